# Optimizing a Trainium2 kernel written in Bass

```python
import math
import jax
import jax.numpy as jnp
from jax import lax
import numpy as np

D_MODEL = 2048
BATCH = 4
SEQ = 2048
DEPTH = 4

GRID_W = 64
CTX_LEN = 256
D_HEAD = 128
ATTN_Q_HEADS = 8
ATTN_KV_HEADS = 2
ATTN_GROUP = ATTN_Q_HEADS // ATTN_KV_HEADS
WINDOW = 128
ATTN_BLOCK = 128
GDN_HEADS = 8
GDN_DK = 128
GDN_DV = 128
GDN_CHUNK = 64
CONV_W = 5
D_FF = 4 * D_MODEL
ROPE_BASE = 10000.0
AXIS_ROT = D_HEAD // 2
EPS = 1e-6
NEG_INF = -1e30

ATTN_Q = ATTN_Q_HEADS * D_HEAD
ATTN_KV = ATTN_KV_HEADS * D_HEAD
GDN_QK = GDN_HEADS * GDN_DK
GDN_V = GDN_HEADS * GDN_DV
GDN_QKV = 2 * GDN_QK + GDN_V
GDN_AB = 4 * GDN_HEADS
IN_OFFSETS = (ATTN_Q, ATTN_Q + ATTN_KV, ATTN_Q + 2 * ATTN_KV, ATTN_Q + 2 * ATTN_KV + GDN_QKV,
              ATTN_Q + 2 * ATTN_KV + GDN_QKV + GDN_V)
D_IN = ATTN_Q + 2 * ATTN_KV + GDN_QKV + GDN_V + GDN_AB
MIX_OUT = ATTN_Q + GDN_V

kernel_name = "hymba_swa_gdn_sqrelu_prefix_dit"


def rmsnorm(x, gain):
    xf = x.astype(jnp.float32)
    y = xf * lax.rsqrt(jnp.mean(xf * xf, axis=-1, keepdims=True) + EPS)
    return (y * gain.astype(jnp.float32)).astype(x.dtype)


def modulate(h, shift, scale):
    return h * (1 + scale) + shift


def axial_rope_tables(n_tokens, dtype):
    rows = n_tokens // GRID_W
    row = jnp.repeat(jnp.arange(rows, dtype=jnp.float32), GRID_W)
    col = jnp.tile(jnp.arange(GRID_W, dtype=jnp.float32), rows)
    inv_freq = ROPE_BASE ** (-jnp.arange(0, AXIS_ROT, 2, dtype=jnp.float32) / AXIS_ROT)
    ang = jnp.concatenate([row[:, None] * inv_freq, col[:, None] * inv_freq], axis=-1)
    return jnp.cos(ang).astype(dtype), jnp.sin(ang).astype(dtype)


def axial_rope(x, cos, sin):
    half = AXIS_ROT // 2

    def rot(t, c, s):
        t1, t2 = t[..., :half], t[..., half:]
        return jnp.concatenate([t1 * c - t2 * s, t2 * c + t1 * s], axis=-1)

    cr, cc = cos[None, :, None, :half], cos[None, :, None, half:]
    sr, sc = sin[None, :, None, :half], sin[None, :, None, half:]
    return jnp.concatenate([rot(x[..., :AXIS_ROT], cr, sr), rot(x[..., AXIS_ROT:], cc, sc)], axis=-1)


def windowed_attention(q, k, v, kc, vc, sink):
    B, L = q.shape[:2]
    LC = kc.shape[1]
    nb = L // ATTN_BLOCK
    scale = D_HEAD ** -0.5
    qb = q.reshape(B, nb, ATTN_BLOCK, ATTN_KV_HEADS, ATTN_GROUP, D_HEAD)

    def band(t):
        tp = jnp.pad(t, ((0, 0), (ATTN_BLOCK, ATTN_BLOCK), (0, 0), (0, 0)))
        parts = [tp[:, s * ATTN_BLOCK: s * ATTN_BLOCK + L].reshape(B, nb, ATTN_BLOCK, ATTN_KV_HEADS, D_HEAD)
                 for s in range(3)]
        return jnp.concatenate(parts, axis=2)

    kb, vb = band(k), band(v)
    qi = jnp.arange(ATTN_BLOCK)
    kj = jnp.arange(3 * ATTN_BLOCK) - ATTN_BLOCK
    rel = kj[None, :] - qi[:, None]
    key_pos = jnp.arange(nb)[:, None] * ATTN_BLOCK + kj[None, :]
    allowed = (jnp.abs(rel) <= WINDOW)[None] & ((key_pos >= 0) & (key_pos < L))[:, None, :]
    s_loc = jnp.einsum('bnqhgd,bnkhd->bnhgqk', qb, kb).astype(jnp.float32) * scale
    s_loc = jnp.where(allowed[None, :, None, None], s_loc, NEG_INF)
    s_ctx = jnp.einsum('bnqhgd,bchd->bnhgqc', qb, kc).astype(jnp.float32) * scale
    s_sink = jnp.broadcast_to(sink.astype(jnp.float32).reshape(ATTN_KV_HEADS, ATTN_GROUP, 1, 1),
                              s_ctx.shape[:-1] + (1,))
    p = jax.nn.softmax(jnp.concatenate([s_loc, s_ctx, s_sink], axis=-1), axis=-1).astype(q.dtype)
    nk = 3 * ATTN_BLOCK
    o = (jnp.einsum('bnhgqk,bnkhd->bnqhgd', p[..., :nk], vb)
         + jnp.einsum('bnhgqc,bchd->bnqhgd', p[..., nk:nk + LC], vc))
    return o.reshape(B, L, ATTN_Q)


def context_attention(q, k, v, sink):
    B, LC = q.shape[:2]
    qg = q.reshape(B, LC, ATTN_KV_HEADS, ATTN_GROUP, D_HEAD)
    s = jnp.einsum('bqhgd,bkhd->bhgqk', qg, k).astype(jnp.float32) * D_HEAD ** -0.5
    s_sink = jnp.broadcast_to(sink.astype(jnp.float32).reshape(1, ATTN_KV_HEADS, ATTN_GROUP, 1, 1),
                              s.shape[:-1] + (1,))
    p = jax.nn.softmax(jnp.concatenate([s, s_sink], axis=-1), axis=-1).astype(q.dtype)
    o = jnp.einsum('bhgqk,bkhd->bqhgd', p[..., :LC], v)
    return o.reshape(B, LC, ATTN_Q)


def short_conv(x, w):
    n = x.shape[1]
    pad = CONV_W // 2
    xp = jnp.pad(x, ((0, 0), (pad, pad), (0, 0)))
    y = xp[:, 0:n] * w[0]
    for j in range(1, CONV_W):
        y = y + xp[:, j:j + n] * w[j]
    return jax.nn.silu(y)


def l2norm(t):
    return t * lax.rsqrt(jnp.sum(t * t, axis=-1, keepdims=True) + EPS)


def gdn_chunk(q, k, v, g, beta, s0):
    B, L, H, DK = q.shape
    DV = v.shape[-1]
    C = GDN_CHUNK
    N = L // C

    def chunks(t):
        return jnp.moveaxis(t.reshape((B, N, C, H) + t.shape[3:]), 3, 1)

    q, k, v, g, beta = chunks(q), chunks(k), chunks(v), chunks(g), chunks(beta)
    gc = jnp.cumsum(g, axis=-1)
    idx = jnp.arange(C)
    lower = idx[:, None] >= idx[None, :]
    strict = idx[:, None] > idx[None, :]
    diff = gc[..., :, None] - gc[..., None, :]
    decay = jnp.where(lower, jnp.exp(jnp.where(lower, diff, 0.0)), 0.0)
    kb = k * beta[..., None]
    vb = v * beta[..., None]
    a = jnp.where(strict, jnp.einsum('bhnid,bhnjd->bhnij', kb, k) * decay, 0.0)
    tmat = a + jnp.eye(C, dtype=a.dtype)
    rhs = jnp.concatenate([vb, kb * jnp.exp(gc)[..., None]], axis=-1)
    sol = lax.linalg.triangular_solve(tmat, rhs, left_side=True, lower=True, unit_diagonal=True)
    u, w = sol[..., :DV], sol[..., DV:]
    qk = jnp.einsum('bhnid,bhnjd->bhnij', q, k) * decay
    q_dec = q * jnp.exp(gc)[..., None]
    k_dec = k * jnp.exp(gc[..., -1:] - gc)[..., None]
    g_last = jnp.exp(gc[..., -1])
    xs = tuple(jnp.moveaxis(t, 2, 0) for t in (q_dec, k_dec, u, w, qk, g_last))

    def step(s, inp):
        qd, kd, ui, wi, qki, gl = inp
        v_new = ui - jnp.einsum('bhcd,bhde->bhce', wi, s)
        o = jnp.einsum('bhcd,bhde->bhce', qd, s) + jnp.einsum('bhij,bhje->bhie', qki, v_new)
        s = s * gl[..., None, None] + jnp.einsum('bhcd,bhce->bhde', kd, v_new)
        return s, o

    s_fin, o = lax.scan(step, s0, xs)
    o = jnp.moveaxis(jnp.moveaxis(o, 0, 2), 1, 3).reshape(B, L, H, DV)
    return o, s_fin


def gated_deltanet(qkv_c, qkv_l, ab_c, ab_l, a_log, dt_bias):
    def prep(qkv, ab):
        B, T = qkv.shape[:2]
        qkv = qkv.astype(jnp.float32)
        ab = ab.astype(jnp.float32).reshape(B, T, 2, 2, GDN_HEADS)
        q, k, v = jnp.split(qkv, [GDN_QK, 2 * GDN_QK], axis=-1)
        q = l2norm(q.reshape(B, T, GDN_HEADS, GDN_DK)) * GDN_DK ** -0.5
        k = l2norm(k.reshape(B, T, GDN_HEADS, GDN_DK))
        v = v.reshape(B, T, GDN_HEADS, GDN_DV)
        g = -jnp.exp(a_log.astype(jnp.float32)) * jax.nn.softplus(ab[:, :, :, 0] + dt_bias.astype(jnp.float32))
        beta = jax.nn.sigmoid(ab[:, :, :, 1])
        return q, k, v, g, beta

    pc, pl = prep(qkv_c, ab_c), prep(qkv_l, ab_l)
    B = qkv_c.shape[0]
    s0 = jnp.zeros((B, GDN_HEADS, GDN_DK, GDN_DV), jnp.float32)
    outs_c, outs_l = [], []
    for d in range(2):
        def sel(p):
            q, k, v, g, beta = p
            t = (q, k, v, g[:, :, d], beta[:, :, d])
            return tuple(jnp.flip(a, axis=1) for a in t) if d == 1 else t
        oc, sc = gdn_chunk(*sel(pc), s0)
        ol, _ = gdn_chunk(*sel(pl), sc)
        if d == 1:
            oc, ol = jnp.flip(oc, axis=1), jnp.flip(ol, axis=1)
        outs_c.append(oc)
        outs_l.append(ol)
    return outs_c[0] + outs_c[1], outs_l[0] + outs_l[1]


def gated_rmsnorm(o, gate, gain):
    B, T = o.shape[:2]
    y = o * lax.rsqrt(jnp.mean(o * o, axis=-1, keepdims=True) + EPS) * gain.astype(jnp.float32)
    y = y * jax.nn.silu(gate.astype(jnp.float32)).reshape(B, T, GDN_HEADS, GDN_DV)
    return y.reshape(B, T, GDN_V).astype(gate.dtype)


def mixer(hc, hl, w_in, conv_w, a_log, dt_bias, gdn_norm, sink, w_out, cos, sin, last):
    B, LC = hc.shape[:2]
    z = jnp.concatenate([hc, hl], axis=1) @ w_in
    T = z.shape[1]
    aq, ak, av, gqkv, ggate, gab = jnp.split(z, IN_OFFSETS, axis=-1)
    aq = aq.reshape(B, T, ATTN_Q_HEADS, D_HEAD)
    ak = ak.reshape(B, T, ATTN_KV_HEADS, D_HEAD)
    av = av.reshape(B, T, ATTN_KV_HEADS, D_HEAD)
    q_l = axial_rope(aq[:, LC:], cos, sin)
    k_l = axial_rope(ak[:, LC:], cos, sin)
    o_attn_l = windowed_attention(q_l, k_l, av[:, LC:], ak[:, :LC], av[:, :LC], sink)
    qkv_c = short_conv(gqkv[:, :LC], conv_w)
    qkv_l = short_conv(gqkv[:, LC:], conv_w)
    o_gdn_c, o_gdn_l = gated_deltanet(qkv_c, qkv_l, gab[:, :LC], gab[:, LC:], a_log, dt_bias)
    o_gdn_l = gated_rmsnorm(o_gdn_l, ggate[:, LC:], gdn_norm)
    out_l = jnp.concatenate([o_attn_l, o_gdn_l], axis=-1) @ w_out
    if last:
        return None, out_l
    o_attn_c = context_attention(aq[:, :LC], ak[:, :LC], av[:, :LC], sink)
    o_gdn_c = gated_rmsnorm(o_gdn_c, ggate[:, :LC], gdn_norm)
    out_c = jnp.concatenate([o_attn_c, o_gdn_c], axis=-1) @ w_out
    return out_c, out_l


def squared_relu_mlp(h, w1, w2):
    return jnp.square(jax.nn.relu(h @ w1)) @ w2


def setup_inputs(seed: int = 0) -> dict:
    key = jax.random.key(seed)
    ks = jax.random.split(key, 18)
    f32 = jnp.float32

    def nrm(k, shape, scale):
        return jax.random.normal(k, shape, f32) * scale

    dt = jnp.exp(jax.random.uniform(ks[9], (DEPTH, 2, GDN_HEADS), f32, math.log(1e-3), math.log(1e-1)))
    return {
        "x": nrm(ks[0], (BATCH, SEQ, D_MODEL), 1.0),
        "c": nrm(ks[1], (BATCH, D_MODEL), 1.0),
        "ctx": nrm(ks[2], (BATCH, CTX_LEN, D_MODEL), 1.0),
        "c_ctx": nrm(ks[3], (D_MODEL,), 1.0),
        "w_ada": nrm(ks[4], (DEPTH, D_MODEL, 6 * D_MODEL), D_MODEL ** -0.5),
        "b_ada": nrm(ks[5], (DEPTH, 6 * D_MODEL), 0.02),
        "norm_mix": 1.0 + nrm(ks[6], (DEPTH, D_MODEL), 0.02),
        "w_in": nrm(ks[7], (DEPTH, D_MODEL, D_IN), D_MODEL ** -0.5),
        "conv_w": nrm(ks[8], (DEPTH, CONV_W, GDN_QKV), CONV_W ** -0.5),
        "a_log": jnp.log(jax.random.uniform(ks[10], (DEPTH, 2, GDN_HEADS), f32, 1.0, 16.0)),
        "dt_bias": dt + jnp.log(-jnp.expm1(-dt)),
        "gdn_norm": 1.0 + nrm(ks[11], (DEPTH, GDN_DV), 0.02),
        "attn_sink": nrm(ks[12], (DEPTH, ATTN_Q_HEADS), 0.5),
        "w_out": nrm(ks[13], (DEPTH, MIX_OUT, D_MODEL), MIX_OUT ** -0.5),
        "norm_ffn": 1.0 + nrm(ks[14], (DEPTH, D_MODEL), 0.02),
        "w_ff1": nrm(ks[15], (DEPTH, D_MODEL, D_FF), D_MODEL ** -0.5),
        "w_ff2": nrm(ks[16], (DEPTH, D_FF, D_MODEL), D_FF ** -0.5),
        "norm_final": 1.0 + nrm(ks[17], (D_MODEL,), 0.02),
    }


def reference(x, c, ctx, c_ctx, w_ada, b_ada, norm_mix, w_in, conv_w, a_log, dt_bias, gdn_norm,
              attn_sink, w_out, norm_ffn, w_ff1, w_ff2, norm_final):
    L = x.shape[1]
    cos, sin = axial_rope_tables(L, x.dtype)
    silu_c = jax.nn.silu(c)
    silu_cc = jax.nn.silu(c_ctx)
    xl, xc = x, ctx
    for i in range(DEPTH):
        last = i == DEPTH - 1
        mod_l = (silu_c @ w_ada[i] + b_ada[i])[:, None, :]
        mod_c = silu_cc @ w_ada[i] + b_ada[i]
        shm_l, scm_l, gm_l, shf_l, scf_l, gf_l = jnp.split(mod_l, 6, axis=-1)
        shm_c, scm_c, gm_c, shf_c, scf_c, gf_c = jnp.split(mod_c, 6, axis=-1)
        hl = modulate(rmsnorm(xl, norm_mix[i]), shm_l, scm_l)
        hc = modulate(rmsnorm(xc, norm_mix[i]), shm_c, scm_c)
        out_c, out_l = mixer(hc, hl, w_in[i], conv_w[i], a_log[i], dt_bias[i], gdn_norm[i],
                             attn_sink[i], w_out[i], cos, sin, last)
        xl = xl + gm_l * out_l
        xl = xl + gf_l * squared_relu_mlp(modulate(rmsnorm(xl, norm_ffn[i]), shf_l, scf_l), w_ff1[i], w_ff2[i])
        if not last:
            xc = xc + gm_c * out_c
            xc = xc + gf_c * squared_relu_mlp(modulate(rmsnorm(xc, norm_ffn[i]), shf_c, scf_c),
                                              w_ff1[i], w_ff2[i])
    return rmsnorm(xl, norm_final)
```

```python
import math
from contextlib import ExitStack
import numpy as np
import concourse.bass as bass
import concourse.mybir as mybir
from concourse.bass_utils import run_bass_kernel_spmd

F32 = mybir.dt.float32
BF16 = mybir.dt.bfloat16
ALU = mybir.AluOpType
AF = mybir.ActivationFunctionType

D = 2048
KC = 16
L = 2048
LC = 256
T = L + LC
NT = T // 128
DEPTH = 4
D_IN = 5664
D_FF = 8192
EPS = 1e-6
TBLK = [(0, 256), (256, 512), (768, 512), (1280, 512), (1792, 512)]
NCORES = 8


class Op:
    __slots__ = ("e", "fn", "deps", "idx", "signal", "dma", "dsem", "dval", "prewait")

    def __init__(self, e, fn, deps, idx, dma=False):
        self.e, self.fn, self.deps, self.idx, self.dma = e, fn, deps, idx, dma
        self.signal = False
        self.dsem = None
        self.dval = 0
        self.prewait = None


class Sch:
    ENG = ("pe", "act", "dve", "pool", "sp")
    NDS = 12

    def __init__(self, nc, es):
        self.nc = nc
        self.eng = {"pe": nc.tensor, "act": nc.scalar, "dve": nc.vector, "pool": nc.gpsimd, "sp": nc.sync}
        self.sem = {e: es.enter_context(nc.semaphore("sem_" + e)) for e in self.ENG}
        self.cnt = {e: 0 for e in self.ENG}
        self.dsem = {q: [es.enter_context(nc.semaphore("dsem_%s_%d" % (q, i))) for i in range(self.NDS)]
                     for q in ("sp", "pool", "act")}
        self.dcnt = {q: 0 for q in ("sp", "pool", "act")}
        self.reset()

    def reset(self):
        self.ops = {e: [] for e in self.ENG}
        self.res = {}

    def _mk(self, e, fn, reads, writes, dma):
        deps = set()
        for r in reads:
            st = self.res.get(r)
            if st is not None and st[0] is not None:
                deps.add(st[0])
        for w in writes:
            st = self.res.get(w)
            if st is not None:
                if st[0] is not None:
                    deps.add(st[0])
                deps.update(st[1].values())
        if e == "pe" and not dma:
            deps = {d for d in deps if d.dma or d.e != "pe"}
        o = Op(e, fn, deps, len(self.ops[e]), dma)
        self.ops[e].append(o)
        for d in deps:
            d.signal = True
        for r in reads:
            st = self.res.setdefault(r, [None, {}])
            st[1][e + ("d" if dma else "")] = o
        for w in writes:
            self.res[w] = [o, {}]
        return o

    limit = None
    nrec = 0

    def op(self, e, fn, reads=(), writes=()):
        if self.limit is not None:
            self.nrec += 1
            if self.nrec > self.limit:
                return None
        return self._mk(e, fn, reads, writes, False)

    def dma(self, q, fn, reads=(), writes=()):
        if self.limit is not None:
            self.nrec += 1
            if self.nrec > self.limit:
                return None
        o = self._mk(q, fn, reads, writes, True)
        m = self.dcnt[q]
        self.dcnt[q] += 1
        o.dsem = self.dsem[q][m % self.NDS]
        o.dval = 16 * (m // self.NDS + 1)
        o.prewait = 16 * (m // self.NDS)
        return o

    def emit(self, block, final_wait=True):
        val = {}
        for e in self.ENG:
            c = self.cnt[e]
            pend = []
            for o in self.ops[e]:
                if o.dma:
                    continue
                pend.append(o)
                if o.signal:
                    c += 1
                    for p in pend:
                        val[p] = c
                    pend = []
            self.cnt[e] = c
        dtot = {}

        def run(e):
            def body(eng):
                waited = {}

                def w(sem, v):
                    k = id(sem)
                    if waited.get(k, -1) >= v:
                        return
                    waited[k] = v
                    eng.wait_ge(sem, v)
                for o in self.ops[e]:
                    for d in o.deps:
                        if d.dma:
                            w(d.dsem, d.dval)
                        else:
                            w(self.sem[d.e], val[d])
                    if o.dma:
                        if o.prewait:
                            w(o.dsem, o.prewait)
                        ins = o.fn(eng)
                        ins.then_inc(o.dsem, 16)
                        dtot[id(o.dsem)] = (o.dsem, o.dval)
                    else:
                        ins = o.fn(eng)
                        if o.signal:
                            ins.then_inc(self.sem[e], 1)
                if e == "sp" and final_wait:
                    pass
            return body

        for e in ("pe", "act", "dve", "pool"):
            if self.ops[e]:
                getattr(block, {"pe": "tensor", "act": "scalar", "dve": "vector", "pool": "gpsimd"}[e])(run(e))
        spbody = run("sp")

        def spfull(eng):
            spbody(eng)
            for q in ("pool", "act"):
                for o in self.ops[q]:
                    if o.dma:
                        dtot[id(o.dsem)] = (o.dsem, max(o.dval, dtot.get(id(o.dsem), (None, 0))[1]))
            for sem, v in dtot.values():
                eng.wait_ge(sem, v)
        block.sync(spfull)
        self.reset()


class Prog:
    def __init__(self, layers, upto="all", dbg=()):
        self.layers = layers
        self.upto = upto
        self.dbg = dbg
        self.nc = bass.Bass("TRN2", target_bir_lowering=False)
        self.es = ExitStack()
        self.uid = 0

    def din(self, name, shape, dt=F32):
        return self.nc.dram_tensor(name, list(shape), dt, kind="ExternalInput").ap()

    def dout(self, name, shape, dt=F32):
        return self.nc.dram_tensor(name, list(shape), dt, kind="ExternalOutput").ap()

    def dscr(self, name, shape, dt=F32):
        return self.nc.dram_tensor(name, list(shape), dt).ap()

    def sb(self, es, name, shape, dt):
        self.uid += 1
        return es.enter_context(self.nc.sbuf_tensor("%s_%d" % (name, self.uid), list(shape), dt))

    def ps(self, es, name, shape, dt=F32):
        self.uid += 1
        return es.enter_context(self.nc.psum_tensor("%s_%d" % (name, self.uid), list(shape), dt))

    def build(self):
        nc = self.nc
        es = self.es
        P = self
        self.xT_in = P.din("xT", [D, T])
        self.c2 = P.din("c2", [128, KC, 2])
        self.consts = P.din("consts", [128, 21, 128])
        self.ropeT = P.din("ropeT", [128, 2, L])
        self.w_ada = {l: P.din("w_ada%d" % l, [D, 6 * D]) for l in self.layers}
        self.b_ada = {l: P.din("b_ada%d" % l, [128, 96]) for l in self.layers}
        self.nrm = {l: P.din("nrm%d" % l, [128, 2, KC]) for l in self.layers}
        self.w_in = {l: P.din("w_in%d" % l, [D, D_IN]) for l in self.layers}
        self.w_out = {l: P.din("w_out%d" % l, [D, D]) for l in self.layers}
        self.w_ff1 = {l: P.din("w_ff1%d" % l, [D, D_FF]) for l in self.layers}
        self.w_ff2 = {l: P.din("w_ff2%d" % l, [D_FF, D]) for l in self.layers}
        self.nfin = P.din("nfin", [128, KC])
        self.outT = P.dout("outT", [D, L])
        self.xT = P.dscr("xTs", [D, T])
        self.zqkT = P.dscr("zqkT", [1280, T], BF16)
        self.zv = P.dscr("zv", [T, 256], BF16)
        self.zg = P.dscr("zg", [3072, T])
        self.zgate = P.dscr("zgate", [1024, T], BF16)
        self.zab = P.dscr("zab", [T, 32])
        self.mixT = P.dscr("mixT", [D, T], BF16)
        self.aT = P.dscr("aT", [D_FF, T], BF16)
        self.dbgout = {}
        for name, shape, dt in self.dbg:
            self.dbgout[name] = P.dout("dbg_" + name, shape, dt)

        self.S = Sch(nc, es)
        self.cst = P.sb(es, "cst", [128, 21, 128], F32)
        self.cstb = P.sb(es, "cstb", [128, 21, 128], BF16)
        self.mod = {l: P.sb(es, "mod%d" % l, [128, 96, 2], F32) for l in self.layers}
        self.gm1 = {l: P.sb(es, "gm1_%d" % l, [128, 2, KC, 2], F32) for l in self.layers}

        self.phase_init()
        for l in self.layers:
            self.phase_mod(l)
        for l in self.layers:
            self.phase_norm_inproj(l)
            if self.upto == "inproj":
                break
            if not getattr(self, "skip_tail", False):
                self.phase_dense_tail(l)
        if self.upto == "all":
            self.phase_final()
        return nc

    def phase_init(self):
        nc, S = self.nc, self.S
        with ExitStack() as es, nc.Block() as block:
            S.dma("sp", lambda e: e.dma_start(out=self.cst[:], in_=self.consts[:, :, :]), writes=["cst"])
            S.op("dve", lambda e: e.tensor_copy(out=self.cstb[:], in_=self.cst[:]), reads=["cst"], writes=["cstb"])
            S.op("pool", lambda e: e.memset(self.epsc[:], EPS), writes=["epsc"])
            for i in range(4):
                S.dma("sp", lambda e, i=i: e.dma_start(out=self.xT[i * 512:(i + 1) * 512, :],
                                                      in_=self.xT_in[i * 512:(i + 1) * 512, :]))
            S.emit(block)

    def phase_mod(self, l):
        nc, S = self.nc, self.S
        with ExitStack() as es, nc.Block() as block:
            c2 = self.sb(es, "c2", [128, KC, 2], F32)
            s2 = self.sb(es, "s2", [128, KC, 2], BF16)
            bt = self.sb(es, "bt", [128, 96], F32)
            nr = self.sb(es, "nr", [128, 2, KC], F32)
            NS = 3
            wsl = [self.sb(es, "wad", [128, KC, 512], BF16) for _ in range(NS)]
            pm = self.ps(es, "pm", [128, 96, 2], F32)
            S.dma("sp", lambda e: e.dma_start(out=c2[:], in_=self.c2[:, :, :]), writes=["c2"])
            S.dma("sp", lambda e: e.dma_start(out=bt[:], in_=self.b_ada[l][:, :]), writes=["bt"])
            S.dma("sp", lambda e: e.dma_start(out=nr[:], in_=self.nrm[l][:, :, :]), writes=["nr"])
            S.op("act", lambda e: e.activation(out=s2[:], in_=c2[:], func=AF.Silu), reads=["c2"], writes=["s2"])
            wv = self.w_ada[l].rearrange("(kc p) n -> p kc n", p=128)
            for s in range(24):
                buf = wsl[s % NS]
                key = "wad%d" % (s % NS)
                S.dma("pool", lambda e, s=s, buf=buf: e.dma_start(out=buf[:], in_=wv[:, :, s * 512:(s + 1) * 512]),
                      writes=[key])
                for j in range(4):
                    cc = s * 4 + j
                    for kc in range(KC):
                        S.op("pe", lambda e, cc=cc, kc=kc, j=j, buf=buf: e.matmul(
                            pm[:, cc, :], buf[:, kc, j * 128:(j + 1) * 128], s2[:, kc, :],
                            start=(kc == 0), stop=(kc == KC - 1)), reads=[key, "s2"], writes=["pm"])
            mod = self.mod[l]
            S.op("dve", lambda e: e.tensor_tensor(out=mod[:], in0=pm[:], in1=bt[:].unsqueeze(2).to_broadcast([128, 96, 2]),
                                                  op=ALU.add), reads=["pm", "bt"], writes=["mod"])
            gm = self.gm1[l]
            for n, which in ((0, 1), (1, 4)):
                S.op("dve", lambda e, n=n, which=which: e.tensor_scalar(
                    out=gm[:, n, :, :], in0=mod[:, which * 16:(which + 1) * 16, :], scalar1=1.0, scalar2=None,
                    op0=ALU.add), reads=["mod"], writes=["gm%d" % n])
                S.op("dve", lambda e, n=n: e.tensor_tensor(
                    out=gm[:, n, :, :], in0=gm[:, n, :, :], in1=nr[:, n, :].unsqueeze(2).to_broadcast([128, KC, 2]),
                    op=ALU.mult), reads=["gm%d" % n, "nr"], writes=["gm%d" % n])
            S.emit(block)

    def norm_to_hT(self, es, l, n, hT, shift_which):
        nc, S = self.nc, self.S
        xb = [self.sb(es, "xb", [128, KC, 256], F32) for _ in range(2)]
        sq = [self.sb(es, "sq", [128, 256], F32) for _ in range(2)]
        rs = [self.sb(es, "rs", [128, 256], F32) for _ in range(2)]
        tmp = [self.sb(es, "tmp", [128, 256], F32) for _ in range(3)]
        pst = [self.ps(es, "pst", [128, 256], F32) for _ in range(2)]
        xv = self.xT.rearrange("(c p) t -> p c t", p=128)
        gm, mod = self.gm1[l], self.mod[l]
        ones = self.cst[:, 1, :]
        for b in range(T // 256):
            j = 1 if b == 0 else 0
            X, Q, R, PS = xb[b % 2], sq[b % 2], rs[b % 2], pst[b % 2]
            kx, kq, kr, kp = "xb%d" % (b % 2), "sq%d" % (b % 2), "rs%d" % (b % 2), "pst%d" % (b % 2)
            S.dma("sp", lambda e, X=X, b=b: e.dma_start(out=X[:], in_=xv[:, :, b * 256:(b + 1) * 256]), writes=[kx])
            for c in range(KC):
                S.op("act", lambda e, X=X, Q=Q, c=c: e.activation(out=Q[:], in_=X[:, c, :], func=AF.Square),
                     reads=[kx], writes=[kq])
                S.op("pe", lambda e, Q=Q, PS=PS, c=c: e.matmul(PS[:], ones, Q[:], start=(c == 0), stop=(c == KC - 1)),
                     reads=[kq, "cst"], writes=[kp])
            S.op("act", lambda e, R=R, PS=PS: e.activation(out=R[:], in_=PS[:], func=AF.Sqrt, scale=1.0 / D, bias=self.epsc[:, 0:1]),
                 reads=[kp], writes=[kr])
            S.op("dve", lambda e, R=R: e.reciprocal(out=R[:], in_=R[:]), reads=[kr], writes=[kr])
            for c in range(KC):
                tb = tmp[c % 3]
                kt = "tmp%d" % (c % 3)
                S.op("dve", lambda e, X=X, R=R, tb=tb, c=c: e.tensor_tensor(out=tb[:], in0=X[:, c, :], in1=R[:],
                                                                          op=ALU.mult), reads=[kx, kr], writes=[kt])
                S.op("act", lambda e, tb=tb, c=c, b=b, j=j: e.activation(
                    out=hT[:, c, b * 256:(b + 1) * 256], in_=tb[:], func=AF.Identity,
                    scale=gm[:, n, c, j:j + 1], bias=mod[:, shift_which * 16 + c, j:j + 1]),
                    reads=[kt, "gm%d" % n, "mod"], writes=["hT"])

    def gemm(self, es, W, ncols, act, actkey, epi, nkc=KC, wname="w", tblk=TBLK):
        S = self.S
        NS = 3
        if not hasattr(self, "_gc"):
            self._gc = {}
        k1, k2 = (id(es), wname, nkc), (id(es), "ps")
        if k1 not in self._gc:
            self._gc[k1] = [self.sb(es, wname, [128, nkc, 256], BF16) for _ in range(NS)]
        if k2 not in self._gc:
            self._gc[k2] = [self.ps(es, "pg", [128, 512], F32) for _ in range(4)]
        wsl, psb = self._gc[k1], self._gc[k2]
        Wv = W.rearrange("(kc p) n -> p kc n", p=128)
        nslab = (ncols + 255) // 256
        pi = 0
        KG = 4
        for s in range(nslab):
            c0 = s * 256
            cw = min(256, ncols - c0)
            buf = wsl[s % NS]
            for kg in range(nkc // KG):
                S.dma("pool", lambda e, buf=buf, kg=kg, c0=c0, cw=cw: e.dma_start(
                    out=buf[:, kg * KG:(kg + 1) * KG, 0:cw], in_=Wv[:, kg * KG:(kg + 1) * KG, c0:c0 + cw]),
                    writes=["%s%d_%d" % (wname, s % NS, kg)])
            for g in range((cw + 127) // 128):
                m = min(128, cw - g * 128)
                for bi, (t0, tn) in enumerate(tblk):
                    PS = psb[pi % 4]
                    pk = "pg%d" % (pi % 4)
                    pi += 1
                    for kc in range(nkc):
                        S.op("pe", lambda e, PS=PS, buf=buf, kc=kc, g=g, m=m, t0=t0, tn=tn: e.matmul(
                            PS[0:m, 0:tn], buf[:, kc, g * 128:g * 128 + m], act[:, kc, t0:t0 + tn],
                            start=(kc == 0), stop=(kc == nkc - 1)),
                            reads=["%s%d_%d" % (wname, s % NS, kc // KG), actkey], writes=[pk])
                    epi(c0 + g * 128, m, bi, t0, tn, PS, pk)

    def phase_norm_inproj(self, l):
        nc, S = self.nc, self.S
        with ExitStack() as es:
            hT = self.sb(es, "hT", [128, KC, T], BF16)
            with ExitStack() as es2, nc.Block() as block:
                self.norm_to_hT(es2, l, 0, hT, 0)
                S.emit(block)
            with ExitStack() as es2, nc.Block() as block:
                stg = [self.sb(es2, "stg", [128, 512], F32) for _ in range(4)]
                cnt = [0]

                def epi(c0, m, bi, t0, tn, PS, pk):
                    i = cnt[0] % 4
                    cnt[0] += 1
                    st = stg[i]
                    S.op("act", lambda e: e.activation(out=st[0:m, 0:tn], in_=PS[0:m, 0:tn], func=AF.Copy),
                         reads=[pk], writes=["stg%d" % i])
                    S.dma("sp", lambda e: e.dma_start(out=self.zT[c0:c0 + m, t0:t0 + tn], in_=st[0:m, 0:tn]),
                          reads=["stg%d" % i])
                self.gemm(es2, self.w_in[l], D_IN, hT, "hT", epi, wname="win")
                S.emit(block)

    def resid_epi(self, es, l, which):
        S = self.S
        NXL = 6
        xl = [self.sb(es, "xl", [128, 512], F32) for _ in range(NXL)]
        cnt = [0]
        mod = self.mod[l]

        def epi(c0, m, bi, t0, tn, PS, pk):
            i = cnt[0] % NXL
            cnt[0] += 1
            xb = xl[i]
            j = 1 if t0 < LC else 0
            c = c0 // 128
            xk = "x_%d_%d" % (c, bi)
            S.dma("act", lambda e: e.dma_start(out=xb[:, 0:tn], in_=self.xT[c0:c0 + 128, t0:t0 + tn]),
                  reads=[xk], writes=["xl%d" % i])
            S.op("dve", lambda e: e.scalar_tensor_tensor(
                out=xb[:, 0:tn], in0=PS[:, 0:tn], scalar=mod[:, which * 16 + c, j:j + 1], in1=xb[:, 0:tn],
                op0=ALU.mult, op1=ALU.add), reads=[pk, "xl%d" % i, "mod"], writes=["xl%d" % i])
            S.dma("sp", lambda e: e.dma_start(out=self.xT[c0:c0 + 128, t0:t0 + tn], in_=xb[:, 0:tn]),
                  reads=["xl%d" % i], writes=[xk])
        return epi

    def phase_dense_tail(self, l):
        nc, S = self.nc, self.S
        with ExitStack() as es, nc.Block() as block:
            mx = self.sb(es, "mx", [128, KC, T], BF16)
            mv = self.mixT.rearrange("(c p) t -> p c t", p=128)
            for c4 in range(4):
                S.dma("sp", lambda e, c4=c4: e.dma_start(out=mx[:, c4 * 4:(c4 + 1) * 4, :], in_=mv[:, c4 * 4:(c4 + 1) * 4, :]),
                      writes=["mx%d" % c4])
            S.op("pool", lambda e: e.memset(self.junk[:], 0.0), reads=["mx0", "mx1", "mx2", "mx3"], writes=["mx"])
            self.gemm(es, self.w_out[l], D, mx, "mx", self.resid_epi(es, l, 2), wname="wout")
            S.emit(block)
        with ExitStack() as es:
            hT = self.sb(es, "hnT", [128, KC, T], BF16)
            with ExitStack() as es2, nc.Block() as block:
                self.norm_to_hT(es2, l, 1, hT, 3)
                S.emit(block)
            with ExitStack() as es2, nc.Block() as block:
                NK2 = 8
                aT = self.sb(es2, "aT", [128, NK2, T], BF16)
                rl = [self.sb(es2, "rl", [128, 512], F32) for _ in range(3)]
                rc = [0]
                repi = self.resid_epi(es2, l, 5)
                for q in range(D_FF // (NK2 * 128)):
                    def epi1(c0, m, bi, t0, tn, PS, pk):
                        i = rc[0] % 3
                        rc[0] += 1
                        r = rl[i]
                        S.op("act", lambda e: e.activation(out=r[:, 0:tn], in_=PS[:, 0:tn], func=AF.Relu),
                             reads=[pk], writes=["rl%d" % i])
                        S.op("pool", lambda e: e.tensor_tensor(out=aT[:, c0 // 128, t0:t0 + tn], in0=r[:, 0:tn],
                                                               in1=r[:, 0:tn], op=ALU.mult),
                             reads=["rl%d" % i], writes=["aT"])
                    self.gemm(es2, self.w_ff1[l][:, q * NK2 * 128:(q + 1) * NK2 * 128], NK2 * 128, hT, "hnT", epi1,
                              wname="wf1")
                    self.gemm(es2, self.w_ff2[l][q * NK2 * 128:(q + 1) * NK2 * 128, :], D, aT, "aT", repi, nkc=NK2,
                              wname="wf2")
                S.emit(block)

    def phase_final(self):
        nc, S = self.nc, self.S
        with ExitStack() as es, nc.Block() as block:
            xb = [self.sb(es, "fxb", [128, KC, 256], F32) for _ in range(2)]
            sq = [self.sb(es, "fsq", [128, 256], F32) for _ in range(2)]
            rs = [self.sb(es, "frs", [128, 256], F32) for _ in range(2)]
            ob = [self.sb(es, "fob", [128, KC, 256], F32) for _ in range(2)]
            nf = self.sb(es, "nf", [128, KC], F32)
            pst = [self.ps(es, "fps", [128, 256], F32) for _ in range(2)]
            xv = self.xT.rearrange("(c p) t -> p c t", p=128)
            ov = self.outT.rearrange("(c p) t -> p c t", p=128)
            ones = self.cst[:, 1, :]
            S.dma("sp", lambda e: e.dma_start(out=nf[:], in_=self.nfin[:, :]), writes=["nf"])
            for b in range(L // 256):
                X, Q, R, PS, O = xb[b % 2], sq[b % 2], rs[b % 2], pst[b % 2], ob[b % 2]
                kx, kq, kr, kp, ko = ["%s%d" % (n, b % 2) for n in ("fx", "fq", "fr", "fp", "fo")]
                S.dma("sp", lambda e, X=X, b=b: e.dma_start(out=X[:], in_=xv[:, :, LC + b * 256:LC + (b + 1) * 256]),
                      writes=[kx])
                for c in range(KC):
                    S.op("act", lambda e, X=X, Q=Q, c=c: e.activation(out=Q[:], in_=X[:, c, :], func=AF.Square),
                         reads=[kx], writes=[kq])
                    S.op("pe", lambda e, Q=Q, PS=PS, c=c: e.matmul(PS[:], ones, Q[:], start=(c == 0), stop=(c == KC - 1)),
                         reads=[kq], writes=[kp])
                S.op("act", lambda e, R=R, PS=PS: e.activation(out=R[:], in_=PS[:], func=AF.Sqrt, scale=1.0 / D,
                                                              bias=self.epsc[:, 0:1]), reads=[kp], writes=[kr])
                S.op("dve", lambda e, R=R: e.reciprocal(out=R[:], in_=R[:]), reads=[kr], writes=[kr])
                for c in range(KC):
                    S.op("dve", lambda e, X=X, R=R, O=O, c=c: e.scalar_tensor_tensor(
                        out=O[:, c, :], in0=X[:, c, :], scalar=nf[:, c:c + 1], in1=R[:], op0=ALU.mult, op1=ALU.mult),
                        reads=[kx, kr, "nf"], writes=[ko])
                S.dma("sp", lambda e, O=O, b=b: e.dma_start(out=ov[:, :, b * 256:(b + 1) * 256], in_=O[:]), reads=[ko])
            S.emit(block)
    def join(self, keys, newkey):
        self.S.op("pool", lambda e: e.memset(self.junk[:], 0.0), reads=list(keys), writes=[newkey])

    def phase_attn(self, l):
        nc, S = self.nc, self.S
        scale = 128 ** -0.5
        cstb = self.cstb
        with ExitStack() as es, nc.Block() as block:
            rope = self.sb(es, "rope", [128, 2, L], F32)
            snk = self.sb(es, "snk", [128, 8], F32)
            esk = self.sb(es, "esk", [128, 8], F32)
            raw = [self.sb(es, "raw", [128, T], F32) for _ in range(2)]
            rb = self.sb(es, "rb", [128, L], BF16)
            t1 = [self.sb(es, "t1", [128, 512], F32) for _ in range(2)]
            t2 = [self.sb(es, "t2", [128, 512], F32) for _ in range(2)]
            kT = self.sb(es, "kT", [128, T], BF16)
            qT = self.sb(es, "qT", [128, T], BF16)
            vT = self.sb(es, "vT", [128, T], BF16)
            V = self.sb(es, "V", [128, NT, 128], BF16)
            PTc = self.sb(es, "PTc", [128, 2, T], BF16)
            PTl = self.sb(es, "PTl", [128, 16, 3, 128], BF16)
            rc = [self.sb(es, "rc", [128, 128], F32) for _ in range(2)]
            ost = [self.sb(es, "ost", [128, 128], BF16) for _ in range(3)]
            psS = [self.ps(es, "psS", [128, 512], F32) for _ in range(2)]
            psO = [self.ps(es, "psO", [128, 128], F32) for _ in range(2)]
            psD = [self.ps(es, "psD", [128, 128], F32) for _ in range(2)]
            psT = [self.ps(es, "psT", [128, 128], BF16) for _ in range(2)]
            S.dma("sp", lambda e: e.dma_start(out=rope[:], in_=self.ropeT[:, :, :]), writes=["rope"])
            S.dma("sp", lambda e: e.dma_start(out=snk[:], in_=self.sink[l][:, :]), writes=["snk"])
            S.op("act", lambda e: e.activation(out=esk[:], in_=snk[:], func=AF.Exp), reads=["snk"], writes=["esk"])
            cn = {"raw": 0, "ps": 0, "t": 0, "o": 0, "ost": 0}

            def load_rope(row0, dst, dkey):
                ri = cn["raw"] % 2
                cn["raw"] += 1
                R, rk = raw[ri], "raw%d" % ri
                S.dma("sp", lambda e: e.dma_start(out=R[:], in_=self.zT[row0:row0 + 128, :]), writes=[rk])
                S.op("act", lambda e: e.activation(out=dst[:, 0:LC], in_=R[:, 0:LC], func=AF.Copy), reads=[rk],
                     writes=[dkey + "c"])
                S.op("act", lambda e: e.activation(out=rb[:], in_=R[:, LC:T], func=AF.Copy), reads=[rk], writes=["rb"])
                for blk in range(4):
                    a, b_ = blk * 512, (blk + 1) * 512
                    pi = cn["ps"] % 2
                    cn["ps"] += 1
                    ti = cn["t"] % 2
                    cn["t"] += 1
                    PS, pk = psS[pi], "psS%d" % pi
                    S.op("pe", lambda e, PS=PS, a=a, b_=b_: e.matmul(PS[:, :], cstb[:, 2, :], rb[:, a:b_], start=True, stop=True),
                         reads=["rb", "cstb"], writes=[pk])
                    S.op("dve", lambda e, a=a, b_=b_, ti=ti: e.tensor_tensor(out=t1[ti][:], in0=R[:, LC + a:LC + b_],
                                                                           in1=rope[:, 0, a:b_], op=ALU.mult),
                         reads=[rk, "rope"], writes=["t1%d" % ti])
                    S.op("dve", lambda e, PS=PS, a=a, b_=b_, ti=ti: e.tensor_tensor(out=t2[ti][:], in0=PS[:, :],
                                                                                  in1=rope[:, 1, a:b_], op=ALU.mult),
                         reads=[pk, "rope"], writes=["t2%d" % ti])
                    S.op("pool", lambda e, a=a, b_=b_, ti=ti: e.tensor_tensor(out=dst[:, LC + a:LC + b_], in0=t1[ti][:],
                                                                            in1=t2[ti][:], op=ALU.add),
                         reads=["t1%d" % ti, "t2%d" % ti], writes=[dkey + str(blk)])
                self.join([dkey + "c"] + [dkey + str(i) for i in range(4)], dkey)

            for g in range(2):
                load_rope(1024 + g * 128, kT, "kT")
                ri = cn["raw"] % 2
                cn["raw"] += 1
                R, rk = raw[ri], "raw%d" % ri
                S.dma("sp", lambda e, R=R, g=g: e.dma_start(out=R[:], in_=self.zT[1280 + g * 128:1408 + g * 128, :]), writes=[rk])
                S.op("act", lambda e, R=R: e.activation(out=vT[:], in_=R[:], func=AF.Copy), reads=[rk], writes=["vT"])
                for t in range(NT):
                    PT_, ptk = psT[t % 2], "psT%d" % (t % 2)
                    S.op("pe", lambda e, PT_=PT_, t=t: e.transpose(out=PT_[:, :], in_=vT[:, t * 128:(t + 1) * 128],
                                                                   identity=cstb[:, 0, :]), reads=["vT", "cstb"], writes=[ptk])
                    S.op("dve", lambda e, PT_=PT_, t=t: e.tensor_copy(out=V[:, t, :], in_=PT_[:, :]), reads=[ptk],
                         writes=["V%d" % t])
                self.join(["V%d" % t for t in range(NT)], "V")
                for hq in range(4):
                    h = g * 4 + hq
                    load_rope(h * 128, qT, "qT")
                    for ck in range(2):
                        for (t0, tn) in TBLK:
                            pi = cn["ps"] % 2
                            cn["ps"] += 1
                            PS, pk = psS[pi], "psS%d" % pi
                            S.op("pe", lambda e, PS=PS, ck=ck, t0=t0, tn=tn: e.matmul(
                                PS[:, 0:tn], kT[:, ck * 128:(ck + 1) * 128], qT[:, t0:t0 + tn], start=True, stop=True),
                                reads=["kT", "qT"], writes=[pk])
                            S.op("act", lambda e, PS=PS, ck=ck, t0=t0, tn=tn: e.activation(
                                out=PTc[:, ck, t0:t0 + tn], in_=PS[:, 0:tn], func=AF.Exp, scale=scale),
                                reads=[pk], writes=["PTc"])
                    for j in range(16):
                        lo, hi = max(j - 1, 0), min(j + 1, 15)
                        nq = hi - lo + 1
                        pi = cn["ps"] % 2
                        cn["ps"] += 1
                        PS, pk = psS[pi], "psS%d" % pi
                        S.op("pe", lambda e, PS=PS, j=j, lo=lo, nq=nq: e.matmul(
                            PS[:, 0:nq * 128], kT[:, LC + j * 128:LC + (j + 1) * 128],
                            qT[:, LC + lo * 128:LC + (lo + nq) * 128], start=True, stop=True),
                            reads=["kT", "qT"], writes=[pk])
                        r0 = lo - j + 1
                        S.op("act", lambda e, PS=PS, j=j, r0=r0, nq=nq: e.activation(
                            out=PTl[:, j, r0:r0 + nq, :], in_=PS[:, 0:nq * 128].rearrange("p (a b) -> p a b", b=128),
                            func=AF.Exp, scale=scale), reads=[pk], writes=["PTl"])
                        if j >= 1:
                            S.op("pool", lambda e, j=j: e.tensor_tensor(out=PTl[:, j, 0, :], in0=PTl[:, j, 0, :],
                                                                        in1=cstb[:, 3, :], op=ALU.mult),
                                 reads=["PTl", "cstb"], writes=["PTl"])
                        if j <= 14:
                            S.op("pool", lambda e, j=j: e.tensor_tensor(out=PTl[:, j, 2, :], in0=PTl[:, j, 2, :],
                                                                        in1=cstb[:, 4, :], op=ALU.mult),
                                 reads=["PTl", "cstb"], writes=["PTl"])
                    for qt in range(NT):
                        contrib = [(ck, PTc[:, ck, qt * 128:(qt + 1) * 128]) for ck in range(2)]
                        if qt >= 2:
                            n = qt - 2
                            for j in range(max(n - 1, 0), min(n + 1, 15) + 1):
                                contrib.append((2 + j, PTl[:, j, n - j + 1, :]))
                        oi = cn["o"] % 2
                        cn["o"] += 1
                        O, Dn = psO[oi], psD[oi]
                        for idx, (vt, rhs) in enumerate(contrib):
                            st_, sp_ = (idx == 0), (idx == len(contrib) - 1)
                            S.op("pe", lambda e, O=O, vt=vt, rhs=rhs, st_=st_, sp_=sp_: e.matmul(
                                O[:, :], V[:, vt, :], rhs, start=st_, stop=sp_), reads=["V", "PTc", "PTl"], writes=["psO%d" % oi])
                            S.op("pe", lambda e, Dn=Dn, rhs=rhs, st_=st_, sp_=sp_: e.matmul(
                                Dn[:, :], cstb[:, 1, :], rhs, start=st_, stop=sp_), reads=["PTc", "PTl"], writes=["psD%d" % oi])
                        RC = rc[oi]
                        S.op("dve", lambda e, RC=RC, Dn=Dn, h=h: e.tensor_scalar(out=RC[:], in0=Dn[:, :], scalar1=esk[:, h:h + 1],
                                                                             scalar2=None, op0=ALU.add),
                             reads=["psD%d" % oi, "esk"], writes=["rc%d" % oi])
                        S.op("dve", lambda e, RC=RC: e.reciprocal(out=RC[:], in_=RC[:]), reads=["rc%d" % oi], writes=["rc%d" % oi])
                        si = cn["ost"] % 3
                        cn["ost"] += 1
                        OS = ost[si]
                        S.op("dve", lambda e, OS=OS, O=O, RC=RC: e.tensor_tensor(out=OS[:], in0=O[:, :], in1=RC[:], op=ALU.mult),
                             reads=["psO%d" % oi, "rc%d" % oi], writes=["ost%d" % si])
                        S.dma("sp", lambda e, OS=OS, h=h, qt=qt: e.dma_start(
                            out=self.mixT[h * 128:(h + 1) * 128, qt * 128:(qt + 1) * 128], in_=OS[:]), reads=["ost%d" % si])
            S.emit(block)
    def phase_gdn(self, l):
        nc, S = self.nc, self.S
        cst, cstb = self.cst, self.cstb
        ORD = [list(range(NT)), [1, 0] + list(range(NT - 1, 1, -1))]
        with ExitStack() as eso:
            G = self.sb(eso, "G", [128, NT, 16], F32)
            BETA = self.sb(eso, "BETA", [128, NT, 16], F32)
            GC = self.sb(eso, "GC", [128, NT, 16], F32)
            TOT = self.sb(eso, "TOT", [128, NT, 16], F32)
            E = self.sb(eso, "E", [128, NT, 16], F32)
            Fd = self.sb(eso, "Fd", [128, NT, 16], F32)
            BE = self.sb(eso, "BE", [128, NT, 16], F32)
            GL = self.sb(eso, "GL", [128, NT, 16], F32)
            onec = self.sb(eso, "onec", [128, 1], F32)
            with ExitStack() as es, nc.Block() as block:
                abT = self.sb(es, "abT", [32, T], F32)
                abm = self.sb(es, "abm", [128, NT, 32], F32)
                al = self.sb(es, "al", [128, 16], F32)
                dtb = self.sb(es, "dtb", [128, 16], F32)
                cw = self.sb(es, "cw", [128, 24, 5], F32)
                xg = self.sb(es, "xg", [128, NT, 16], F32)
                Zp = [self.sb(es, "Zp", [128, T + 8], F32) for _ in range(2)]
                acc = [self.sb(es, "acc", [128, T], F32) for _ in range(2)]
                sl_ = [self.sb(es, "sl", [128, T], F32) for _ in range(2)]
                sq = [self.sb(es, "gsq", [128, 512], F32) for _ in range(2)]
                ri = [self.sb(es, "gri", [128, 512], F32) for _ in range(2)]
                ob = [self.sb(es, "gob", [128, T], BF16) for _ in range(2)]
                tm = [self.sb(es, "gtm", [128, NT, 128], BF16) for _ in range(2)]
                pabA = self.ps(es, "pabA", [128, 512], F32)
                pabB = self.ps(es, "pabB", [128, 512], F32)
                pgcb = self.ps(es, "pgc", [128, 512], F32)
                ptotb = self.ps(es, "ptot", [128, 512], F32)
                pgc = pgcb[:, 0:NT * 16].rearrange("p (n c) -> p n c", c=16)
                ptot = ptotb[:, 0:NT * 16].rearrange("p (n c) -> p n c", c=16)
                pss = [self.ps(es, "pss", [128, 512], F32) for _ in range(2)]
                ptrb = [self.ps(es, "ptr", [128, 1024], BF16) for _ in range(2)]
                ptr = [ptrb[0][:, 0:128], ptrb[1][:, 0:128]]
                S.op("pool", lambda e: e.memset(onec[:], 1.0), writes=["onec"])
                for z in Zp:
                    S.op("pool", lambda e, z=z: e.memset(z[:], 0.0), writes=["zpc%d" % Zp.index(z), "zpl%d" % Zp.index(z)])
                S.dma("sp", lambda e: e.dma_start(out=abT[:], in_=self.zT[5632:5664, :]), writes=["abT"])
                S.dma("sp", lambda e: e.dma_start(out=al[:], in_=self.alog[l][:, :]), writes=["al"])
                S.dma("sp", lambda e: e.dma_start(out=dtb[:], in_=self.dtb[l][:, :]), writes=["dtb"])
                S.dma("sp", lambda e: e.dma_start(out=cw[:], in_=self.convw[l][:, :, :]), writes=["cw"])
                for t in range(NT):
                    pb_ = pabA if t < 9 else pabB
                    S.op("pe", lambda e, t=t, pb_=pb_: e.transpose(out=pb_[:, (t % 9) * 32:(t % 9 + 1) * 32], in_=abT[:, t * 128:(t + 1) * 128],
                                                                   identity=cst[0:32, 0, 0:32]), reads=["abT", "cst"], writes=["pab"])
                S.op("dve", lambda e: e.tensor_copy(out=abm[:, 0:9, :], in_=pabA[:, 0:288].rearrange("p (n c) -> p n c", c=32)), reads=["pab"], writes=["abm0"])
                S.op("dve", lambda e: e.tensor_copy(out=abm[:, 9:18, :], in_=pabB[:, 0:288].rearrange("p (n c) -> p n c", c=32)), reads=["pab"], writes=["abm1"])
                self.join(["abm0", "abm1"], "abm")
                ab5 = abm[:].rearrange("p n (d t h) -> p n d t h", d=2, t=2)
                v4 = lambda a: a[:].rearrange("p n (d h) -> p n d h", d=2)
                b4 = lambda a: a[:].rearrange("p (d h) -> p d h", d=2).unsqueeze(1).to_broadcast([128, NT, 2, 8])
                S.op("dve", lambda e: e.tensor_tensor(out=v4(xg), in0=ab5[:, :, :, 0, :], in1=b4(dtb), op=ALU.add),
                     reads=["abm", "dtb"], writes=["xg"])
                S.op("act", lambda e: e.activation(out=xg[:], in_=xg[:], func=AF.Exp), reads=["xg"], writes=["xg"])
                S.op("act", lambda e: e.activation(out=xg[:], in_=xg[:], func=AF.Ln, bias=onec[:, 0:1]), reads=["xg", "onec"], writes=["xg"])
                S.op("act", lambda e: e.activation(out=al[:], in_=al[:], func=AF.Exp), reads=["al"], writes=["al"])
                S.op("dve", lambda e: e.tensor_scalar(out=al[:], in0=al[:], scalar1=-1.0, scalar2=None, op0=ALU.mult),
                     reads=["al"], writes=["al"])
                S.op("dve", lambda e: e.tensor_tensor(out=v4(G), in0=v4(xg), in1=b4(al), op=ALU.mult), reads=["xg", "al"], writes=["G"])
                S.op("act", lambda e: e.activation(out=v4(BETA), in_=ab5[:, :, :, 1, :], func=AF.Sigmoid), reads=["abm"], writes=["BETA"])
                for t in range(NT):
                    for d in range(2):
                        S.op("pe", lambda e, t=t, d=d: e.matmul(pgc[:, t, d * 8:(d + 1) * 8], cst[:, 3 + d, :],
                                                                G[:, t, d * 8:(d + 1) * 8], start=True, stop=True),
                             reads=["G", "cst"], writes=["pgc"])
                    S.op("pe", lambda e, t=t: e.matmul(ptot[:, t, :], cst[:, 1, :], G[:, t, :], start=True, stop=True),
                         reads=["G", "cst"], writes=["ptot"])
                S.op("dve", lambda e: e.tensor_copy(out=GC[:], in_=pgc), reads=["pgc"], writes=["GC"])
                S.op("dve", lambda e: e.tensor_copy(out=TOT[:], in_=ptot), reads=["ptot"], writes=["TOT"])
                S.op("act", lambda e: e.activation(out=E[:], in_=GC[:], func=AF.Exp), reads=["GC"], writes=["E"])
                S.op("dve", lambda e: e.tensor_tensor(out=Fd[:], in0=TOT[:], in1=GC[:], op=ALU.subtract), reads=["TOT", "GC"], writes=["Fd"])
                S.op("act", lambda e: e.activation(out=Fd[:], in_=Fd[:], func=AF.Exp), reads=["Fd"], writes=["Fd"])
                S.op("act", lambda e: e.activation(out=GL[:], in_=TOT[:], func=AF.Exp), reads=["TOT"], writes=["GL"])
                S.op("dve", lambda e: e.tensor_tensor(out=BE[:], in0=BETA[:], in1=E[:], op=ALU.mult), reads=["BETA", "E"], writes=["BE"])
                cnt = [0]
                for h in range(8):
                    for which in range(3):
                        i = cnt[0] % 2
                        cnt[0] += 1
                        row0 = 1536 + which * 1024 + h * 128
                        ch = which * 8 + h
                        z, A_, SL, OB, TM = Zp[i], acc[i], sl_[i], ob[i], tm[i]
                        S.dma("sp", lambda e, z=z, row0=row0: e.dma_start(out=z[:, 2:2 + LC], in_=self.zT[row0:row0 + 128, 0:LC]),
                              writes=["zpc%d" % i])
                        S.dma("sp", lambda e, z=z, row0=row0: e.dma_start(out=z[:, 262:262 + L], in_=self.zT[row0:row0 + 128, LC:T]),
                              writes=["zpl%d" % i])
                        for (o0, n, zo, zk) in ((0, LC, 0, "zpc%d" % i), (LC, L, 260, "zpl%d" % i)):
                            S.op("dve", lambda e, z=z, A_=A_, o0=o0, n=n, zo=zo, ch=ch: e.tensor_scalar(
                                out=A_[:, o0:o0 + n], in0=z[:, zo:zo + n], scalar1=cw[:, ch, 0:1], scalar2=None, op0=ALU.mult),
                                reads=[zk, "cw"], writes=["acc%d" % i])
                            for j in range(1, 5):
                                S.op("dve", lambda e, z=z, A_=A_, o0=o0, n=n, zo=zo, ch=ch, j=j: e.scalar_tensor_tensor(
                                    out=A_[:, o0:o0 + n], in0=z[:, zo + j:zo + j + n], scalar=cw[:, ch, j:j + 1],
                                    in1=A_[:, o0:o0 + n], op0=ALU.mult, op1=ALU.add), reads=[zk, "cw", "acc%d" % i], writes=["acc%d" % i])
                        S.op("act", lambda e, A_=A_, SL=SL: e.activation(out=SL[:], in_=A_[:], func=AF.Silu),
                             reads=["acc%d" % i], writes=["sl%d" % i])
                        if which < 2:
                            qs = (128 ** -0.5) if which == 0 else 1.0
                            for bi, (t0, tn) in enumerate(TBLK):
                                bb = bi % 2
                                S.op("act", lambda e, SL=SL, t0=t0, tn=tn, bb=bb: e.activation(out=sq[bb][:, 0:tn], in_=SL[:, t0:t0 + tn],
                                                                                             func=AF.Square), reads=["sl%d" % i], writes=["gsq%d" % bb])
                                S.op("pe", lambda e, t0=t0, tn=tn, bb=bb: e.matmul(pss[bb][:, 0:tn], cst[:, 1, :], sq[bb][:, 0:tn],
                                                                                 start=True, stop=True), reads=["gsq%d" % bb], writes=["pss%d" % bb])
                                S.op("act", lambda e, tn=tn, bb=bb: e.activation(out=ri[bb][:, 0:tn], in_=pss[bb][:, 0:tn], func=AF.Sqrt,
                                                                               bias=self.epsc[:, 0:1]), reads=["pss%d" % bb], writes=["gri%d" % bb])
                                S.op("dve", lambda e, tn=tn, bb=bb: e.reciprocal(out=ri[bb][:, 0:tn], in_=ri[bb][:, 0:tn]),
                                     reads=["gri%d" % bb], writes=["gri%d" % bb])
                                S.op("dve", lambda e, SL=SL, OB=OB, t0=t0, tn=tn, bb=bb, qs=qs: e.scalar_tensor_tensor(
                                    out=OB[:, t0:t0 + tn], in0=SL[:, t0:t0 + tn], scalar=qs, in1=ri[bb][:, 0:tn],
                                    op0=ALU.mult, op1=ALU.mult), reads=["sl%d" % i, "gri%d" % bb], writes=["gob%d" % i])
                            dst = self.gq if which == 0 else self.gk
                            S.dma("sp", lambda e, OB=OB, dst=dst, h=h: e.dma_start(out=dst[h, :, :], in_=OB[:]), reads=["gob%d" % i])
                        else:
                            S.op("act", lambda e, SL=SL, OB=OB: e.activation(out=OB[:], in_=SL[:], func=AF.Copy),
                                 reads=["sl%d" % i], writes=["gob%d" % i])
                        if which >= 1:
                            for t in range(NT):
                                S.op("pe", lambda e, OB=OB, t=t: e.transpose(out=ptr[t % 2], in_=OB[:, t * 128:(t + 1) * 128],
                                                                             identity=cstb[:, 0, :]), reads=["gob%d" % i], writes=["ptr%d" % (t % 2)])
                                S.op("dve", lambda e, TM=TM, t=t: e.tensor_copy(out=TM[:, t, :], in_=ptr[t % 2]),
                                     reads=["ptr%d" % (t % 2)], writes=["gtm%d" % i])
                            dst = self.gkm if which == 1 else self.gvm
                            S.dma("sp", lambda e, TM=TM, dst=dst, h=h: e.dma_start(
                                out=dst[h].rearrange("(n p) d -> p n d", p=128), in_=TM[:]), reads=["gtm%d" % i])
                S.emit(block)
            if getattr(self, "gdn_stop", 9) < 2:
                return
            with ExitStack() as es, nc.Block() as block:
                NSET = 8
                f32t = lambda n: [self.sb(es, n, [128, 128], F32) for _ in range(NSET)]
                b16t = lambda n: [self.sb(es, n, [128, 128], BF16) for _ in range(NSET)]
                rhsG = [self.sb(es, "rhsG", [128, 256], F32) for _ in range(NSET)]
                t1, Ex, t2, Ey, EB, Mx, XA, My, QA, usb, osb, gcs = [f32t(n) for n in
                    ("t1", "Ex", "t2", "Ey", "EB", "Mx", "XA", "My", "QA", "usb", "osb", "gcs")]
                QKm, qdec, TT, bv, kbe, kdec, wsb, vnew, kt, qt, km, vm = [b16t(n) for n in
                    ("QKm", "qdec", "TT", "bv", "kbe", "kdec", "wsb", "vnew", "kt", "qt", "km", "vm")]
                Pm = [[self.sb(es, "Pm", [128, 128], F32) for _ in range(2)] for _ in range(NSET)]
                Xm = [[self.sb(es, "Xm", [128, 128], F32) for _ in range(2)] for _ in range(NSET)]
                Yb = [[self.sb(es, "Yb", [128, 128], F32) for _ in range(2)] for _ in range(NSET)]
                Yo = [[self.sb(es, "Yo", [128, 128], F32) for _ in range(3)] for _ in range(NSET)]
                XF, YF, Qs, M1s = [f32t(n) for n in ("XF", "YF", "Qs", "M1s")]
                Yoh = [[self.sb(es, "Yoh", [128, 128], BF16) for _ in range(3)] for _ in range(NSET)]
                Rb, Qh, M1h_s = [b16t(n) for n in ("Rb", "Qh", "M1hs")]
                Sf = [[self.sb(es, "Sf", [128, 128], F32) for _ in range(2)] for _ in range(8)]
                Sb = [[self.sb(es, "Sb", [128, 128], BF16) for _ in range(2)] for _ in range(8)]
                bk = [[self.ps(es, "bk", [128, 512], F32) for _ in range(1)] for _ in range(NSET)]
                for h in range(8):
                    for d in range(2):
                        S.op("pool", lambda e, h=h, d=d: e.memset(Sf[h][d][:], 0.0), writes=["Sf%d%d" % (h, d)])
                        S.op("pool", lambda e, h=h, d=d: e.memset(Sb[h][d][:], 0.0), writes=["Sb%d%d" % (h, d)])
                def body(i, h, d, t):
                    K = lambda n: "%s%d" % (n, i)
                    col = d * 8 + h
                    gcol, bcol, gccol = G[:, t, col:col + 1], BETA[:, t, col:col + 1], GC[:, t, col:col + 1]
                    becol, fcol, glcol = BE[:, t, col:col + 1], Fd[:, t, col:col + 1], GL[:, t, col:col + 1]
                    tsl = slice(t * 128, (t + 1) * 128)
                    Bk = bk[i][0]
                    A, B, C = Bk[:, 0:128], Bk[:, 128:256], Bk[:, 256:512]
                    OP, M1, OX, Yp = Bk[:, 0:128], Bk[:, 128:256], Bk[:, 256:384], Bk[:, 384:512]
                    Sp, U, W, O2 = Bk[:, 0:128], Bk[:, 128:256], Bk[:, 256:384], Bk[:, 384:512]
                    Dp = Bk[:, 128:256]
                    yield S.dma("sp", lambda e, i=i, h=h, tsl=tsl: e.dma_start(out=kt[i][:], in_=self.gk[h, :, tsl]), writes=[K("kt")])
                    yield S.dma("sp", lambda e, i=i, h=h, tsl=tsl: e.dma_start(out=qt[i][:], in_=self.gq[h, :, tsl]), writes=[K("qt")])
                    yield S.dma("sp", lambda e, i=i, h=h, tsl=tsl: e.dma_start(out=km[i][:], in_=self.gkm[h, tsl, :]), writes=[K("km")])
                    yield S.dma("sp", lambda e, i=i, h=h, tsl=tsl: e.dma_start(out=vm[i][:], in_=self.gvm[h, tsl, :]), writes=[K("vm")])
                    yield S.op("pe", lambda e, A=A, i=i: e.matmul(A, kt[i][:], kt[i][:], start=True, stop=True), reads=[K("kt")], writes=[K("bank0"), K("A")])
                    yield S.op("pe", lambda e, B=B, i=i: e.matmul(B, kt[i][:], qt[i][:], start=True, stop=True), reads=[K("kt"), K("qt")], writes=[K("bank0"), K("B")])
                    yield S.op("dve", lambda e, i=i, d=d, gcol=gcol: e.tensor_scalar(out=rhsG[i][:, 0:128], in0=cst[:, 3 + d, :], scalar1=gcol,
                                                                             scalar2=None, op0=ALU.mult), reads=["G"], writes=[K("rhsGa")])
                    yield S.op("act", lambda e, i=i, bcol=bcol: e.activation(out=rhsG[i][:, 128:256], in_=cst[:, 0, :], func=AF.Copy, scale=bcol),
                               reads=["BETA"], writes=[K("rhsGb")])
                    yield S.op("pe", lambda e, C=C, i=i: e.matmul(C, cst[:, 1, :], rhsG[i][:], start=True, stop=True), reads=[K("rhsGa"), K("rhsGb")],
                               writes=[K("bank0"), K("C")])
                    yield S.op("dve", lambda e, C=C, i=i, gccol=gccol: e.tensor_scalar(out=t1[i][:], in0=C[:, 0:128], scalar1=gccol, scalar2=0.0,
                                                                               op0=ALU.subtract, op1=ALU.min), reads=[K("C")], writes=[K("bank0"), K("t1")])
                    yield S.op("act", lambda e, i=i: e.activation(out=Ex[i][:], in_=t1[i][:], func=AF.Exp), reads=[K("t1")], writes=[K("Ex")])
                    yield S.op("dve", lambda e, C=C, i=i, gccol=gccol: e.tensor_scalar(out=t2[i][:], in0=C[:, 0:128], scalar1=gccol, scalar2=0.0,
                                                                               op0=ALU.subtract, op1=ALU.max), reads=[K("C")], writes=[K("bank0"), K("t2")])
                    yield S.op("act", lambda e, i=i: e.activation(out=Ey[i][:], in_=t2[i][:], func=AF.Exp, scale=-1.0), reads=[K("t2")], writes=[K("Ey")])
                    yield S.op("dve", lambda e, C=C, i=i: e.tensor_copy(out=gcs[i][:], in_=C[:, 0:128]), reads=[K("C")], writes=[K("bank0"), K("gcs")])
                    yield S.op("act", lambda e, i=i: e.activation(out=EB[i][:], in_=gcs[i][:], func=AF.Exp), reads=[K("gcs")], writes=[K("EB")])
                    yield S.op("dve", lambda e, C=C, i=i, d=d: e.tensor_tensor(out=Mx[i][:], in0=C[:, 128:256], in1=cst[:, 11 + d, :], op=ALU.mult),
                               reads=[K("C")], writes=[K("bank0"), K("Mx")])
                    yield S.op("dve", lambda e, A=A, i=i: e.tensor_tensor(out=XA[i][:], in0=A, in1=Ex[i][:], op=ALU.mult), reads=[K("A"), K("Ex")],
                               writes=[K("bank0"), K("XA")])
                    yield S.op("pool", lambda e, i=i: e.tensor_tensor(out=Xm[i][0][:], in0=XA[i][:], in1=Mx[i][:], op=ALU.mult),
                               reads=[K("XA"), K("Mx")], writes=[K("X0")])
                    yield S.op("pool", lambda e, i=i: e.tensor_tensor(out=Pm[i][1][:], in0=Xm[i][0][:], in1=cst[:, 0, :], op=ALU.add),
                               reads=[K("X0")], writes=[K("P1")])
                    yield S.op("dve", lambda e, A=A, i=i: e.tensor_tensor(out=YF[i][:], in0=A, in1=Ey[i][:], op=ALU.mult), reads=[K("A"), K("Ey")],
                               writes=[K("bank0"), K("YF")])
                    yield S.op("dve", lambda e, i=i, d=d, bcol=bcol: e.scalar_tensor_tensor(out=Yb[i][0][:], in0=YF[i][:], scalar=bcol, in1=cst[:, 13 + d * 4, :],
                                                                                     op0=ALU.mult, op1=ALU.mult), reads=[K("YF"), "BETA"], writes=[K("Y0")])
                    for lv in range(3):
                        yield S.op("dve", lambda e, i=i, d=d, lv=lv, bcol=bcol: e.scalar_tensor_tensor(
                            out=Yoh[i][lv][:], in0=YF[i][:], scalar=bcol, in1=cst[:, 14 + d * 4 + lv, :], op0=ALU.mult, op1=ALU.mult),
                            reads=[K("YF"), "BETA"], writes=[K("Yo%d" % lv)])
                    yield S.op("dve", lambda e, B=B, i=i: e.tensor_tensor(out=QA[i][:], in0=B, in1=Ex[i][:], op=ALU.mult), reads=[K("B"), K("Ex")],
                               writes=[K("bank0"), K("QA")])
                    yield S.op("pool", lambda e, i=i, d=d: e.tensor_tensor(out=QKm[i][:], in0=QA[i][:], in1=cst[:, 3 + d, :], op=ALU.mult),
                               reads=[K("QA")], writes=[K("QKm")])
                    yield S.op("pool", lambda e, i=i: e.tensor_tensor(out=qdec[i][:], in0=qt[i][:], in1=EB[i][:], op=ALU.mult), reads=[K("qt"), K("EB")],
                               writes=[K("qdec")])
                    yield S.op("dve", lambda e, i=i, bcol=bcol: e.tensor_scalar(out=bv[i][:], in0=vm[i][:], scalar1=bcol, scalar2=None, op0=ALU.mult),
                               reads=[K("vm"), "BETA"], writes=[K("bv")])
                    yield S.op("dve", lambda e, i=i, becol=becol: e.tensor_scalar(out=kbe[i][:], in0=km[i][:], scalar1=becol, scalar2=None, op0=ALU.mult),
                               reads=[K("km"), "BE"], writes=[K("kbe")])
                    yield S.op("dve", lambda e, i=i, fcol=fcol: e.tensor_scalar(out=kdec[i][:], in0=km[i][:], scalar1=fcol, scalar2=None, op0=ALU.mult),
                               reads=[K("km"), "Fd"], writes=[K("kdec")])
                    a, b = 0, 1
                    for m in range(4):
                        ka, kb_ = str(a), str(b)
                        if m > 0:
                            yield S.op("pe", lambda e, OP=OP, i=i, a=a: e.matmul(OP, Yb[i][a][:], Pm[i][a][:], start=True, stop=True),
                                       reads=[K("Y" + ka), K("P" + ka)], writes=[K("bank0"), K("OP")])
                        if m < 3:
                            yield S.op("pe", lambda e, OX=OX, i=i, a=a: e.matmul(OX, Yb[i][a][:], Xm[i][a][:], start=True, stop=True),
                                       reads=[K("Y" + ka), K("X" + ka)], writes=[K("bank0"), K("OX")])
                            yield S.op("pe", lambda e, Yp=Yp, i=i, a=a: e.matmul(Yp, Xm[i][a][:], Yb[i][a][:], start=True, stop=True),
                                       reads=[K("Y" + ka), K("X" + ka)], writes=[K("bank0"), K("Yp")])
                        if m > 0:
                            yield S.op("dve", lambda e, OP=OP, i=i, a=a, b=b: e.tensor_tensor(out=Pm[i][b][:], in0=OP, in1=Pm[i][a][:], op=ALU.add),
                                       reads=[K("OP"), K("P" + ka)], writes=[K("bank0"), K("P" + kb_)])
                        if m < 3:
                            yield S.op("act", lambda e, OX=OX, i=i, b=b: e.activation(out=Xm[i][b][:], in_=OX, func=AF.Copy),
                                       reads=[K("OX")], writes=[K("bank0"), K("X" + kb_)])
                            yield S.op("act", lambda e, Yp=Yp, i=i, b=b: e.activation(out=Yb[i][b][:], in_=Yp, func=AF.Copy),
                                       reads=[K("Yp")], writes=[K("bank0"), K("Y" + kb_)])
                        a, b = b, a
                    M1h = Bk[:, 128:256].bitcast(BF16)[:, 0:128]
                    yield S.op("dve", lambda e, i=i, a=a: e.tensor_copy(out=Rb[i][:], in_=Pm[i][a][:]), reads=[K("P" + str(a))], writes=[K("Rb")])
                    yield S.op("pe", lambda e, M1h=M1h, i=i: e.transpose(out=M1h, in_=Rb[i][:], identity=cstb[:, 0, :]),
                               reads=[K("Rb")], writes=[K("bank0"), K("M1")])
                    yield S.op("act", lambda e, M1h=M1h, i=i: e.activation(out=Qh[i][:], in_=M1h, func=AF.Copy), reads=[K("M1")], writes=[K("bank0"), K("Qs")])
                    for lv in range(3):
                        ka, kb_ = str(a), str(b)
                        yield S.op("pe", lambda e, M1=M1, i=i, lv=lv: e.matmul(M1, Yoh[i][lv][:], Rb[i][:], start=True, stop=True),
                                   reads=[K("Yo%d" % lv), K("Rb")], writes=[K("bank0"), K("M1")])
                        yield S.op("act", lambda e, M1=M1, i=i: e.activation(out=M1h_s[i][:], in_=M1, func=AF.Copy), reads=[K("M1")], writes=[K("bank0"), K("M1s")])
                        yield S.op("pe", lambda e, OP=OP, i=i: e.matmul(OP, Qh[i][:], M1h_s[i][:], start=True, stop=True),
                                   reads=[K("Qs"), K("M1s")], writes=[K("bank0"), K("OP")])
                        if lv < 2:
                            yield S.op("dve", lambda e, OP=OP, i=i, a=a, b=b: e.tensor_tensor(out=Pm[i][b][:], in0=OP, in1=Pm[i][a][:], op=ALU.add),
                                       reads=[K("OP"), K("P" + ka)], writes=[K("bank0"), K("P" + kb_)])
                            yield S.op("dve", lambda e, i=i, b=b: e.tensor_copy(out=Rb[i][:], in_=Pm[i][b][:]), reads=[K("P" + kb_)], writes=[K("Rb")])
                            yield S.op("pe", lambda e, M1h=M1h, i=i: e.transpose(out=M1h, in_=Rb[i][:], identity=cstb[:, 0, :]),
                                       reads=[K("Rb")], writes=[K("bank0"), K("M1")])
                            yield S.op("act", lambda e, M1h=M1h, i=i: e.activation(out=Qh[i][:], in_=M1h, func=AF.Copy), reads=[K("M1")], writes=[K("bank0"), K("Qs")])
                        else:
                            yield S.op("dve", lambda e, OP=OP, i=i, a=a: e.tensor_tensor(out=TT[i][:], in0=OP, in1=Pm[i][a][:], op=ALU.add),
                                       reads=[K("OP"), K("P" + ka)], writes=[K("bank0"), K("TT")])
                        a, b = b, a
                    yield S.op("pe", lambda e, U=U, i=i: e.matmul(U, TT[i][:], bv[i][:], start=True, stop=True), reads=[K("TT"), K("bv")], writes=[K("bank0"), K("U")])
                    yield S.op("pe", lambda e, W=W, i=i: e.matmul(W, kbe[i][:], TT[i][:], start=True, stop=True), reads=[K("TT"), K("kbe")], writes=[K("bank0"), K("W")])
                    yield S.op("dve", lambda e, U=U, i=i: e.tensor_copy(out=usb[i][:], in_=U), reads=[K("U")], writes=[K("bank0"), K("usb")])
                    yield S.op("act", lambda e, W=W, i=i: e.activation(out=wsb[i][:], in_=W, func=AF.Copy), reads=[K("W")], writes=[K("bank0"), K("wsb")])
                    sk, sbk = "Sf%d%d" % (h, d), "Sb%d%d" % (h, d)
                    SF, SB = Sf[h][d], Sb[h][d]
                    yield S.op("pe", lambda e, Sp=Sp, i=i, SB=SB: e.matmul(Sp, wsb[i][:], SB[:], start=True, stop=True), reads=[K("wsb"), sbk], writes=[K("bank0"), K("Sp")])
                    yield S.op("dve", lambda e, Sp=Sp, i=i: e.tensor_tensor(out=vnew[i][:], in0=usb[i][:], in1=Sp, op=ALU.subtract),
                         reads=[K("usb"), K("Sp")], writes=[K("bank0"), K("vnew")])
                    yield S.op("pe", lambda e, O2=O2, i=i, SB=SB: e.matmul(O2, SB[:], qdec[i][:], start=True, stop=False), reads=[sbk, K("qdec")], writes=[K("bank0"), K("O2")])
                    yield S.op("pe", lambda e, O2=O2, i=i: e.matmul(O2, vnew[i][:], QKm[i][:], start=False, stop=True), reads=[K("vnew"), K("QKm")], writes=[K("bank0"), K("O2")])
                    yield S.op("act", lambda e, O2=O2, i=i: e.activation(out=osb[i][:], in_=O2, func=AF.Copy), reads=[K("O2")], writes=[K("bank0"), K("osb")])
                    yield S.dma("act", lambda e, i=i, d=d, h=h, tsl=tsl: e.dma_start(out=self.oT[d, h, :, tsl], in_=osb[i][:]), reads=[K("osb")])
                    yield S.op("pe", lambda e, Dp=Dp, i=i: e.matmul(Dp, kdec[i][:], vnew[i][:], start=True, stop=True), reads=[K("kdec"), K("vnew")], writes=[K("bank0"), K("Dp")])
                    yield S.op("dve", lambda e, Dp=Dp, SF=SF, glcol=glcol: e.scalar_tensor_tensor(out=SF[:], in0=SF[:], scalar=glcol, in1=Dp,
                                                                                           op0=ALU.mult, op1=ALU.add), reads=[sk, K("Dp"), "GL"], writes=[K("bank0"), sk])
                    yield S.op("act", lambda e, SF=SF, SB=SB: e.activation(out=SB[:], in_=SF[:], func=AF.Copy), reads=[sk], writes=[sbk])

                order = [(s_, h_, d_) for s_ in range(getattr(self, 'scan_steps', NT))
                         for h_ in range(getattr(self, 'scan_heads', 8)) for d_ in range(2)]
                for g0 in range(0, len(order), NSET):
                    gens = [body(k, h_, d_, ORD[d_][s_]) for k, (s_, h_, d_) in enumerate(order[g0:g0 + NSET])]
                    while gens:
                        for g in list(gens):
                            try:
                                next(g)
                            except StopIteration:
                                gens.remove(g)
                S.limit = None
                S.emit(block)
        if getattr(self, "gdn_stop", 9) < 3:
            return
        with ExitStack() as es, nc.Block() as block:
            o0 = [self.sb(es, "o0", [128, T], F32) for _ in range(2)]
            o1 = [self.sb(es, "o1", [128, T], F32) for _ in range(2)]
            gt = [self.sb(es, "gt", [128, T], F32) for _ in range(2)]
            sq = [self.sb(es, "nsq", [128, 512], F32) for _ in range(2)]
            ri = [self.sb(es, "nri", [128, 512], F32) for _ in range(2)]
            yb = [self.sb(es, "nyb", [128, T], BF16) for _ in range(2)]
            gn = self.sb(es, "gn", [128, 1], F32)
            pss = [self.ps(es, "npss", [128, 512], F32) for _ in range(2)]
            S.dma("sp", lambda e: e.dma_start(out=gn[:], in_=self.gnorm[l][:, :]), writes=["gn"])
            for h in range(8):
                i = h % 2
                S.dma("sp", lambda e, i=i, h=h: e.dma_start(out=o0[i][:], in_=self.oT[0, h, :, :]), writes=["o0%d" % i])
                S.dma("sp", lambda e, i=i, h=h: e.dma_start(out=o1[i][:], in_=self.oT[1, h, :, :]), writes=["o1%d" % i])
                S.dma("sp", lambda e, i=i, h=h: e.dma_start(out=gt[i][:], in_=self.zT[4608 + h * 128:4736 + h * 128, :]), writes=["gt%d" % i])
                S.op("pool", lambda e, i=i: e.tensor_tensor(out=o0[i][:], in0=o0[i][:], in1=o1[i][:], op=ALU.add), reads=["o0%d" % i, "o1%d" % i], writes=["o0%d" % i])
                S.op("act", lambda e, i=i: e.activation(out=gt[i][:], in_=gt[i][:], func=AF.Silu), reads=["gt%d" % i], writes=["gt%d" % i])
                for bi, (t0, tn) in enumerate(TBLK):
                    bb = bi % 2
                    S.op("act", lambda e, i=i, t0=t0, tn=tn, bb=bb: e.activation(out=sq[bb][:, 0:tn], in_=o0[i][:, t0:t0 + tn], func=AF.Square),
                         reads=["o0%d" % i], writes=["nsq%d" % bb])
                    S.op("pe", lambda e, tn=tn, bb=bb: e.matmul(pss[bb][:, 0:tn], cst[:, 1, :], sq[bb][:, 0:tn], start=True, stop=True),
                         reads=["nsq%d" % bb], writes=["npss%d" % bb])
                    S.op("act", lambda e, tn=tn, bb=bb: e.activation(out=ri[bb][:, 0:tn], in_=pss[bb][:, 0:tn], func=AF.Sqrt, scale=1.0 / 128,
                                                                   bias=self.epsc[:, 0:1]), reads=["npss%d" % bb], writes=["nri%d" % bb])
                    S.op("dve", lambda e, tn=tn, bb=bb: e.reciprocal(out=ri[bb][:, 0:tn], in_=ri[bb][:, 0:tn]), reads=["nri%d" % bb], writes=["nri%d" % bb])
                    S.op("dve", lambda e, i=i, t0=t0, tn=tn, bb=bb: e.scalar_tensor_tensor(out=ri[bb][:, 0:tn], in0=o0[i][:, t0:t0 + tn], scalar=gn[:, 0:1],
                                                                                         in1=ri[bb][:, 0:tn], op0=ALU.mult, op1=ALU.mult),
                         reads=["o0%d" % i, "nri%d" % bb, "gn"], writes=["nri%d" % bb])
                    S.op("pool", lambda e, i=i, t0=t0, tn=tn, bb=bb: e.tensor_tensor(out=yb[i][:, t0:t0 + tn], in0=ri[bb][:, 0:tn], in1=gt[i][:, t0:t0 + tn],
                                                                                   op=ALU.mult), reads=["nri%d" % bb, "gt%d" % i], writes=["nyb%d" % i])
                S.dma("sp", lambda e, i=i, h=h: e.dma_start(out=self.mixT[1024 + h * 128:1152 + h * 128, :], in_=yb[i][:]), reads=["nyb%d" % i])
            S.emit(block)

    def phase_mixer(self, l):
        if not getattr(self, "skip_attn", False):
            self.phase_attn(l)
        if not getattr(self, "skip_gdn", False):
            self.phase_gdn(l)
    def build_gdnprobe(self):
        nc, es, P = self.nc, self.es, self
        l = 0
        self.consts = P.din("consts", [128, 21, 128])
        self.zT = P.din("zT", [D_IN, T])
        self.alog = {l: P.din("alog0", [128, 16])}
        self.dtb = {l: P.din("dtb0", [128, 16])}
        self.convw = {l: P.din("convw0", [128, 24, 5])}
        self.gnorm = {l: P.din("gnorm0", [128, 1])}
        self.gq = P.dout("gq", [8, 128, T], BF16)
        self.gk = P.dout("gk", [8, 128, T], BF16)
        self.gkm = P.dout("gkm", [8, T, 128], BF16)
        self.gvm = P.dout("gvm", [8, T, 128], BF16)
        self.oT = P.dout("oT", [2, 8, 128, T])
        self.mixT = P.dout("mixT", [D, T], BF16)
        self.S = Sch(nc, es)
        self.cst = P.sb(es, "cst", [128, 21, 128], F32)
        self.cstb = P.sb(es, "cstb", [128, 21, 128], BF16)
        self.junk = P.sb(es, "junk", [128, 8], F32)
        self.epsc = P.sb(es, "epsc", [128, 1], F32)
        S = self.S
        with nc.Block() as block:
            S.dma("sp", lambda e: e.dma_start(out=self.cst[:], in_=self.consts[:, :, :]), writes=["cst"])
            S.op("dve", lambda e: e.tensor_copy(out=self.cstb[:], in_=self.cst[:]), reads=["cst"], writes=["cstb"])
            S.op("pool", lambda e: e.memset(self.epsc[:], EPS), writes=["epsc"])
            S.emit(block)
        self.phase_gdn(0)
        es.close()
        return nc

    def build(self):
        nc = self.nc
        es = self.es
        P = self
        self.xT_in = P.din("xT", [D, T])
        self.c2 = P.din("c2", [128, KC, 2])
        self.consts = P.din("consts", [128, 21, 128])
        self.w_ada = {l: P.din("w_ada%d" % l, [D, 6 * D]) for l in self.layers}
        self.b_ada = {l: P.din("b_ada%d" % l, [128, 96]) for l in self.layers}
        self.nrm = {l: P.din("nrm%d" % l, [128, 2, KC]) for l in self.layers}
        self.w_in = {l: P.din("w_in%d" % l, [D, D_IN]) for l in self.layers}
        if self.upto != "inproj":
            self.w_out = {l: P.din("w_out%d" % l, [D, D]) for l in self.layers}
            self.w_ff1 = {l: P.din("w_ff1%d" % l, [D, D_FF]) for l in self.layers}
            self.w_ff2 = {l: P.din("w_ff2%d" % l, [D_FF, D]) for l in self.layers}
            self.nfin = P.din("nfin", [128, KC])
            self.outT = P.dout("outT", [D, L])
        if self.upto != "inproj":
            self.ropeT = P.din("ropeT", [128, 2, L])
            self.sink = {l: P.din("sink%d" % l, [128, 8]) for l in self.layers}
            self.alog = {l: P.din("alog%d" % l, [128, 16]) for l in self.layers}
            self.dtb = {l: P.din("dtb%d" % l, [128, 16]) for l in self.layers}
            self.convw = {l: P.din("convw%d" % l, [128, 24, 5]) for l in self.layers}
            self.gnorm = {l: P.din("gnorm%d" % l, [128, 1]) for l in self.layers}
        self.gq = P.dscr("gq", [8, 128, T], BF16)
        self.gk = P.dscr("gk", [8, 128, T], BF16)
        self.gkm = P.dscr("gkm", [8, T, 128], BF16)
        self.gvm = P.dscr("gvm", [8, T, 128], BF16)
        self.oT = P.dout("oT", [2, 8, 128, T]) if "oT" in [d[0] for d in self.dbg] else P.dscr("oT", [2, 8, 128, T])
        dn = [d[0] for d in self.dbg]
        self.xT = P.dout("xTs", [D, T]) if "xT" in dn else P.dscr("xTs", [D, T])
        self.zT = P.dscr("zT", [D_IN, T]) if "zT" not in [d[0] for d in self.dbg] else P.dout("zT", [D_IN, T])
        self.mixT = P.dout("mixT", [D, T], BF16) if "mixT" in dn else P.dscr("mixT", [D, T], BF16)
        self.S = Sch(nc, es)
        self.cst = P.sb(es, "cst", [128, 21, 128], F32)
        self.cstb = P.sb(es, "cstb", [128, 21, 128], BF16)
        self.junk = P.sb(es, "junk", [128, 8], F32)
        self.epsc = P.sb(es, "epsc", [128, 1], F32)
        self.mod = {l: P.sb(es, "mod%d" % l, [128, 96, 2], F32) for l in self.layers}
        self.gm1 = {l: P.sb(es, "gm1_%d" % l, [128, 2, KC, 2], F32) for l in self.layers}
        self.phase_init()
        for l in self.layers:
            self.phase_mod(l)
        for l in self.layers:
            self.phase_norm_inproj(l)
            if self.upto == "inproj":
                break
            self.phase_mixer(l)
            if not getattr(self, "skip_tail", False):
                self.phase_dense_tail(l)
        if self.upto == "all":
            self.phase_final()
        es.close()
        return nc


def make_consts():
    c = np.zeros((128, 21, 128), np.float32)
    p = np.arange(128)[:, None]
    f = np.arange(128)[None, :]
    c[:, 0, :] = (p == f)
    c[:, 1, :] = 1.0
    partner = np.where((np.arange(128) % 64) < 32, np.arange(128) + 32, np.arange(128) - 32)
    c[:, 2, :] = (p == partner[None, :])
    c[:, 3, :] = (p <= f)
    c[:, 4, :] = (p >= f)
    c[:, 5, :] = -(p < f).astype(np.float32)
    c[:, 6, :] = -(p > f).astype(np.float32)
    c[:, 7, :] = (p // 16 == f // 16)
    for lv, sz in enumerate((16, 32, 64)):
        c[:, 8 + lv, :] = (p // (2 * sz) == f // (2 * sz)) & (p // sz != f // sz)
    for d in range(2):
        c[:, 11 + d, :] = c[:, 5 + d, :] * c[:, 7, :]
        for k in range(4):
            c[:, 13 + d * 4 + k, :] = c[:, 6 - d, :] * c[:, 7 + k, :]
    return c


def rope_tables():
    pos = np.arange(L)
    row = (pos // 64).astype(np.float32)
    col = (pos % 64).astype(np.float32)
    inv = (10000.0 ** (-np.arange(0, 64, 2, dtype=np.float32) / 64.0)).astype(np.float32)
    ang = np.concatenate([row[:, None] * inv[None, :], col[:, None] * inv[None, :]], axis=-1).astype(np.float32)
    cos, sin = np.cos(ang).astype(np.float32), np.sin(ang).astype(np.float32)
    out = np.zeros((128, 2, L), np.float32)
    for m_ in range(128):
        idx = (m_ % 32) if m_ < 64 else 32 + ((m_ - 64) % 32)
        sign = -1.0 if (m_ % 64) < 32 else 1.0
        out[m_, 0, :] = cos[:, idx]
        out[m_, 1, :] = sign * sin[:, idx]
    return out


def fm_vec(v, nchunk):
    return np.ascontiguousarray(np.asarray(v, np.float32).reshape(nchunk, 128).T)


def core_inputs(inp, b, layers, upto="all"):
    m = {}
    xt = np.concatenate([inp["ctx"][b], inp["x"][b]], axis=0)
    m["xT"] = np.ascontiguousarray(xt.T)
    c2 = np.stack([fm_vec(inp["c"][b], KC), fm_vec(inp["c_ctx"], KC)], axis=-1)
    m["c2"] = np.ascontiguousarray(c2)
    m["consts"] = make_consts()
    for l in layers:
        m["w_ada%d" % l] = inp["w_ada"][l]
        m["b_ada%d" % l] = fm_vec(inp["b_ada"][l], 96)
        m["nrm%d" % l] = np.ascontiguousarray(np.stack([fm_vec(inp["norm_mix"][l], KC), fm_vec(inp["norm_ffn"][l], KC)], axis=1))
        m["w_in%d" % l] = inp["w_in"][l]
        if upto != "inproj":
            m["w_out%d" % l] = inp["w_out"][l]
            m["w_ff1%d" % l] = inp["w_ff1"][l]
            m["w_ff2%d" % l] = inp["w_ff2"][l]
    if upto != "inproj":
        m["nfin"] = fm_vec(inp["norm_final"], KC)
        m["ropeT"] = rope_tables()
        for l in layers:
            m["sink%d" % l] = np.ascontiguousarray(np.broadcast_to(inp["attn_sink"][l][None, :], (128, 8))).astype(np.float32)
            m["alog%d" % l] = np.ascontiguousarray(np.broadcast_to(inp["a_log"][l].reshape(1, 16), (128, 16))).astype(np.float32)
            m["dtb%d" % l] = np.ascontiguousarray(np.broadcast_to(inp["dt_bias"][l].reshape(1, 16), (128, 16))).astype(np.float32)
            cw = inp["conv_w"][l]
            m["convw%d" % l] = np.ascontiguousarray(cw.T.reshape(24, 128, 5).transpose(1, 0, 2)).astype(np.float32)
            m["gnorm%d" % l] = np.ascontiguousarray(inp["gdn_norm"][l].reshape(128, 1)).astype(np.float32)
    return m


_PROG = {}


def kernel(**inputs):
    inp = {k: np.asarray(v) for k, v in inputs.items()}
    layers = list(range(DEPTH))
    if "nc" not in _PROG:
        _PROG["nc"] = Prog(layers).build()
    nc = _PROG["nc"]
    n = 4
    maps = [core_inputs(inp, b, layers) for b in range(n)]
    res = run_bass_kernel_spmd(nc, maps, core_ids=list(range(n)))
    out = np.stack([res.results[b]["outT"].T for b in range(n)], axis=0)
    return np.ascontiguousarray(out.astype(np.float32))
```

```python
import math
from contextlib import ExitStack
import numpy as np
import concourse.bass as bass
import concourse.mybir as mybir
from concourse.bass_utils import run_bass_kernel_spmd

F32 = mybir.dt.float32
BF16 = mybir.dt.bfloat16
ALU = mybir.AluOpType
AF = mybir.ActivationFunctionType

D = 2048
KC = 16
L = 2048
LC = 256
T = L + LC
NT = T // 128
DEPTH = 4
D_IN = 5664
D_FF = 8192
EPS = 1e-6
TBLK = [(0, 256), (256, 512), (768, 512), (1280, 512), (1792, 512)]
NCORES = 8


class Op:
    __slots__ = ("e", "fn", "deps", "idx", "signal", "dma", "dsem", "dval", "prewait")

    def __init__(self, e, fn, deps, idx, dma=False):
        self.e, self.fn, self.deps, self.idx, self.dma = e, fn, deps, idx, dma
        self.signal = False
        self.dsem = None
        self.dval = 0
        self.prewait = None


class Sch:
    ENG = ("pe", "act", "dve", "pool", "sp")
    NDS = 12

    def __init__(self, nc, es):
        self.nc = nc
        self.eng = {"pe": nc.tensor, "act": nc.scalar, "dve": nc.vector, "pool": nc.gpsimd, "sp": nc.sync}
        self.sem = {e: es.enter_context(nc.semaphore("sem_" + e)) for e in self.ENG}
        self.cnt = {e: 0 for e in self.ENG}
        self.dsem = {q: [es.enter_context(nc.semaphore("dsem_%s_%d" % (q, i))) for i in range(self.NDS)]
                     for q in ("sp", "pool", "act")}
        self.dcnt = {q: 0 for q in ("sp", "pool", "act")}
        self.reset()

    def reset(self):
        self.ops = {e: [] for e in self.ENG}
        self.res = {}

    def _mk(self, e, fn, reads, writes, dma):
        deps = set()
        for r in reads:
            st = self.res.get(r)
            if st is not None and st[0] is not None:
                deps.add(st[0])
        for w in writes:
            st = self.res.get(w)
            if st is not None:
                if st[0] is not None:
                    deps.add(st[0])
                deps.update(st[1].values())
        if e == "pe" and not dma:
            deps = {d for d in deps if d.dma or d.e != "pe"}
        o = Op(e, fn, deps, len(self.ops[e]), dma)
        self.ops[e].append(o)
        for d in deps:
            d.signal = True
        for r in reads:
            st = self.res.setdefault(r, [None, {}])
            st[1][e + ("d" if dma else "")] = o
        for w in writes:
            self.res[w] = [o, {}]
        return o

    limit = None
    nrec = 0

    def op(self, e, fn, reads=(), writes=()):
        if self.limit is not None:
            self.nrec += 1
            if self.nrec > self.limit:
                return None
        return self._mk(e, fn, reads, writes, False)

    def dma(self, q, fn, reads=(), writes=()):
        if self.limit is not None:
            self.nrec += 1
            if self.nrec > self.limit:
                return None
        o = self._mk(q, fn, reads, writes, True)
        m = self.dcnt[q]
        self.dcnt[q] += 1
        o.dsem = self.dsem[q][m % self.NDS]
        o.dval = 16 * (m // self.NDS + 1)
        o.prewait = 16 * (m // self.NDS)
        return o

    def emit(self, block, final_wait=True):
        val = {}
        for e in self.ENG:
            c = self.cnt[e]
            pend = []
            for o in self.ops[e]:
                if o.dma:
                    continue
                pend.append(o)
                if o.signal:
                    c += 1
                    for p in pend:
                        val[p] = c
                    pend = []
            self.cnt[e] = c
        dtot = {}

        def run(e):
            def body(eng):
                waited = {}

                def w(sem, v):
                    k = id(sem)
                    if waited.get(k, -1) >= v:
                        return
                    waited[k] = v
                    eng.wait_ge(sem, v)
                for o in self.ops[e]:
                    for d in o.deps:
                        if d.dma:
                            w(d.dsem, d.dval)
                        else:
                            w(self.sem[d.e], val[d])
                    if o.dma:
                        if o.prewait:
                            w(o.dsem, o.prewait)
                        ins = o.fn(eng)
                        ins.then_inc(o.dsem, 16)
                        dtot[id(o.dsem)] = (o.dsem, o.dval)
                    else:
                        ins = o.fn(eng)
                        if o.signal:
                            ins.then_inc(self.sem[e], 1)
                if e == "sp" and final_wait:
                    pass
            return body

        for e in ("pe", "act", "dve", "pool"):
            if self.ops[e]:
                getattr(block, {"pe": "tensor", "act": "scalar", "dve": "vector", "pool": "gpsimd"}[e])(run(e))
        spbody = run("sp")

        def spfull(eng):
            spbody(eng)
            for q in ("pool", "act"):
                for o in self.ops[q]:
                    if o.dma:
                        dtot[id(o.dsem)] = (o.dsem, max(o.dval, dtot.get(id(o.dsem), (None, 0))[1]))
            for sem, v in dtot.values():
                eng.wait_ge(sem, v)
        block.sync(spfull)
        self.reset()


class Prog:
    def __init__(self, layers, upto="all", dbg=()):
        self.layers = layers
        self.upto = upto
        self.dbg = dbg
        self.nc = bass.Bass("TRN2", target_bir_lowering=False)
        self.es = ExitStack()
        self.uid = 0

    def din(self, name, shape, dt=F32):
        return self.nc.dram_tensor(name, list(shape), dt, kind="ExternalInput").ap()

    def dout(self, name, shape, dt=F32):
        return self.nc.dram_tensor(name, list(shape), dt, kind="ExternalOutput").ap()

    def dscr(self, name, shape, dt=F32):
        return self.nc.dram_tensor(name, list(shape), dt).ap()

    def sb(self, es, name, shape, dt):
        self.uid += 1
        return es.enter_context(self.nc.sbuf_tensor("%s_%d" % (name, self.uid), list(shape), dt))

    def ps(self, es, name, shape, dt=F32):
        self.uid += 1
        return es.enter_context(self.nc.psum_tensor("%s_%d" % (name, self.uid), list(shape), dt))

    def build(self):
        nc = self.nc
        es = self.es
        P = self
        self.xT_in = P.din("xT", [D, T])
        self.c2 = P.din("c2", [128, KC, 2])
        self.consts = P.din("consts", [128, 21, 128])
        self.ropeT = P.din("ropeT", [128, 2, L])
        self.w_ada = {l: P.din("w_ada%d" % l, [D, 6 * D]) for l in self.layers}
        self.b_ada = {l: P.din("b_ada%d" % l, [128, 96]) for l in self.layers}
        self.nrm = {l: P.din("nrm%d" % l, [128, 2, KC]) for l in self.layers}
        self.w_in = {l: P.din("w_in%d" % l, [D, D_IN]) for l in self.layers}
        self.w_out = {l: P.din("w_out%d" % l, [D, D]) for l in self.layers}
        self.w_ff1 = {l: P.din("w_ff1%d" % l, [D, D_FF]) for l in self.layers}
        self.w_ff2 = {l: P.din("w_ff2%d" % l, [D_FF, D]) for l in self.layers}
        self.nfin = P.din("nfin", [128, KC])
        self.outT = P.dout("outT", [D, L])
        self.xT = P.dscr("xTs", [D, T])
        self.zqkT = P.dscr("zqkT", [1280, T], BF16)
        self.zv = P.dscr("zv", [T, 256], BF16)
        self.zg = P.dscr("zg", [3072, T])
        self.zgate = P.dscr("zgate", [1024, T], BF16)
        self.zab = P.dscr("zab", [T, 32])
        self.mixT = P.dscr("mixT", [D, T], BF16)
        self.aT = P.dscr("aT", [D_FF, T], BF16)
        self.dbgout = {}
        for name, shape, dt in self.dbg:
            self.dbgout[name] = P.dout("dbg_" + name, shape, dt)

        self.S = Sch(nc, es)
        self.cst = P.sb(es, "cst", [128, 21, 128], F32)
        self.cstb = P.sb(es, "cstb", [128, 21, 128], BF16)
        self.mod = {l: P.sb(es, "mod%d" % l, [128, 96, 2], F32) for l in self.layers}
        self.gm1 = {l: P.sb(es, "gm1_%d" % l, [128, 2, KC, 2], F32) for l in self.layers}

        self.phase_init()
        for l in self.layers:
            self.phase_mod(l)
        for l in self.layers:
            self.phase_norm_inproj(l)
            if self.upto == "inproj":
                break
            if not getattr(self, "skip_tail", False):
                self.phase_dense_tail(l)
        if self.upto == "all":
            self.phase_final()
        return nc

    def phase_init(self):
        nc, S = self.nc, self.S
        with ExitStack() as es, nc.Block() as block:
            S.dma("sp", lambda e: e.dma_start(out=self.cst[:], in_=self.consts[:, :, :]), writes=["cst"])
            S.op("dve", lambda e: e.tensor_copy(out=self.cstb[:], in_=self.cst[:]), reads=["cst"], writes=["cstb"])
            S.op("pool", lambda e: e.memset(self.epsc[:], EPS), writes=["epsc"])
            for i in range(4):
                S.dma("sp", lambda e, i=i: e.dma_start(out=self.xT[i * 512:(i + 1) * 512, :],
                                                      in_=self.xT_in[i * 512:(i + 1) * 512, :]))
            S.emit(block)

    def phase_mod(self, l):
        nc, S = self.nc, self.S
        with ExitStack() as es, nc.Block() as block:
            c2 = self.sb(es, "c2", [128, KC, 2], F32)
            s2 = self.sb(es, "s2", [128, KC, 2], BF16)
            bt = self.sb(es, "bt", [128, 96], F32)
            nr = self.sb(es, "nr", [128, 2, KC], F32)
            NS = 3
            wsl = [self.sb(es, "wad", [128, KC, 512], BF16) for _ in range(NS)]
            pm = self.ps(es, "pm", [128, 96, 2], F32)
            S.dma("sp", lambda e: e.dma_start(out=c2[:], in_=self.c2[:, :, :]), writes=["c2"])
            S.dma("sp", lambda e: e.dma_start(out=bt[:], in_=self.b_ada[l][:, :]), writes=["bt"])
            S.dma("sp", lambda e: e.dma_start(out=nr[:], in_=self.nrm[l][:, :, :]), writes=["nr"])
            S.op("act", lambda e: e.activation(out=s2[:], in_=c2[:], func=AF.Silu), reads=["c2"], writes=["s2"])
            wv = self.w_ada[l].rearrange("(kc p) n -> p kc n", p=128)
            for s in range(24):
                buf = wsl[s % NS]
                key = "wad%d" % (s % NS)
                S.dma("pool", lambda e, s=s, buf=buf: e.dma_start(out=buf[:], in_=wv[:, :, s * 512:(s + 1) * 512]),
                      writes=[key])
                for j in range(4):
                    cc = s * 4 + j
                    for kc in range(KC):
                        S.op("pe", lambda e, cc=cc, kc=kc, j=j, buf=buf: e.matmul(
                            pm[:, cc, :], buf[:, kc, j * 128:(j + 1) * 128], s2[:, kc, :],
                            start=(kc == 0), stop=(kc == KC - 1)), reads=[key, "s2"], writes=["pm"])
            mod = self.mod[l]
            S.op("dve", lambda e: e.tensor_tensor(out=mod[:], in0=pm[:], in1=bt[:].unsqueeze(2).to_broadcast([128, 96, 2]),
                                                  op=ALU.add), reads=["pm", "bt"], writes=["mod"])
            gm = self.gm1[l]
            for n, which in ((0, 1), (1, 4)):
                S.op("dve", lambda e, n=n, which=which: e.tensor_scalar(
                    out=gm[:, n, :, :], in0=mod[:, which * 16:(which + 1) * 16, :], scalar1=1.0, scalar2=None,
                    op0=ALU.add), reads=["mod"], writes=["gm%d" % n])
                S.op("dve", lambda e, n=n: e.tensor_tensor(
                    out=gm[:, n, :, :], in0=gm[:, n, :, :], in1=nr[:, n, :].unsqueeze(2).to_broadcast([128, KC, 2]),
                    op=ALU.mult), reads=["gm%d" % n, "nr"], writes=["gm%d" % n])
            S.emit(block)

    def norm_to_hT(self, es, l, n, hT, shift_which):
        nc, S = self.nc, self.S
        xb = [self.sb(es, "xb", [128, KC, 256], F32) for _ in range(2)]
        sq = [self.sb(es, "sq", [128, 256], F32) for _ in range(2)]
        rs = [self.sb(es, "rs", [128, 256], F32) for _ in range(2)]
        tmp = [self.sb(es, "tmp", [128, 256], F32) for _ in range(3)]
        pst = [self.ps(es, "pst", [128, 256], F32) for _ in range(2)]
        xv = self.xT.rearrange("(c p) t -> p c t", p=128)
        gm, mod = self.gm1[l], self.mod[l]
        ones = self.cst[:, 1, :]
        for b in range(T // 256):
            j = 1 if b == 0 else 0
            X, Q, R, PS = xb[b % 2], sq[b % 2], rs[b % 2], pst[b % 2]
            kx, kq, kr, kp = "xb%d" % (b % 2), "sq%d" % (b % 2), "rs%d" % (b % 2), "pst%d" % (b % 2)
            S.dma("sp", lambda e, X=X, b=b: e.dma_start(out=X[:], in_=xv[:, :, b * 256:(b + 1) * 256]), writes=[kx])
            for c in range(KC):
                S.op("act", lambda e, X=X, Q=Q, c=c: e.activation(out=Q[:], in_=X[:, c, :], func=AF.Square),
                     reads=[kx], writes=[kq])
                S.op("pe", lambda e, Q=Q, PS=PS, c=c: e.matmul(PS[:], ones, Q[:], start=(c == 0), stop=(c == KC - 1)),
                     reads=[kq, "cst"], writes=[kp])
            S.op("act", lambda e, R=R, PS=PS: e.activation(out=R[:], in_=PS[:], func=AF.Ln, scale=1.0 / D, bias=self.epsc[:, 0:1]),
                 reads=[kp], writes=[kr])
            S.op("act", lambda e, R=R: e.activation(out=R[:], in_=R[:], func=AF.Exp, scale=-0.5), reads=[kr], writes=[kr])
            for c in range(KC):
                tb = tmp[c % 3]
                kt = "tmp%d" % (c % 3)
                S.op("dve", lambda e, X=X, R=R, tb=tb, c=c: e.tensor_tensor(out=tb[:], in0=X[:, c, :], in1=R[:],
                                                                          op=ALU.mult), reads=[kx, kr], writes=[kt])
                S.op("act", lambda e, tb=tb, c=c, b=b, j=j: e.activation(
                    out=hT[:, c, b * 256:(b + 1) * 256], in_=tb[:], func=AF.Identity,
                    scale=gm[:, n, c, j:j + 1], bias=mod[:, shift_which * 16 + c, j:j + 1]),
                    reads=[kt, "gm%d" % n, "mod"], writes=["hT"])

    def gemm(self, es, W, ncols, act, actkey, epi, nkc=KC, wname="w", tblk=TBLK):
        S = self.S
        NS = 3
        if not hasattr(self, "_gc"):
            self._gc = {}
        k1, k2 = (id(es), wname, nkc), (id(es), "ps")
        if k1 not in self._gc:
            self._gc[k1] = [self.sb(es, wname, [128, nkc, 256], BF16) for _ in range(NS)]
        if k2 not in self._gc:
            self._gc[k2] = [self.ps(es, "pg", [128, 512], F32) for _ in range(4)]
        wsl, psb = self._gc[k1], self._gc[k2]
        Wv = W.rearrange("(kc p) n -> p kc n", p=128)
        nslab = (ncols + 255) // 256
        pi = 0
        KG = 4
        for s in range(nslab):
            c0 = s * 256
            cw = min(256, ncols - c0)
            buf = wsl[s % NS]
            for kg in range(nkc // KG):
                S.dma("pool", lambda e, buf=buf, kg=kg, c0=c0, cw=cw: e.dma_start(
                    out=buf[:, kg * KG:(kg + 1) * KG, 0:cw], in_=Wv[:, kg * KG:(kg + 1) * KG, c0:c0 + cw]),
                    writes=["%s%d_%d" % (wname, s % NS, kg)])
            for g in range((cw + 127) // 128):
                m = min(128, cw - g * 128)
                for bi, (t0, tn) in enumerate(tblk):
                    PS = psb[pi % 4]
                    pk = "pg%d" % (pi % 4)
                    pi += 1
                    for kc in range(nkc):
                        S.op("pe", lambda e, PS=PS, buf=buf, kc=kc, g=g, m=m, t0=t0, tn=tn: e.matmul(
                            PS[0:m, 0:tn], buf[:, kc, g * 128:g * 128 + m], act[:, kc, t0:t0 + tn],
                            start=(kc == 0), stop=(kc == nkc - 1)),
                            reads=["%s%d_%d" % (wname, s % NS, kc // KG), actkey], writes=[pk])
                    epi(c0 + g * 128, m, bi, t0, tn, PS, pk)

    def phase_norm_inproj(self, l):
        nc, S = self.nc, self.S
        with ExitStack() as es:
            hT = self.sb(es, "hT", [128, KC, T], BF16)
            with ExitStack() as es2, nc.Block() as block:
                self.norm_to_hT(es2, l, 0, hT, 0)
                S.emit(block)
            with ExitStack() as es2, nc.Block() as block:
                stg = [self.sb(es2, "stg", [128, 512], F32) for _ in range(4)]
                cnt = [0]

                def epi(c0, m, bi, t0, tn, PS, pk):
                    i = cnt[0] % 4
                    cnt[0] += 1
                    st = stg[i]
                    S.op("act", lambda e: e.activation(out=st[0:m, 0:tn], in_=PS[0:m, 0:tn], func=AF.Copy),
                         reads=[pk], writes=["stg%d" % i])
                    S.dma("sp", lambda e: e.dma_start(out=self.zT[c0:c0 + m, t0:t0 + tn], in_=st[0:m, 0:tn]),
                          reads=["stg%d" % i])
                self.gemm(es2, self.w_in[l], D_IN, hT, "hT", epi, wname="win")
                S.emit(block)

    def resid_epi(self, es, l, which):
        S = self.S
        NXL = 6
        xl = [self.sb(es, "xl", [128, 512], F32) for _ in range(NXL)]
        cnt = [0]
        mod = self.mod[l]

        def epi(c0, m, bi, t0, tn, PS, pk):
            i = cnt[0] % NXL
            cnt[0] += 1
            xb = xl[i]
            j = 1 if t0 < LC else 0
            c = c0 // 128
            xk = "x_%d_%d" % (c, bi)
            S.dma("act", lambda e: e.dma_start(out=xb[:, 0:tn], in_=self.xT[c0:c0 + 128, t0:t0 + tn]),
                  reads=[xk], writes=["xl%d" % i])
            S.op("dve", lambda e: e.scalar_tensor_tensor(
                out=xb[:, 0:tn], in0=PS[:, 0:tn], scalar=mod[:, which * 16 + c, j:j + 1], in1=xb[:, 0:tn],
                op0=ALU.mult, op1=ALU.add), reads=[pk, "xl%d" % i, "mod"], writes=["xl%d" % i])
            S.dma("sp", lambda e: e.dma_start(out=self.xT[c0:c0 + 128, t0:t0 + tn], in_=xb[:, 0:tn]),
                  reads=["xl%d" % i], writes=[xk])
        return epi

    def phase_dense_tail(self, l):
        nc, S = self.nc, self.S
        with ExitStack() as es, nc.Block() as block:
            mx = self.sb(es, "mx", [128, KC, T], BF16)
            mv = self.mixT.rearrange("(c p) t -> p c t", p=128)
            for c4 in range(4):
                S.dma("sp", lambda e, c4=c4: e.dma_start(out=mx[:, c4 * 4:(c4 + 1) * 4, :], in_=mv[:, c4 * 4:(c4 + 1) * 4, :]),
                      writes=["mx%d" % c4])
            S.op("pool", lambda e: e.memset(self.junk[:], 0.0), reads=["mx0", "mx1", "mx2", "mx3"], writes=["mx"])
            self.gemm(es, self.w_out[l], D, mx, "mx", self.resid_epi(es, l, 2), wname="wout")
            S.emit(block)
        with ExitStack() as es:
            hT = self.sb(es, "hnT", [128, KC, T], BF16)
            with ExitStack() as es2, nc.Block() as block:
                self.norm_to_hT(es2, l, 1, hT, 3)
                S.emit(block)
            with ExitStack() as es2, nc.Block() as block:
                NK2 = 8
                aT = self.sb(es2, "aT", [128, NK2, T], BF16)
                rl = [self.sb(es2, "rl", [128, 512], F32) for _ in range(3)]
                rc = [0]
                repi = self.resid_epi(es2, l, 5)
                for q in range(D_FF // (NK2 * 128)):
                    def epi1(c0, m, bi, t0, tn, PS, pk):
                        i = rc[0] % 3
                        rc[0] += 1
                        r = rl[i]
                        S.op("act", lambda e: e.activation(out=r[:, 0:tn], in_=PS[:, 0:tn], func=AF.Relu),
                             reads=[pk], writes=["rl%d" % i])
                        S.op("pool", lambda e: e.tensor_tensor(out=aT[:, c0 // 128, t0:t0 + tn], in0=r[:, 0:tn],
                                                               in1=r[:, 0:tn], op=ALU.mult),
                             reads=["rl%d" % i], writes=["aT"])
                    self.gemm(es2, self.w_ff1[l][:, q * NK2 * 128:(q + 1) * NK2 * 128], NK2 * 128, hT, "hnT", epi1,
                              wname="wf1")
                    self.gemm(es2, self.w_ff2[l][q * NK2 * 128:(q + 1) * NK2 * 128, :], D, aT, "aT", repi, nkc=NK2,
                              wname="wf2")
                S.emit(block)

    def phase_final(self):
        nc, S = self.nc, self.S
        with ExitStack() as es, nc.Block() as block:
            xb = [self.sb(es, "fxb", [128, KC, 256], F32) for _ in range(2)]
            sq = [self.sb(es, "fsq", [128, 256], F32) for _ in range(2)]
            rs = [self.sb(es, "frs", [128, 256], F32) for _ in range(2)]
            ob = [self.sb(es, "fob", [128, KC, 256], F32) for _ in range(2)]
            nf = self.sb(es, "nf", [128, KC], F32)
            pst = [self.ps(es, "fps", [128, 256], F32) for _ in range(2)]
            xv = self.xT.rearrange("(c p) t -> p c t", p=128)
            ov = self.outT.rearrange("(c p) t -> p c t", p=128)
            ones = self.cst[:, 1, :]
            S.dma("sp", lambda e: e.dma_start(out=nf[:], in_=self.nfin[:, :]), writes=["nf"])
            for b in range(L // 256):
                X, Q, R, PS, O = xb[b % 2], sq[b % 2], rs[b % 2], pst[b % 2], ob[b % 2]
                kx, kq, kr, kp, ko = ["%s%d" % (n, b % 2) for n in ("fx", "fq", "fr", "fp", "fo")]
                S.dma("sp", lambda e, X=X, b=b: e.dma_start(out=X[:], in_=xv[:, :, LC + b * 256:LC + (b + 1) * 256]),
                      writes=[kx])
                for c in range(KC):
                    S.op("act", lambda e, X=X, Q=Q, c=c: e.activation(out=Q[:], in_=X[:, c, :], func=AF.Square),
                         reads=[kx], writes=[kq])
                    S.op("pe", lambda e, Q=Q, PS=PS, c=c: e.matmul(PS[:], ones, Q[:], start=(c == 0), stop=(c == KC - 1)),
                         reads=[kq], writes=[kp])
                S.op("act", lambda e, R=R, PS=PS: e.activation(out=R[:], in_=PS[:], func=AF.Ln, scale=1.0 / D,
                                                              bias=self.epsc[:, 0:1]), reads=[kp], writes=[kr])
                S.op("act", lambda e, R=R: e.activation(out=R[:], in_=R[:], func=AF.Exp, scale=-0.5), reads=[kr], writes=[kr])
                for c in range(KC):
                    S.op("dve", lambda e, X=X, R=R, O=O, c=c: e.scalar_tensor_tensor(
                        out=O[:, c, :], in0=X[:, c, :], scalar=nf[:, c:c + 1], in1=R[:], op0=ALU.mult, op1=ALU.mult),
                        reads=[kx, kr, "nf"], writes=[ko])
                S.dma("sp", lambda e, O=O, b=b: e.dma_start(out=ov[:, :, b * 256:(b + 1) * 256], in_=O[:]), reads=[ko])
            S.emit(block)
    def join(self, keys, newkey):
        self.S.op("pool", lambda e: e.memset(self.junk[:], 0.0), reads=list(keys), writes=[newkey])

    def phase_attn(self, l):
        nc, S = self.nc, self.S
        scale = 128 ** -0.5
        cstb = self.cstb
        with ExitStack() as es, nc.Block() as block:
            rope = self.sb(es, "rope", [128, 2, L], F32)
            snk = self.sb(es, "snk", [128, 8], F32)
            esk = self.sb(es, "esk", [128, 8], F32)
            raw = [self.sb(es, "raw", [128, T], F32) for _ in range(2)]
            rb = self.sb(es, "rb", [128, L], BF16)
            t1 = [self.sb(es, "t1", [128, 512], F32) for _ in range(2)]
            t2 = [self.sb(es, "t2", [128, 512], F32) for _ in range(2)]
            kT = self.sb(es, "kT", [128, T], BF16)
            qT = self.sb(es, "qT", [128, T], BF16)
            vT = self.sb(es, "vT", [128, T], BF16)
            V = self.sb(es, "V", [128, NT, 128], BF16)
            PTc = self.sb(es, "PTc", [128, 2, T], BF16)
            PTl = self.sb(es, "PTl", [128, 16, 3, 128], BF16)
            rc = [self.sb(es, "rc", [128, 128], F32) for _ in range(2)]
            ost = [self.sb(es, "ost", [128, 128], BF16) for _ in range(3)]
            psS = [self.ps(es, "psS", [128, 512], F32) for _ in range(2)]
            psO = [self.ps(es, "psO", [128, 128], F32) for _ in range(2)]
            psD = [self.ps(es, "psD", [128, 128], F32) for _ in range(2)]
            psT = [self.ps(es, "psT", [128, 128], BF16) for _ in range(2)]
            S.dma("sp", lambda e: e.dma_start(out=rope[:], in_=self.ropeT[:, :, :]), writes=["rope"])
            S.dma("sp", lambda e: e.dma_start(out=snk[:], in_=self.sink[l][:, :]), writes=["snk"])
            S.op("act", lambda e: e.activation(out=esk[:], in_=snk[:], func=AF.Exp), reads=["snk"], writes=["esk"])
            cn = {"raw": 0, "ps": 0, "t": 0, "o": 0, "ost": 0}

            def load_rope(row0, dst, dkey):
                ri = cn["raw"] % 2
                cn["raw"] += 1
                R, rk = raw[ri], "raw%d" % ri
                S.dma("sp", lambda e: e.dma_start(out=R[:], in_=self.zT[row0:row0 + 128, :]), writes=[rk])
                S.op("act", lambda e: e.activation(out=dst[:, 0:LC], in_=R[:, 0:LC], func=AF.Copy), reads=[rk],
                     writes=[dkey + "c"])
                S.op("act", lambda e: e.activation(out=rb[:], in_=R[:, LC:T], func=AF.Copy), reads=[rk], writes=["rb"])
                for blk in range(4):
                    a, b_ = blk * 512, (blk + 1) * 512
                    pi = cn["ps"] % 2
                    cn["ps"] += 1
                    ti = cn["t"] % 2
                    cn["t"] += 1
                    PS, pk = psS[pi], "psS%d" % pi
                    S.op("pe", lambda e, PS=PS, a=a, b_=b_: e.matmul(PS[:, :], cstb[:, 2, :], rb[:, a:b_], start=True, stop=True),
                         reads=["rb", "cstb"], writes=[pk])
                    S.op("dve", lambda e, a=a, b_=b_, ti=ti: e.tensor_tensor(out=t1[ti][:], in0=R[:, LC + a:LC + b_],
                                                                           in1=rope[:, 0, a:b_], op=ALU.mult),
                         reads=[rk, "rope"], writes=["t1%d" % ti])
                    S.op("dve", lambda e, PS=PS, a=a, b_=b_, ti=ti: e.tensor_tensor(out=t2[ti][:], in0=PS[:, :],
                                                                                  in1=rope[:, 1, a:b_], op=ALU.mult),
                         reads=[pk, "rope"], writes=["t2%d" % ti])
                    S.op("pool", lambda e, a=a, b_=b_, ti=ti: e.tensor_tensor(out=dst[:, LC + a:LC + b_], in0=t1[ti][:],
                                                                            in1=t2[ti][:], op=ALU.add),
                         reads=["t1%d" % ti, "t2%d" % ti], writes=[dkey + str(blk)])
                self.join([dkey + "c"] + [dkey + str(i) for i in range(4)], dkey)

            for g in range(2):
                load_rope(1024 + g * 128, kT, "kT")
                ri = cn["raw"] % 2
                cn["raw"] += 1
                R, rk = raw[ri], "raw%d" % ri
                S.dma("sp", lambda e, R=R, g=g: e.dma_start(out=R[:], in_=self.zT[1280 + g * 128:1408 + g * 128, :]), writes=[rk])
                S.op("act", lambda e, R=R: e.activation(out=vT[:], in_=R[:], func=AF.Copy), reads=[rk], writes=["vT"])
                for t in range(NT):
                    PT_, ptk = psT[t % 2], "psT%d" % (t % 2)
                    S.op("pe", lambda e, PT_=PT_, t=t: e.transpose(out=PT_[:, :], in_=vT[:, t * 128:(t + 1) * 128],
                                                                   identity=cstb[:, 0, :]), reads=["vT", "cstb"], writes=[ptk])
                    S.op("dve", lambda e, PT_=PT_, t=t: e.tensor_copy(out=V[:, t, :], in_=PT_[:, :]), reads=[ptk],
                         writes=["V%d" % t])
                self.join(["V%d" % t for t in range(NT)], "V")
                for hq in range(4):
                    h = g * 4 + hq
                    load_rope(h * 128, qT, "qT")
                    for ck in range(2):
                        for (t0, tn) in TBLK:
                            pi = cn["ps"] % 2
                            cn["ps"] += 1
                            PS, pk = psS[pi], "psS%d" % pi
                            S.op("pe", lambda e, PS=PS, ck=ck, t0=t0, tn=tn: e.matmul(
                                PS[:, 0:tn], kT[:, ck * 128:(ck + 1) * 128], qT[:, t0:t0 + tn], start=True, stop=True),
                                reads=["kT", "qT"], writes=[pk])
                            S.op("act", lambda e, PS=PS, ck=ck, t0=t0, tn=tn: e.activation(
                                out=PTc[:, ck, t0:t0 + tn], in_=PS[:, 0:tn], func=AF.Exp, scale=scale),
                                reads=[pk], writes=["PTc"])
                    for j in range(16):
                        lo, hi = max(j - 1, 0), min(j + 1, 15)
                        nq = hi - lo + 1
                        pi = cn["ps"] % 2
                        cn["ps"] += 1
                        PS, pk = psS[pi], "psS%d" % pi
                        S.op("pe", lambda e, PS=PS, j=j, lo=lo, nq=nq: e.matmul(
                            PS[:, 0:nq * 128], kT[:, LC + j * 128:LC + (j + 1) * 128],
                            qT[:, LC + lo * 128:LC + (lo + nq) * 128], start=True, stop=True),
                            reads=["kT", "qT"], writes=[pk])
                        r0 = lo - j + 1
                        S.op("act", lambda e, PS=PS, j=j, r0=r0, nq=nq: e.activation(
                            out=PTl[:, j, r0:r0 + nq, :], in_=PS[:, 0:nq * 128].rearrange("p (a b) -> p a b", b=128),
                            func=AF.Exp, scale=scale), reads=[pk], writes=["PTl"])
                        if j >= 1:
                            S.op("pool", lambda e, j=j: e.tensor_tensor(out=PTl[:, j, 0, :], in0=PTl[:, j, 0, :],
                                                                        in1=cstb[:, 3, :], op=ALU.mult),
                                 reads=["PTl", "cstb"], writes=["PTl"])
                        if j <= 14:
                            S.op("pool", lambda e, j=j: e.tensor_tensor(out=PTl[:, j, 2, :], in0=PTl[:, j, 2, :],
                                                                        in1=cstb[:, 4, :], op=ALU.mult),
                                 reads=["PTl", "cstb"], writes=["PTl"])
                    for qt in range(NT):
                        contrib = [(ck, PTc[:, ck, qt * 128:(qt + 1) * 128]) for ck in range(2)]
                        if qt >= 2:
                            n = qt - 2
                            for j in range(max(n - 1, 0), min(n + 1, 15) + 1):
                                contrib.append((2 + j, PTl[:, j, n - j + 1, :]))
                        oi = cn["o"] % 2
                        cn["o"] += 1
                        O, Dn = psO[oi], psD[oi]
                        for idx, (vt, rhs) in enumerate(contrib):
                            st_, sp_ = (idx == 0), (idx == len(contrib) - 1)
                            S.op("pe", lambda e, O=O, vt=vt, rhs=rhs, st_=st_, sp_=sp_: e.matmul(
                                O[:, :], V[:, vt, :], rhs, start=st_, stop=sp_), reads=["V", "PTc", "PTl"], writes=["psO%d" % oi])
                            S.op("pe", lambda e, Dn=Dn, rhs=rhs, st_=st_, sp_=sp_: e.matmul(
                                Dn[:, :], cstb[:, 1, :], rhs, start=st_, stop=sp_), reads=["PTc", "PTl"], writes=["psD%d" % oi])
                        RC = rc[oi]
                        S.op("dve", lambda e, RC=RC, Dn=Dn, h=h: e.tensor_scalar(out=RC[:], in0=Dn[:, :], scalar1=esk[:, h:h + 1],
                                                                             scalar2=None, op0=ALU.add),
                             reads=["psD%d" % oi, "esk"], writes=["rc%d" % oi])
                        S.op("dve", lambda e, RC=RC: e.reciprocal(out=RC[:], in_=RC[:]), reads=["rc%d" % oi], writes=["rc%d" % oi])
                        si = cn["ost"] % 3
                        cn["ost"] += 1
                        OS = ost[si]
                        S.op("dve", lambda e, OS=OS, O=O, RC=RC: e.tensor_tensor(out=OS[:], in0=O[:, :], in1=RC[:], op=ALU.mult),
                             reads=["psO%d" % oi, "rc%d" % oi], writes=["ost%d" % si])
                        S.dma("pool", lambda e, OS=OS, h=h, qt=qt: e.dma_start(
                            out=self.mixT[h * 128:(h + 1) * 128, qt * 128:(qt + 1) * 128], in_=OS[:]), reads=["ost%d" % si])
            S.emit(block)
    def phase_gdn(self, l):
        nc, S = self.nc, self.S
        cst, cstb = self.cst, self.cstb
        ORD = [list(range(NT)), [1, 0] + list(range(NT - 1, 1, -1))]
        with ExitStack() as eso:
            G = self.sb(eso, "G", [128, NT, 16], F32)
            BETA = self.sb(eso, "BETA", [128, NT, 16], F32)
            GC = self.sb(eso, "GC", [128, NT, 16], F32)
            TOT = self.sb(eso, "TOT", [128, NT, 16], F32)
            E = self.sb(eso, "E", [128, NT, 16], F32)
            Fd = self.sb(eso, "Fd", [128, NT, 16], F32)
            BE = self.sb(eso, "BE", [128, NT, 16], F32)
            GL = self.sb(eso, "GL", [128, NT, 16], F32)
            onec = self.sb(eso, "onec", [128, 1], F32)
            with ExitStack() as es, nc.Block() as block:
                abT = self.sb(es, "abT", [32, T], F32)
                abm = self.sb(es, "abm", [128, NT, 32], F32)
                al = self.sb(es, "al", [128, 16], F32)
                dtb = self.sb(es, "dtb", [128, 16], F32)
                cw = self.sb(es, "cw", [128, 24, 5], F32)
                xg = self.sb(es, "xg", [128, NT, 16], F32)
                Zp = [self.sb(es, "Zp", [128, T + 8], F32) for _ in range(2)]
                acc = [self.sb(es, "acc", [128, T], F32) for _ in range(2)]
                sl_ = [self.sb(es, "sl", [128, T], F32) for _ in range(2)]
                sq = [self.sb(es, "gsq", [128, 512], F32) for _ in range(2)]
                ri = [self.sb(es, "gri", [128, 512], F32) for _ in range(2)]
                ob = [self.sb(es, "gob", [128, T], BF16) for _ in range(2)]
                tm = [self.sb(es, "gtm", [128, NT, 128], BF16) for _ in range(2)]
                pabA = self.ps(es, "pabA", [128, 512], F32)
                pabB = self.ps(es, "pabB", [128, 512], F32)
                pgcb = self.ps(es, "pgc", [128, 512], F32)
                ptotb = self.ps(es, "ptot", [128, 512], F32)
                pgc = pgcb[:, 0:NT * 16].rearrange("p (n c) -> p n c", c=16)
                ptot = ptotb[:, 0:NT * 16].rearrange("p (n c) -> p n c", c=16)
                pss = [self.ps(es, "pss", [128, 512], F32) for _ in range(2)]
                ptrb = [self.ps(es, "ptr", [128, 1024], BF16) for _ in range(2)]
                ptr = [ptrb[0][:, 0:128], ptrb[1][:, 0:128]]
                S.op("pool", lambda e: e.memset(onec[:], 1.0), writes=["onec"])
                for z in Zp:
                    S.op("pool", lambda e, z=z: e.memset(z[:], 0.0), writes=["zpc%d" % Zp.index(z), "zpl%d" % Zp.index(z)])
                S.dma("sp", lambda e: e.dma_start(out=abT[:], in_=self.zT[5632:5664, :]), writes=["abT"])
                S.dma("sp", lambda e: e.dma_start(out=al[:], in_=self.alog[l][:, :]), writes=["al"])
                S.dma("sp", lambda e: e.dma_start(out=dtb[:], in_=self.dtb[l][:, :]), writes=["dtb"])
                S.dma("sp", lambda e: e.dma_start(out=cw[:], in_=self.convw[l][:, :, :]), writes=["cw"])
                for t in range(NT):
                    pb_ = pabA if t < 9 else pabB
                    S.op("pe", lambda e, t=t, pb_=pb_: e.transpose(out=pb_[:, (t % 9) * 32:(t % 9 + 1) * 32], in_=abT[:, t * 128:(t + 1) * 128],
                                                                   identity=cst[0:32, 0, 0:32]), reads=["abT", "cst"], writes=["pab"])
                S.op("dve", lambda e: e.tensor_copy(out=abm[:, 0:9, :], in_=pabA[:, 0:288].rearrange("p (n c) -> p n c", c=32)), reads=["pab"], writes=["abm0"])
                S.op("dve", lambda e: e.tensor_copy(out=abm[:, 9:18, :], in_=pabB[:, 0:288].rearrange("p (n c) -> p n c", c=32)), reads=["pab"], writes=["abm1"])
                self.join(["abm0", "abm1"], "abm")
                ab5 = abm[:].rearrange("p n (d t h) -> p n d t h", d=2, t=2)
                v4 = lambda a: a[:].rearrange("p n (d h) -> p n d h", d=2)
                b4 = lambda a: a[:].rearrange("p (d h) -> p d h", d=2).unsqueeze(1).to_broadcast([128, NT, 2, 8])
                S.op("dve", lambda e: e.tensor_tensor(out=v4(xg), in0=ab5[:, :, :, 0, :], in1=b4(dtb), op=ALU.add),
                     reads=["abm", "dtb"], writes=["xg"])
                S.op("act", lambda e: e.activation(out=xg[:], in_=xg[:], func=AF.Exp), reads=["xg"], writes=["xg"])
                S.op("act", lambda e: e.activation(out=xg[:], in_=xg[:], func=AF.Ln, bias=onec[:, 0:1]), reads=["xg", "onec"], writes=["xg"])
                S.op("act", lambda e: e.activation(out=al[:], in_=al[:], func=AF.Exp), reads=["al"], writes=["al"])
                S.op("dve", lambda e: e.tensor_scalar(out=al[:], in0=al[:], scalar1=-1.0, scalar2=None, op0=ALU.mult),
                     reads=["al"], writes=["al"])
                S.op("dve", lambda e: e.tensor_tensor(out=v4(G), in0=v4(xg), in1=b4(al), op=ALU.mult), reads=["xg", "al"], writes=["G"])
                S.op("act", lambda e: e.activation(out=v4(BETA), in_=ab5[:, :, :, 1, :], func=AF.Sigmoid), reads=["abm"], writes=["BETA"])
                for t in range(NT):
                    for d in range(2):
                        S.op("pe", lambda e, t=t, d=d: e.matmul(pgc[:, t, d * 8:(d + 1) * 8], cst[:, 3 + d, :],
                                                                G[:, t, d * 8:(d + 1) * 8], start=True, stop=True),
                             reads=["G", "cst"], writes=["pgc"])
                    S.op("pe", lambda e, t=t: e.matmul(ptot[:, t, :], cst[:, 1, :], G[:, t, :], start=True, stop=True),
                         reads=["G", "cst"], writes=["ptot"])
                S.op("dve", lambda e: e.tensor_copy(out=GC[:], in_=pgc), reads=["pgc"], writes=["GC"])
                S.op("dve", lambda e: e.tensor_copy(out=TOT[:], in_=ptot), reads=["ptot"], writes=["TOT"])
                S.op("act", lambda e: e.activation(out=E[:], in_=GC[:], func=AF.Exp), reads=["GC"], writes=["E"])
                S.op("dve", lambda e: e.tensor_tensor(out=Fd[:], in0=TOT[:], in1=GC[:], op=ALU.subtract), reads=["TOT", "GC"], writes=["Fd"])
                S.op("act", lambda e: e.activation(out=Fd[:], in_=Fd[:], func=AF.Exp), reads=["Fd"], writes=["Fd"])
                S.op("act", lambda e: e.activation(out=GL[:], in_=TOT[:], func=AF.Exp), reads=["TOT"], writes=["GL"])
                S.op("dve", lambda e: e.tensor_tensor(out=BE[:], in0=BETA[:], in1=E[:], op=ALU.mult), reads=["BETA", "E"], writes=["BE"])
                cnt = [0]
                for h in range(8):
                    for which in range(3):
                        i = cnt[0] % 2
                        cnt[0] += 1
                        row0 = 1536 + which * 1024 + h * 128
                        ch = which * 8 + h
                        z, A_, SL, OB, TM = Zp[i], acc[i], sl_[i], ob[i], tm[i]
                        S.dma("sp", lambda e, z=z, row0=row0: e.dma_start(out=z[:, 2:2 + LC], in_=self.zT[row0:row0 + 128, 0:LC]),
                              writes=["zpc%d" % i])
                        S.dma("sp", lambda e, z=z, row0=row0: e.dma_start(out=z[:, 262:262 + L], in_=self.zT[row0:row0 + 128, LC:T]),
                              writes=["zpl%d" % i])
                        for (o0, n, zo, zk) in ((0, LC, 0, "zpc%d" % i), (LC, L, 260, "zpl%d" % i)):
                            S.op("dve", lambda e, z=z, A_=A_, o0=o0, n=n, zo=zo, ch=ch: e.tensor_scalar(
                                out=A_[:, o0:o0 + n], in0=z[:, zo:zo + n], scalar1=cw[:, ch, 0:1], scalar2=None, op0=ALU.mult),
                                reads=[zk, "cw"], writes=["acc%d" % i])
                            for j in range(1, 5):
                                S.op("dve", lambda e, z=z, A_=A_, o0=o0, n=n, zo=zo, ch=ch, j=j: e.scalar_tensor_tensor(
                                    out=A_[:, o0:o0 + n], in0=z[:, zo + j:zo + j + n], scalar=cw[:, ch, j:j + 1],
                                    in1=A_[:, o0:o0 + n], op0=ALU.mult, op1=ALU.add), reads=[zk, "cw", "acc%d" % i], writes=["acc%d" % i])
                        S.op("act", lambda e, A_=A_, SL=SL: e.activation(out=SL[:], in_=A_[:], func=AF.Silu),
                             reads=["acc%d" % i], writes=["sl%d" % i])
                        if which < 2:
                            qs = (128 ** -0.5) if which == 0 else 1.0
                            for bi, (t0, tn) in enumerate(TBLK):
                                bb = bi % 2
                                S.op("act", lambda e, SL=SL, t0=t0, tn=tn, bb=bb: e.activation(out=sq[bb][:, 0:tn], in_=SL[:, t0:t0 + tn],
                                                                                             func=AF.Square), reads=["sl%d" % i], writes=["gsq%d" % bb])
                                S.op("pe", lambda e, t0=t0, tn=tn, bb=bb: e.matmul(pss[bb][:, 0:tn], cst[:, 1, :], sq[bb][:, 0:tn],
                                                                                 start=True, stop=True), reads=["gsq%d" % bb], writes=["pss%d" % bb])
                                S.op("act", lambda e, tn=tn, bb=bb: e.activation(out=ri[bb][:, 0:tn], in_=pss[bb][:, 0:tn], func=AF.Ln,
                                                                               bias=self.epsc[:, 0:1]), reads=["pss%d" % bb], writes=["gri%d" % bb])
                                S.op("act", lambda e, tn=tn, bb=bb: e.activation(out=ri[bb][:, 0:tn], in_=ri[bb][:, 0:tn], func=AF.Exp, scale=-0.5),
                                     reads=["gri%d" % bb], writes=["gri%d" % bb])
                                S.op("dve", lambda e, SL=SL, OB=OB, t0=t0, tn=tn, bb=bb, qs=qs: e.scalar_tensor_tensor(
                                    out=OB[:, t0:t0 + tn], in0=SL[:, t0:t0 + tn], scalar=qs, in1=ri[bb][:, 0:tn],
                                    op0=ALU.mult, op1=ALU.mult), reads=["sl%d" % i, "gri%d" % bb], writes=["gob%d" % i])
                            dst = self.gq if which == 0 else self.gk
                            S.dma("act", lambda e, OB=OB, dst=dst, h=h: e.dma_start(out=dst[h, :, :], in_=OB[:]), reads=["gob%d" % i])
                        else:
                            S.op("act", lambda e, SL=SL, OB=OB: e.activation(out=OB[:], in_=SL[:], func=AF.Copy),
                                 reads=["sl%d" % i], writes=["gob%d" % i])
                        if which >= 1:
                            for t in range(NT):
                                S.op("pe", lambda e, OB=OB, t=t: e.transpose(out=ptr[t % 2], in_=OB[:, t * 128:(t + 1) * 128],
                                                                             identity=cstb[:, 0, :]), reads=["gob%d" % i], writes=["ptr%d" % (t % 2)])
                                S.op("dve", lambda e, TM=TM, t=t: e.tensor_copy(out=TM[:, t, :], in_=ptr[t % 2]),
                                     reads=["ptr%d" % (t % 2)], writes=["gtm%d" % i])
                            dst = self.gkm if which == 1 else self.gvm
                            S.dma("act", lambda e, TM=TM, dst=dst, h=h: e.dma_start(
                                out=dst[h].rearrange("(n p) d -> p n d", p=128), in_=TM[:]), reads=["gtm%d" % i])
                S.emit(block)
            if getattr(self, "gdn_stop", 9) < 2:
                return
            with ExitStack() as es, nc.Block() as block:
                NSET = 8
                f32t = lambda n: [self.sb(es, n, [128, 128], F32) for _ in range(NSET)]
                b16t = lambda n: [self.sb(es, n, [128, 128], BF16) for _ in range(NSET)]
                rhsG = [self.sb(es, "rhsG", [128, 256], F32) for _ in range(NSET)]
                t1, Ex, t2, Ey, EB, Mx, XA, My, QA, usb, osb, gcs = [f32t(n) for n in
                    ("t1", "Ex", "t2", "Ey", "EB", "Mx", "XA", "My", "QA", "usb", "osb", "gcs")]
                QKm, qdec, TT, bv, kbe, kdec, wsb, vnew, kt, qt, km, vm = [b16t(n) for n in
                    ("QKm", "qdec", "TT", "bv", "kbe", "kdec", "wsb", "vnew", "kt", "qt", "km", "vm")]
                Pm = [[self.sb(es, "Pm", [128, 128], F32) for _ in range(2)] for _ in range(NSET)]
                Xm = [[self.sb(es, "Xm", [128, 128], F32) for _ in range(2)] for _ in range(NSET)]
                Yb = [[self.sb(es, "Yb", [128, 128], F32) for _ in range(2)] for _ in range(NSET)]
                Yo = [[self.sb(es, "Yo", [128, 128], F32) for _ in range(3)] for _ in range(NSET)]
                XF, YF, Qs, M1s = [f32t(n) for n in ("XF", "YF", "Qs", "M1s")]
                Sf = [[self.sb(es, "Sf", [128, 128], F32) for _ in range(2)] for _ in range(8)]
                Sb = [[self.sb(es, "Sb", [128, 128], BF16) for _ in range(2)] for _ in range(8)]
                bk = [[self.ps(es, "bk", [128, 512], F32) for _ in range(1)] for _ in range(NSET)]
                for h in range(8):
                    for d in range(2):
                        S.op("pool", lambda e, h=h, d=d: e.memset(Sf[h][d][:], 0.0), writes=["Sf%d%d" % (h, d)])
                        S.op("pool", lambda e, h=h, d=d: e.memset(Sb[h][d][:], 0.0), writes=["Sb%d%d" % (h, d)])
                def body(i, h, d, t):
                    K = lambda n: "%s%d" % (n, i)
                    col = d * 8 + h
                    gcol, bcol, gccol = G[:, t, col:col + 1], BETA[:, t, col:col + 1], GC[:, t, col:col + 1]
                    becol, fcol, glcol = BE[:, t, col:col + 1], Fd[:, t, col:col + 1], GL[:, t, col:col + 1]
                    tsl = slice(t * 128, (t + 1) * 128)
                    Bk = bk[i][0]
                    A, B, C = Bk[:, 0:128], Bk[:, 128:256], Bk[:, 256:512]
                    OP, M1, OX, Yp = Bk[:, 0:128], Bk[:, 128:256], Bk[:, 256:384], Bk[:, 384:512]
                    Sp, U, W, O2 = Bk[:, 0:128], Bk[:, 128:256], Bk[:, 256:384], Bk[:, 384:512]
                    Dp = Bk[:, 128:256]
                    yield S.dma("sp", lambda e, i=i, h=h, tsl=tsl: e.dma_start(out=kt[i][:], in_=self.gk[h, :, tsl]), writes=[K("kt")])
                    yield S.dma("sp", lambda e, i=i, h=h, tsl=tsl: e.dma_start(out=qt[i][:], in_=self.gq[h, :, tsl]), writes=[K("qt")])
                    yield S.dma("sp", lambda e, i=i, h=h, tsl=tsl: e.dma_start(out=km[i][:], in_=self.gkm[h, tsl, :]), writes=[K("km")])
                    yield S.dma("sp", lambda e, i=i, h=h, tsl=tsl: e.dma_start(out=vm[i][:], in_=self.gvm[h, tsl, :]), writes=[K("vm")])
                    yield S.op("pe", lambda e, A=A, i=i: e.matmul(A, kt[i][:], kt[i][:], start=True, stop=True), reads=[K("kt")], writes=[K("bank0"), K("A")])
                    yield S.op("pe", lambda e, B=B, i=i: e.matmul(B, kt[i][:], qt[i][:], start=True, stop=True), reads=[K("kt"), K("qt")], writes=[K("bank0"), K("B")])
                    yield S.op("pool", lambda e, i=i, d=d, gcol=gcol: e.tensor_scalar(out=rhsG[i][:, 0:128], in0=cst[:, 3 + d, :], scalar1=gcol,
                                                                             scalar2=None, op0=ALU.mult), reads=["G"], writes=[K("rhsGa")])
                    yield S.op("act", lambda e, i=i, bcol=bcol: e.activation(out=rhsG[i][:, 128:256], in_=cst[:, 0, :], func=AF.Copy, scale=bcol),
                               reads=["BETA"], writes=[K("rhsGb")])
                    yield S.op("pe", lambda e, C=C, i=i: e.matmul(C, cst[:, 1, :], rhsG[i][:], start=True, stop=True), reads=[K("rhsGa"), K("rhsGb")],
                               writes=[K("bank0"), K("C")])
                    yield S.op("dve", lambda e, C=C, i=i, gccol=gccol: e.tensor_scalar(out=t1[i][:], in0=C[:, 0:128], scalar1=gccol, scalar2=0.0,
                                                                               op0=ALU.subtract, op1=ALU.min), reads=[K("C")], writes=[K("bank0"), K("t1")])
                    yield S.op("act", lambda e, i=i: e.activation(out=Ex[i][:], in_=t1[i][:], func=AF.Exp), reads=[K("t1")], writes=[K("Ex")])
                    yield S.op("dve", lambda e, C=C, i=i, gccol=gccol: e.tensor_scalar(out=t2[i][:], in0=C[:, 0:128], scalar1=gccol, scalar2=0.0,
                                                                               op0=ALU.subtract, op1=ALU.max), reads=[K("C")], writes=[K("bank0"), K("t2")])
                    yield S.op("act", lambda e, i=i: e.activation(out=Ey[i][:], in_=t2[i][:], func=AF.Exp, scale=-1.0), reads=[K("t2")], writes=[K("Ey")])
                    yield S.op("dve", lambda e, C=C, i=i: e.tensor_copy(out=gcs[i][:], in_=C[:, 0:128]), reads=[K("C")], writes=[K("bank0"), K("gcs")])
                    yield S.op("act", lambda e, i=i: e.activation(out=EB[i][:], in_=gcs[i][:], func=AF.Exp), reads=[K("gcs")], writes=[K("EB")])
                    yield S.op("dve", lambda e, C=C, i=i, d=d: e.tensor_tensor(out=Mx[i][:], in0=C[:, 128:256], in1=cst[:, 11 + d, :], op=ALU.mult),
                               reads=[K("C")], writes=[K("bank0"), K("Mx")])
                    yield S.op("dve", lambda e, A=A, i=i: e.tensor_tensor(out=XA[i][:], in0=A, in1=Ex[i][:], op=ALU.mult), reads=[K("A"), K("Ex")],
                               writes=[K("bank0"), K("XA")])
                    yield S.op("pool", lambda e, i=i: e.tensor_tensor(out=Xm[i][0][:], in0=XA[i][:], in1=Mx[i][:], op=ALU.mult),
                               reads=[K("XA"), K("Mx")], writes=[K("X0")])
                    yield S.op("pool", lambda e, i=i: e.tensor_tensor(out=Pm[i][1][:], in0=Xm[i][0][:], in1=cst[:, 0, :], op=ALU.add),
                               reads=[K("X0")], writes=[K("P1")])
                    yield S.op("dve", lambda e, A=A, i=i: e.tensor_tensor(out=YF[i][:], in0=A, in1=Ey[i][:], op=ALU.mult), reads=[K("A"), K("Ey")],
                               writes=[K("bank0"), K("YF")])
                    yield S.op("dve", lambda e, i=i, d=d, bcol=bcol: e.scalar_tensor_tensor(out=Yb[i][0][:], in0=YF[i][:], scalar=bcol, in1=cst[:, 13 + d * 4, :],
                                                                                     op0=ALU.mult, op1=ALU.mult), reads=[K("YF"), "BETA"], writes=[K("Y0")])
                    for lv in range(3):
                        yield S.op("dve", lambda e, i=i, d=d, lv=lv, bcol=bcol: e.scalar_tensor_tensor(
                            out=Yo[i][lv][:], in0=YF[i][:], scalar=bcol, in1=cst[:, 14 + d * 4 + lv, :], op0=ALU.mult, op1=ALU.mult),
                            reads=[K("YF"), "BETA"], writes=[K("Yo%d" % lv)])
                    yield S.op("dve", lambda e, B=B, i=i: e.tensor_tensor(out=QA[i][:], in0=B, in1=Ex[i][:], op=ALU.mult), reads=[K("B"), K("Ex")],
                               writes=[K("bank0"), K("QA")])
                    yield S.op("pool", lambda e, i=i, d=d: e.tensor_tensor(out=QKm[i][:], in0=QA[i][:], in1=cst[:, 3 + d, :], op=ALU.mult),
                               reads=[K("QA")], writes=[K("QKm")])
                    yield S.op("pool", lambda e, i=i: e.tensor_tensor(out=qdec[i][:], in0=qt[i][:], in1=EB[i][:], op=ALU.mult), reads=[K("qt"), K("EB")],
                               writes=[K("qdec")])
                    yield S.op("act", lambda e, i=i, bcol=bcol: e.activation(out=bv[i][:], in_=vm[i][:], func=AF.Copy, scale=bcol),
                               reads=[K("vm"), "BETA"], writes=[K("bv")])
                    yield S.op("act", lambda e, i=i, becol=becol: e.activation(out=kbe[i][:], in_=km[i][:], func=AF.Copy, scale=becol),
                               reads=[K("km"), "BE"], writes=[K("kbe")])
                    yield S.op("act", lambda e, i=i, fcol=fcol: e.activation(out=kdec[i][:], in_=km[i][:], func=AF.Copy, scale=fcol),
                               reads=[K("km"), "Fd"], writes=[K("kdec")])
                    a, b = 0, 1
                    for m in range(4):
                        ka, kb_ = str(a), str(b)
                        if m > 0:
                            yield S.op("pe", lambda e, OP=OP, i=i, a=a: e.matmul(OP, Yb[i][a][:], Pm[i][a][:], start=True, stop=True),
                                       reads=[K("Y" + ka), K("P" + ka)], writes=[K("bank0"), K("OP")])
                        if m < 3:
                            yield S.op("pe", lambda e, OX=OX, i=i, a=a: e.matmul(OX, Yb[i][a][:], Xm[i][a][:], start=True, stop=True),
                                       reads=[K("Y" + ka), K("X" + ka)], writes=[K("bank0"), K("OX")])
                            yield S.op("pe", lambda e, Yp=Yp, i=i, a=a: e.matmul(Yp, Xm[i][a][:], Yb[i][a][:], start=True, stop=True),
                                       reads=[K("Y" + ka), K("X" + ka)], writes=[K("bank0"), K("Yp")])
                        if m > 0:
                            yield S.op("dve", lambda e, OP=OP, i=i, a=a, b=b: e.tensor_tensor(out=Pm[i][b][:], in0=OP, in1=Pm[i][a][:], op=ALU.add),
                                       reads=[K("OP"), K("P" + ka)], writes=[K("bank0"), K("P" + kb_)])
                        if m < 3:
                            yield S.op("act", lambda e, OX=OX, i=i, b=b: e.activation(out=Xm[i][b][:], in_=OX, func=AF.Copy),
                                       reads=[K("OX")], writes=[K("bank0"), K("X" + kb_)])
                            yield S.op("act", lambda e, Yp=Yp, i=i, b=b: e.activation(out=Yb[i][b][:], in_=Yp, func=AF.Copy),
                                       reads=[K("Yp")], writes=[K("bank0"), K("Y" + kb_)])
                        a, b = b, a
                    yield S.op("pe", lambda e, M1=M1, i=i, a=a: e.transpose(out=M1, in_=Pm[i][a][:], identity=cst[:, 0, :]),
                         reads=[K("P" + str(a))], writes=[K("bank0"), K("M1")])
                    yield S.op("act", lambda e, M1=M1, i=i: e.activation(out=Qs[i][:], in_=M1, func=AF.Copy), reads=[K("M1")], writes=[K("bank0"), K("Qs")])
                    for lv in range(3):
                        ka, kb_ = str(a), str(b)
                        yield S.op("pe", lambda e, M1=M1, i=i, a=a, lv=lv: e.matmul(M1, Yo[i][lv][:], Pm[i][a][:], start=True, stop=True),
                             reads=[K("Yo%d" % lv), K("P" + ka)], writes=[K("bank0"), K("M1")])
                        yield S.op("act", lambda e, M1=M1, i=i: e.activation(out=M1s[i][:], in_=M1, func=AF.Copy), reads=[K("M1")], writes=[K("bank0"), K("M1s")])
                        yield S.op("pe", lambda e, OP=OP, i=i: e.matmul(OP, Qs[i][:], M1s[i][:], start=True, stop=True),
                             reads=[K("Qs"), K("M1s")], writes=[K("bank0"), K("OP")])
                        if lv < 2:
                            yield S.op("dve", lambda e, OP=OP, i=i, a=a, b=b: e.tensor_tensor(out=Pm[i][b][:], in0=OP, in1=Pm[i][a][:], op=ALU.add),
                                 reads=[K("OP"), K("P" + ka)], writes=[K("bank0"), K("P" + kb_)])
                            yield S.op("pe", lambda e, M1=M1, i=i, b=b: e.transpose(out=M1, in_=Pm[i][b][:], identity=cst[:, 0, :]),
                                 reads=[K("P" + kb_)], writes=[K("bank0"), K("M1")])
                            yield S.op("act", lambda e, M1=M1, i=i: e.activation(out=Qs[i][:], in_=M1, func=AF.Copy), reads=[K("M1")], writes=[K("bank0"), K("Qs")])
                        else:
                            yield S.op("dve", lambda e, OP=OP, i=i, a=a: e.tensor_tensor(out=TT[i][:], in0=OP, in1=Pm[i][a][:], op=ALU.add),
                                 reads=[K("OP"), K("P" + ka)], writes=[K("bank0"), K("TT")])
                        a, b = b, a
                    yield S.op("pe", lambda e, U=U, i=i: e.matmul(U, TT[i][:], bv[i][:], start=True, stop=True), reads=[K("TT"), K("bv")], writes=[K("bank0"), K("U")])
                    yield S.op("pe", lambda e, W=W, i=i: e.matmul(W, kbe[i][:], TT[i][:], start=True, stop=True), reads=[K("TT"), K("kbe")], writes=[K("bank0"), K("W")])
                    yield S.op("dve", lambda e, U=U, i=i: e.tensor_copy(out=usb[i][:], in_=U), reads=[K("U")], writes=[K("bank0"), K("usb")])
                    yield S.op("act", lambda e, W=W, i=i: e.activation(out=wsb[i][:], in_=W, func=AF.Copy), reads=[K("W")], writes=[K("bank0"), K("wsb")])
                    sk, sbk = "Sf%d%d" % (h, d), "Sb%d%d" % (h, d)
                    SF, SB = Sf[h][d], Sb[h][d]
                    yield S.op("pe", lambda e, Sp=Sp, i=i, SB=SB: e.matmul(Sp, wsb[i][:], SB[:], start=True, stop=True), reads=[K("wsb"), sbk], writes=[K("bank0"), K("Sp")])
                    yield S.op("dve", lambda e, Sp=Sp, i=i: e.tensor_tensor(out=vnew[i][:], in0=usb[i][:], in1=Sp, op=ALU.subtract),
                         reads=[K("usb"), K("Sp")], writes=[K("bank0"), K("vnew")])
                    yield S.op("pe", lambda e, O2=O2, i=i, SB=SB: e.matmul(O2, SB[:], qdec[i][:], start=True, stop=False), reads=[sbk, K("qdec")], writes=[K("bank0"), K("O2")])
                    yield S.op("pe", lambda e, O2=O2, i=i: e.matmul(O2, vnew[i][:], QKm[i][:], start=False, stop=True), reads=[K("vnew"), K("QKm")], writes=[K("bank0"), K("O2")])
                    yield S.op("act", lambda e, O2=O2, i=i: e.activation(out=osb[i][:], in_=O2, func=AF.Copy), reads=[K("O2")], writes=[K("bank0"), K("osb")])
                    yield S.dma("act", lambda e, i=i, d=d, h=h, tsl=tsl: e.dma_start(out=self.oT[d, h, :, tsl], in_=osb[i][:]), reads=[K("osb")])
                    yield S.op("pe", lambda e, Dp=Dp, i=i: e.matmul(Dp, kdec[i][:], vnew[i][:], start=True, stop=True), reads=[K("kdec"), K("vnew")], writes=[K("bank0"), K("Dp")])
                    yield S.op("dve", lambda e, Dp=Dp, SF=SF, glcol=glcol: e.scalar_tensor_tensor(out=SF[:], in0=SF[:], scalar=glcol, in1=Dp,
                                                                                           op0=ALU.mult, op1=ALU.add), reads=[sk, K("Dp"), "GL"], writes=[K("bank0"), sk])
                    yield S.op("act", lambda e, SF=SF, SB=SB: e.activation(out=SB[:], in_=SF[:], func=AF.Copy), reads=[sk], writes=[sbk])

                order = [(s_, h_, d_) for s_ in range(getattr(self, 'scan_steps', NT))
                         for h_ in range(getattr(self, 'scan_heads', 8)) for d_ in range(2)]
                for g0 in range(0, len(order), NSET):
                    gens = [body(k, h_, d_, ORD[d_][s_]) for k, (s_, h_, d_) in enumerate(order[g0:g0 + NSET])]
                    while gens:
                        for g in list(gens):
                            try:
                                next(g)
                            except StopIteration:
                                gens.remove(g)
                S.limit = None
                S.emit(block)
        if getattr(self, "gdn_stop", 9) < 3:
            return
        with ExitStack() as es, nc.Block() as block:
            o0 = [self.sb(es, "o0", [128, T], F32) for _ in range(2)]
            o1 = [self.sb(es, "o1", [128, T], F32) for _ in range(2)]
            gt = [self.sb(es, "gt", [128, T], F32) for _ in range(2)]
            sq = [self.sb(es, "nsq", [128, 512], F32) for _ in range(2)]
            ri = [self.sb(es, "nri", [128, 512], F32) for _ in range(2)]
            yb = [self.sb(es, "nyb", [128, T], BF16) for _ in range(2)]
            gn = self.sb(es, "gn", [128, 1], F32)
            pss = [self.ps(es, "npss", [128, 512], F32) for _ in range(2)]
            S.dma("sp", lambda e: e.dma_start(out=gn[:], in_=self.gnorm[l][:, :]), writes=["gn"])
            for h in range(8):
                i = h % 2
                S.dma("sp", lambda e, i=i, h=h: e.dma_start(out=o0[i][:], in_=self.oT[0, h, :, :]), writes=["o0%d" % i])
                S.dma("sp", lambda e, i=i, h=h: e.dma_start(out=o1[i][:], in_=self.oT[1, h, :, :]), writes=["o1%d" % i])
                S.dma("sp", lambda e, i=i, h=h: e.dma_start(out=gt[i][:], in_=self.zT[4608 + h * 128:4736 + h * 128, :]), writes=["gt%d" % i])
                S.op("pool", lambda e, i=i: e.tensor_tensor(out=o0[i][:], in0=o0[i][:], in1=o1[i][:], op=ALU.add), reads=["o0%d" % i, "o1%d" % i], writes=["o0%d" % i])
                S.op("act", lambda e, i=i: e.activation(out=gt[i][:], in_=gt[i][:], func=AF.Silu), reads=["gt%d" % i], writes=["gt%d" % i])
                for bi, (t0, tn) in enumerate(TBLK):
                    bb = bi % 2
                    S.op("act", lambda e, i=i, t0=t0, tn=tn, bb=bb: e.activation(out=sq[bb][:, 0:tn], in_=o0[i][:, t0:t0 + tn], func=AF.Square),
                         reads=["o0%d" % i], writes=["nsq%d" % bb])
                    S.op("pe", lambda e, tn=tn, bb=bb: e.matmul(pss[bb][:, 0:tn], cst[:, 1, :], sq[bb][:, 0:tn], start=True, stop=True),
                         reads=["nsq%d" % bb], writes=["npss%d" % bb])
                    S.op("act", lambda e, tn=tn, bb=bb: e.activation(out=ri[bb][:, 0:tn], in_=pss[bb][:, 0:tn], func=AF.Ln, scale=1.0 / 128,
                                                                   bias=self.epsc[:, 0:1]), reads=["npss%d" % bb], writes=["nri%d" % bb])
                    S.op("act", lambda e, tn=tn, bb=bb: e.activation(out=ri[bb][:, 0:tn], in_=ri[bb][:, 0:tn], func=AF.Exp, scale=-0.5), reads=["nri%d" % bb], writes=["nri%d" % bb])
                    S.op("dve", lambda e, i=i, t0=t0, tn=tn, bb=bb: e.scalar_tensor_tensor(out=ri[bb][:, 0:tn], in0=o0[i][:, t0:t0 + tn], scalar=gn[:, 0:1],
                                                                                         in1=ri[bb][:, 0:tn], op0=ALU.mult, op1=ALU.mult),
                         reads=["o0%d" % i, "nri%d" % bb, "gn"], writes=["nri%d" % bb])
                    S.op("pool", lambda e, i=i, t0=t0, tn=tn, bb=bb: e.tensor_tensor(out=yb[i][:, t0:t0 + tn], in0=ri[bb][:, 0:tn], in1=gt[i][:, t0:t0 + tn],
                                                                                   op=ALU.mult), reads=["nri%d" % bb, "gt%d" % i], writes=["nyb%d" % i])
                S.dma("act", lambda e, i=i, h=h: e.dma_start(out=self.mixT[1024 + h * 128:1152 + h * 128, :], in_=yb[i][:]), reads=["nyb%d" % i])
            S.emit(block)

    def phase_mixer(self, l):
        if not getattr(self, "skip_attn", False):
            self.phase_attn(l)
        if not getattr(self, "skip_gdn", False):
            self.phase_gdn(l)
    def build_gdnprobe(self):
        nc, es, P = self.nc, self.es, self
        l = 0
        self.consts = P.din("consts", [128, 21, 128])
        self.zT = P.din("zT", [D_IN, T])
        self.alog = {l: P.din("alog0", [128, 16])}
        self.dtb = {l: P.din("dtb0", [128, 16])}
        self.convw = {l: P.din("convw0", [128, 24, 5])}
        self.gnorm = {l: P.din("gnorm0", [128, 1])}
        self.gq = P.dout("gq", [8, 128, T], BF16)
        self.gk = P.dout("gk", [8, 128, T], BF16)
        self.gkm = P.dout("gkm", [8, T, 128], BF16)
        self.gvm = P.dout("gvm", [8, T, 128], BF16)
        self.oT = P.dout("oT", [2, 8, 128, T])
        self.mixT = P.dout("mixT", [D, T], BF16)
        self.S = Sch(nc, es)
        self.cst = P.sb(es, "cst", [128, 21, 128], F32)
        self.cstb = P.sb(es, "cstb", [128, 21, 128], BF16)
        self.junk = P.sb(es, "junk", [128, 8], F32)
        self.epsc = P.sb(es, "epsc", [128, 1], F32)
        S = self.S
        with nc.Block() as block:
            S.dma("sp", lambda e: e.dma_start(out=self.cst[:], in_=self.consts[:, :, :]), writes=["cst"])
            S.op("dve", lambda e: e.tensor_copy(out=self.cstb[:], in_=self.cst[:]), reads=["cst"], writes=["cstb"])
            S.op("pool", lambda e: e.memset(self.epsc[:], EPS), writes=["epsc"])
            S.emit(block)
        self.phase_gdn(0)
        es.close()
        return nc

    def build(self):
        nc = self.nc
        es = self.es
        P = self
        self.xT_in = P.din("xT", [D, T])
        self.c2 = P.din("c2", [128, KC, 2])
        self.consts = P.din("consts", [128, 21, 128])
        self.w_ada = {l: P.din("w_ada%d" % l, [D, 6 * D]) for l in self.layers}
        self.b_ada = {l: P.din("b_ada%d" % l, [128, 96]) for l in self.layers}
        self.nrm = {l: P.din("nrm%d" % l, [128, 2, KC]) for l in self.layers}
        self.w_in = {l: P.din("w_in%d" % l, [D, D_IN]) for l in self.layers}
        if self.upto != "inproj":
            self.w_out = {l: P.din("w_out%d" % l, [D, D]) for l in self.layers}
            self.w_ff1 = {l: P.din("w_ff1%d" % l, [D, D_FF]) for l in self.layers}
            self.w_ff2 = {l: P.din("w_ff2%d" % l, [D_FF, D]) for l in self.layers}
            self.nfin = P.din("nfin", [128, KC])
            self.outT = P.dout("outT", [D, L])
        if self.upto != "inproj":
            self.ropeT = P.din("ropeT", [128, 2, L])
            self.sink = {l: P.din("sink%d" % l, [128, 8]) for l in self.layers}
            self.alog = {l: P.din("alog%d" % l, [128, 16]) for l in self.layers}
            self.dtb = {l: P.din("dtb%d" % l, [128, 16]) for l in self.layers}
            self.convw = {l: P.din("convw%d" % l, [128, 24, 5]) for l in self.layers}
            self.gnorm = {l: P.din("gnorm%d" % l, [128, 1]) for l in self.layers}
        self.gq = P.dscr("gq", [8, 128, T], BF16)
        self.gk = P.dscr("gk", [8, 128, T], BF16)
        self.gkm = P.dscr("gkm", [8, T, 128], BF16)
        self.gvm = P.dscr("gvm", [8, T, 128], BF16)
        self.oT = P.dout("oT", [2, 8, 128, T]) if "oT" in [d[0] for d in self.dbg] else P.dscr("oT", [2, 8, 128, T])
        dn = [d[0] for d in self.dbg]
        self.xT = P.dout("xTs", [D, T]) if "xT" in dn else P.dscr("xTs", [D, T])
        self.zT = P.dscr("zT", [D_IN, T]) if "zT" not in [d[0] for d in self.dbg] else P.dout("zT", [D_IN, T])
        self.mixT = P.dout("mixT", [D, T], BF16) if "mixT" in dn else P.dscr("mixT", [D, T], BF16)
        self.S = Sch(nc, es)
        self.cst = P.sb(es, "cst", [128, 21, 128], F32)
        self.cstb = P.sb(es, "cstb", [128, 21, 128], BF16)
        self.junk = P.sb(es, "junk", [128, 8], F32)
        self.epsc = P.sb(es, "epsc", [128, 1], F32)
        self.mod = {l: P.sb(es, "mod%d" % l, [128, 96, 2], F32) for l in self.layers}
        self.gm1 = {l: P.sb(es, "gm1_%d" % l, [128, 2, KC, 2], F32) for l in self.layers}
        self.phase_init()
        for l in self.layers:
            self.phase_mod(l)
        for l in self.layers:
            self.phase_norm_inproj(l)
            if self.upto == "inproj":
                break
            self.phase_mixer(l)
            if not getattr(self, "skip_tail", False):
                self.phase_dense_tail(l)
        if self.upto == "all":
            self.phase_final()
        es.close()
        return nc


def make_consts():
    c = np.zeros((128, 21, 128), np.float32)
    p = np.arange(128)[:, None]
    f = np.arange(128)[None, :]
    c[:, 0, :] = (p == f)
    c[:, 1, :] = 1.0
    partner = np.where((np.arange(128) % 64) < 32, np.arange(128) + 32, np.arange(128) - 32)
    c[:, 2, :] = (p == partner[None, :])
    c[:, 3, :] = (p <= f)
    c[:, 4, :] = (p >= f)
    c[:, 5, :] = -(p < f).astype(np.float32)
    c[:, 6, :] = -(p > f).astype(np.float32)
    c[:, 7, :] = (p // 16 == f // 16)
    for lv, sz in enumerate((16, 32, 64)):
        c[:, 8 + lv, :] = (p // (2 * sz) == f // (2 * sz)) & (p // sz != f // sz)
    for d in range(2):
        c[:, 11 + d, :] = c[:, 5 + d, :] * c[:, 7, :]
        for k in range(4):
            c[:, 13 + d * 4 + k, :] = c[:, 6 - d, :] * c[:, 7 + k, :]
    return c


def rope_tables():
    pos = np.arange(L)
    row = (pos // 64).astype(np.float32)
    col = (pos % 64).astype(np.float32)
    inv = (10000.0 ** (-np.arange(0, 64, 2, dtype=np.float32) / 64.0)).astype(np.float32)
    ang = np.concatenate([row[:, None] * inv[None, :], col[:, None] * inv[None, :]], axis=-1).astype(np.float32)
    cos, sin = np.cos(ang).astype(np.float32), np.sin(ang).astype(np.float32)
    out = np.zeros((128, 2, L), np.float32)
    for m_ in range(128):
        idx = (m_ % 32) if m_ < 64 else 32 + ((m_ - 64) % 32)
        sign = -1.0 if (m_ % 64) < 32 else 1.0
        out[m_, 0, :] = cos[:, idx]
        out[m_, 1, :] = sign * sin[:, idx]
    return out


def fm_vec(v, nchunk):
    return np.ascontiguousarray(np.asarray(v, np.float32).reshape(nchunk, 128).T)


def core_inputs(inp, b, layers, upto="all"):
    m = {}
    xt = np.concatenate([inp["ctx"][b], inp["x"][b]], axis=0)
    m["xT"] = np.ascontiguousarray(xt.T)
    c2 = np.stack([fm_vec(inp["c"][b], KC), fm_vec(inp["c_ctx"], KC)], axis=-1)
    m["c2"] = np.ascontiguousarray(c2)
    m["consts"] = make_consts()
    for l in layers:
        m["w_ada%d" % l] = inp["w_ada"][l]
        m["b_ada%d" % l] = fm_vec(inp["b_ada"][l], 96)
        m["nrm%d" % l] = np.ascontiguousarray(np.stack([fm_vec(inp["norm_mix"][l], KC), fm_vec(inp["norm_ffn"][l], KC)], axis=1))
        m["w_in%d" % l] = inp["w_in"][l]
        if upto != "inproj":
            m["w_out%d" % l] = inp["w_out"][l]
            m["w_ff1%d" % l] = inp["w_ff1"][l]
            m["w_ff2%d" % l] = inp["w_ff2"][l]
    if upto != "inproj":
        m["nfin"] = fm_vec(inp["norm_final"], KC)
        m["ropeT"] = rope_tables()
        for l in layers:
            m["sink%d" % l] = np.ascontiguousarray(np.broadcast_to(inp["attn_sink"][l][None, :], (128, 8))).astype(np.float32)
            m["alog%d" % l] = np.ascontiguousarray(np.broadcast_to(inp["a_log"][l].reshape(1, 16), (128, 16))).astype(np.float32)
            m["dtb%d" % l] = np.ascontiguousarray(np.broadcast_to(inp["dt_bias"][l].reshape(1, 16), (128, 16))).astype(np.float32)
            cw = inp["conv_w"][l]
            m["convw%d" % l] = np.ascontiguousarray(cw.T.reshape(24, 128, 5).transpose(1, 0, 2)).astype(np.float32)
            m["gnorm%d" % l] = np.ascontiguousarray(inp["gdn_norm"][l].reshape(128, 1)).astype(np.float32)
    return m


_PROG = {}


def kernel(**inputs):
    inp = {k: np.asarray(v) for k, v in inputs.items()}
    layers = list(range(DEPTH))
    if "nc" not in _PROG:
        _PROG["nc"] = Prog(layers).build()
    nc = _PROG["nc"]
    n = 4
    maps = [core_inputs(inp, b, layers) for b in range(n)]
    res = run_bass_kernel_spmd(nc, maps, core_ids=list(range(n)))
    out = np.stack([res.results[b]["outT"].T for b in range(n)], axis=0)
    return np.ascontiguousarray(out.astype(np.float32))
```

```python
import math
from contextlib import ExitStack
import numpy as np
import concourse.bass as bass
import concourse.mybir as mybir
from concourse.bass_utils import run_bass_kernel_spmd

F32 = mybir.dt.float32
BF16 = mybir.dt.bfloat16
ALU = mybir.AluOpType
AF = mybir.ActivationFunctionType

D = 2048
KC = 16
L = 2048
LC = 256
T = L + LC
NT = T // 128
DEPTH = 4
D_IN = 5664
D_FF = 8192
EPS = 1e-6
TBLK = [(0, 256), (256, 512), (768, 512), (1280, 512), (1792, 512)]
NCORES = 8


class Op:
    __slots__ = ("e", "fn", "deps", "idx", "signal", "dma", "dsem", "dval", "prewait")

    def __init__(self, e, fn, deps, idx, dma=False):
        self.e, self.fn, self.deps, self.idx, self.dma = e, fn, deps, idx, dma
        self.signal = False
        self.dsem = None
        self.dval = 0
        self.prewait = None


class Sch:
    ENG = ("pe", "act", "dve", "pool", "sp")
    NDS = 12

    def __init__(self, nc, es):
        self.nc = nc
        self.eng = {"pe": nc.tensor, "act": nc.scalar, "dve": nc.vector, "pool": nc.gpsimd, "sp": nc.sync}
        self.sem = {e: es.enter_context(nc.semaphore("sem_" + e)) for e in self.ENG}
        self.cnt = {e: 0 for e in self.ENG}
        self.dsem = {q: [es.enter_context(nc.semaphore("dsem_%s_%d" % (q, i))) for i in range(self.NDS)]
                     for q in ("sp", "pool", "act")}
        self.dcnt = {q: 0 for q in ("sp", "pool", "act")}
        self.reset()

    def reset(self):
        self.ops = {e: [] for e in self.ENG}
        self.res = {}

    def _mk(self, e, fn, reads, writes, dma):
        deps = set()
        for r in reads:
            st = self.res.get(r)
            if st is not None and st[0] is not None:
                deps.add(st[0])
        for w in writes:
            st = self.res.get(w)
            if st is not None:
                if st[0] is not None:
                    deps.add(st[0])
                deps.update(st[1].values())
        if e == "pe" and not dma:
            deps = {d for d in deps if d.dma or d.e != "pe"}
        o = Op(e, fn, deps, len(self.ops[e]), dma)
        self.ops[e].append(o)
        for d in deps:
            d.signal = True
        for r in reads:
            st = self.res.setdefault(r, [None, {}])
            st[1][e + ("d" if dma else "")] = o
        for w in writes:
            self.res[w] = [o, {}]
        return o

    limit = None
    nrec = 0

    def op(self, e, fn, reads=(), writes=()):
        if self.limit is not None:
            self.nrec += 1
            if self.nrec > self.limit:
                return None
        return self._mk(e, fn, reads, writes, False)

    def dma(self, q, fn, reads=(), writes=()):
        if self.limit is not None:
            self.nrec += 1
            if self.nrec > self.limit:
                return None
        o = self._mk(q, fn, reads, writes, True)
        m = self.dcnt[q]
        self.dcnt[q] += 1
        o.dsem = self.dsem[q][m % self.NDS]
        o.dval = 16 * (m // self.NDS + 1)
        o.prewait = 16 * (m // self.NDS)
        return o

    def emit(self, block, final_wait=True):
        val = {}
        for e in self.ENG:
            c = self.cnt[e]
            pend = []
            for o in self.ops[e]:
                if o.dma:
                    continue
                pend.append(o)
                if o.signal:
                    c += 1
                    for p in pend:
                        val[p] = c
                    pend = []
            self.cnt[e] = c
        dtot = {}

        def run(e):
            def body(eng):
                waited = {}

                def w(sem, v):
                    k = id(sem)
                    if waited.get(k, -1) >= v:
                        return
                    waited[k] = v
                    eng.wait_ge(sem, v)
                for o in self.ops[e]:
                    for d in o.deps:
                        if d.dma:
                            w(d.dsem, d.dval)
                        else:
                            w(self.sem[d.e], val[d])
                    if o.dma:
                        if o.prewait:
                            w(o.dsem, o.prewait)
                        ins = o.fn(eng)
                        ins.then_inc(o.dsem, 16)
                        dtot[id(o.dsem)] = (o.dsem, o.dval)
                    else:
                        ins = o.fn(eng)
                        if o.signal:
                            ins.then_inc(self.sem[e], 1)
                if e == "sp" and final_wait:
                    pass
            return body

        for e in ("pe", "act", "dve", "pool"):
            if self.ops[e]:
                getattr(block, {"pe": "tensor", "act": "scalar", "dve": "vector", "pool": "gpsimd"}[e])(run(e))
        spbody = run("sp")

        def spfull(eng):
            spbody(eng)
            for q in ("pool", "act"):
                for o in self.ops[q]:
                    if o.dma:
                        dtot[id(o.dsem)] = (o.dsem, max(o.dval, dtot.get(id(o.dsem), (None, 0))[1]))
            for sem, v in dtot.values():
                eng.wait_ge(sem, v)
        block.sync(spfull)
        self.reset()


class Prog:
    def __init__(self, layers, upto="all", dbg=()):
        self.layers = layers
        self.upto = upto
        self.dbg = dbg
        self.nc = bass.Bass("TRN2", target_bir_lowering=False)
        self.es = ExitStack()
        self.uid = 0

    def din(self, name, shape, dt=F32):
        return self.nc.dram_tensor(name, list(shape), dt, kind="ExternalInput").ap()

    def dout(self, name, shape, dt=F32):
        return self.nc.dram_tensor(name, list(shape), dt, kind="ExternalOutput").ap()

    def dscr(self, name, shape, dt=F32):
        return self.nc.dram_tensor(name, list(shape), dt).ap()

    def sb(self, es, name, shape, dt):
        self.uid += 1
        return es.enter_context(self.nc.sbuf_tensor("%s_%d" % (name, self.uid), list(shape), dt))

    def ps(self, es, name, shape, dt=F32):
        self.uid += 1
        return es.enter_context(self.nc.psum_tensor("%s_%d" % (name, self.uid), list(shape), dt))

    def build(self):
        nc = self.nc
        es = self.es
        P = self
        self.xT_in = P.din("xT", [D, T])
        self.c2 = P.din("c2", [128, KC, 2])
        self.consts = P.din("consts", [128, 21, 128])
        self.ropeT = P.din("ropeT", [128, 2, L])
        self.w_ada = {l: P.din("w_ada%d" % l, [D, 6 * D]) for l in self.layers}
        self.b_ada = {l: P.din("b_ada%d" % l, [128, 96]) for l in self.layers}
        self.nrm = {l: P.din("nrm%d" % l, [128, 2, KC]) for l in self.layers}
        self.w_in = {l: P.din("w_in%d" % l, [D, D_IN]) for l in self.layers}
        self.w_out = {l: P.din("w_out%d" % l, [D, D]) for l in self.layers}
        self.w_ff1 = {l: P.din("w_ff1%d" % l, [D, D_FF]) for l in self.layers}
        self.w_ff2 = {l: P.din("w_ff2%d" % l, [D_FF, D]) for l in self.layers}
        self.nfin = P.din("nfin", [128, KC])
        self.outT = P.dout("outT", [D, L])
        self.xT = P.dscr("xTs", [D, T])
        self.zqkT = P.dscr("zqkT", [1280, T], BF16)
        self.zv = P.dscr("zv", [T, 256], BF16)
        self.zg = P.dscr("zg", [3072, T])
        self.zgate = P.dscr("zgate", [1024, T], BF16)
        self.zab = P.dscr("zab", [T, 32])
        self.mixT = P.dscr("mixT", [D, T], BF16)
        self.aT = P.dscr("aT", [D_FF, T], BF16)
        self.dbgout = {}
        for name, shape, dt in self.dbg:
            self.dbgout[name] = P.dout("dbg_" + name, shape, dt)

        self.S = Sch(nc, es)
        self.cst = P.sb(es, "cst", [128, 21, 128], F32)
        self.cstb = P.sb(es, "cstb", [128, 21, 128], BF16)
        self.mod = {l: P.sb(es, "mod%d" % l, [128, 96, 2], F32) for l in self.layers}
        self.gm1 = {l: P.sb(es, "gm1_%d" % l, [128, 2, KC, 2], F32) for l in self.layers}

        self.phase_init()
        for l in self.layers:
            self.phase_mod(l)
        for l in self.layers:
            self.phase_norm_inproj(l)
            if self.upto == "inproj":
                break
            if not getattr(self, "skip_tail", False):
                self.phase_dense_tail(l)
        if self.upto == "all":
            self.phase_final()
        return nc

    def phase_init(self):
        nc, S = self.nc, self.S
        with ExitStack() as es, nc.Block() as block:
            S.dma("sp", lambda e: e.dma_start(out=self.cst[:], in_=self.consts[:, :, :]), writes=["cst"])
            S.op("dve", lambda e: e.tensor_copy(out=self.cstb[:], in_=self.cst[:]), reads=["cst"], writes=["cstb"])
            S.op("pool", lambda e: e.memset(self.epsc[:], EPS), writes=["epsc"])
            for i in range(4):
                S.dma("sp", lambda e, i=i: e.dma_start(out=self.xT[i * 512:(i + 1) * 512, :],
                                                      in_=self.xT_in[i * 512:(i + 1) * 512, :]))
            S.emit(block)

    def phase_mod(self, l):
        nc, S = self.nc, self.S
        with ExitStack() as es, nc.Block() as block:
            c2 = self.sb(es, "c2", [128, KC, 2], F32)
            s2 = self.sb(es, "s2", [128, KC, 2], BF16)
            bt = self.sb(es, "bt", [128, 96], F32)
            nr = self.sb(es, "nr", [128, 2, KC], F32)
            NS = 3
            wsl = [self.sb(es, "wad", [128, KC, 512], BF16) for _ in range(NS)]
            pm = self.ps(es, "pm", [128, 96, 2], F32)
            S.dma("sp", lambda e: e.dma_start(out=c2[:], in_=self.c2[:, :, :]), writes=["c2"])
            S.dma("sp", lambda e: e.dma_start(out=bt[:], in_=self.b_ada[l][:, :]), writes=["bt"])
            S.dma("sp", lambda e: e.dma_start(out=nr[:], in_=self.nrm[l][:, :, :]), writes=["nr"])
            S.op("act", lambda e: e.activation(out=s2[:], in_=c2[:], func=AF.Silu), reads=["c2"], writes=["s2"])
            wv = self.w_ada[l].rearrange("(kc p) n -> p kc n", p=128)
            for s in range(24):
                buf = wsl[s % NS]
                key = "wad%d" % (s % NS)
                S.dma("pool", lambda e, s=s, buf=buf: e.dma_start(out=buf[:], in_=wv[:, :, s * 512:(s + 1) * 512]),
                      writes=[key])
                for j in range(4):
                    cc = s * 4 + j
                    for kc in range(KC):
                        S.op("pe", lambda e, cc=cc, kc=kc, j=j, buf=buf: e.matmul(
                            pm[:, cc, :], buf[:, kc, j * 128:(j + 1) * 128], s2[:, kc, :],
                            start=(kc == 0), stop=(kc == KC - 1)), reads=[key, "s2"], writes=["pm"])
            mod = self.mod[l]
            S.op("dve", lambda e: e.tensor_tensor(out=mod[:], in0=pm[:], in1=bt[:].unsqueeze(2).to_broadcast([128, 96, 2]),
                                                  op=ALU.add), reads=["pm", "bt"], writes=["mod"])
            gm = self.gm1[l]
            for n, which in ((0, 1), (1, 4)):
                S.op("dve", lambda e, n=n, which=which: e.tensor_scalar(
                    out=gm[:, n, :, :], in0=mod[:, which * 16:(which + 1) * 16, :], scalar1=1.0, scalar2=None,
                    op0=ALU.add), reads=["mod"], writes=["gm%d" % n])
                S.op("dve", lambda e, n=n: e.tensor_tensor(
                    out=gm[:, n, :, :], in0=gm[:, n, :, :], in1=nr[:, n, :].unsqueeze(2).to_broadcast([128, KC, 2]),
                    op=ALU.mult), reads=["gm%d" % n, "nr"], writes=["gm%d" % n])
            S.emit(block)

    def norm_to_hT(self, es, l, n, hT, shift_which):
        nc, S = self.nc, self.S
        xb = [self.sb(es, "xb", [128, KC, 256], F32) for _ in range(2)]
        sq = [self.sb(es, "sq", [128, 256], F32) for _ in range(2)]
        rs = [self.sb(es, "rs", [128, 256], F32) for _ in range(2)]
        tmp = [self.sb(es, "tmp", [128, 256], F32) for _ in range(3)]
        pst = [self.ps(es, "pst", [128, 256], F32) for _ in range(2)]
        xv = self.xT.rearrange("(c p) t -> p c t", p=128)
        gm, mod = self.gm1[l], self.mod[l]
        ones = self.cst[:, 1, :]
        for b in range(T // 256):
            j = 1 if b == 0 else 0
            X, Q, R, PS = xb[b % 2], sq[b % 2], rs[b % 2], pst[b % 2]
            kx, kq, kr, kp = "xb%d" % (b % 2), "sq%d" % (b % 2), "rs%d" % (b % 2), "pst%d" % (b % 2)
            S.dma("sp", lambda e, X=X, b=b: e.dma_start(out=X[:], in_=xv[:, :, b * 256:(b + 1) * 256]), writes=[kx])
            for c in range(KC):
                S.op("act", lambda e, X=X, Q=Q, c=c: e.activation(out=Q[:], in_=X[:, c, :], func=AF.Square),
                     reads=[kx], writes=[kq])
                S.op("pe", lambda e, Q=Q, PS=PS, c=c: e.matmul(PS[:], ones, Q[:], start=(c == 0), stop=(c == KC - 1)),
                     reads=[kq, "cst"], writes=[kp])
            S.op("act", lambda e, R=R, PS=PS: e.activation(out=R[:], in_=PS[:], func=AF.Ln, scale=1.0 / D, bias=self.epsc[:, 0:1]),
                 reads=[kp], writes=[kr])
            S.op("act", lambda e, R=R: e.activation(out=R[:], in_=R[:], func=AF.Exp, scale=-0.5), reads=[kr], writes=[kr])
            for c in range(KC):
                tb = tmp[c % 3]
                kt = "tmp%d" % (c % 3)
                S.op("dve", lambda e, X=X, R=R, tb=tb, c=c: e.tensor_tensor(out=tb[:], in0=X[:, c, :], in1=R[:],
                                                                          op=ALU.mult), reads=[kx, kr], writes=[kt])
                S.op("act", lambda e, tb=tb, c=c, b=b, j=j: e.activation(
                    out=hT[:, c, b * 256:(b + 1) * 256], in_=tb[:], func=AF.Identity,
                    scale=gm[:, n, c, j:j + 1], bias=mod[:, shift_which * 16 + c, j:j + 1]),
                    reads=[kt, "gm%d" % n, "mod"], writes=["hT"])

    def gemm(self, es, W, ncols, act, actkey, epi, nkc=KC, wname="w", tblk=TBLK):
        S = self.S
        NS = 3
        if not hasattr(self, "_gc"):
            self._gc = {}
        k1, k2 = (id(es), wname, nkc), (id(es), "ps")
        if k1 not in self._gc:
            self._gc[k1] = [self.sb(es, wname, [128, nkc, 256], BF16) for _ in range(NS)]
        if k2 not in self._gc:
            self._gc[k2] = [self.ps(es, "pg", [128, 512], F32) for _ in range(4)]
        wsl, psb = self._gc[k1], self._gc[k2]
        Wv = W.rearrange("(kc p) n -> p kc n", p=128)
        nslab = (ncols + 255) // 256
        pi = 0
        KG = 4
        for s in range(nslab):
            c0 = s * 256
            cw = min(256, ncols - c0)
            buf = wsl[s % NS]
            for kg in range(nkc // KG):
                S.dma("pool", lambda e, buf=buf, kg=kg, c0=c0, cw=cw: e.dma_start(
                    out=buf[:, kg * KG:(kg + 1) * KG, 0:cw], in_=Wv[:, kg * KG:(kg + 1) * KG, c0:c0 + cw]),
                    writes=["%s%d_%d" % (wname, s % NS, kg)])
            for g in range((cw + 127) // 128):
                m = min(128, cw - g * 128)
                for bi, (t0, tn) in enumerate(tblk):
                    PS = psb[pi % 4]
                    pk = "pg%d" % (pi % 4)
                    pi += 1
                    for kc in range(nkc):
                        S.op("pe", lambda e, PS=PS, buf=buf, kc=kc, g=g, m=m, t0=t0, tn=tn: e.matmul(
                            PS[0:m, 0:tn], buf[:, kc, g * 128:g * 128 + m], act[:, kc, t0:t0 + tn],
                            start=(kc == 0), stop=(kc == nkc - 1)),
                            reads=["%s%d_%d" % (wname, s % NS, kc // KG), actkey], writes=[pk])
                    epi(c0 + g * 128, m, bi, t0, tn, PS, pk)

    def phase_norm_inproj(self, l):
        nc, S = self.nc, self.S
        with ExitStack() as es:
            hT = self.sb(es, "hT", [128, KC, T], BF16)
            with ExitStack() as es2, nc.Block() as block:
                self.norm_to_hT(es2, l, 0, hT, 0)
                S.emit(block)
            with ExitStack() as es2, nc.Block() as block:
                stg = [self.sb(es2, "stg", [128, 512], F32) for _ in range(4)]
                cnt = [0]

                def epi(c0, m, bi, t0, tn, PS, pk):
                    i = cnt[0] % 4
                    cnt[0] += 1
                    st = stg[i]
                    S.op("act", lambda e: e.activation(out=st[0:m, 0:tn], in_=PS[0:m, 0:tn], func=AF.Copy),
                         reads=[pk], writes=["stg%d" % i])
                    S.dma("sp", lambda e: e.dma_start(out=self.zT[c0:c0 + m, t0:t0 + tn], in_=st[0:m, 0:tn]),
                          reads=["stg%d" % i])
                self.gemm(es2, self.w_in[l], D_IN, hT, "hT", epi, wname="win")
                S.emit(block)

    def resid_epi(self, es, l, which):
        S = self.S
        NXL = 6
        xl = [self.sb(es, "xl", [128, 512], F32) for _ in range(NXL)]
        cnt = [0]
        mod = self.mod[l]

        def epi(c0, m, bi, t0, tn, PS, pk):
            i = cnt[0] % NXL
            cnt[0] += 1
            xb = xl[i]
            j = 1 if t0 < LC else 0
            c = c0 // 128
            xk = "x_%d_%d" % (c, bi)
            S.dma("act", lambda e: e.dma_start(out=xb[:, 0:tn], in_=self.xT[c0:c0 + 128, t0:t0 + tn]),
                  reads=[xk], writes=["xl%d" % i])
            S.op("dve", lambda e: e.scalar_tensor_tensor(
                out=xb[:, 0:tn], in0=PS[:, 0:tn], scalar=mod[:, which * 16 + c, j:j + 1], in1=xb[:, 0:tn],
                op0=ALU.mult, op1=ALU.add), reads=[pk, "xl%d" % i, "mod"], writes=["xl%d" % i])
            S.dma("sp", lambda e: e.dma_start(out=self.xT[c0:c0 + 128, t0:t0 + tn], in_=xb[:, 0:tn]),
                  reads=["xl%d" % i], writes=[xk])
        return epi

    def phase_dense_tail(self, l):
        nc, S = self.nc, self.S
        with ExitStack() as es, nc.Block() as block:
            mx = self.sb(es, "mx", [128, KC, T], BF16)
            mv = self.mixT.rearrange("(c p) t -> p c t", p=128)
            for c4 in range(4):
                S.dma("sp", lambda e, c4=c4: e.dma_start(out=mx[:, c4 * 4:(c4 + 1) * 4, :], in_=mv[:, c4 * 4:(c4 + 1) * 4, :]),
                      writes=["mx%d" % c4])
            S.op("pool", lambda e: e.memset(self.junk[:], 0.0), reads=["mx0", "mx1", "mx2", "mx3"], writes=["mx"])
            self.gemm(es, self.w_out[l], D, mx, "mx", self.resid_epi(es, l, 2), wname="wout")
            S.emit(block)
        with ExitStack() as es:
            hT = self.sb(es, "hnT", [128, KC, T], BF16)
            with ExitStack() as es2, nc.Block() as block:
                self.norm_to_hT(es2, l, 1, hT, 3)
                S.emit(block)
            with ExitStack() as es2, nc.Block() as block:
                NK2 = 8
                aT = self.sb(es2, "aT", [128, NK2, T], BF16)
                rl = [self.sb(es2, "rl", [128, 512], F32) for _ in range(3)]
                rc = [0]
                repi = self.resid_epi(es2, l, 5)
                for q in range(D_FF // (NK2 * 128)):
                    def epi1(c0, m, bi, t0, tn, PS, pk):
                        i = rc[0] % 3
                        rc[0] += 1
                        r = rl[i]
                        S.op("act", lambda e: e.activation(out=r[:, 0:tn], in_=PS[:, 0:tn], func=AF.Relu),
                             reads=[pk], writes=["rl%d" % i])
                        S.op("pool", lambda e: e.tensor_tensor(out=aT[:, c0 // 128, t0:t0 + tn], in0=r[:, 0:tn],
                                                               in1=r[:, 0:tn], op=ALU.mult),
                             reads=["rl%d" % i], writes=["aT"])
                    self.gemm(es2, self.w_ff1[l][:, q * NK2 * 128:(q + 1) * NK2 * 128], NK2 * 128, hT, "hnT", epi1,
                              wname="wf1")
                    self.gemm(es2, self.w_ff2[l][q * NK2 * 128:(q + 1) * NK2 * 128, :], D, aT, "aT", repi, nkc=NK2,
                              wname="wf2")
                S.emit(block)

    def phase_final(self):
        nc, S = self.nc, self.S
        with ExitStack() as es, nc.Block() as block:
            xb = [self.sb(es, "fxb", [128, KC, 256], F32) for _ in range(2)]
            sq = [self.sb(es, "fsq", [128, 256], F32) for _ in range(2)]
            rs = [self.sb(es, "frs", [128, 256], F32) for _ in range(2)]
            ob = [self.sb(es, "fob", [128, KC, 256], F32) for _ in range(2)]
            nf = self.sb(es, "nf", [128, KC], F32)
            pst = [self.ps(es, "fps", [128, 256], F32) for _ in range(2)]
            xv = self.xT.rearrange("(c p) t -> p c t", p=128)
            ov = self.outT.rearrange("(c p) t -> p c t", p=128)
            ones = self.cst[:, 1, :]
            S.dma("sp", lambda e: e.dma_start(out=nf[:], in_=self.nfin[:, :]), writes=["nf"])
            for b in range(L // 256):
                X, Q, R, PS, O = xb[b % 2], sq[b % 2], rs[b % 2], pst[b % 2], ob[b % 2]
                kx, kq, kr, kp, ko = ["%s%d" % (n, b % 2) for n in ("fx", "fq", "fr", "fp", "fo")]
                S.dma("sp", lambda e, X=X, b=b: e.dma_start(out=X[:], in_=xv[:, :, LC + b * 256:LC + (b + 1) * 256]),
                      writes=[kx])
                for c in range(KC):
                    S.op("act", lambda e, X=X, Q=Q, c=c: e.activation(out=Q[:], in_=X[:, c, :], func=AF.Square),
                         reads=[kx], writes=[kq])
                    S.op("pe", lambda e, Q=Q, PS=PS, c=c: e.matmul(PS[:], ones, Q[:], start=(c == 0), stop=(c == KC - 1)),
                         reads=[kq], writes=[kp])
                S.op("act", lambda e, R=R, PS=PS: e.activation(out=R[:], in_=PS[:], func=AF.Ln, scale=1.0 / D,
                                                              bias=self.epsc[:, 0:1]), reads=[kp], writes=[kr])
                S.op("act", lambda e, R=R: e.activation(out=R[:], in_=R[:], func=AF.Exp, scale=-0.5), reads=[kr], writes=[kr])
                for c in range(KC):
                    S.op("dve", lambda e, X=X, R=R, O=O, c=c: e.scalar_tensor_tensor(
                        out=O[:, c, :], in0=X[:, c, :], scalar=nf[:, c:c + 1], in1=R[:], op0=ALU.mult, op1=ALU.mult),
                        reads=[kx, kr, "nf"], writes=[ko])
                S.dma("sp", lambda e, O=O, b=b: e.dma_start(out=ov[:, :, b * 256:(b + 1) * 256], in_=O[:]), reads=[ko])
            S.emit(block)
    def join(self, keys, newkey):
        self.S.op("pool", lambda e: e.memset(self.junk[:], 0.0), reads=list(keys), writes=[newkey])

    def phase_attn(self, l):
        nc, S = self.nc, self.S
        scale = 128 ** -0.5
        cstb = self.cstb
        with ExitStack() as es, nc.Block() as block:
            rope = self.sb(es, "rope", [128, 2, L], F32)
            snk = self.sb(es, "snk", [128, 8], F32)
            esk = self.sb(es, "esk", [128, 8], F32)
            raw = [self.sb(es, "raw", [128, T], F32) for _ in range(2)]
            rb = self.sb(es, "rb", [128, L], BF16)
            t1 = [self.sb(es, "t1", [128, 512], F32) for _ in range(2)]
            t2 = [self.sb(es, "t2", [128, 512], F32) for _ in range(2)]
            kT = self.sb(es, "kT", [128, T], BF16)
            qT = self.sb(es, "qT", [128, T], BF16)
            vT = self.sb(es, "vT", [128, T], BF16)
            V = self.sb(es, "V", [128, NT, 128], BF16)
            PTc = self.sb(es, "PTc", [128, 2, T], BF16)
            PTl = self.sb(es, "PTl", [128, 16, 3, 128], BF16)
            rc = [self.sb(es, "rc", [128, 128], F32) for _ in range(2)]
            ost = [self.sb(es, "ost", [128, 128], BF16) for _ in range(3)]
            psS = [self.ps(es, "psS", [128, 512], F32) for _ in range(2)]
            psO = [self.ps(es, "psO", [128, 128], F32) for _ in range(2)]
            psD = [self.ps(es, "psD", [128, 128], F32) for _ in range(2)]
            psT = [self.ps(es, "psT", [128, 128], BF16) for _ in range(2)]
            S.dma("sp", lambda e: e.dma_start(out=rope[:], in_=self.ropeT[:, :, :]), writes=["rope"])
            S.dma("sp", lambda e: e.dma_start(out=snk[:], in_=self.sink[l][:, :]), writes=["snk"])
            S.op("act", lambda e: e.activation(out=esk[:], in_=snk[:], func=AF.Exp), reads=["snk"], writes=["esk"])
            cn = {"raw": 0, "ps": 0, "t": 0, "o": 0, "ost": 0}

            def load_rope(row0, dst, dkey):
                ri = cn["raw"] % 2
                cn["raw"] += 1
                R, rk = raw[ri], "raw%d" % ri
                S.dma("sp", lambda e: e.dma_start(out=R[:], in_=self.zT[row0:row0 + 128, :]), writes=[rk])
                S.op("act", lambda e: e.activation(out=dst[:, 0:LC], in_=R[:, 0:LC], func=AF.Copy), reads=[rk],
                     writes=[dkey + "c"])
                S.op("act", lambda e: e.activation(out=rb[:], in_=R[:, LC:T], func=AF.Copy), reads=[rk], writes=["rb"])
                for blk in range(4):
                    a, b_ = blk * 512, (blk + 1) * 512
                    pi = cn["ps"] % 2
                    cn["ps"] += 1
                    ti = cn["t"] % 2
                    cn["t"] += 1
                    PS, pk = psS[pi], "psS%d" % pi
                    S.op("pe", lambda e, PS=PS, a=a, b_=b_: e.matmul(PS[:, :], cstb[:, 2, :], rb[:, a:b_], start=True, stop=True),
                         reads=["rb", "cstb"], writes=[pk])
                    S.op("dve", lambda e, a=a, b_=b_, ti=ti: e.tensor_tensor(out=t1[ti][:], in0=R[:, LC + a:LC + b_],
                                                                           in1=rope[:, 0, a:b_], op=ALU.mult),
                         reads=[rk, "rope"], writes=["t1%d" % ti])
                    S.op("dve", lambda e, PS=PS, a=a, b_=b_, ti=ti: e.tensor_tensor(out=t2[ti][:], in0=PS[:, :],
                                                                                  in1=rope[:, 1, a:b_], op=ALU.mult),
                         reads=[pk, "rope"], writes=["t2%d" % ti])
                    S.op("pool", lambda e, a=a, b_=b_, ti=ti: e.tensor_tensor(out=dst[:, LC + a:LC + b_], in0=t1[ti][:],
                                                                            in1=t2[ti][:], op=ALU.add),
                         reads=["t1%d" % ti, "t2%d" % ti], writes=[dkey + str(blk)])
                self.join([dkey + "c"] + [dkey + str(i) for i in range(4)], dkey)

            for g in range(2):
                load_rope(1024 + g * 128, kT, "kT")
                ri = cn["raw"] % 2
                cn["raw"] += 1
                R, rk = raw[ri], "raw%d" % ri
                S.dma("sp", lambda e, R=R, g=g: e.dma_start(out=R[:], in_=self.zT[1280 + g * 128:1408 + g * 128, :]), writes=[rk])
                S.op("act", lambda e, R=R: e.activation(out=vT[:], in_=R[:], func=AF.Copy), reads=[rk], writes=["vT"])
                for t in range(NT):
                    PT_, ptk = psT[t % 2], "psT%d" % (t % 2)
                    S.op("pe", lambda e, PT_=PT_, t=t: e.transpose(out=PT_[:, :], in_=vT[:, t * 128:(t + 1) * 128],
                                                                   identity=cstb[:, 0, :]), reads=["vT", "cstb"], writes=[ptk])
                    S.op("dve", lambda e, PT_=PT_, t=t: e.tensor_copy(out=V[:, t, :], in_=PT_[:, :]), reads=[ptk],
                         writes=["V%d" % t])
                self.join(["V%d" % t for t in range(NT)], "V")
                for hq in range(4):
                    h = g * 4 + hq
                    load_rope(h * 128, qT, "qT")
                    for ck in range(2):
                        for (t0, tn) in TBLK:
                            pi = cn["ps"] % 2
                            cn["ps"] += 1
                            PS, pk = psS[pi], "psS%d" % pi
                            S.op("pe", lambda e, PS=PS, ck=ck, t0=t0, tn=tn: e.matmul(
                                PS[:, 0:tn], kT[:, ck * 128:(ck + 1) * 128], qT[:, t0:t0 + tn], start=True, stop=True),
                                reads=["kT", "qT"], writes=[pk])
                            S.op("act", lambda e, PS=PS, ck=ck, t0=t0, tn=tn: e.activation(
                                out=PTc[:, ck, t0:t0 + tn], in_=PS[:, 0:tn], func=AF.Exp, scale=scale),
                                reads=[pk], writes=["PTc"])
                    for j in range(16):
                        lo, hi = max(j - 1, 0), min(j + 1, 15)
                        nq = hi - lo + 1
                        pi = cn["ps"] % 2
                        cn["ps"] += 1
                        PS, pk = psS[pi], "psS%d" % pi
                        S.op("pe", lambda e, PS=PS, j=j, lo=lo, nq=nq: e.matmul(
                            PS[:, 0:nq * 128], kT[:, LC + j * 128:LC + (j + 1) * 128],
                            qT[:, LC + lo * 128:LC + (lo + nq) * 128], start=True, stop=True),
                            reads=["kT", "qT"], writes=[pk])
                        r0 = lo - j + 1
                        S.op("act", lambda e, PS=PS, j=j, r0=r0, nq=nq: e.activation(
                            out=PTl[:, j, r0:r0 + nq, :], in_=PS[:, 0:nq * 128].rearrange("p (a b) -> p a b", b=128),
                            func=AF.Exp, scale=scale), reads=[pk], writes=["PTl"])
                        if j >= 1:
                            S.op("pool", lambda e, j=j: e.tensor_tensor(out=PTl[:, j, 0, :], in0=PTl[:, j, 0, :],
                                                                        in1=cstb[:, 3, :], op=ALU.mult),
                                 reads=["PTl", "cstb"], writes=["PTl"])
                        if j <= 14:
                            S.op("pool", lambda e, j=j: e.tensor_tensor(out=PTl[:, j, 2, :], in0=PTl[:, j, 2, :],
                                                                        in1=cstb[:, 4, :], op=ALU.mult),
                                 reads=["PTl", "cstb"], writes=["PTl"])
                    for qt in range(NT):
                        contrib = [(ck, PTc[:, ck, qt * 128:(qt + 1) * 128]) for ck in range(2)]
                        if qt >= 2:
                            n = qt - 2
                            for j in range(max(n - 1, 0), min(n + 1, 15) + 1):
                                contrib.append((2 + j, PTl[:, j, n - j + 1, :]))
                        oi = cn["o"] % 2
                        cn["o"] += 1
                        O, Dn = psO[oi], psD[oi]
                        for idx, (vt, rhs) in enumerate(contrib):
                            st_, sp_ = (idx == 0), (idx == len(contrib) - 1)
                            S.op("pe", lambda e, O=O, vt=vt, rhs=rhs, st_=st_, sp_=sp_: e.matmul(
                                O[:, :], V[:, vt, :], rhs, start=st_, stop=sp_), reads=["V", "PTc", "PTl"], writes=["psO%d" % oi])
                            S.op("pe", lambda e, Dn=Dn, rhs=rhs, st_=st_, sp_=sp_: e.matmul(
                                Dn[:, :], cstb[:, 1, :], rhs, start=st_, stop=sp_), reads=["PTc", "PTl"], writes=["psD%d" % oi])
                        RC = rc[oi]
                        S.op("dve", lambda e, RC=RC, Dn=Dn, h=h: e.tensor_scalar(out=RC[:], in0=Dn[:, :], scalar1=esk[:, h:h + 1],
                                                                             scalar2=None, op0=ALU.add),
                             reads=["psD%d" % oi, "esk"], writes=["rc%d" % oi])
                        S.op("dve", lambda e, RC=RC: e.reciprocal(out=RC[:], in_=RC[:]), reads=["rc%d" % oi], writes=["rc%d" % oi])
                        si = cn["ost"] % 3
                        cn["ost"] += 1
                        OS = ost[si]
                        S.op("dve", lambda e, OS=OS, O=O, RC=RC: e.tensor_tensor(out=OS[:], in0=O[:, :], in1=RC[:], op=ALU.mult),
                             reads=["psO%d" % oi, "rc%d" % oi], writes=["ost%d" % si])
                        S.dma("pool", lambda e, OS=OS, h=h, qt=qt: e.dma_start(
                            out=self.mixT[h * 128:(h + 1) * 128, qt * 128:(qt + 1) * 128], in_=OS[:]), reads=["ost%d" % si])
            S.emit(block)
    def phase_gdn(self, l):
        nc, S = self.nc, self.S
        cst, cstb = self.cst, self.cstb
        ORD = [list(range(NT)), [1, 0] + list(range(NT - 1, 1, -1))]
        with ExitStack() as eso:
            G = self.sb(eso, "G", [128, NT, 16], F32)
            BETA = self.sb(eso, "BETA", [128, NT, 16], F32)
            GC = self.sb(eso, "GC", [128, NT, 16], F32)
            TOT = self.sb(eso, "TOT", [128, NT, 16], F32)
            E = self.sb(eso, "E", [128, NT, 16], F32)
            Fd = self.sb(eso, "Fd", [128, NT, 16], F32)
            BE = self.sb(eso, "BE", [128, NT, 16], F32)
            GL = self.sb(eso, "GL", [128, NT, 16], F32)
            onec = self.sb(eso, "onec", [128, 1], F32)
            with ExitStack() as es, nc.Block() as block:
                abT = self.sb(es, "abT", [32, T], F32)
                abm = self.sb(es, "abm", [128, NT, 32], F32)
                al = self.sb(es, "al", [128, 16], F32)
                dtb = self.sb(es, "dtb", [128, 16], F32)
                cw = self.sb(es, "cw", [128, 24, 5], F32)
                xg = self.sb(es, "xg", [128, NT, 16], F32)
                Zp = [self.sb(es, "Zp", [128, T + 8], F32) for _ in range(2)]
                acc = [self.sb(es, "acc", [128, T], F32) for _ in range(2)]
                sl_ = [self.sb(es, "sl", [128, T], F32) for _ in range(2)]
                sq = [self.sb(es, "gsq", [128, 512], F32) for _ in range(2)]
                ri = [self.sb(es, "gri", [128, 512], F32) for _ in range(2)]
                ob = [self.sb(es, "gob", [128, T], BF16) for _ in range(2)]
                tm = [self.sb(es, "gtm", [128, NT, 128], BF16) for _ in range(2)]
                pabA = self.ps(es, "pabA", [128, 512], F32)
                pabB = self.ps(es, "pabB", [128, 512], F32)
                pgcb = self.ps(es, "pgc", [128, 512], F32)
                ptotb = self.ps(es, "ptot", [128, 512], F32)
                pgc = pgcb[:, 0:NT * 16].rearrange("p (n c) -> p n c", c=16)
                ptot = ptotb[:, 0:NT * 16].rearrange("p (n c) -> p n c", c=16)
                pss = [self.ps(es, "pss", [128, 512], F32) for _ in range(2)]
                ptrb = [self.ps(es, "ptr", [128, 1024], BF16) for _ in range(2)]
                ptr = [ptrb[0][:, 0:128], ptrb[1][:, 0:128]]
                S.op("pool", lambda e: e.memset(onec[:], 1.0), writes=["onec"])
                for z in Zp:
                    S.op("pool", lambda e, z=z: e.memset(z[:], 0.0), writes=["zpc%d" % Zp.index(z), "zpl%d" % Zp.index(z)])
                S.dma("sp", lambda e: e.dma_start(out=abT[:], in_=self.zT[5632:5664, :]), writes=["abT"])
                S.dma("sp", lambda e: e.dma_start(out=al[:], in_=self.alog[l][:, :]), writes=["al"])
                S.dma("sp", lambda e: e.dma_start(out=dtb[:], in_=self.dtb[l][:, :]), writes=["dtb"])
                S.dma("sp", lambda e: e.dma_start(out=cw[:], in_=self.convw[l][:, :, :]), writes=["cw"])
                for t in range(NT):
                    pb_ = pabA if t < 9 else pabB
                    S.op("pe", lambda e, t=t, pb_=pb_: e.transpose(out=pb_[:, (t % 9) * 32:(t % 9 + 1) * 32], in_=abT[:, t * 128:(t + 1) * 128],
                                                                   identity=cst[0:32, 0, 0:32]), reads=["abT", "cst"], writes=["pab"])
                S.op("dve", lambda e: e.tensor_copy(out=abm[:, 0:9, :], in_=pabA[:, 0:288].rearrange("p (n c) -> p n c", c=32)), reads=["pab"], writes=["abm0"])
                S.op("dve", lambda e: e.tensor_copy(out=abm[:, 9:18, :], in_=pabB[:, 0:288].rearrange("p (n c) -> p n c", c=32)), reads=["pab"], writes=["abm1"])
                self.join(["abm0", "abm1"], "abm")
                ab5 = abm[:].rearrange("p n (d t h) -> p n d t h", d=2, t=2)
                v4 = lambda a: a[:].rearrange("p n (d h) -> p n d h", d=2)
                b4 = lambda a: a[:].rearrange("p (d h) -> p d h", d=2).unsqueeze(1).to_broadcast([128, NT, 2, 8])
                S.op("dve", lambda e: e.tensor_tensor(out=v4(xg), in0=ab5[:, :, :, 0, :], in1=b4(dtb), op=ALU.add),
                     reads=["abm", "dtb"], writes=["xg"])
                S.op("act", lambda e: e.activation(out=xg[:], in_=xg[:], func=AF.Exp), reads=["xg"], writes=["xg"])
                S.op("act", lambda e: e.activation(out=xg[:], in_=xg[:], func=AF.Ln, bias=onec[:, 0:1]), reads=["xg", "onec"], writes=["xg"])
                S.op("act", lambda e: e.activation(out=al[:], in_=al[:], func=AF.Exp), reads=["al"], writes=["al"])
                S.op("dve", lambda e: e.tensor_scalar(out=al[:], in0=al[:], scalar1=-1.0, scalar2=None, op0=ALU.mult),
                     reads=["al"], writes=["al"])
                S.op("dve", lambda e: e.tensor_tensor(out=v4(G), in0=v4(xg), in1=b4(al), op=ALU.mult), reads=["xg", "al"], writes=["G"])
                S.op("act", lambda e: e.activation(out=v4(BETA), in_=ab5[:, :, :, 1, :], func=AF.Sigmoid), reads=["abm"], writes=["BETA"])
                for t in range(NT):
                    for d in range(2):
                        S.op("pe", lambda e, t=t, d=d: e.matmul(pgc[:, t, d * 8:(d + 1) * 8], cst[:, 3 + d, :],
                                                                G[:, t, d * 8:(d + 1) * 8], start=True, stop=True),
                             reads=["G", "cst"], writes=["pgc"])
                    S.op("pe", lambda e, t=t: e.matmul(ptot[:, t, :], cst[:, 1, :], G[:, t, :], start=True, stop=True),
                         reads=["G", "cst"], writes=["ptot"])
                S.op("dve", lambda e: e.tensor_copy(out=GC[:], in_=pgc), reads=["pgc"], writes=["GC"])
                S.op("dve", lambda e: e.tensor_copy(out=TOT[:], in_=ptot), reads=["ptot"], writes=["TOT"])
                S.op("act", lambda e: e.activation(out=E[:], in_=GC[:], func=AF.Exp), reads=["GC"], writes=["E"])
                S.op("dve", lambda e: e.tensor_tensor(out=Fd[:], in0=TOT[:], in1=GC[:], op=ALU.subtract), reads=["TOT", "GC"], writes=["Fd"])
                S.op("act", lambda e: e.activation(out=Fd[:], in_=Fd[:], func=AF.Exp), reads=["Fd"], writes=["Fd"])
                S.op("act", lambda e: e.activation(out=GL[:], in_=TOT[:], func=AF.Exp), reads=["TOT"], writes=["GL"])
                S.op("dve", lambda e: e.tensor_tensor(out=BE[:], in0=BETA[:], in1=E[:], op=ALU.mult), reads=["BETA", "E"], writes=["BE"])
                cnt = [0]
                for h in range(8):
                    for which in range(3):
                        i = cnt[0] % 2
                        cnt[0] += 1
                        row0 = 1536 + which * 1024 + h * 128
                        ch = which * 8 + h
                        z, A_, SL, OB, TM = Zp[i], acc[i], sl_[i], ob[i], tm[i]
                        S.dma("sp", lambda e, z=z, row0=row0: e.dma_start(out=z[:, 2:2 + LC], in_=self.zT[row0:row0 + 128, 0:LC]),
                              writes=["zpc%d" % i])
                        S.dma("sp", lambda e, z=z, row0=row0: e.dma_start(out=z[:, 262:262 + L], in_=self.zT[row0:row0 + 128, LC:T]),
                              writes=["zpl%d" % i])
                        for (o0, n, zo, zk) in ((0, LC, 0, "zpc%d" % i), (LC, L, 260, "zpl%d" % i)):
                            S.op("dve", lambda e, z=z, A_=A_, o0=o0, n=n, zo=zo, ch=ch: e.tensor_scalar(
                                out=A_[:, o0:o0 + n], in0=z[:, zo:zo + n], scalar1=cw[:, ch, 0:1], scalar2=None, op0=ALU.mult),
                                reads=[zk, "cw"], writes=["acc%d" % i])
                            for j in range(1, 5):
                                S.op("dve", lambda e, z=z, A_=A_, o0=o0, n=n, zo=zo, ch=ch, j=j: e.scalar_tensor_tensor(
                                    out=A_[:, o0:o0 + n], in0=z[:, zo + j:zo + j + n], scalar=cw[:, ch, j:j + 1],
                                    in1=A_[:, o0:o0 + n], op0=ALU.mult, op1=ALU.add), reads=[zk, "cw", "acc%d" % i], writes=["acc%d" % i])
                        S.op("act", lambda e, A_=A_, SL=SL: e.activation(out=SL[:], in_=A_[:], func=AF.Silu),
                             reads=["acc%d" % i], writes=["sl%d" % i])
                        if which < 2:
                            qs = (128 ** -0.5) if which == 0 else 1.0
                            for bi, (t0, tn) in enumerate(TBLK):
                                bb = bi % 2
                                S.op("act", lambda e, SL=SL, t0=t0, tn=tn, bb=bb: e.activation(out=sq[bb][:, 0:tn], in_=SL[:, t0:t0 + tn],
                                                                                             func=AF.Square), reads=["sl%d" % i], writes=["gsq%d" % bb])
                                S.op("pe", lambda e, t0=t0, tn=tn, bb=bb: e.matmul(pss[bb][:, 0:tn], cst[:, 1, :], sq[bb][:, 0:tn],
                                                                                 start=True, stop=True), reads=["gsq%d" % bb], writes=["pss%d" % bb])
                                S.op("act", lambda e, tn=tn, bb=bb: e.activation(out=ri[bb][:, 0:tn], in_=pss[bb][:, 0:tn], func=AF.Ln,
                                                                               bias=self.epsc[:, 0:1]), reads=["pss%d" % bb], writes=["gri%d" % bb])
                                S.op("act", lambda e, tn=tn, bb=bb: e.activation(out=ri[bb][:, 0:tn], in_=ri[bb][:, 0:tn], func=AF.Exp, scale=-0.5),
                                     reads=["gri%d" % bb], writes=["gri%d" % bb])
                                S.op("dve", lambda e, SL=SL, OB=OB, t0=t0, tn=tn, bb=bb, qs=qs: e.scalar_tensor_tensor(
                                    out=OB[:, t0:t0 + tn], in0=SL[:, t0:t0 + tn], scalar=qs, in1=ri[bb][:, 0:tn],
                                    op0=ALU.mult, op1=ALU.mult), reads=["sl%d" % i, "gri%d" % bb], writes=["gob%d" % i])
                            dst = self.gq if which == 0 else self.gk
                            S.dma("act", lambda e, OB=OB, dst=dst, h=h: e.dma_start(out=dst[h, :, :], in_=OB[:]), reads=["gob%d" % i])
                        else:
                            S.op("act", lambda e, SL=SL, OB=OB: e.activation(out=OB[:], in_=SL[:], func=AF.Copy),
                                 reads=["sl%d" % i], writes=["gob%d" % i])
                        if which >= 1:
                            for t in range(NT):
                                S.op("pe", lambda e, OB=OB, t=t: e.transpose(out=ptr[t % 2], in_=OB[:, t * 128:(t + 1) * 128],
                                                                             identity=cstb[:, 0, :]), reads=["gob%d" % i], writes=["ptr%d" % (t % 2)])
                                S.op("dve", lambda e, TM=TM, t=t: e.tensor_copy(out=TM[:, t, :], in_=ptr[t % 2]),
                                     reads=["ptr%d" % (t % 2)], writes=["gtm%d" % i])
                            dst = self.gkm if which == 1 else self.gvm
                            S.dma("act", lambda e, TM=TM, dst=dst, h=h: e.dma_start(
                                out=dst[h].rearrange("(n p) d -> p n d", p=128), in_=TM[:]), reads=["gtm%d" % i])
                S.emit(block)
            if getattr(self, "gdn_stop", 9) < 2:
                return
            with ExitStack() as es, nc.Block() as block:
                NSET = 8
                f32t = lambda n: [self.sb(es, n, [128, 128], F32) for _ in range(NSET)]
                b16t = lambda n: [self.sb(es, n, [128, 128], BF16) for _ in range(NSET)]
                rhsG = [self.sb(es, "rhsG", [128, 256], F32) for _ in range(NSET)]
                t1, Ex, t2, Ey, EB, Mx, XA, My, QA, usb, osb, gcs = [f32t(n) for n in
                    ("t1", "Ex", "t2", "Ey", "EB", "Mx", "XA", "My", "QA", "usb", "osb", "gcs")]
                QKm, qdec, TT, bv, kbe, kdec, wsb, vnew, kt, qt, km, vm = [b16t(n) for n in
                    ("QKm", "qdec", "TT", "bv", "kbe", "kdec", "wsb", "vnew", "kt", "qt", "km", "vm")]
                Pm = [[self.sb(es, "Pm", [128, 128], F32) for _ in range(2)] for _ in range(NSET)]
                Xm = [[self.sb(es, "Xm", [128, 128], F32) for _ in range(2)] for _ in range(NSET)]
                Yb = [[self.sb(es, "Yb", [128, 128], F32) for _ in range(2)] for _ in range(NSET)]
                Yo = [[self.sb(es, "Yo", [128, 128], F32) for _ in range(3)] for _ in range(NSET)]
                XF, YF, Qs, M1s = [f32t(n) for n in ("XF", "YF", "Qs", "M1s")]
                Sf = [[self.sb(es, "Sf", [128, 128], F32) for _ in range(2)] for _ in range(8)]
                Sb = [[self.sb(es, "Sb", [128, 128], BF16) for _ in range(2)] for _ in range(8)]
                bk = [[self.ps(es, "bk", [128, 512], F32) for _ in range(1)] for _ in range(NSET)]
                for h in range(8):
                    for d in range(2):
                        S.op("pool", lambda e, h=h, d=d: e.memset(Sf[h][d][:], 0.0), writes=["Sf%d%d" % (h, d)])
                        S.op("pool", lambda e, h=h, d=d: e.memset(Sb[h][d][:], 0.0), writes=["Sb%d%d" % (h, d)])
                def body(i, h, d, t):
                    K = lambda n: "%s%d" % (n, i)
                    col = d * 8 + h
                    gcol, bcol, gccol = G[:, t, col:col + 1], BETA[:, t, col:col + 1], GC[:, t, col:col + 1]
                    becol, fcol, glcol = BE[:, t, col:col + 1], Fd[:, t, col:col + 1], GL[:, t, col:col + 1]
                    tsl = slice(t * 128, (t + 1) * 128)
                    Bk = bk[i][0]
                    A, B, C = Bk[:, 0:128], Bk[:, 128:256], Bk[:, 256:512]
                    OP, M1, OX, Yp = Bk[:, 0:128], Bk[:, 128:256], Bk[:, 256:384], Bk[:, 384:512]
                    Sp, U, W, O2 = Bk[:, 0:128], Bk[:, 128:256], Bk[:, 256:384], Bk[:, 384:512]
                    Dp = Bk[:, 128:256]
                    yield S.dma("sp", lambda e, i=i, h=h, tsl=tsl: e.dma_start(out=kt[i][:], in_=self.gk[h, :, tsl]), writes=[K("kt")])
                    yield S.dma("sp", lambda e, i=i, h=h, tsl=tsl: e.dma_start(out=qt[i][:], in_=self.gq[h, :, tsl]), writes=[K("qt")])
                    yield S.dma("sp", lambda e, i=i, h=h, tsl=tsl: e.dma_start(out=km[i][:], in_=self.gkm[h, tsl, :]), writes=[K("km")])
                    yield S.dma("sp", lambda e, i=i, h=h, tsl=tsl: e.dma_start(out=vm[i][:], in_=self.gvm[h, tsl, :]), writes=[K("vm")])
                    yield S.op("pe", lambda e, A=A, i=i: e.matmul(A, kt[i][:], kt[i][:], start=True, stop=True), reads=[K("kt")], writes=[K("bank0"), K("A")])
                    yield S.op("pe", lambda e, B=B, i=i: e.matmul(B, kt[i][:], qt[i][:], start=True, stop=True), reads=[K("kt"), K("qt")], writes=[K("bank0"), K("B")])
                    yield S.op("pool", lambda e, i=i, d=d, gcol=gcol: e.tensor_scalar(out=rhsG[i][:, 0:128], in0=cst[:, 3 + d, :], scalar1=gcol,
                                                                             scalar2=None, op0=ALU.mult), reads=["G"], writes=[K("rhsGa")])
                    yield S.op("act", lambda e, i=i, bcol=bcol: e.activation(out=rhsG[i][:, 128:256], in_=cst[:, 0, :], func=AF.Copy, scale=bcol),
                               reads=["BETA"], writes=[K("rhsGb")])
                    yield S.op("pe", lambda e, C=C, i=i: e.matmul(C, cst[:, 1, :], rhsG[i][:], start=True, stop=True), reads=[K("rhsGa"), K("rhsGb")],
                               writes=[K("bank0"), K("C")])
                    yield S.op("dve", lambda e, C=C, i=i, gccol=gccol: e.tensor_scalar(out=t1[i][:], in0=C[:, 0:128], scalar1=gccol, scalar2=0.0,
                                                                               op0=ALU.subtract, op1=ALU.min), reads=[K("C")], writes=[K("bank0"), K("t1")])
                    yield S.op("act", lambda e, i=i: e.activation(out=Ex[i][:], in_=t1[i][:], func=AF.Exp), reads=[K("t1")], writes=[K("Ex")])
                    yield S.op("dve", lambda e, C=C, i=i, gccol=gccol: e.tensor_scalar(out=t2[i][:], in0=C[:, 0:128], scalar1=gccol, scalar2=0.0,
                                                                               op0=ALU.subtract, op1=ALU.max), reads=[K("C")], writes=[K("bank0"), K("t2")])
                    yield S.op("act", lambda e, i=i: e.activation(out=Ey[i][:], in_=t2[i][:], func=AF.Exp, scale=-1.0), reads=[K("t2")], writes=[K("Ey")])
                    yield S.op("dve", lambda e, C=C, i=i: e.tensor_copy(out=gcs[i][:], in_=C[:, 0:128]), reads=[K("C")], writes=[K("bank0"), K("gcs")])
                    yield S.op("act", lambda e, i=i: e.activation(out=EB[i][:], in_=gcs[i][:], func=AF.Exp), reads=[K("gcs")], writes=[K("EB")])
                    yield S.op("dve", lambda e, C=C, i=i, d=d: e.tensor_tensor(out=Mx[i][:], in0=C[:, 128:256], in1=cst[:, 11 + d, :], op=ALU.mult),
                               reads=[K("C")], writes=[K("bank0"), K("Mx")])
                    yield S.op("dve", lambda e, A=A, i=i: e.tensor_tensor(out=XA[i][:], in0=A, in1=Ex[i][:], op=ALU.mult), reads=[K("A"), K("Ex")],
                               writes=[K("bank0"), K("XA")])
                    yield S.op("pool", lambda e, i=i: e.tensor_tensor(out=Xm[i][0][:], in0=XA[i][:], in1=Mx[i][:], op=ALU.mult),
                               reads=[K("XA"), K("Mx")], writes=[K("X0")])
                    yield S.op("pool", lambda e, i=i: e.tensor_tensor(out=Pm[i][1][:], in0=Xm[i][0][:], in1=cst[:, 0, :], op=ALU.add),
                               reads=[K("X0")], writes=[K("P1")])
                    yield S.op("dve", lambda e, A=A, i=i: e.tensor_tensor(out=YF[i][:], in0=A, in1=Ey[i][:], op=ALU.mult), reads=[K("A"), K("Ey")],
                               writes=[K("bank0"), K("YF")])
                    yield S.op("dve", lambda e, i=i, d=d, bcol=bcol: e.scalar_tensor_tensor(out=Yb[i][0][:], in0=YF[i][:], scalar=bcol, in1=cst[:, 13 + d * 4, :],
                                                                                     op0=ALU.mult, op1=ALU.mult), reads=[K("YF"), "BETA"], writes=[K("Y0")])
                    for lv in range(3):
                        yield S.op("dve", lambda e, i=i, d=d, lv=lv, bcol=bcol: e.scalar_tensor_tensor(
                            out=Yo[i][lv][:], in0=YF[i][:], scalar=bcol, in1=cst[:, 14 + d * 4 + lv, :], op0=ALU.mult, op1=ALU.mult),
                            reads=[K("YF"), "BETA"], writes=[K("Yo%d" % lv)])
                    yield S.op("dve", lambda e, B=B, i=i: e.tensor_tensor(out=QA[i][:], in0=B, in1=Ex[i][:], op=ALU.mult), reads=[K("B"), K("Ex")],
                               writes=[K("bank0"), K("QA")])
                    yield S.op("pool", lambda e, i=i, d=d: e.tensor_tensor(out=QKm[i][:], in0=QA[i][:], in1=cst[:, 3 + d, :], op=ALU.mult),
                               reads=[K("QA")], writes=[K("QKm")])
                    yield S.op("pool", lambda e, i=i: e.tensor_tensor(out=qdec[i][:], in0=qt[i][:], in1=EB[i][:], op=ALU.mult), reads=[K("qt"), K("EB")],
                               writes=[K("qdec")])
                    yield S.op("act", lambda e, i=i, bcol=bcol: e.activation(out=bv[i][:], in_=vm[i][:], func=AF.Copy, scale=bcol),
                               reads=[K("vm"), "BETA"], writes=[K("bv")])
                    yield S.op("act", lambda e, i=i, becol=becol: e.activation(out=kbe[i][:], in_=km[i][:], func=AF.Copy, scale=becol),
                               reads=[K("km"), "BE"], writes=[K("kbe")])
                    yield S.op("act", lambda e, i=i, fcol=fcol: e.activation(out=kdec[i][:], in_=km[i][:], func=AF.Copy, scale=fcol),
                               reads=[K("km"), "Fd"], writes=[K("kdec")])
                    a, b = 0, 1
                    for m in range(4):
                        ka, kb_ = str(a), str(b)
                        if m > 0:
                            yield S.op("pe", lambda e, OP=OP, i=i, a=a: e.matmul(OP, Yb[i][a][:], Pm[i][a][:], start=True, stop=True),
                                       reads=[K("Y" + ka), K("P" + ka)], writes=[K("bank0"), K("OP")])
                        if m < 3:
                            yield S.op("pe", lambda e, OX=OX, i=i, a=a: e.matmul(OX, Yb[i][a][:], Xm[i][a][:], start=True, stop=True),
                                       reads=[K("Y" + ka), K("X" + ka)], writes=[K("bank0"), K("OX")])
                            yield S.op("pe", lambda e, Yp=Yp, i=i, a=a: e.matmul(Yp, Xm[i][a][:], Yb[i][a][:], start=True, stop=True),
                                       reads=[K("Y" + ka), K("X" + ka)], writes=[K("bank0"), K("Yp")])
                        if m > 0:
                            yield S.op("dve", lambda e, OP=OP, i=i, a=a, b=b: e.tensor_tensor(out=Pm[i][b][:], in0=OP, in1=Pm[i][a][:], op=ALU.add),
                                       reads=[K("OP"), K("P" + ka)], writes=[K("bank0"), K("P" + kb_)])
                        if m < 3:
                            yield S.op("act", lambda e, OX=OX, i=i, b=b: e.activation(out=Xm[i][b][:], in_=OX, func=AF.Copy),
                                       reads=[K("OX")], writes=[K("bank0"), K("X" + kb_)])
                            yield S.op("act", lambda e, Yp=Yp, i=i, b=b: e.activation(out=Yb[i][b][:], in_=Yp, func=AF.Copy),
                                       reads=[K("Yp")], writes=[K("bank0"), K("Y" + kb_)])
                        a, b = b, a
                    yield S.op("pe", lambda e, M1=M1, i=i, a=a: e.transpose(out=M1, in_=Pm[i][a][:], identity=cst[:, 0, :]),
                         reads=[K("P" + str(a))], writes=[K("bank0"), K("M1")])
                    yield S.op("act", lambda e, M1=M1, i=i: e.activation(out=Qs[i][:], in_=M1, func=AF.Copy), reads=[K("M1")], writes=[K("bank0"), K("Qs")])
                    for lv in range(3):
                        ka, kb_ = str(a), str(b)
                        yield S.op("pe", lambda e, M1=M1, i=i, a=a, lv=lv: e.matmul(M1, Yo[i][lv][:], Pm[i][a][:], start=True, stop=True),
                             reads=[K("Yo%d" % lv), K("P" + ka)], writes=[K("bank0"), K("M1")])
                        yield S.op("act", lambda e, M1=M1, i=i: e.activation(out=M1s[i][:], in_=M1, func=AF.Copy), reads=[K("M1")], writes=[K("bank0"), K("M1s")])
                        yield S.op("pe", lambda e, OP=OP, i=i: e.matmul(OP, Qs[i][:], M1s[i][:], start=True, stop=True),
                             reads=[K("Qs"), K("M1s")], writes=[K("bank0"), K("OP")])
                        if lv < 2:
                            yield S.op("dve", lambda e, OP=OP, i=i, a=a, b=b: e.tensor_tensor(out=Pm[i][b][:], in0=OP, in1=Pm[i][a][:], op=ALU.add),
                                 reads=[K("OP"), K("P" + ka)], writes=[K("bank0"), K("P" + kb_)])
                            yield S.op("pe", lambda e, M1=M1, i=i, b=b: e.transpose(out=M1, in_=Pm[i][b][:], identity=cst[:, 0, :]),
                                 reads=[K("P" + kb_)], writes=[K("bank0"), K("M1")])
                            yield S.op("act", lambda e, M1=M1, i=i: e.activation(out=Qs[i][:], in_=M1, func=AF.Copy), reads=[K("M1")], writes=[K("bank0"), K("Qs")])
                        else:
                            yield S.op("dve", lambda e, OP=OP, i=i, a=a: e.tensor_tensor(out=TT[i][:], in0=OP, in1=Pm[i][a][:], op=ALU.add),
                                 reads=[K("OP"), K("P" + ka)], writes=[K("bank0"), K("TT")])
                        a, b = b, a
                    yield S.op("pe", lambda e, U=U, i=i: e.matmul(U, TT[i][:], bv[i][:], start=True, stop=True), reads=[K("TT"), K("bv")], writes=[K("bank0"), K("U")])
                    yield S.op("pe", lambda e, W=W, i=i: e.matmul(W, kbe[i][:], TT[i][:], start=True, stop=True), reads=[K("TT"), K("kbe")], writes=[K("bank0"), K("W")])
                    yield S.op("dve", lambda e, U=U, i=i: e.tensor_copy(out=usb[i][:], in_=U), reads=[K("U")], writes=[K("bank0"), K("usb")])
                    yield S.op("act", lambda e, W=W, i=i: e.activation(out=wsb[i][:], in_=W, func=AF.Copy), reads=[K("W")], writes=[K("bank0"), K("wsb")])
                    sk, sbk = "Sf%d%d" % (h, d), "Sb%d%d" % (h, d)
                    SF, SB = Sf[h][d], Sb[h][d]
                    yield S.op("pe", lambda e, Sp=Sp, i=i, SB=SB: e.matmul(Sp, wsb[i][:], SB[:], start=True, stop=True), reads=[K("wsb"), sbk], writes=[K("bank0"), K("Sp")])
                    yield S.op("dve", lambda e, Sp=Sp, i=i: e.tensor_tensor(out=vnew[i][:], in0=usb[i][:], in1=Sp, op=ALU.subtract),
                         reads=[K("usb"), K("Sp")], writes=[K("bank0"), K("vnew")])
                    yield S.op("pe", lambda e, O2=O2, i=i, SB=SB: e.matmul(O2, SB[:], qdec[i][:], start=True, stop=False), reads=[sbk, K("qdec")], writes=[K("bank0"), K("O2")])
                    yield S.op("pe", lambda e, O2=O2, i=i: e.matmul(O2, vnew[i][:], QKm[i][:], start=False, stop=True), reads=[K("vnew"), K("QKm")], writes=[K("bank0"), K("O2")])
                    yield S.op("act", lambda e, O2=O2, i=i: e.activation(out=osb[i][:], in_=O2, func=AF.Copy), reads=[K("O2")], writes=[K("bank0"), K("osb")])
                    yield S.dma("act", lambda e, i=i, d=d, h=h, tsl=tsl: e.dma_start(out=self.oT[d, h, :, tsl], in_=osb[i][:]), reads=[K("osb")])
                    yield S.op("pe", lambda e, Dp=Dp, i=i: e.matmul(Dp, kdec[i][:], vnew[i][:], start=True, stop=True), reads=[K("kdec"), K("vnew")], writes=[K("bank0"), K("Dp")])
                    yield S.op("dve", lambda e, Dp=Dp, SF=SF, glcol=glcol: e.scalar_tensor_tensor(out=SF[:], in0=SF[:], scalar=glcol, in1=Dp,
                                                                                           op0=ALU.mult, op1=ALU.add), reads=[sk, K("Dp"), "GL"], writes=[K("bank0"), sk])
                    yield S.op("act", lambda e, SF=SF, SB=SB: e.activation(out=SB[:], in_=SF[:], func=AF.Copy), reads=[sk], writes=[sbk])

                order = [(s_, h_, d_) for s_ in range(getattr(self, 'scan_steps', NT))
                         for h_ in range(getattr(self, 'scan_heads', 8)) for d_ in range(2)]
                STAG = getattr(self, "scan_stag", 9)
                slots = [order[k::NSET] for k in range(NSET)]
                gens = [None] * NSET
                nxt = [0] * NSET
                rnd = 0
                while True:
                    alive = False
                    for k in range(NSET):
                        if gens[k] is None and nxt[k] < len(slots[k]) and rnd >= k * STAG:
                            s_, h_, d_ = slots[k][nxt[k]]
                            nxt[k] += 1
                            gens[k] = body(k, h_, d_, ORD[d_][s_])
                        if gens[k] is not None:
                            alive = True
                            try:
                                next(gens[k])
                            except StopIteration:
                                gens[k] = None
                        elif nxt[k] < len(slots[k]):
                            alive = True
                    rnd += 1
                    if not alive:
                        break
                S.limit = None
                S.emit(block)
        if getattr(self, "gdn_stop", 9) < 3:
            return
        with ExitStack() as es, nc.Block() as block:
            o0 = [self.sb(es, "o0", [128, T], F32) for _ in range(2)]
            o1 = [self.sb(es, "o1", [128, T], F32) for _ in range(2)]
            gt = [self.sb(es, "gt", [128, T], F32) for _ in range(2)]
            sq = [self.sb(es, "nsq", [128, 512], F32) for _ in range(2)]
            ri = [self.sb(es, "nri", [128, 512], F32) for _ in range(2)]
            yb = [self.sb(es, "nyb", [128, T], BF16) for _ in range(2)]
            gn = self.sb(es, "gn", [128, 1], F32)
            pss = [self.ps(es, "npss", [128, 512], F32) for _ in range(2)]
            S.dma("sp", lambda e: e.dma_start(out=gn[:], in_=self.gnorm[l][:, :]), writes=["gn"])
            for h in range(8):
                i = h % 2
                S.dma("sp", lambda e, i=i, h=h: e.dma_start(out=o0[i][:], in_=self.oT[0, h, :, :]), writes=["o0%d" % i])
                S.dma("sp", lambda e, i=i, h=h: e.dma_start(out=o1[i][:], in_=self.oT[1, h, :, :]), writes=["o1%d" % i])
                S.dma("sp", lambda e, i=i, h=h: e.dma_start(out=gt[i][:], in_=self.zT[4608 + h * 128:4736 + h * 128, :]), writes=["gt%d" % i])
                S.op("pool", lambda e, i=i: e.tensor_tensor(out=o0[i][:], in0=o0[i][:], in1=o1[i][:], op=ALU.add), reads=["o0%d" % i, "o1%d" % i], writes=["o0%d" % i])
                S.op("act", lambda e, i=i: e.activation(out=gt[i][:], in_=gt[i][:], func=AF.Silu), reads=["gt%d" % i], writes=["gt%d" % i])
                for bi, (t0, tn) in enumerate(TBLK):
                    bb = bi % 2
                    S.op("act", lambda e, i=i, t0=t0, tn=tn, bb=bb: e.activation(out=sq[bb][:, 0:tn], in_=o0[i][:, t0:t0 + tn], func=AF.Square),
                         reads=["o0%d" % i], writes=["nsq%d" % bb])
                    S.op("pe", lambda e, tn=tn, bb=bb: e.matmul(pss[bb][:, 0:tn], cst[:, 1, :], sq[bb][:, 0:tn], start=True, stop=True),
                         reads=["nsq%d" % bb], writes=["npss%d" % bb])
                    S.op("act", lambda e, tn=tn, bb=bb: e.activation(out=ri[bb][:, 0:tn], in_=pss[bb][:, 0:tn], func=AF.Ln, scale=1.0 / 128,
                                                                   bias=self.epsc[:, 0:1]), reads=["npss%d" % bb], writes=["nri%d" % bb])
                    S.op("act", lambda e, tn=tn, bb=bb: e.activation(out=ri[bb][:, 0:tn], in_=ri[bb][:, 0:tn], func=AF.Exp, scale=-0.5), reads=["nri%d" % bb], writes=["nri%d" % bb])
                    S.op("dve", lambda e, i=i, t0=t0, tn=tn, bb=bb: e.scalar_tensor_tensor(out=ri[bb][:, 0:tn], in0=o0[i][:, t0:t0 + tn], scalar=gn[:, 0:1],
                                                                                         in1=ri[bb][:, 0:tn], op0=ALU.mult, op1=ALU.mult),
                         reads=["o0%d" % i, "nri%d" % bb, "gn"], writes=["nri%d" % bb])
                    S.op("pool", lambda e, i=i, t0=t0, tn=tn, bb=bb: e.tensor_tensor(out=yb[i][:, t0:t0 + tn], in0=ri[bb][:, 0:tn], in1=gt[i][:, t0:t0 + tn],
                                                                                   op=ALU.mult), reads=["nri%d" % bb, "gt%d" % i], writes=["nyb%d" % i])
                S.dma("act", lambda e, i=i, h=h: e.dma_start(out=self.mixT[1024 + h * 128:1152 + h * 128, :], in_=yb[i][:]), reads=["nyb%d" % i])
            S.emit(block)

    def phase_mixer(self, l):
        if not getattr(self, "skip_attn", False):
            self.phase_attn(l)
        if not getattr(self, "skip_gdn", False):
            self.phase_gdn(l)
    def build_gdnprobe(self):
        nc, es, P = self.nc, self.es, self
        l = 0
        self.consts = P.din("consts", [128, 21, 128])
        self.zT = P.din("zT", [D_IN, T])
        self.alog = {l: P.din("alog0", [128, 16])}
        self.dtb = {l: P.din("dtb0", [128, 16])}
        self.convw = {l: P.din("convw0", [128, 24, 5])}
        self.gnorm = {l: P.din("gnorm0", [128, 1])}
        self.gq = P.dout("gq", [8, 128, T], BF16)
        self.gk = P.dout("gk", [8, 128, T], BF16)
        self.gkm = P.dout("gkm", [8, T, 128], BF16)
        self.gvm = P.dout("gvm", [8, T, 128], BF16)
        self.oT = P.dout("oT", [2, 8, 128, T])
        self.mixT = P.dout("mixT", [D, T], BF16)
        self.S = Sch(nc, es)
        self.cst = P.sb(es, "cst", [128, 21, 128], F32)
        self.cstb = P.sb(es, "cstb", [128, 21, 128], BF16)
        self.junk = P.sb(es, "junk", [128, 8], F32)
        self.epsc = P.sb(es, "epsc", [128, 1], F32)
        S = self.S
        with nc.Block() as block:
            S.dma("sp", lambda e: e.dma_start(out=self.cst[:], in_=self.consts[:, :, :]), writes=["cst"])
            S.op("dve", lambda e: e.tensor_copy(out=self.cstb[:], in_=self.cst[:]), reads=["cst"], writes=["cstb"])
            S.op("pool", lambda e: e.memset(self.epsc[:], EPS), writes=["epsc"])
            S.emit(block)
        self.phase_gdn(0)
        es.close()
        return nc

    def build(self):
        nc = self.nc
        es = self.es
        P = self
        self.xT_in = P.din("xT", [D, T])
        self.c2 = P.din("c2", [128, KC, 2])
        self.consts = P.din("consts", [128, 21, 128])
        self.w_ada = {l: P.din("w_ada%d" % l, [D, 6 * D]) for l in self.layers}
        self.b_ada = {l: P.din("b_ada%d" % l, [128, 96]) for l in self.layers}
        self.nrm = {l: P.din("nrm%d" % l, [128, 2, KC]) for l in self.layers}
        self.w_in = {l: P.din("w_in%d" % l, [D, D_IN]) for l in self.layers}
        if self.upto != "inproj":
            self.w_out = {l: P.din("w_out%d" % l, [D, D]) for l in self.layers}
            self.w_ff1 = {l: P.din("w_ff1%d" % l, [D, D_FF]) for l in self.layers}
            self.w_ff2 = {l: P.din("w_ff2%d" % l, [D_FF, D]) for l in self.layers}
            self.nfin = P.din("nfin", [128, KC])
            self.outT = P.dout("outT", [D, L])
        if self.upto != "inproj":
            self.ropeT = P.din("ropeT", [128, 2, L])
            self.sink = {l: P.din("sink%d" % l, [128, 8]) for l in self.layers}
            self.alog = {l: P.din("alog%d" % l, [128, 16]) for l in self.layers}
            self.dtb = {l: P.din("dtb%d" % l, [128, 16]) for l in self.layers}
            self.convw = {l: P.din("convw%d" % l, [128, 24, 5]) for l in self.layers}
            self.gnorm = {l: P.din("gnorm%d" % l, [128, 1]) for l in self.layers}
        self.gq = P.dscr("gq", [8, 128, T], BF16)
        self.gk = P.dscr("gk", [8, 128, T], BF16)
        self.gkm = P.dscr("gkm", [8, T, 128], BF16)
        self.gvm = P.dscr("gvm", [8, T, 128], BF16)
        self.oT = P.dout("oT", [2, 8, 128, T]) if "oT" in [d[0] for d in self.dbg] else P.dscr("oT", [2, 8, 128, T])
        dn = [d[0] for d in self.dbg]
        self.xT = P.dout("xTs", [D, T]) if "xT" in dn else P.dscr("xTs", [D, T])
        self.zT = P.dscr("zT", [D_IN, T]) if "zT" not in [d[0] for d in self.dbg] else P.dout("zT", [D_IN, T])
        self.mixT = P.dout("mixT", [D, T], BF16) if "mixT" in dn else P.dscr("mixT", [D, T], BF16)
        self.S = Sch(nc, es)
        self.cst = P.sb(es, "cst", [128, 21, 128], F32)
        self.cstb = P.sb(es, "cstb", [128, 21, 128], BF16)
        self.junk = P.sb(es, "junk", [128, 8], F32)
        self.epsc = P.sb(es, "epsc", [128, 1], F32)
        self.mod = {l: P.sb(es, "mod%d" % l, [128, 96, 2], F32) for l in self.layers}
        self.gm1 = {l: P.sb(es, "gm1_%d" % l, [128, 2, KC, 2], F32) for l in self.layers}
        self.phase_init()
        for l in self.layers:
            self.phase_mod(l)
        for l in self.layers:
            self.phase_norm_inproj(l)
            if self.upto == "inproj":
                break
            self.phase_mixer(l)
            if not getattr(self, "skip_tail", False):
                self.phase_dense_tail(l)
        if self.upto == "all":
            self.phase_final()
        es.close()
        return nc


def make_consts():
    c = np.zeros((128, 21, 128), np.float32)
    p = np.arange(128)[:, None]
    f = np.arange(128)[None, :]
    c[:, 0, :] = (p == f)
    c[:, 1, :] = 1.0
    partner = np.where((np.arange(128) % 64) < 32, np.arange(128) + 32, np.arange(128) - 32)
    c[:, 2, :] = (p == partner[None, :])
    c[:, 3, :] = (p <= f)
    c[:, 4, :] = (p >= f)
    c[:, 5, :] = -(p < f).astype(np.float32)
    c[:, 6, :] = -(p > f).astype(np.float32)
    c[:, 7, :] = (p // 16 == f // 16)
    for lv, sz in enumerate((16, 32, 64)):
        c[:, 8 + lv, :] = (p // (2 * sz) == f // (2 * sz)) & (p // sz != f // sz)
    for d in range(2):
        c[:, 11 + d, :] = c[:, 5 + d, :] * c[:, 7, :]
        for k in range(4):
            c[:, 13 + d * 4 + k, :] = c[:, 6 - d, :] * c[:, 7 + k, :]
    return c


def rope_tables():
    pos = np.arange(L)
    row = (pos // 64).astype(np.float32)
    col = (pos % 64).astype(np.float32)
    inv = (10000.0 ** (-np.arange(0, 64, 2, dtype=np.float32) / 64.0)).astype(np.float32)
    ang = np.concatenate([row[:, None] * inv[None, :], col[:, None] * inv[None, :]], axis=-1).astype(np.float32)
    cos, sin = np.cos(ang).astype(np.float32), np.sin(ang).astype(np.float32)
    out = np.zeros((128, 2, L), np.float32)
    for m_ in range(128):
        idx = (m_ % 32) if m_ < 64 else 32 + ((m_ - 64) % 32)
        sign = -1.0 if (m_ % 64) < 32 else 1.0
        out[m_, 0, :] = cos[:, idx]
        out[m_, 1, :] = sign * sin[:, idx]
    return out


def fm_vec(v, nchunk):
    return np.ascontiguousarray(np.asarray(v, np.float32).reshape(nchunk, 128).T)


def core_inputs(inp, b, layers, upto="all"):
    m = {}
    xt = np.concatenate([inp["ctx"][b], inp["x"][b]], axis=0)
    m["xT"] = np.ascontiguousarray(xt.T)
    c2 = np.stack([fm_vec(inp["c"][b], KC), fm_vec(inp["c_ctx"], KC)], axis=-1)
    m["c2"] = np.ascontiguousarray(c2)
    m["consts"] = make_consts()
    for l in layers:
        m["w_ada%d" % l] = inp["w_ada"][l]
        m["b_ada%d" % l] = fm_vec(inp["b_ada"][l], 96)
        m["nrm%d" % l] = np.ascontiguousarray(np.stack([fm_vec(inp["norm_mix"][l], KC), fm_vec(inp["norm_ffn"][l], KC)], axis=1))
        m["w_in%d" % l] = inp["w_in"][l]
        if upto != "inproj":
            m["w_out%d" % l] = inp["w_out"][l]
            m["w_ff1%d" % l] = inp["w_ff1"][l]
            m["w_ff2%d" % l] = inp["w_ff2"][l]
    if upto != "inproj":
        m["nfin"] = fm_vec(inp["norm_final"], KC)
        m["ropeT"] = rope_tables()
        for l in layers:
            m["sink%d" % l] = np.ascontiguousarray(np.broadcast_to(inp["attn_sink"][l][None, :], (128, 8))).astype(np.float32)
            m["alog%d" % l] = np.ascontiguousarray(np.broadcast_to(inp["a_log"][l].reshape(1, 16), (128, 16))).astype(np.float32)
            m["dtb%d" % l] = np.ascontiguousarray(np.broadcast_to(inp["dt_bias"][l].reshape(1, 16), (128, 16))).astype(np.float32)
            cw = inp["conv_w"][l]
            m["convw%d" % l] = np.ascontiguousarray(cw.T.reshape(24, 128, 5).transpose(1, 0, 2)).astype(np.float32)
            m["gnorm%d" % l] = np.ascontiguousarray(inp["gdn_norm"][l].reshape(128, 1)).astype(np.float32)
    return m


_PROG = {}


def kernel(**inputs):
    inp = {k: np.asarray(v) for k, v in inputs.items()}
    layers = list(range(DEPTH))
    if "nc" not in _PROG:
        _PROG["nc"] = Prog(layers).build()
    nc = _PROG["nc"]
    n = 4
    maps = [core_inputs(inp, b, layers) for b in range(n)]
    res = run_bass_kernel_spmd(nc, maps, core_ids=list(range(n)))
    out = np.stack([res.results[b]["outT"].T for b in range(n)], axis=0)
    return np.ascontiguousarray(out.astype(np.float32))
```

```python
import math
from contextlib import ExitStack
import numpy as np
import concourse.bass as bass
import concourse.mybir as mybir
from concourse.bass_utils import run_bass_kernel_spmd

F32 = mybir.dt.float32
BF16 = mybir.dt.bfloat16
ALU = mybir.AluOpType
AF = mybir.ActivationFunctionType

D = 2048
KC = 16
L = 2048
LC = 256
T = L + LC
NT = T // 128
DEPTH = 4
D_IN = 5664
D_FF = 8192
EPS = 1e-6
TBLK = [(0, 256), (256, 512), (768, 512), (1280, 512), (1792, 512)]
NCORES = 8


class Op:
    __slots__ = ("e", "fn", "deps", "idx", "signal", "dma", "dsem", "dval", "prewait")

    def __init__(self, e, fn, deps, idx, dma=False):
        self.e, self.fn, self.deps, self.idx, self.dma = e, fn, deps, idx, dma
        self.signal = False
        self.dsem = None
        self.dval = 0
        self.prewait = None


class Sch:
    ENG = ("pe", "act", "dve", "pool", "sp")
    NDS = 12

    def __init__(self, nc, es):
        self.nc = nc
        self.eng = {"pe": nc.tensor, "act": nc.scalar, "dve": nc.vector, "pool": nc.gpsimd, "sp": nc.sync}
        self.sem = {e: es.enter_context(nc.semaphore("sem_" + e)) for e in self.ENG}
        self.cnt = {e: 0 for e in self.ENG}
        self.dsem = {q: [es.enter_context(nc.semaphore("dsem_%s_%d" % (q, i))) for i in range(self.NDS)]
                     for q in ("sp", "pool", "act")}
        self.dcnt = {q: 0 for q in ("sp", "pool", "act")}
        self.reset()

    def reset(self):
        self.ops = {e: [] for e in self.ENG}
        self.res = {}

    def _mk(self, e, fn, reads, writes, dma):
        deps = set()
        for r in reads:
            st = self.res.get(r)
            if st is not None and st[0] is not None:
                deps.add(st[0])
        for w in writes:
            st = self.res.get(w)
            if st is not None:
                if st[0] is not None:
                    deps.add(st[0])
                deps.update(st[1].values())
        if e == "pe" and not dma:
            deps = {d for d in deps if d.dma or d.e != "pe"}
        o = Op(e, fn, deps, len(self.ops[e]), dma)
        self.ops[e].append(o)
        for d in deps:
            d.signal = True
        for r in reads:
            st = self.res.setdefault(r, [None, {}])
            st[1][e + ("d" if dma else "")] = o
        for w in writes:
            self.res[w] = [o, {}]
        return o

    limit = None
    nrec = 0

    def op(self, e, fn, reads=(), writes=()):
        if self.limit is not None:
            self.nrec += 1
            if self.nrec > self.limit:
                return None
        return self._mk(e, fn, reads, writes, False)

    def dma(self, q, fn, reads=(), writes=()):
        if self.limit is not None:
            self.nrec += 1
            if self.nrec > self.limit:
                return None
        o = self._mk(q, fn, reads, writes, True)
        m = self.dcnt[q]
        self.dcnt[q] += 1
        o.dsem = self.dsem[q][m % self.NDS]
        o.dval = 16 * (m // self.NDS + 1)
        o.prewait = 16 * (m // self.NDS)
        return o

    def emit(self, block, final_wait=True):
        val = {}
        for e in self.ENG:
            c = self.cnt[e]
            pend = []
            for o in self.ops[e]:
                if o.dma:
                    continue
                pend.append(o)
                if o.signal:
                    c += 1
                    for p in pend:
                        val[p] = c
                    pend = []
            self.cnt[e] = c
        dtot = {}

        def run(e):
            def body(eng):
                waited = {}

                def w(sem, v):
                    k = id(sem)
                    if waited.get(k, -1) >= v:
                        return
                    waited[k] = v
                    eng.wait_ge(sem, v)
                for o in self.ops[e]:
                    for d in o.deps:
                        if d.dma:
                            w(d.dsem, d.dval)
                        else:
                            w(self.sem[d.e], val[d])
                    if o.dma:
                        if o.prewait:
                            w(o.dsem, o.prewait)
                        ins = o.fn(eng)
                        ins.then_inc(o.dsem, 16)
                        dtot[id(o.dsem)] = (o.dsem, o.dval)
                    else:
                        ins = o.fn(eng)
                        if o.signal:
                            ins.then_inc(self.sem[e], 1)
                if e == "sp" and final_wait:
                    pass
            return body

        for e in ("pe", "act", "dve", "pool"):
            if self.ops[e]:
                getattr(block, {"pe": "tensor", "act": "scalar", "dve": "vector", "pool": "gpsimd"}[e])(run(e))
        spbody = run("sp")

        def spfull(eng):
            spbody(eng)
            for q in ("pool", "act"):
                for o in self.ops[q]:
                    if o.dma:
                        dtot[id(o.dsem)] = (o.dsem, max(o.dval, dtot.get(id(o.dsem), (None, 0))[1]))
            for sem, v in dtot.values():
                eng.wait_ge(sem, v)
        block.sync(spfull)
        self.reset()


class Prog:
    def __init__(self, layers, upto="all", dbg=()):
        self.layers = layers
        self.upto = upto
        self.dbg = dbg
        self.nc = bass.Bass("TRN2", target_bir_lowering=False)
        self.es = ExitStack()
        self.uid = 0

    def din(self, name, shape, dt=F32):
        return self.nc.dram_tensor(name, list(shape), dt, kind="ExternalInput").ap()

    def dout(self, name, shape, dt=F32):
        return self.nc.dram_tensor(name, list(shape), dt, kind="ExternalOutput").ap()

    def dscr(self, name, shape, dt=F32):
        return self.nc.dram_tensor(name, list(shape), dt).ap()

    def sb(self, es, name, shape, dt):
        self.uid += 1
        return es.enter_context(self.nc.sbuf_tensor("%s_%d" % (name, self.uid), list(shape), dt))

    def ps(self, es, name, shape, dt=F32):
        self.uid += 1
        return es.enter_context(self.nc.psum_tensor("%s_%d" % (name, self.uid), list(shape), dt))

    def build(self):
        nc = self.nc
        es = self.es
        P = self
        self.xT_in = P.din("xT", [D, T])
        self.c2 = P.din("c2", [128, KC, 2])
        self.consts = P.din("consts", [128, 21, 128])
        self.ropeT = P.din("ropeT", [128, 2, L])
        self.w_ada = {l: P.din("w_ada%d" % l, [D, 6 * D]) for l in self.layers}
        self.b_ada = {l: P.din("b_ada%d" % l, [128, 96]) for l in self.layers}
        self.nrm = {l: P.din("nrm%d" % l, [128, 2, KC]) for l in self.layers}
        self.w_in = {l: P.din("w_in%d" % l, [D, D_IN]) for l in self.layers}
        self.w_out = {l: P.din("w_out%d" % l, [D, D]) for l in self.layers}
        self.w_ff1 = {l: P.din("w_ff1%d" % l, [D, D_FF]) for l in self.layers}
        self.w_ff2 = {l: P.din("w_ff2%d" % l, [D_FF, D]) for l in self.layers}
        self.nfin = P.din("nfin", [128, KC])
        self.outT = P.dout("outT", [D, L])
        self.xT = P.dscr("xTs", [D, T])
        self.zqkT = P.dscr("zqkT", [1280, T], BF16)
        self.zv = P.dscr("zv", [T, 256], BF16)
        self.zg = P.dscr("zg", [3072, T])
        self.zgate = P.dscr("zgate", [1024, T], BF16)
        self.zab = P.dscr("zab", [T, 32])
        self.mixT = P.dscr("mixT", [D, T], BF16)
        self.aT = P.dscr("aT", [D_FF, T], BF16)
        self.dbgout = {}
        for name, shape, dt in self.dbg:
            self.dbgout[name] = P.dout("dbg_" + name, shape, dt)

        self.S = Sch(nc, es)
        self.cst = P.sb(es, "cst", [128, 21, 128], F32)
        self.cstb = P.sb(es, "cstb", [128, 21, 128], BF16)
        self.mod = {l: P.sb(es, "mod%d" % l, [128, 96, 2], F32) for l in self.layers}
        self.gm1 = {l: P.sb(es, "gm1_%d" % l, [128, 2, KC, 2], F32) for l in self.layers}

        self.phase_init()
        for l in self.layers:
            self.phase_mod(l)
        for l in self.layers:
            self.phase_norm_inproj(l)
            if self.upto == "inproj":
                break
            if not getattr(self, "skip_tail", False):
                self.phase_dense_tail(l)
        if self.upto == "all":
            self.phase_final()
        return nc

    def phase_init(self):
        nc, S = self.nc, self.S
        with ExitStack() as es, nc.Block() as block:
            S.dma("sp", lambda e: e.dma_start(out=self.cst[:], in_=self.consts[:, :, :]), writes=["cst"])
            S.op("dve", lambda e: e.tensor_copy(out=self.cstb[:], in_=self.cst[:]), reads=["cst"], writes=["cstb"])
            S.op("pool", lambda e: e.memset(self.epsc[:], EPS), writes=["epsc"])
            for i in range(4):
                S.dma("sp", lambda e, i=i: e.dma_start(out=self.xT[i * 512:(i + 1) * 512, :],
                                                      in_=self.xT_in[i * 512:(i + 1) * 512, :]))
            S.emit(block)

    def phase_mod(self, l):
        nc, S = self.nc, self.S
        with ExitStack() as es, nc.Block() as block:
            c2 = self.sb(es, "c2", [128, KC, 2], F32)
            s2 = self.sb(es, "s2", [128, KC, 2], BF16)
            bt = self.sb(es, "bt", [128, 96], F32)
            nr = self.sb(es, "nr", [128, 2, KC], F32)
            NS = 3
            wsl = [self.sb(es, "wad", [128, KC, 512], BF16) for _ in range(NS)]
            pm = self.ps(es, "pm", [128, 96, 2], F32)
            S.dma("sp", lambda e: e.dma_start(out=c2[:], in_=self.c2[:, :, :]), writes=["c2"])
            S.dma("sp", lambda e: e.dma_start(out=bt[:], in_=self.b_ada[l][:, :]), writes=["bt"])
            S.dma("sp", lambda e: e.dma_start(out=nr[:], in_=self.nrm[l][:, :, :]), writes=["nr"])
            S.op("act", lambda e: e.activation(out=s2[:], in_=c2[:], func=AF.Silu), reads=["c2"], writes=["s2"])
            wv = self.w_ada[l].rearrange("(kc p) n -> p kc n", p=128)
            for s in range(24):
                buf = wsl[s % NS]
                key = "wad%d" % (s % NS)
                S.dma("pool", lambda e, s=s, buf=buf: e.dma_start(out=buf[:], in_=wv[:, :, s * 512:(s + 1) * 512]),
                      writes=[key])
                for j in range(4):
                    cc = s * 4 + j
                    for kc in range(KC):
                        S.op("pe", lambda e, cc=cc, kc=kc, j=j, buf=buf: e.matmul(
                            pm[:, cc, :], buf[:, kc, j * 128:(j + 1) * 128], s2[:, kc, :],
                            start=(kc == 0), stop=(kc == KC - 1)), reads=[key, "s2"], writes=["pm"])
            mod = self.mod[l]
            S.op("dve", lambda e: e.tensor_tensor(out=mod[:], in0=pm[:], in1=bt[:].unsqueeze(2).to_broadcast([128, 96, 2]),
                                                  op=ALU.add), reads=["pm", "bt"], writes=["mod"])
            gm = self.gm1[l]
            for n, which in ((0, 1), (1, 4)):
                S.op("dve", lambda e, n=n, which=which: e.tensor_scalar(
                    out=gm[:, n, :, :], in0=mod[:, which * 16:(which + 1) * 16, :], scalar1=1.0, scalar2=None,
                    op0=ALU.add), reads=["mod"], writes=["gm%d" % n])
                S.op("dve", lambda e, n=n: e.tensor_tensor(
                    out=gm[:, n, :, :], in0=gm[:, n, :, :], in1=nr[:, n, :].unsqueeze(2).to_broadcast([128, KC, 2]),
                    op=ALU.mult), reads=["gm%d" % n, "nr"], writes=["gm%d" % n])
            S.emit(block)

    def norm_to_hT(self, es, l, n, hT, shift_which):
        nc, S = self.nc, self.S
        xb = [self.sb(es, "xb", [128, KC, 256], F32) for _ in range(2)]
        sq = [self.sb(es, "sq", [128, 256], F32) for _ in range(2)]
        rs = [self.sb(es, "rs", [128, 256], F32) for _ in range(2)]
        tmp = [self.sb(es, "tmp", [128, 256], F32) for _ in range(3)]
        pst = [self.ps(es, "pst", [128, 256], F32) for _ in range(2)]
        xv = self.xT.rearrange("(c p) t -> p c t", p=128)
        gm, mod = self.gm1[l], self.mod[l]
        ones = self.cst[:, 1, :]
        for b in range(T // 256):
            j = 1 if b == 0 else 0
            X, Q, R, PS = xb[b % 2], sq[b % 2], rs[b % 2], pst[b % 2]
            kx, kq, kr, kp = "xb%d" % (b % 2), "sq%d" % (b % 2), "rs%d" % (b % 2), "pst%d" % (b % 2)
            S.dma("sp", lambda e, X=X, b=b: e.dma_start(out=X[:], in_=xv[:, :, b * 256:(b + 1) * 256]), writes=[kx])
            for c in range(KC):
                S.op("act", lambda e, X=X, Q=Q, c=c: e.activation(out=Q[:], in_=X[:, c, :], func=AF.Square),
                     reads=[kx], writes=[kq])
                S.op("pe", lambda e, Q=Q, PS=PS, c=c: e.matmul(PS[:], ones, Q[:], start=(c == 0), stop=(c == KC - 1)),
                     reads=[kq, "cst"], writes=[kp])
            S.op("act", lambda e, R=R, PS=PS: e.activation(out=R[:], in_=PS[:], func=AF.Ln, scale=1.0 / D, bias=self.epsc[:, 0:1]),
                 reads=[kp], writes=[kr])
            S.op("act", lambda e, R=R: e.activation(out=R[:], in_=R[:], func=AF.Exp, scale=-0.5), reads=[kr], writes=[kr])
            for c in range(KC):
                tb = tmp[c % 3]
                kt = "tmp%d" % (c % 3)
                S.op("dve", lambda e, X=X, R=R, tb=tb, c=c: e.tensor_tensor(out=tb[:], in0=X[:, c, :], in1=R[:],
                                                                          op=ALU.mult), reads=[kx, kr], writes=[kt])
                S.op("act", lambda e, tb=tb, c=c, b=b, j=j: e.activation(
                    out=hT[:, c, b * 256:(b + 1) * 256], in_=tb[:], func=AF.Identity,
                    scale=gm[:, n, c, j:j + 1], bias=mod[:, shift_which * 16 + c, j:j + 1]),
                    reads=[kt, "gm%d" % n, "mod"], writes=["hT"])

    def gemm(self, es, W, ncols, act, actkey, epi, nkc=KC, wname="w", tblk=TBLK):
        S = self.S
        NS = 3
        if not hasattr(self, "_gc"):
            self._gc = {}
        k1, k2 = (id(es), wname, nkc), (id(es), "ps")
        if k1 not in self._gc:
            self._gc[k1] = [self.sb(es, wname, [128, nkc, 256], BF16) for _ in range(NS)]
        if k2 not in self._gc:
            self._gc[k2] = [self.ps(es, "pg", [128, 512], F32) for _ in range(4)]
        wsl, psb = self._gc[k1], self._gc[k2]
        Wv = W.rearrange("(kc p) n -> p kc n", p=128)
        nslab = (ncols + 255) // 256
        pi = 0
        KG = 4
        for s in range(nslab):
            c0 = s * 256
            cw = min(256, ncols - c0)
            buf = wsl[s % NS]
            for kg in range(nkc // KG):
                S.dma("pool", lambda e, buf=buf, kg=kg, c0=c0, cw=cw: e.dma_start(
                    out=buf[:, kg * KG:(kg + 1) * KG, 0:cw], in_=Wv[:, kg * KG:(kg + 1) * KG, c0:c0 + cw]),
                    writes=["%s%d_%d" % (wname, s % NS, kg)])
            for g in range((cw + 127) // 128):
                m = min(128, cw - g * 128)
                for bi, (t0, tn) in enumerate(tblk):
                    PS = psb[pi % 4]
                    pk = "pg%d" % (pi % 4)
                    pi += 1
                    for kc in range(nkc):
                        S.op("pe", lambda e, PS=PS, buf=buf, kc=kc, g=g, m=m, t0=t0, tn=tn: e.matmul(
                            PS[0:m, 0:tn], buf[:, kc, g * 128:g * 128 + m], act[:, kc, t0:t0 + tn],
                            start=(kc == 0), stop=(kc == nkc - 1)),
                            reads=["%s%d_%d" % (wname, s % NS, kc // KG), actkey], writes=[pk])
                    epi(c0 + g * 128, m, bi, t0, tn, PS, pk)

    def phase_norm_inproj(self, l):
        nc, S = self.nc, self.S
        with ExitStack() as es:
            hT = self.sb(es, "hT", [128, KC, T], BF16)
            with ExitStack() as es2, nc.Block() as block:
                self.norm_to_hT(es2, l, 0, hT, 0)
                S.emit(block)
            with ExitStack() as es2, nc.Block() as block:
                stg = [self.sb(es2, "stg", [128, 512], F32) for _ in range(4)]
                cnt = [0]

                def epi(c0, m, bi, t0, tn, PS, pk):
                    i = cnt[0] % 4
                    cnt[0] += 1
                    st = stg[i]
                    S.op("act", lambda e: e.activation(out=st[0:m, 0:tn], in_=PS[0:m, 0:tn], func=AF.Copy),
                         reads=[pk], writes=["stg%d" % i])
                    S.dma("sp", lambda e: e.dma_start(out=self.zT[c0:c0 + m, t0:t0 + tn], in_=st[0:m, 0:tn]),
                          reads=["stg%d" % i])
                self.gemm(es2, self.w_in[l], D_IN, hT, "hT", epi, wname="win")
                S.emit(block)

    def resid_epi(self, es, l, which):
        S = self.S
        NXL = 6
        xl = [self.sb(es, "xl", [128, 512], F32) for _ in range(NXL)]
        cnt = [0]
        mod = self.mod[l]

        def epi(c0, m, bi, t0, tn, PS, pk):
            i = cnt[0] % NXL
            cnt[0] += 1
            xb = xl[i]
            j = 1 if t0 < LC else 0
            c = c0 // 128
            xk = "x_%d_%d" % (c, bi)
            S.dma("act", lambda e: e.dma_start(out=xb[:, 0:tn], in_=self.xT[c0:c0 + 128, t0:t0 + tn]),
                  reads=[xk], writes=["xl%d" % i])
            S.op("dve", lambda e: e.scalar_tensor_tensor(
                out=xb[:, 0:tn], in0=PS[:, 0:tn], scalar=mod[:, which * 16 + c, j:j + 1], in1=xb[:, 0:tn],
                op0=ALU.mult, op1=ALU.add), reads=[pk, "xl%d" % i, "mod"], writes=["xl%d" % i])
            S.dma("sp", lambda e: e.dma_start(out=self.xT[c0:c0 + 128, t0:t0 + tn], in_=xb[:, 0:tn]),
                  reads=["xl%d" % i], writes=[xk])
        return epi

    def phase_dense_tail(self, l):
        nc, S = self.nc, self.S
        with ExitStack() as es, nc.Block() as block:
            mx = self.sb(es, "mx", [128, KC, T], BF16)
            mv = self.mixT.rearrange("(c p) t -> p c t", p=128)
            for c4 in range(4):
                S.dma("sp", lambda e, c4=c4: e.dma_start(out=mx[:, c4 * 4:(c4 + 1) * 4, :], in_=mv[:, c4 * 4:(c4 + 1) * 4, :]),
                      writes=["mx%d" % c4])
            S.op("pool", lambda e: e.memset(self.junk[:], 0.0), reads=["mx0", "mx1", "mx2", "mx3"], writes=["mx"])
            self.gemm(es, self.w_out[l], D, mx, "mx", self.resid_epi(es, l, 2), wname="wout")
            S.emit(block)
        with ExitStack() as es:
            hT = self.sb(es, "hnT", [128, KC, T], BF16)
            with ExitStack() as es2, nc.Block() as block:
                self.norm_to_hT(es2, l, 1, hT, 3)
                S.emit(block)
            with ExitStack() as es2, nc.Block() as block:
                NK2 = 8
                aT = self.sb(es2, "aT", [128, NK2, T], BF16)
                rl = [self.sb(es2, "rl", [128, 512], F32) for _ in range(3)]
                rc = [0]
                repi = self.resid_epi(es2, l, 5)
                for q in range(D_FF // (NK2 * 128)):
                    def epi1(c0, m, bi, t0, tn, PS, pk):
                        i = rc[0] % 3
                        rc[0] += 1
                        r = rl[i]
                        S.op("act", lambda e: e.activation(out=r[:, 0:tn], in_=PS[:, 0:tn], func=AF.Relu),
                             reads=[pk], writes=["rl%d" % i])
                        S.op("pool", lambda e: e.tensor_tensor(out=aT[:, c0 // 128, t0:t0 + tn], in0=r[:, 0:tn],
                                                               in1=r[:, 0:tn], op=ALU.mult),
                             reads=["rl%d" % i], writes=["aT"])
                    self.gemm(es2, self.w_ff1[l][:, q * NK2 * 128:(q + 1) * NK2 * 128], NK2 * 128, hT, "hnT", epi1,
                              wname="wf1")
                    self.gemm(es2, self.w_ff2[l][q * NK2 * 128:(q + 1) * NK2 * 128, :], D, aT, "aT", repi, nkc=NK2,
                              wname="wf2")
                S.emit(block)

    def phase_final(self):
        nc, S = self.nc, self.S
        with ExitStack() as es, nc.Block() as block:
            xb = [self.sb(es, "fxb", [128, KC, 256], F32) for _ in range(2)]
            sq = [self.sb(es, "fsq", [128, 256], F32) for _ in range(2)]
            rs = [self.sb(es, "frs", [128, 256], F32) for _ in range(2)]
            ob = [self.sb(es, "fob", [128, KC, 256], F32) for _ in range(2)]
            nf = self.sb(es, "nf", [128, KC], F32)
            pst = [self.ps(es, "fps", [128, 256], F32) for _ in range(2)]
            xv = self.xT.rearrange("(c p) t -> p c t", p=128)
            ov = self.outT.rearrange("(c p) t -> p c t", p=128)
            ones = self.cst[:, 1, :]
            S.dma("sp", lambda e: e.dma_start(out=nf[:], in_=self.nfin[:, :]), writes=["nf"])
            for b in range(L // 256):
                X, Q, R, PS, O = xb[b % 2], sq[b % 2], rs[b % 2], pst[b % 2], ob[b % 2]
                kx, kq, kr, kp, ko = ["%s%d" % (n, b % 2) for n in ("fx", "fq", "fr", "fp", "fo")]
                S.dma("sp", lambda e, X=X, b=b: e.dma_start(out=X[:], in_=xv[:, :, LC + b * 256:LC + (b + 1) * 256]),
                      writes=[kx])
                for c in range(KC):
                    S.op("act", lambda e, X=X, Q=Q, c=c: e.activation(out=Q[:], in_=X[:, c, :], func=AF.Square),
                         reads=[kx], writes=[kq])
                    S.op("pe", lambda e, Q=Q, PS=PS, c=c: e.matmul(PS[:], ones, Q[:], start=(c == 0), stop=(c == KC - 1)),
                         reads=[kq], writes=[kp])
                S.op("act", lambda e, R=R, PS=PS: e.activation(out=R[:], in_=PS[:], func=AF.Ln, scale=1.0 / D,
                                                              bias=self.epsc[:, 0:1]), reads=[kp], writes=[kr])
                S.op("act", lambda e, R=R: e.activation(out=R[:], in_=R[:], func=AF.Exp, scale=-0.5), reads=[kr], writes=[kr])
                for c in range(KC):
                    S.op("dve", lambda e, X=X, R=R, O=O, c=c: e.scalar_tensor_tensor(
                        out=O[:, c, :], in0=X[:, c, :], scalar=nf[:, c:c + 1], in1=R[:], op0=ALU.mult, op1=ALU.mult),
                        reads=[kx, kr, "nf"], writes=[ko])
                S.dma("sp", lambda e, O=O, b=b: e.dma_start(out=ov[:, :, b * 256:(b + 1) * 256], in_=O[:]), reads=[ko])
            S.emit(block)
    def join(self, keys, newkey):
        self.S.op("pool", lambda e: e.memset(self.junk[:], 0.0), reads=list(keys), writes=[newkey])

    def phase_attn(self, l):
        nc, S = self.nc, self.S
        scale = 128 ** -0.5
        cstb = self.cstb
        with ExitStack() as es, nc.Block() as block:
            rope = self.sb(es, "rope", [128, 2, L], F32)
            snk = self.sb(es, "snk", [128, 8], F32)
            esk = self.sb(es, "esk", [128, 8], F32)
            raw = [self.sb(es, "raw", [128, T], F32) for _ in range(2)]
            rb = self.sb(es, "rb", [128, L], BF16)
            t1 = [self.sb(es, "t1", [128, 512], F32) for _ in range(2)]
            t2 = [self.sb(es, "t2", [128, 512], F32) for _ in range(2)]
            kT = self.sb(es, "kT", [128, T], BF16)
            qT = self.sb(es, "qT", [128, T], BF16)
            vT = self.sb(es, "vT", [128, T], BF16)
            V = self.sb(es, "V", [128, NT, 128], BF16)
            PTc = self.sb(es, "PTc", [128, 2, T], BF16)
            PTl = self.sb(es, "PTl", [128, 16, 3, 128], BF16)
            rc = [self.sb(es, "rc", [128, 128], F32) for _ in range(2)]
            ost = [self.sb(es, "ost", [128, 128], BF16) for _ in range(3)]
            psS = [self.ps(es, "psS", [128, 512], F32) for _ in range(2)]
            psO = [self.ps(es, "psO", [128, 128], F32) for _ in range(2)]
            psD = [self.ps(es, "psD", [128, 128], F32) for _ in range(2)]
            psT = [self.ps(es, "psT", [128, 128], BF16) for _ in range(2)]
            S.dma("sp", lambda e: e.dma_start(out=rope[:], in_=self.ropeT[:, :, :]), writes=["rope"])
            S.dma("sp", lambda e: e.dma_start(out=snk[:], in_=self.sink[l][:, :]), writes=["snk"])
            S.op("act", lambda e: e.activation(out=esk[:], in_=snk[:], func=AF.Exp), reads=["snk"], writes=["esk"])
            cn = {"raw": 0, "ps": 0, "t": 0, "o": 0, "ost": 0}

            def load_rope(row0, dst, dkey):
                ri = cn["raw"] % 2
                cn["raw"] += 1
                R, rk = raw[ri], "raw%d" % ri
                S.dma("sp", lambda e: e.dma_start(out=R[:], in_=self.zT[row0:row0 + 128, :]), writes=[rk])
                S.op("act", lambda e: e.activation(out=dst[:, 0:LC], in_=R[:, 0:LC], func=AF.Copy), reads=[rk],
                     writes=[dkey + "c"])
                S.op("act", lambda e: e.activation(out=rb[:], in_=R[:, LC:T], func=AF.Copy), reads=[rk], writes=["rb"])
                for blk in range(4):
                    a, b_ = blk * 512, (blk + 1) * 512
                    pi = cn["ps"] % 2
                    cn["ps"] += 1
                    ti = cn["t"] % 2
                    cn["t"] += 1
                    PS, pk = psS[pi], "psS%d" % pi
                    S.op("pe", lambda e, PS=PS, a=a, b_=b_: e.matmul(PS[:, :], cstb[:, 2, :], rb[:, a:b_], start=True, stop=True),
                         reads=["rb", "cstb"], writes=[pk])
                    S.op("dve", lambda e, a=a, b_=b_, ti=ti: e.tensor_tensor(out=t1[ti][:], in0=R[:, LC + a:LC + b_],
                                                                           in1=rope[:, 0, a:b_], op=ALU.mult),
                         reads=[rk, "rope"], writes=["t1%d" % ti])
                    S.op("dve", lambda e, PS=PS, a=a, b_=b_, ti=ti: e.tensor_tensor(out=t2[ti][:], in0=PS[:, :],
                                                                                  in1=rope[:, 1, a:b_], op=ALU.mult),
                         reads=[pk, "rope"], writes=["t2%d" % ti])
                    S.op("pool", lambda e, a=a, b_=b_, ti=ti: e.tensor_tensor(out=dst[:, LC + a:LC + b_], in0=t1[ti][:],
                                                                            in1=t2[ti][:], op=ALU.add),
                         reads=["t1%d" % ti, "t2%d" % ti], writes=[dkey + str(blk)])
                self.join([dkey + "c"] + [dkey + str(i) for i in range(4)], dkey)

            for g in range(2):
                load_rope(1024 + g * 128, kT, "kT")
                ri = cn["raw"] % 2
                cn["raw"] += 1
                R, rk = raw[ri], "raw%d" % ri
                S.dma("sp", lambda e, R=R, g=g: e.dma_start(out=R[:], in_=self.zT[1280 + g * 128:1408 + g * 128, :]), writes=[rk])
                S.op("act", lambda e, R=R: e.activation(out=vT[:], in_=R[:], func=AF.Copy), reads=[rk], writes=["vT"])
                for t in range(NT):
                    PT_, ptk = psT[t % 2], "psT%d" % (t % 2)
                    S.op("pe", lambda e, PT_=PT_, t=t: e.transpose(out=PT_[:, :], in_=vT[:, t * 128:(t + 1) * 128],
                                                                   identity=cstb[:, 0, :]), reads=["vT", "cstb"], writes=[ptk])
                    S.op("dve", lambda e, PT_=PT_, t=t: e.tensor_copy(out=V[:, t, :], in_=PT_[:, :]), reads=[ptk],
                         writes=["V%d" % t])
                self.join(["V%d" % t for t in range(NT)], "V")
                for hq in range(4):
                    h = g * 4 + hq
                    load_rope(h * 128, qT, "qT")
                    for ck in range(2):
                        for (t0, tn) in TBLK:
                            pi = cn["ps"] % 2
                            cn["ps"] += 1
                            PS, pk = psS[pi], "psS%d" % pi
                            S.op("pe", lambda e, PS=PS, ck=ck, t0=t0, tn=tn: e.matmul(
                                PS[:, 0:tn], kT[:, ck * 128:(ck + 1) * 128], qT[:, t0:t0 + tn], start=True, stop=True),
                                reads=["kT", "qT"], writes=[pk])
                            S.op("act", lambda e, PS=PS, ck=ck, t0=t0, tn=tn: e.activation(
                                out=PTc[:, ck, t0:t0 + tn], in_=PS[:, 0:tn], func=AF.Exp, scale=scale),
                                reads=[pk], writes=["PTc"])
                    for j in range(16):
                        lo, hi = max(j - 1, 0), min(j + 1, 15)
                        nq = hi - lo + 1
                        pi = cn["ps"] % 2
                        cn["ps"] += 1
                        PS, pk = psS[pi], "psS%d" % pi
                        S.op("pe", lambda e, PS=PS, j=j, lo=lo, nq=nq: e.matmul(
                            PS[:, 0:nq * 128], kT[:, LC + j * 128:LC + (j + 1) * 128],
                            qT[:, LC + lo * 128:LC + (lo + nq) * 128], start=True, stop=True),
                            reads=["kT", "qT"], writes=[pk])
                        r0 = lo - j + 1
                        S.op("act", lambda e, PS=PS, j=j, r0=r0, nq=nq: e.activation(
                            out=PTl[:, j, r0:r0 + nq, :], in_=PS[:, 0:nq * 128].rearrange("p (a b) -> p a b", b=128),
                            func=AF.Exp, scale=scale), reads=[pk], writes=["PTl"])
                        if j >= 1:
                            S.op("pool", lambda e, j=j: e.tensor_tensor(out=PTl[:, j, 0, :], in0=PTl[:, j, 0, :],
                                                                        in1=cstb[:, 3, :], op=ALU.mult),
                                 reads=["PTl", "cstb"], writes=["PTl"])
                        if j <= 14:
                            S.op("pool", lambda e, j=j: e.tensor_tensor(out=PTl[:, j, 2, :], in0=PTl[:, j, 2, :],
                                                                        in1=cstb[:, 4, :], op=ALU.mult),
                                 reads=["PTl", "cstb"], writes=["PTl"])
                    for qt in range(NT):
                        contrib = [(ck, PTc[:, ck, qt * 128:(qt + 1) * 128]) for ck in range(2)]
                        if qt >= 2:
                            n = qt - 2
                            for j in range(max(n - 1, 0), min(n + 1, 15) + 1):
                                contrib.append((2 + j, PTl[:, j, n - j + 1, :]))
                        oi = cn["o"] % 2
                        cn["o"] += 1
                        O, Dn = psO[oi], psD[oi]
                        for idx, (vt, rhs) in enumerate(contrib):
                            st_, sp_ = (idx == 0), (idx == len(contrib) - 1)
                            S.op("pe", lambda e, O=O, vt=vt, rhs=rhs, st_=st_, sp_=sp_: e.matmul(
                                O[:, :], V[:, vt, :], rhs, start=st_, stop=sp_), reads=["V", "PTc", "PTl"], writes=["psO%d" % oi])
                            S.op("pe", lambda e, Dn=Dn, rhs=rhs, st_=st_, sp_=sp_: e.matmul(
                                Dn[:, :], cstb[:, 1, :], rhs, start=st_, stop=sp_), reads=["PTc", "PTl"], writes=["psD%d" % oi])
                        RC = rc[oi]
                        S.op("dve", lambda e, RC=RC, Dn=Dn, h=h: e.tensor_scalar(out=RC[:], in0=Dn[:, :], scalar1=esk[:, h:h + 1],
                                                                             scalar2=None, op0=ALU.add),
                             reads=["psD%d" % oi, "esk"], writes=["rc%d" % oi])
                        S.op("dve", lambda e, RC=RC: e.reciprocal(out=RC[:], in_=RC[:]), reads=["rc%d" % oi], writes=["rc%d" % oi])
                        si = cn["ost"] % 3
                        cn["ost"] += 1
                        OS = ost[si]
                        S.op("dve", lambda e, OS=OS, O=O, RC=RC: e.tensor_tensor(out=OS[:], in0=O[:, :], in1=RC[:], op=ALU.mult),
                             reads=["psO%d" % oi, "rc%d" % oi], writes=["ost%d" % si])
                        S.dma("pool", lambda e, OS=OS, h=h, qt=qt: e.dma_start(
                            out=self.mixT[h * 128:(h + 1) * 128, qt * 128:(qt + 1) * 128], in_=OS[:]), reads=["ost%d" % si])
            S.emit(block)
    def phase_gdn(self, l):
        nc, S = self.nc, self.S
        cst, cstb = self.cst, self.cstb
        ORD = [list(range(NT)), [1, 0] + list(range(NT - 1, 1, -1))]
        with ExitStack() as eso:
            G = self.sb(eso, "G", [128, NT, 16], F32)
            BETA = self.sb(eso, "BETA", [128, NT, 16], F32)
            GC = self.sb(eso, "GC", [128, NT, 16], F32)
            TOT = self.sb(eso, "TOT", [128, NT, 16], F32)
            E = self.sb(eso, "E", [128, NT, 16], F32)
            Fd = self.sb(eso, "Fd", [128, NT, 16], F32)
            BE = self.sb(eso, "BE", [128, NT, 16], F32)
            GL = self.sb(eso, "GL", [128, NT, 16], F32)
            onec = self.sb(eso, "onec", [128, 1], F32)
            with ExitStack() as es, nc.Block() as block:
                abT = self.sb(es, "abT", [32, T], F32)
                abm = self.sb(es, "abm", [128, NT, 32], F32)
                al = self.sb(es, "al", [128, 16], F32)
                dtb = self.sb(es, "dtb", [128, 16], F32)
                cw = self.sb(es, "cw", [128, 24, 5], F32)
                xg = self.sb(es, "xg", [128, NT, 16], F32)
                Zp = [self.sb(es, "Zp", [128, T + 8], F32) for _ in range(3)]
                acc = [self.sb(es, "acc", [128, T], F32) for _ in range(3)]
                sl_ = [self.sb(es, "sl", [128, T], F32) for _ in range(3)]
                sq = [self.sb(es, "gsq", [128, 512], F32) for _ in range(6)]
                ri = [self.sb(es, "gri", [128, 512], F32) for _ in range(6)]
                ob = [self.sb(es, "gob", [128, T], BF16) for _ in range(3)]
                tm = [self.sb(es, "gtm", [128, NT, 128], BF16) for _ in range(3)]
                pabA = self.ps(es, "pabA", [128, 512], F32)
                pabB = self.ps(es, "pabB", [128, 512], F32)
                pgcb = self.ps(es, "pgc", [128, 512], F32)
                ptotb = self.ps(es, "ptot", [128, 512], F32)
                pgc = pgcb[:, 0:NT * 16].rearrange("p (n c) -> p n c", c=16)
                ptot = ptotb[:, 0:NT * 16].rearrange("p (n c) -> p n c", c=16)
                pss = [self.ps(es, "pss", [128, 512], F32) for _ in range(2)]
                ptrb = [self.ps(es, "ptr", [128, 1024], BF16) for _ in range(2)]
                ptr = [ptrb[0][:, 0:128], ptrb[1][:, 0:128]]
                S.op("pool", lambda e: e.memset(onec[:], 1.0), writes=["onec"])
                for z in Zp:
                    S.op("pool", lambda e, z=z: e.memset(z[:], 0.0), writes=["zpc%d" % Zp.index(z), "zpl%d" % Zp.index(z)])
                S.dma("sp", lambda e: e.dma_start(out=abT[:], in_=self.zT[5632:5664, :]), writes=["abT"])
                S.dma("sp", lambda e: e.dma_start(out=al[:], in_=self.alog[l][:, :]), writes=["al"])
                S.dma("sp", lambda e: e.dma_start(out=dtb[:], in_=self.dtb[l][:, :]), writes=["dtb"])
                S.dma("sp", lambda e: e.dma_start(out=cw[:], in_=self.convw[l][:, :, :]), writes=["cw"])
                for t in range(NT):
                    pb_ = pabA if t < 9 else pabB
                    S.op("pe", lambda e, t=t, pb_=pb_: e.transpose(out=pb_[:, (t % 9) * 32:(t % 9 + 1) * 32], in_=abT[:, t * 128:(t + 1) * 128],
                                                                   identity=cst[0:32, 0, 0:32]), reads=["abT", "cst"], writes=["pab"])
                S.op("dve", lambda e: e.tensor_copy(out=abm[:, 0:9, :], in_=pabA[:, 0:288].rearrange("p (n c) -> p n c", c=32)), reads=["pab"], writes=["abm0"])
                S.op("dve", lambda e: e.tensor_copy(out=abm[:, 9:18, :], in_=pabB[:, 0:288].rearrange("p (n c) -> p n c", c=32)), reads=["pab"], writes=["abm1"])
                self.join(["abm0", "abm1"], "abm")
                ab5 = abm[:].rearrange("p n (d t h) -> p n d t h", d=2, t=2)
                v4 = lambda a: a[:].rearrange("p n (d h) -> p n d h", d=2)
                b4 = lambda a: a[:].rearrange("p (d h) -> p d h", d=2).unsqueeze(1).to_broadcast([128, NT, 2, 8])
                S.op("dve", lambda e: e.tensor_tensor(out=v4(xg), in0=ab5[:, :, :, 0, :], in1=b4(dtb), op=ALU.add),
                     reads=["abm", "dtb"], writes=["xg"])
                S.op("act", lambda e: e.activation(out=xg[:], in_=xg[:], func=AF.Exp), reads=["xg"], writes=["xg"])
                S.op("act", lambda e: e.activation(out=xg[:], in_=xg[:], func=AF.Ln, bias=onec[:, 0:1]), reads=["xg", "onec"], writes=["xg"])
                S.op("act", lambda e: e.activation(out=al[:], in_=al[:], func=AF.Exp), reads=["al"], writes=["al"])
                S.op("dve", lambda e: e.tensor_scalar(out=al[:], in0=al[:], scalar1=-1.0, scalar2=None, op0=ALU.mult),
                     reads=["al"], writes=["al"])
                S.op("dve", lambda e: e.tensor_tensor(out=v4(G), in0=v4(xg), in1=b4(al), op=ALU.mult), reads=["xg", "al"], writes=["G"])
                S.op("act", lambda e: e.activation(out=v4(BETA), in_=ab5[:, :, :, 1, :], func=AF.Sigmoid), reads=["abm"], writes=["BETA"])
                for t in range(NT):
                    for d in range(2):
                        S.op("pe", lambda e, t=t, d=d: e.matmul(pgc[:, t, d * 8:(d + 1) * 8], cst[:, 3 + d, :],
                                                                G[:, t, d * 8:(d + 1) * 8], start=True, stop=True),
                             reads=["G", "cst"], writes=["pgc"])
                    S.op("pe", lambda e, t=t: e.matmul(ptot[:, t, :], cst[:, 1, :], G[:, t, :], start=True, stop=True),
                         reads=["G", "cst"], writes=["ptot"])
                S.op("dve", lambda e: e.tensor_copy(out=GC[:], in_=pgc), reads=["pgc"], writes=["GC"])
                S.op("dve", lambda e: e.tensor_copy(out=TOT[:], in_=ptot), reads=["ptot"], writes=["TOT"])
                S.op("act", lambda e: e.activation(out=E[:], in_=GC[:], func=AF.Exp), reads=["GC"], writes=["E"])
                S.op("dve", lambda e: e.tensor_tensor(out=Fd[:], in0=TOT[:], in1=GC[:], op=ALU.subtract), reads=["TOT", "GC"], writes=["Fd"])
                S.op("act", lambda e: e.activation(out=Fd[:], in_=Fd[:], func=AF.Exp), reads=["Fd"], writes=["Fd"])
                S.op("act", lambda e: e.activation(out=GL[:], in_=TOT[:], func=AF.Exp), reads=["TOT"], writes=["GL"])
                S.op("dve", lambda e: e.tensor_tensor(out=BE[:], in0=BETA[:], in1=E[:], op=ALU.mult), reads=["BETA", "E"], writes=["BE"])
                NPS = 2

                def pbody(i, h, which):
                    row0 = 1536 + which * 1024 + h * 128
                    ch = which * 8 + h
                    z, A_, SL, OB, TM = Zp[i], acc[i], sl_[i], ob[i], tm[i]
                    yield S.dma("sp", lambda e, z=z, row0=row0: e.dma_start(out=z[:, 2:2 + LC], in_=self.zT[row0:row0 + 128, 0:LC]),
                          writes=["zpc%d" % i])
                    yield S.dma("sp", lambda e, z=z, row0=row0: e.dma_start(out=z[:, 262:262 + L], in_=self.zT[row0:row0 + 128, LC:T]),
                          writes=["zpl%d" % i])
                    for (o0, n, zo, zk) in ((0, LC, 0, "zpc%d" % i), (LC, L, 260, "zpl%d" % i)):
                        yield S.op("dve", lambda e, z=z, A_=A_, o0=o0, n=n, zo=zo, ch=ch: e.tensor_scalar(
                            out=A_[:, o0:o0 + n], in0=z[:, zo:zo + n], scalar1=cw[:, ch, 0:1], scalar2=None, op0=ALU.mult),
                            reads=[zk, "cw"], writes=["acc%d" % i])
                        for j in range(1, 5):
                            yield S.op("dve", lambda e, z=z, A_=A_, o0=o0, n=n, zo=zo, ch=ch, j=j: e.scalar_tensor_tensor(
                                out=A_[:, o0:o0 + n], in0=z[:, zo + j:zo + j + n], scalar=cw[:, ch, j:j + 1],
                                in1=A_[:, o0:o0 + n], op0=ALU.mult, op1=ALU.add), reads=[zk, "cw", "acc%d" % i], writes=["acc%d" % i])
                    yield S.op("act", lambda e, A_=A_, SL=SL: e.activation(out=SL[:], in_=A_[:], func=AF.Silu),
                         reads=["acc%d" % i], writes=["sl%d" % i])
                    if which < 2:
                        qs = (128 ** -0.5) if which == 0 else 1.0
                        for bi, (t0, tn) in enumerate(TBLK):
                            bb = bi % 2
                            yield S.op("act", lambda e, SL=SL, t0=t0, tn=tn, bb=bb: e.activation(out=sq[i * 2 + bb][:, 0:tn], in_=SL[:, t0:t0 + tn],
                                                                                         func=AF.Square), reads=["sl%d" % i], writes=["gsq%d_%d" % (i, bb)])
                            yield S.op("pe", lambda e, t0=t0, tn=tn, bb=bb: e.matmul(pss[i][:, 0:tn], cst[:, 1, :], sq[i * 2 + bb][:, 0:tn],
                                                                             start=True, stop=True), reads=["gsq%d_%d" % (i, bb)], writes=["pss%d" % i])
                            yield S.op("act", lambda e, tn=tn, bb=bb: e.activation(out=ri[i * 2 + bb][:, 0:tn], in_=pss[i][:, 0:tn], func=AF.Ln,
                                                                           bias=self.epsc[:, 0:1]), reads=["pss%d" % i], writes=["gri%d_%d" % (i, bb)])
                            yield S.op("act", lambda e, tn=tn, bb=bb: e.activation(out=ri[i * 2 + bb][:, 0:tn], in_=ri[i * 2 + bb][:, 0:tn], func=AF.Exp, scale=-0.5),
                                 reads=["gri%d_%d" % (i, bb)], writes=["gri%d_%d" % (i, bb)])
                            yield S.op("dve", lambda e, SL=SL, OB=OB, t0=t0, tn=tn, bb=bb, qs=qs: e.scalar_tensor_tensor(
                                out=OB[:, t0:t0 + tn], in0=SL[:, t0:t0 + tn], scalar=qs, in1=ri[i * 2 + bb][:, 0:tn],
                                op0=ALU.mult, op1=ALU.mult), reads=["sl%d" % i, "gri%d_%d" % (i, bb)], writes=["gob%d" % i])
                        dst = self.gq if which == 0 else self.gk
                        yield S.dma("act", lambda e, OB=OB, dst=dst, h=h: e.dma_start(out=dst[h, :, :], in_=OB[:]), reads=["gob%d" % i])
                    else:
                        yield S.op("act", lambda e, SL=SL, OB=OB: e.activation(out=OB[:], in_=SL[:], func=AF.Copy),
                             reads=["sl%d" % i], writes=["gob%d" % i])
                    if which >= 1:
                        for t in range(NT):
                            yield S.op("pe", lambda e, OB=OB, t=t: e.transpose(out=ptr[i], in_=OB[:, t * 128:(t + 1) * 128],
                                                                         identity=cstb[:, 0, :]), reads=["gob%d" % i], writes=["ptr%d" % i])
                            yield S.op("dve", lambda e, TM=TM, t=t: e.tensor_copy(out=TM[:, t, :], in_=ptr[i]),
                                 reads=["ptr%d" % i], writes=["gtm%d" % i])
                        dst = self.gkm if which == 1 else self.gvm
                        yield S.dma("act", lambda e, TM=TM, dst=dst, h=h: e.dma_start(
                            out=dst[h].rearrange("(n p) d -> p n d", p=128), in_=TM[:]), reads=["gtm%d" % i])

                items = [(h_, w_) for h_ in range(8) for w_ in range(3)]
                pslots = [items[k::NPS] for k in range(NPS)]
                pg = [None] * NPS
                pn = [0] * NPS
                rnd = 0
                while True:
                    alive = False
                    for k in range(NPS):
                        if pg[k] is None and pn[k] < len(pslots[k]) and rnd >= k * 25:
                            pg[k] = pbody(k, *pslots[k][pn[k]])
                            pn[k] += 1
                        if pg[k] is not None:
                            alive = True
                            try:
                                next(pg[k])
                            except StopIteration:
                                pg[k] = None
                        elif pn[k] < len(pslots[k]):
                            alive = True
                    rnd += 1
                    if not alive:
                        break
                S.emit(block)
            if getattr(self, "gdn_stop", 9) < 2:
                return
            with ExitStack() as es, nc.Block() as block:
                NSET = 8
                f32t = lambda n: [self.sb(es, n, [128, 128], F32) for _ in range(NSET)]
                b16t = lambda n: [self.sb(es, n, [128, 128], BF16) for _ in range(NSET)]
                rhsG = [self.sb(es, "rhsG", [128, 256], F32) for _ in range(NSET)]
                t1, Ex, t2, Ey, EB, Mx, XA, My, QA, usb, osb, gcs = [f32t(n) for n in
                    ("t1", "Ex", "t2", "Ey", "EB", "Mx", "XA", "My", "QA", "usb", "osb", "gcs")]
                QKm, qdec, TT, bv, kbe, kdec, wsb, vnew, kt, qt, km, vm = [b16t(n) for n in
                    ("QKm", "qdec", "TT", "bv", "kbe", "kdec", "wsb", "vnew", "kt", "qt", "km", "vm")]
                Pm = [[self.sb(es, "Pm", [128, 128], F32) for _ in range(2)] for _ in range(NSET)]
                Xm = [[self.sb(es, "Xm", [128, 128], F32) for _ in range(2)] for _ in range(NSET)]
                Yb = [[self.sb(es, "Yb", [128, 128], F32) for _ in range(2)] for _ in range(NSET)]
                Yo = [[self.sb(es, "Yo", [128, 128], F32) for _ in range(3)] for _ in range(NSET)]
                XF, YF, Qs, M1s = [f32t(n) for n in ("XF", "YF", "Qs", "M1s")]
                Sf = [[self.sb(es, "Sf", [128, 128], F32) for _ in range(2)] for _ in range(8)]
                Sb = [[self.sb(es, "Sb", [128, 128], BF16) for _ in range(2)] for _ in range(8)]
                bk = [[self.ps(es, "bk", [128, 512], F32) for _ in range(1)] for _ in range(NSET)]
                for h in range(8):
                    for d in range(2):
                        S.op("pool", lambda e, h=h, d=d: e.memset(Sf[h][d][:], 0.0), writes=["Sf%d%d" % (h, d)])
                        S.op("pool", lambda e, h=h, d=d: e.memset(Sb[h][d][:], 0.0), writes=["Sb%d%d" % (h, d)])
                def body(i, h, d, t):
                    K = lambda n: "%s%d" % (n, i)
                    col = d * 8 + h
                    gcol, bcol, gccol = G[:, t, col:col + 1], BETA[:, t, col:col + 1], GC[:, t, col:col + 1]
                    becol, fcol, glcol = BE[:, t, col:col + 1], Fd[:, t, col:col + 1], GL[:, t, col:col + 1]
                    tsl = slice(t * 128, (t + 1) * 128)
                    Bk = bk[i][0]
                    A, B, C = Bk[:, 0:128], Bk[:, 128:256], Bk[:, 256:512]
                    OP, M1, OX, Yp = Bk[:, 0:128], Bk[:, 128:256], Bk[:, 256:384], Bk[:, 384:512]
                    Sp, U, W, O2 = Bk[:, 0:128], Bk[:, 128:256], Bk[:, 256:384], Bk[:, 384:512]
                    Dp = Bk[:, 128:256]
                    yield S.dma("sp", lambda e, i=i, h=h, tsl=tsl: e.dma_start(out=kt[i][:], in_=self.gk[h, :, tsl]), writes=[K("kt")])
                    yield S.dma("sp", lambda e, i=i, h=h, tsl=tsl: e.dma_start(out=qt[i][:], in_=self.gq[h, :, tsl]), writes=[K("qt")])
                    yield S.dma("sp", lambda e, i=i, h=h, tsl=tsl: e.dma_start(out=km[i][:], in_=self.gkm[h, tsl, :]), writes=[K("km")])
                    yield S.dma("sp", lambda e, i=i, h=h, tsl=tsl: e.dma_start(out=vm[i][:], in_=self.gvm[h, tsl, :]), writes=[K("vm")])
                    yield S.op("pe", lambda e, A=A, i=i: e.matmul(A, kt[i][:], kt[i][:], start=True, stop=True), reads=[K("kt")], writes=[K("bank0"), K("A")])
                    yield S.op("pe", lambda e, B=B, i=i: e.matmul(B, kt[i][:], qt[i][:], start=True, stop=True), reads=[K("kt"), K("qt")], writes=[K("bank0"), K("B")])
                    yield S.op("pool", lambda e, i=i, d=d, gcol=gcol: e.tensor_scalar(out=rhsG[i][:, 0:128], in0=cst[:, 3 + d, :], scalar1=gcol,
                                                                             scalar2=None, op0=ALU.mult), reads=["G"], writes=[K("rhsGa")])
                    yield S.op("act", lambda e, i=i, bcol=bcol: e.activation(out=rhsG[i][:, 128:256], in_=cst[:, 0, :], func=AF.Copy, scale=bcol),
                               reads=["BETA"], writes=[K("rhsGb")])
                    yield S.op("pe", lambda e, C=C, i=i: e.matmul(C, cst[:, 1, :], rhsG[i][:], start=True, stop=True), reads=[K("rhsGa"), K("rhsGb")],
                               writes=[K("bank0"), K("C")])
                    yield S.op("dve", lambda e, C=C, i=i, gccol=gccol: e.tensor_scalar(out=t1[i][:], in0=C[:, 0:128], scalar1=gccol, scalar2=0.0,
                                                                               op0=ALU.subtract, op1=ALU.min), reads=[K("C")], writes=[K("bank0"), K("t1")])
                    yield S.op("act", lambda e, i=i: e.activation(out=Ex[i][:], in_=t1[i][:], func=AF.Exp), reads=[K("t1")], writes=[K("Ex")])
                    yield S.op("dve", lambda e, C=C, i=i, gccol=gccol: e.tensor_scalar(out=t2[i][:], in0=C[:, 0:128], scalar1=gccol, scalar2=0.0,
                                                                               op0=ALU.subtract, op1=ALU.max), reads=[K("C")], writes=[K("bank0"), K("t2")])
                    yield S.op("act", lambda e, i=i: e.activation(out=Ey[i][:], in_=t2[i][:], func=AF.Exp, scale=-1.0), reads=[K("t2")], writes=[K("Ey")])
                    yield S.op("dve", lambda e, C=C, i=i: e.tensor_copy(out=gcs[i][:], in_=C[:, 0:128]), reads=[K("C")], writes=[K("bank0"), K("gcs")])
                    yield S.op("act", lambda e, i=i: e.activation(out=EB[i][:], in_=gcs[i][:], func=AF.Exp), reads=[K("gcs")], writes=[K("EB")])
                    yield S.op("dve", lambda e, C=C, i=i, d=d: e.tensor_tensor(out=Mx[i][:], in0=C[:, 128:256], in1=cst[:, 11 + d, :], op=ALU.mult),
                               reads=[K("C")], writes=[K("bank0"), K("Mx")])
                    yield S.op("dve", lambda e, A=A, i=i: e.tensor_tensor(out=XA[i][:], in0=A, in1=Ex[i][:], op=ALU.mult), reads=[K("A"), K("Ex")],
                               writes=[K("bank0"), K("XA")])
                    yield S.op("pool", lambda e, i=i: e.tensor_tensor(out=Xm[i][0][:], in0=XA[i][:], in1=Mx[i][:], op=ALU.mult),
                               reads=[K("XA"), K("Mx")], writes=[K("X0")])
                    yield S.op("pool", lambda e, i=i: e.tensor_tensor(out=Pm[i][1][:], in0=Xm[i][0][:], in1=cst[:, 0, :], op=ALU.add),
                               reads=[K("X0")], writes=[K("P1")])
                    yield S.op("dve", lambda e, A=A, i=i: e.tensor_tensor(out=YF[i][:], in0=A, in1=Ey[i][:], op=ALU.mult), reads=[K("A"), K("Ey")],
                               writes=[K("bank0"), K("YF")])
                    yield S.op("dve", lambda e, i=i, d=d, bcol=bcol: e.scalar_tensor_tensor(out=Yb[i][0][:], in0=YF[i][:], scalar=bcol, in1=cst[:, 13 + d * 4, :],
                                                                                     op0=ALU.mult, op1=ALU.mult), reads=[K("YF"), "BETA"], writes=[K("Y0")])
                    for lv in range(3):
                        yield S.op("dve", lambda e, i=i, d=d, lv=lv, bcol=bcol: e.scalar_tensor_tensor(
                            out=Yo[i][lv][:], in0=YF[i][:], scalar=bcol, in1=cst[:, 14 + d * 4 + lv, :], op0=ALU.mult, op1=ALU.mult),
                            reads=[K("YF"), "BETA"], writes=[K("Yo%d" % lv)])
                    yield S.op("dve", lambda e, B=B, i=i: e.tensor_tensor(out=QA[i][:], in0=B, in1=Ex[i][:], op=ALU.mult), reads=[K("B"), K("Ex")],
                               writes=[K("bank0"), K("QA")])
                    yield S.op("pool", lambda e, i=i, d=d: e.tensor_tensor(out=QKm[i][:], in0=QA[i][:], in1=cst[:, 3 + d, :], op=ALU.mult),
                               reads=[K("QA")], writes=[K("QKm")])
                    yield S.op("pool", lambda e, i=i: e.tensor_tensor(out=qdec[i][:], in0=qt[i][:], in1=EB[i][:], op=ALU.mult), reads=[K("qt"), K("EB")],
                               writes=[K("qdec")])
                    yield S.op("act", lambda e, i=i, bcol=bcol: e.activation(out=bv[i][:], in_=vm[i][:], func=AF.Copy, scale=bcol),
                               reads=[K("vm"), "BETA"], writes=[K("bv")])
                    yield S.op("act", lambda e, i=i, becol=becol: e.activation(out=kbe[i][:], in_=km[i][:], func=AF.Copy, scale=becol),
                               reads=[K("km"), "BE"], writes=[K("kbe")])
                    yield S.op("act", lambda e, i=i, fcol=fcol: e.activation(out=kdec[i][:], in_=km[i][:], func=AF.Copy, scale=fcol),
                               reads=[K("km"), "Fd"], writes=[K("kdec")])
                    a, b = 0, 1
                    for m in range(4):
                        ka, kb_ = str(a), str(b)
                        if m > 0:
                            yield S.op("pe", lambda e, OP=OP, i=i, a=a: e.matmul(OP, Yb[i][a][:], Pm[i][a][:], start=True, stop=True),
                                       reads=[K("Y" + ka), K("P" + ka)], writes=[K("bank0"), K("OP")])
                        if m < 3:
                            yield S.op("pe", lambda e, OX=OX, i=i, a=a: e.matmul(OX, Yb[i][a][:], Xm[i][a][:], start=True, stop=True),
                                       reads=[K("Y" + ka), K("X" + ka)], writes=[K("bank0"), K("OX")])
                            yield S.op("pe", lambda e, Yp=Yp, i=i, a=a: e.matmul(Yp, Xm[i][a][:], Yb[i][a][:], start=True, stop=True),
                                       reads=[K("Y" + ka), K("X" + ka)], writes=[K("bank0"), K("Yp")])
                        if m > 0:
                            yield S.op("dve", lambda e, OP=OP, i=i, a=a, b=b: e.tensor_tensor(out=Pm[i][b][:], in0=OP, in1=Pm[i][a][:], op=ALU.add),
                                       reads=[K("OP"), K("P" + ka)], writes=[K("bank0"), K("P" + kb_)])
                        if m < 3:
                            yield S.op("act", lambda e, OX=OX, i=i, b=b: e.activation(out=Xm[i][b][:], in_=OX, func=AF.Copy),
                                       reads=[K("OX")], writes=[K("bank0"), K("X" + kb_)])
                            yield S.op("act", lambda e, Yp=Yp, i=i, b=b: e.activation(out=Yb[i][b][:], in_=Yp, func=AF.Copy),
                                       reads=[K("Yp")], writes=[K("bank0"), K("Y" + kb_)])
                        a, b = b, a
                    yield S.op("pe", lambda e, M1=M1, i=i, a=a: e.transpose(out=M1, in_=Pm[i][a][:], identity=cst[:, 0, :]),
                         reads=[K("P" + str(a))], writes=[K("bank0"), K("M1")])
                    yield S.op("act", lambda e, M1=M1, i=i: e.activation(out=Qs[i][:], in_=M1, func=AF.Copy), reads=[K("M1")], writes=[K("bank0"), K("Qs")])
                    for lv in range(3):
                        ka, kb_ = str(a), str(b)
                        yield S.op("pe", lambda e, M1=M1, i=i, a=a, lv=lv: e.matmul(M1, Yo[i][lv][:], Pm[i][a][:], start=True, stop=True),
                             reads=[K("Yo%d" % lv), K("P" + ka)], writes=[K("bank0"), K("M1")])
                        yield S.op("act", lambda e, M1=M1, i=i: e.activation(out=M1s[i][:], in_=M1, func=AF.Copy), reads=[K("M1")], writes=[K("bank0"), K("M1s")])
                        yield S.op("pe", lambda e, OP=OP, i=i: e.matmul(OP, Qs[i][:], M1s[i][:], start=True, stop=True),
                             reads=[K("Qs"), K("M1s")], writes=[K("bank0"), K("OP")])
                        if lv < 2:
                            yield S.op("dve", lambda e, OP=OP, i=i, a=a, b=b: e.tensor_tensor(out=Pm[i][b][:], in0=OP, in1=Pm[i][a][:], op=ALU.add),
                                 reads=[K("OP"), K("P" + ka)], writes=[K("bank0"), K("P" + kb_)])
                            yield S.op("pe", lambda e, M1=M1, i=i, b=b: e.transpose(out=M1, in_=Pm[i][b][:], identity=cst[:, 0, :]),
                                 reads=[K("P" + kb_)], writes=[K("bank0"), K("M1")])
                            yield S.op("act", lambda e, M1=M1, i=i: e.activation(out=Qs[i][:], in_=M1, func=AF.Copy), reads=[K("M1")], writes=[K("bank0"), K("Qs")])
                        else:
                            yield S.op("dve", lambda e, OP=OP, i=i, a=a: e.tensor_tensor(out=TT[i][:], in0=OP, in1=Pm[i][a][:], op=ALU.add),
                                 reads=[K("OP"), K("P" + ka)], writes=[K("bank0"), K("TT")])
                        a, b = b, a
                    yield S.op("pe", lambda e, U=U, i=i: e.matmul(U, TT[i][:], bv[i][:], start=True, stop=True), reads=[K("TT"), K("bv")], writes=[K("bank0"), K("U")])
                    yield S.op("pe", lambda e, W=W, i=i: e.matmul(W, kbe[i][:], TT[i][:], start=True, stop=True), reads=[K("TT"), K("kbe")], writes=[K("bank0"), K("W")])
                    yield S.op("dve", lambda e, U=U, i=i: e.tensor_copy(out=usb[i][:], in_=U), reads=[K("U")], writes=[K("bank0"), K("usb")])
                    yield S.op("act", lambda e, W=W, i=i: e.activation(out=wsb[i][:], in_=W, func=AF.Copy), reads=[K("W")], writes=[K("bank0"), K("wsb")])
                    sk, sbk = "Sf%d%d" % (h, d), "Sb%d%d" % (h, d)
                    SF, SB = Sf[h][d], Sb[h][d]
                    yield S.op("pe", lambda e, Sp=Sp, i=i, SB=SB: e.matmul(Sp, wsb[i][:], SB[:], start=True, stop=True), reads=[K("wsb"), sbk], writes=[K("bank0"), K("Sp")])
                    yield S.op("dve", lambda e, Sp=Sp, i=i: e.tensor_tensor(out=vnew[i][:], in0=usb[i][:], in1=Sp, op=ALU.subtract),
                         reads=[K("usb"), K("Sp")], writes=[K("bank0"), K("vnew")])
                    yield S.op("pe", lambda e, O2=O2, i=i, SB=SB: e.matmul(O2, SB[:], qdec[i][:], start=True, stop=False), reads=[sbk, K("qdec")], writes=[K("bank0"), K("O2")])
                    yield S.op("pe", lambda e, O2=O2, i=i: e.matmul(O2, vnew[i][:], QKm[i][:], start=False, stop=True), reads=[K("vnew"), K("QKm")], writes=[K("bank0"), K("O2")])
                    yield S.op("act", lambda e, O2=O2, i=i: e.activation(out=osb[i][:], in_=O2, func=AF.Copy), reads=[K("O2")], writes=[K("bank0"), K("osb")])
                    yield S.dma("act", lambda e, i=i, d=d, h=h, tsl=tsl: e.dma_start(out=self.oT[d, h, :, tsl], in_=osb[i][:]), reads=[K("osb")])
                    yield S.op("pe", lambda e, Dp=Dp, i=i: e.matmul(Dp, kdec[i][:], vnew[i][:], start=True, stop=True), reads=[K("kdec"), K("vnew")], writes=[K("bank0"), K("Dp")])
                    yield S.op("dve", lambda e, Dp=Dp, SF=SF, glcol=glcol: e.scalar_tensor_tensor(out=SF[:], in0=SF[:], scalar=glcol, in1=Dp,
                                                                                           op0=ALU.mult, op1=ALU.add), reads=[sk, K("Dp"), "GL"], writes=[K("bank0"), sk])
                    yield S.op("act", lambda e, SF=SF, SB=SB: e.activation(out=SB[:], in_=SF[:], func=AF.Copy), reads=[sk], writes=[sbk])

                order = [(s_, h_, d_) for s_ in range(getattr(self, 'scan_steps', NT))
                         for h_ in range(getattr(self, 'scan_heads', 8)) for d_ in range(2)]
                STAG = getattr(self, "scan_stag", 9)
                slots = [order[k::NSET] for k in range(NSET)]
                gens = [None] * NSET
                nxt = [0] * NSET
                rnd = 0
                while True:
                    alive = False
                    for k in range(NSET):
                        if gens[k] is None and nxt[k] < len(slots[k]) and rnd >= k * STAG:
                            s_, h_, d_ = slots[k][nxt[k]]
                            nxt[k] += 1
                            gens[k] = body(k, h_, d_, ORD[d_][s_])
                        if gens[k] is not None:
                            alive = True
                            try:
                                next(gens[k])
                            except StopIteration:
                                gens[k] = None
                        elif nxt[k] < len(slots[k]):
                            alive = True
                    rnd += 1
                    if not alive:
                        break
                S.limit = None
                S.emit(block)
        if getattr(self, "gdn_stop", 9) < 3:
            return
        with ExitStack() as es, nc.Block() as block:
            o0 = [self.sb(es, "o0", [128, T], F32) for _ in range(2)]
            o1 = [self.sb(es, "o1", [128, T], F32) for _ in range(2)]
            gt = [self.sb(es, "gt", [128, T], F32) for _ in range(2)]
            sq = [self.sb(es, "nsq", [128, 512], F32) for _ in range(2)]
            ri = [self.sb(es, "nri", [128, 512], F32) for _ in range(2)]
            yb = [self.sb(es, "nyb", [128, T], BF16) for _ in range(2)]
            gn = self.sb(es, "gn", [128, 1], F32)
            pss = [self.ps(es, "npss", [128, 512], F32) for _ in range(2)]
            S.dma("sp", lambda e: e.dma_start(out=gn[:], in_=self.gnorm[l][:, :]), writes=["gn"])
            for h in range(8):
                i = h % 2
                S.dma("sp", lambda e, i=i, h=h: e.dma_start(out=o0[i][:], in_=self.oT[0, h, :, :]), writes=["o0%d" % i])
                S.dma("sp", lambda e, i=i, h=h: e.dma_start(out=o1[i][:], in_=self.oT[1, h, :, :]), writes=["o1%d" % i])
                S.dma("sp", lambda e, i=i, h=h: e.dma_start(out=gt[i][:], in_=self.zT[4608 + h * 128:4736 + h * 128, :]), writes=["gt%d" % i])
                S.op("pool", lambda e, i=i: e.tensor_tensor(out=o0[i][:], in0=o0[i][:], in1=o1[i][:], op=ALU.add), reads=["o0%d" % i, "o1%d" % i], writes=["o0%d" % i])
                S.op("act", lambda e, i=i: e.activation(out=gt[i][:], in_=gt[i][:], func=AF.Silu), reads=["gt%d" % i], writes=["gt%d" % i])
                for bi, (t0, tn) in enumerate(TBLK):
                    bb = bi % 2
                    S.op("act", lambda e, i=i, t0=t0, tn=tn, bb=bb: e.activation(out=sq[bb][:, 0:tn], in_=o0[i][:, t0:t0 + tn], func=AF.Square),
                         reads=["o0%d" % i], writes=["nsq%d" % bb])
                    S.op("pe", lambda e, tn=tn, bb=bb: e.matmul(pss[bb][:, 0:tn], cst[:, 1, :], sq[bb][:, 0:tn], start=True, stop=True),
                         reads=["nsq%d" % bb], writes=["npss%d" % bb])
                    S.op("act", lambda e, tn=tn, bb=bb: e.activation(out=ri[bb][:, 0:tn], in_=pss[bb][:, 0:tn], func=AF.Ln, scale=1.0 / 128,
                                                                   bias=self.epsc[:, 0:1]), reads=["npss%d" % bb], writes=["nri%d" % bb])
                    S.op("act", lambda e, tn=tn, bb=bb: e.activation(out=ri[bb][:, 0:tn], in_=ri[bb][:, 0:tn], func=AF.Exp, scale=-0.5), reads=["nri%d" % bb], writes=["nri%d" % bb])
                    S.op("dve", lambda e, i=i, t0=t0, tn=tn, bb=bb: e.scalar_tensor_tensor(out=ri[bb][:, 0:tn], in0=o0[i][:, t0:t0 + tn], scalar=gn[:, 0:1],
                                                                                         in1=ri[bb][:, 0:tn], op0=ALU.mult, op1=ALU.mult),
                         reads=["o0%d" % i, "nri%d" % bb, "gn"], writes=["nri%d" % bb])
                    S.op("pool", lambda e, i=i, t0=t0, tn=tn, bb=bb: e.tensor_tensor(out=yb[i][:, t0:t0 + tn], in0=ri[bb][:, 0:tn], in1=gt[i][:, t0:t0 + tn],
                                                                                   op=ALU.mult), reads=["nri%d" % bb, "gt%d" % i], writes=["nyb%d" % i])
                S.dma("act", lambda e, i=i, h=h: e.dma_start(out=self.mixT[1024 + h * 128:1152 + h * 128, :], in_=yb[i][:]), reads=["nyb%d" % i])
            S.emit(block)

    def phase_mixer(self, l):
        if not getattr(self, "skip_attn", False):
            self.phase_attn(l)
        if not getattr(self, "skip_gdn", False):
            self.phase_gdn(l)
    def build_gdnprobe(self):
        nc, es, P = self.nc, self.es, self
        l = 0
        self.consts = P.din("consts", [128, 21, 128])
        self.zT = P.din("zT", [D_IN, T])
        self.alog = {l: P.din("alog0", [128, 16])}
        self.dtb = {l: P.din("dtb0", [128, 16])}
        self.convw = {l: P.din("convw0", [128, 24, 5])}
        self.gnorm = {l: P.din("gnorm0", [128, 1])}
        self.gq = P.dout("gq", [8, 128, T], BF16)
        self.gk = P.dout("gk", [8, 128, T], BF16)
        self.gkm = P.dout("gkm", [8, T, 128], BF16)
        self.gvm = P.dout("gvm", [8, T, 128], BF16)
        self.oT = P.dout("oT", [2, 8, 128, T])
        self.mixT = P.dout("mixT", [D, T], BF16)
        self.S = Sch(nc, es)
        self.cst = P.sb(es, "cst", [128, 21, 128], F32)
        self.cstb = P.sb(es, "cstb", [128, 21, 128], BF16)
        self.junk = P.sb(es, "junk", [128, 8], F32)
        self.epsc = P.sb(es, "epsc", [128, 1], F32)
        S = self.S
        with nc.Block() as block:
            S.dma("sp", lambda e: e.dma_start(out=self.cst[:], in_=self.consts[:, :, :]), writes=["cst"])
            S.op("dve", lambda e: e.tensor_copy(out=self.cstb[:], in_=self.cst[:]), reads=["cst"], writes=["cstb"])
            S.op("pool", lambda e: e.memset(self.epsc[:], EPS), writes=["epsc"])
            S.emit(block)
        self.phase_gdn(0)
        es.close()
        return nc

    def build(self):
        nc = self.nc
        es = self.es
        P = self
        self.xT_in = P.din("xT", [D, T])
        self.c2 = P.din("c2", [128, KC, 2])
        self.consts = P.din("consts", [128, 21, 128])
        self.w_ada = {l: P.din("w_ada%d" % l, [D, 6 * D]) for l in self.layers}
        self.b_ada = {l: P.din("b_ada%d" % l, [128, 96]) for l in self.layers}
        self.nrm = {l: P.din("nrm%d" % l, [128, 2, KC]) for l in self.layers}
        self.w_in = {l: P.din("w_in%d" % l, [D, D_IN]) for l in self.layers}
        if self.upto != "inproj":
            self.w_out = {l: P.din("w_out%d" % l, [D, D]) for l in self.layers}
            self.w_ff1 = {l: P.din("w_ff1%d" % l, [D, D_FF]) for l in self.layers}
            self.w_ff2 = {l: P.din("w_ff2%d" % l, [D_FF, D]) for l in self.layers}
            self.nfin = P.din("nfin", [128, KC])
            self.outT = P.dout("outT", [D, L])
        if self.upto != "inproj":
            self.ropeT = P.din("ropeT", [128, 2, L])
            self.sink = {l: P.din("sink%d" % l, [128, 8]) for l in self.layers}
            self.alog = {l: P.din("alog%d" % l, [128, 16]) for l in self.layers}
            self.dtb = {l: P.din("dtb%d" % l, [128, 16]) for l in self.layers}
            self.convw = {l: P.din("convw%d" % l, [128, 24, 5]) for l in self.layers}
            self.gnorm = {l: P.din("gnorm%d" % l, [128, 1]) for l in self.layers}
        self.gq = P.dscr("gq", [8, 128, T], BF16)
        self.gk = P.dscr("gk", [8, 128, T], BF16)
        self.gkm = P.dscr("gkm", [8, T, 128], BF16)
        self.gvm = P.dscr("gvm", [8, T, 128], BF16)
        self.oT = P.dout("oT", [2, 8, 128, T]) if "oT" in [d[0] for d in self.dbg] else P.dscr("oT", [2, 8, 128, T])
        dn = [d[0] for d in self.dbg]
        self.xT = P.dout("xTs", [D, T]) if "xT" in dn else P.dscr("xTs", [D, T])
        self.zT = P.dscr("zT", [D_IN, T]) if "zT" not in [d[0] for d in self.dbg] else P.dout("zT", [D_IN, T])
        self.mixT = P.dout("mixT", [D, T], BF16) if "mixT" in dn else P.dscr("mixT", [D, T], BF16)
        self.S = Sch(nc, es)
        self.cst = P.sb(es, "cst", [128, 21, 128], F32)
        self.cstb = P.sb(es, "cstb", [128, 21, 128], BF16)
        self.junk = P.sb(es, "junk", [128, 8], F32)
        self.epsc = P.sb(es, "epsc", [128, 1], F32)
        self.mod = {l: P.sb(es, "mod%d" % l, [128, 96, 2], F32) for l in self.layers}
        self.gm1 = {l: P.sb(es, "gm1_%d" % l, [128, 2, KC, 2], F32) for l in self.layers}
        self.phase_init()
        for l in self.layers:
            self.phase_mod(l)
        for l in self.layers:
            self.phase_norm_inproj(l)
            if self.upto == "inproj":
                break
            self.phase_mixer(l)
            if not getattr(self, "skip_tail", False):
                self.phase_dense_tail(l)
        if self.upto == "all":
            self.phase_final()
        es.close()
        return nc


def make_consts():
    c = np.zeros((128, 21, 128), np.float32)
    p = np.arange(128)[:, None]
    f = np.arange(128)[None, :]
    c[:, 0, :] = (p == f)
    c[:, 1, :] = 1.0
    partner = np.where((np.arange(128) % 64) < 32, np.arange(128) + 32, np.arange(128) - 32)
    c[:, 2, :] = (p == partner[None, :])
    c[:, 3, :] = (p <= f)
    c[:, 4, :] = (p >= f)
    c[:, 5, :] = -(p < f).astype(np.float32)
    c[:, 6, :] = -(p > f).astype(np.float32)
    c[:, 7, :] = (p // 16 == f // 16)
    for lv, sz in enumerate((16, 32, 64)):
        c[:, 8 + lv, :] = (p // (2 * sz) == f // (2 * sz)) & (p // sz != f // sz)
    for d in range(2):
        c[:, 11 + d, :] = c[:, 5 + d, :] * c[:, 7, :]
        for k in range(4):
            c[:, 13 + d * 4 + k, :] = c[:, 6 - d, :] * c[:, 7 + k, :]
    return c


def rope_tables():
    pos = np.arange(L)
    row = (pos // 64).astype(np.float32)
    col = (pos % 64).astype(np.float32)
    inv = (10000.0 ** (-np.arange(0, 64, 2, dtype=np.float32) / 64.0)).astype(np.float32)
    ang = np.concatenate([row[:, None] * inv[None, :], col[:, None] * inv[None, :]], axis=-1).astype(np.float32)
    cos, sin = np.cos(ang).astype(np.float32), np.sin(ang).astype(np.float32)
    out = np.zeros((128, 2, L), np.float32)
    for m_ in range(128):
        idx = (m_ % 32) if m_ < 64 else 32 + ((m_ - 64) % 32)
        sign = -1.0 if (m_ % 64) < 32 else 1.0
        out[m_, 0, :] = cos[:, idx]
        out[m_, 1, :] = sign * sin[:, idx]
    return out


def fm_vec(v, nchunk):
    return np.ascontiguousarray(np.asarray(v, np.float32).reshape(nchunk, 128).T)


def core_inputs(inp, b, layers, upto="all"):
    m = {}
    xt = np.concatenate([inp["ctx"][b], inp["x"][b]], axis=0)
    m["xT"] = np.ascontiguousarray(xt.T)
    c2 = np.stack([fm_vec(inp["c"][b], KC), fm_vec(inp["c_ctx"], KC)], axis=-1)
    m["c2"] = np.ascontiguousarray(c2)
    m["consts"] = make_consts()
    for l in layers:
        m["w_ada%d" % l] = inp["w_ada"][l]
        m["b_ada%d" % l] = fm_vec(inp["b_ada"][l], 96)
        m["nrm%d" % l] = np.ascontiguousarray(np.stack([fm_vec(inp["norm_mix"][l], KC), fm_vec(inp["norm_ffn"][l], KC)], axis=1))
        m["w_in%d" % l] = inp["w_in"][l]
        if upto != "inproj":
            m["w_out%d" % l] = inp["w_out"][l]
            m["w_ff1%d" % l] = inp["w_ff1"][l]
            m["w_ff2%d" % l] = inp["w_ff2"][l]
    if upto != "inproj":
        m["nfin"] = fm_vec(inp["norm_final"], KC)
        m["ropeT"] = rope_tables()
        for l in layers:
            m["sink%d" % l] = np.ascontiguousarray(np.broadcast_to(inp["attn_sink"][l][None, :], (128, 8))).astype(np.float32)
            m["alog%d" % l] = np.ascontiguousarray(np.broadcast_to(inp["a_log"][l].reshape(1, 16), (128, 16))).astype(np.float32)
            m["dtb%d" % l] = np.ascontiguousarray(np.broadcast_to(inp["dt_bias"][l].reshape(1, 16), (128, 16))).astype(np.float32)
            cw = inp["conv_w"][l]
            m["convw%d" % l] = np.ascontiguousarray(cw.T.reshape(24, 128, 5).transpose(1, 0, 2)).astype(np.float32)
            m["gnorm%d" % l] = np.ascontiguousarray(inp["gdn_norm"][l].reshape(128, 1)).astype(np.float32)
    return m


_PROG = {}


def kernel(**inputs):
    inp = {k: np.asarray(v) for k, v in inputs.items()}
    layers = list(range(DEPTH))
    if "nc" not in _PROG:
        _PROG["nc"] = Prog(layers).build()
    nc = _PROG["nc"]
    n = 4
    maps = [core_inputs(inp, b, layers) for b in range(n)]
    res = run_bass_kernel_spmd(nc, maps, core_ids=list(range(n)))
    out = np.stack([res.results[b]["outT"].T for b in range(n)], axis=0)
    return np.ascontiguousarray(out.astype(np.float32))
```

```python
import math
from contextlib import ExitStack
import numpy as np
import concourse.bass as bass
import concourse.mybir as mybir
from concourse.bass_utils import run_bass_kernel_spmd

F32 = mybir.dt.float32
BF16 = mybir.dt.bfloat16
ALU = mybir.AluOpType
AF = mybir.ActivationFunctionType

D = 2048
KC = 16
L = 2048
LC = 256
T = L + LC
NT = T // 128
DEPTH = 4
D_IN = 5664
D_FF = 8192
EPS = 1e-6
TBLK = [(0, 256), (256, 512), (768, 512), (1280, 512), (1792, 512)]
NCORES = 8


class Op:
    __slots__ = ("e", "fn", "deps", "idx", "signal", "dma", "dsem", "dval", "prewait")

    def __init__(self, e, fn, deps, idx, dma=False):
        self.e, self.fn, self.deps, self.idx, self.dma = e, fn, deps, idx, dma
        self.signal = False
        self.dsem = None
        self.dval = 0
        self.prewait = None


class Sch:
    ENG = ("pe", "act", "dve", "pool", "sp")
    NDS = 12

    def __init__(self, nc, es):
        self.nc = nc
        self.eng = {"pe": nc.tensor, "act": nc.scalar, "dve": nc.vector, "pool": nc.gpsimd, "sp": nc.sync}
        self.sem = {e: es.enter_context(nc.semaphore("sem_" + e)) for e in self.ENG}
        self.cnt = {e: 0 for e in self.ENG}
        self.dsem = {q: [es.enter_context(nc.semaphore("dsem_%s_%d" % (q, i))) for i in range(self.NDS)]
                     for q in ("sp", "pool", "act")}
        self.dcnt = {q: 0 for q in ("sp", "pool", "act")}
        self.reset()

    def reset(self):
        self.ops = {e: [] for e in self.ENG}
        self.res = {}

    def _mk(self, e, fn, reads, writes, dma):
        deps = set()
        for r in reads:
            st = self.res.get(r)
            if st is not None and st[0] is not None:
                deps.add(st[0])
        for w in writes:
            st = self.res.get(w)
            if st is not None:
                if st[0] is not None:
                    deps.add(st[0])
                deps.update(st[1].values())
        if e == "pe" and not dma:
            deps = {d for d in deps if d.dma or d.e != "pe"}
        o = Op(e, fn, deps, len(self.ops[e]), dma)
        self.ops[e].append(o)
        for d in deps:
            d.signal = True
        for r in reads:
            st = self.res.setdefault(r, [None, {}])
            st[1][e + ("d" if dma else "")] = o
        for w in writes:
            self.res[w] = [o, {}]
        return o

    limit = None
    nrec = 0

    def op(self, e, fn, reads=(), writes=()):
        if self.limit is not None:
            self.nrec += 1
            if self.nrec > self.limit:
                return None
        return self._mk(e, fn, reads, writes, False)

    def dma(self, q, fn, reads=(), writes=()):
        if self.limit is not None:
            self.nrec += 1
            if self.nrec > self.limit:
                return None
        o = self._mk(q, fn, reads, writes, True)
        m = self.dcnt[q]
        self.dcnt[q] += 1
        o.dsem = self.dsem[q][m % self.NDS]
        o.dval = 16 * (m // self.NDS + 1)
        o.prewait = 16 * (m // self.NDS)
        return o

    def emit(self, block, final_wait=True):
        val = {}
        for e in self.ENG:
            c = self.cnt[e]
            pend = []
            for o in self.ops[e]:
                if o.dma:
                    continue
                pend.append(o)
                if o.signal:
                    c += 1
                    for p in pend:
                        val[p] = c
                    pend = []
            self.cnt[e] = c
        dtot = {}

        def run(e):
            def body(eng):
                waited = {}

                def w(sem, v):
                    k = id(sem)
                    if waited.get(k, -1) >= v:
                        return
                    waited[k] = v
                    eng.wait_ge(sem, v)
                for o in self.ops[e]:
                    for d in o.deps:
                        if d.dma:
                            w(d.dsem, d.dval)
                        else:
                            w(self.sem[d.e], val[d])
                    if o.dma:
                        if o.prewait:
                            w(o.dsem, o.prewait)
                        ins = o.fn(eng)
                        ins.then_inc(o.dsem, 16)
                        dtot[id(o.dsem)] = (o.dsem, o.dval)
                    else:
                        ins = o.fn(eng)
                        if o.signal:
                            ins.then_inc(self.sem[e], 1)
                if e == "sp" and final_wait:
                    pass
            return body

        for e in ("pe", "act", "dve", "pool"):
            if self.ops[e]:
                getattr(block, {"pe": "tensor", "act": "scalar", "dve": "vector", "pool": "gpsimd"}[e])(run(e))
        spbody = run("sp")

        def spfull(eng):
            spbody(eng)
            for q in ("pool", "act"):
                for o in self.ops[q]:
                    if o.dma:
                        dtot[id(o.dsem)] = (o.dsem, max(o.dval, dtot.get(id(o.dsem), (None, 0))[1]))
            for sem, v in dtot.values():
                eng.wait_ge(sem, v)
        block.sync(spfull)
        self.reset()


class Prog:
    def __init__(self, layers, upto="all", dbg=()):
        self.layers = layers
        self.upto = upto
        self.dbg = dbg
        self.nc = bass.Bass("TRN2", target_bir_lowering=False)
        self.es = ExitStack()
        self.uid = 0

    def din(self, name, shape, dt=F32):
        return self.nc.dram_tensor(name, list(shape), dt, kind="ExternalInput").ap()

    def dout(self, name, shape, dt=F32):
        return self.nc.dram_tensor(name, list(shape), dt, kind="ExternalOutput").ap()

    def dscr(self, name, shape, dt=F32):
        return self.nc.dram_tensor(name, list(shape), dt).ap()

    def sb(self, es, name, shape, dt):
        self.uid += 1
        return es.enter_context(self.nc.sbuf_tensor("%s_%d" % (name, self.uid), list(shape), dt))

    def ps(self, es, name, shape, dt=F32):
        self.uid += 1
        return es.enter_context(self.nc.psum_tensor("%s_%d" % (name, self.uid), list(shape), dt))

    def build(self):
        nc = self.nc
        es = self.es
        P = self
        self.xT_in = P.din("xT", [D, T])
        self.c2 = P.din("c2", [128, KC, 2])
        self.consts = P.din("consts", [128, 21, 128])
        self.ropeT = P.din("ropeT", [128, 2, L])
        self.w_ada = {l: P.din("w_ada%d" % l, [D, 6 * D]) for l in self.layers}
        self.b_ada = {l: P.din("b_ada%d" % l, [128, 96]) for l in self.layers}
        self.nrm = {l: P.din("nrm%d" % l, [128, 2, KC]) for l in self.layers}
        self.w_in = {l: P.din("w_in%d" % l, [D, D_IN]) for l in self.layers}
        self.w_out = {l: P.din("w_out%d" % l, [D, D]) for l in self.layers}
        self.w_ff1 = {l: P.din("w_ff1%d" % l, [D, D_FF]) for l in self.layers}
        self.w_ff2 = {l: P.din("w_ff2%d" % l, [D_FF, D]) for l in self.layers}
        self.nfin = P.din("nfin", [128, KC])
        self.outT = P.dout("outT", [D, L])
        self.xT = P.dscr("xTs", [D, T])
        self.zqkT = P.dscr("zqkT", [1280, T], BF16)
        self.zv = P.dscr("zv", [T, 256], BF16)
        self.zg = P.dscr("zg", [3072, T])
        self.zgate = P.dscr("zgate", [1024, T], BF16)
        self.zab = P.dscr("zab", [T, 32])
        self.mixT = P.dscr("mixT", [D, T], BF16)
        self.aT = P.dscr("aT", [D_FF, T], BF16)
        self.dbgout = {}
        for name, shape, dt in self.dbg:
            self.dbgout[name] = P.dout("dbg_" + name, shape, dt)

        self.S = Sch(nc, es)
        self.cst = P.sb(es, "cst", [128, 21, 128], F32)
        self.cstb = P.sb(es, "cstb", [128, 21, 128], BF16)
        self.mod = {l: P.sb(es, "mod%d" % l, [128, 96, 2], F32) for l in self.layers}
        self.gm1 = {l: P.sb(es, "gm1_%d" % l, [128, 2, KC, 2], F32) for l in self.layers}

        self.phase_init()
        for l in self.layers:
            self.phase_mod(l)
        for l in self.layers:
            self.phase_norm_inproj(l)
            if self.upto == "inproj":
                break
            if not getattr(self, "skip_tail", False):
                self.phase_dense_tail(l)
        if self.upto == "all":
            self.phase_final()
        return nc

    def phase_init(self):
        nc, S = self.nc, self.S
        with ExitStack() as es, nc.Block() as block:
            S.dma("sp", lambda e: e.dma_start(out=self.cst[:], in_=self.consts[:, :, :]), writes=["cst"])
            S.op("dve", lambda e: e.tensor_copy(out=self.cstb[:], in_=self.cst[:]), reads=["cst"], writes=["cstb"])
            S.op("pool", lambda e: e.memset(self.epsc[:], EPS), writes=["epsc"])
            for i in range(4):
                S.dma("sp", lambda e, i=i: e.dma_start(out=self.xT[i * 512:(i + 1) * 512, :],
                                                      in_=self.xT_in[i * 512:(i + 1) * 512, :]))
            S.emit(block)

    def phase_mod(self, l):
        nc, S = self.nc, self.S
        with ExitStack() as es, nc.Block() as block:
            c2 = self.sb(es, "c2", [128, KC, 2], F32)
            s2 = self.sb(es, "s2", [128, KC, 2], BF16)
            bt = self.sb(es, "bt", [128, 96], F32)
            nr = self.sb(es, "nr", [128, 2, KC], F32)
            NS = 3
            wsl = [self.sb(es, "wad", [128, KC, 512], BF16) for _ in range(NS)]
            pm = self.ps(es, "pm", [128, 96, 2], F32)
            S.dma("sp", lambda e: e.dma_start(out=c2[:], in_=self.c2[:, :, :]), writes=["c2"])
            S.dma("sp", lambda e: e.dma_start(out=bt[:], in_=self.b_ada[l][:, :]), writes=["bt"])
            S.dma("sp", lambda e: e.dma_start(out=nr[:], in_=self.nrm[l][:, :, :]), writes=["nr"])
            S.op("act", lambda e: e.activation(out=s2[:], in_=c2[:], func=AF.Silu), reads=["c2"], writes=["s2"])
            wv = self.w_ada[l].rearrange("(kc p) n -> p kc n", p=128)
            for s in range(24):
                buf = wsl[s % NS]
                key = "wad%d" % (s % NS)
                S.dma("pool", lambda e, s=s, buf=buf: e.dma_start(out=buf[:], in_=wv[:, :, s * 512:(s + 1) * 512]),
                      writes=[key])
                for j in range(4):
                    cc = s * 4 + j
                    for kc in range(KC):
                        S.op("pe", lambda e, cc=cc, kc=kc, j=j, buf=buf: e.matmul(
                            pm[:, cc, :], buf[:, kc, j * 128:(j + 1) * 128], s2[:, kc, :],
                            start=(kc == 0), stop=(kc == KC - 1)), reads=[key, "s2"], writes=["pm"])
            mod = self.mod[l]
            S.op("dve", lambda e: e.tensor_tensor(out=mod[:], in0=pm[:], in1=bt[:].unsqueeze(2).to_broadcast([128, 96, 2]),
                                                  op=ALU.add), reads=["pm", "bt"], writes=["mod"])
            gm = self.gm1[l]
            for n, which in ((0, 1), (1, 4)):
                S.op("dve", lambda e, n=n, which=which: e.tensor_scalar(
                    out=gm[:, n, :, :], in0=mod[:, which * 16:(which + 1) * 16, :], scalar1=1.0, scalar2=None,
                    op0=ALU.add), reads=["mod"], writes=["gm%d" % n])
                S.op("dve", lambda e, n=n: e.tensor_tensor(
                    out=gm[:, n, :, :], in0=gm[:, n, :, :], in1=nr[:, n, :].unsqueeze(2).to_broadcast([128, KC, 2]),
                    op=ALU.mult), reads=["gm%d" % n, "nr"], writes=["gm%d" % n])
            S.emit(block)

    def norm_to_hT(self, es, l, n, hT, shift_which):
        nc, S = self.nc, self.S
        xb = [self.sb(es, "xb", [128, KC, 256], F32) for _ in range(2)]
        sq = [self.sb(es, "sq", [128, 256], F32) for _ in range(8)]
        rs = [self.sb(es, "rs", [128, 256], F32) for _ in range(2)]
        tmp = [self.sb(es, "tmp", [128, 256], F32) for _ in range(6)]
        pst = [self.ps(es, "pst", [128, 512], F32) for _ in range(2)]
        xv = self.xT.rearrange("(c p) t -> p c t", p=128)
        gm, mod = self.gm1[l], self.mod[l]
        ones = self.cst[:, 1, :]

        def nb(k, b):
            j = 1 if b == 0 else 0
            X, R, PS = xb[k], rs[k], pst[k][:, 0:256]
            kx, kr, kp = "xb%d" % k, "rs%d" % k, "pst%d" % k
            yield S.dma("sp", lambda e: e.dma_start(out=X[:], in_=xv[:, :, b * 256:(b + 1) * 256]), writes=[kx])
            for c in range(KC):
                Q, kq = sq[k * 4 + c % 4], "sq%d_%d" % (k, c % 4)
                yield S.op("act", lambda e, Q=Q, c=c: e.activation(out=Q[:], in_=X[:, c, :], func=AF.Square),
                           reads=[kx], writes=[kq])
                yield S.op("pe", lambda e, Q=Q, c=c: e.matmul(PS, ones, Q[:], start=(c == 0), stop=(c == KC - 1)),
                           reads=[kq, "cst"], writes=[kp])
            yield S.op("act", lambda e: e.activation(out=R[:], in_=PS, func=AF.Ln, scale=1.0 / D, bias=self.epsc[:, 0:1]),
                       reads=[kp], writes=[kr])
            yield S.op("act", lambda e: e.activation(out=R[:], in_=R[:], func=AF.Exp, scale=-0.5), reads=[kr], writes=[kr])
            for c in range(KC):
                tb, kt = tmp[k * 3 + c % 3], "tmp%d_%d" % (k, c % 3)
                yield S.op("dve", lambda e, tb=tb, c=c: e.tensor_tensor(out=tb[:], in0=X[:, c, :], in1=R[:], op=ALU.mult),
                           reads=[kx, kr], writes=[kt])
                yield S.op("act", lambda e, tb=tb, c=c: e.activation(
                    out=hT[:, c, b * 256:(b + 1) * 256], in_=tb[:], func=AF.Identity,
                    scale=gm[:, n, c, j:j + 1], bias=mod[:, shift_which * 16 + c, j:j + 1]),
                    reads=[kt, "gm%d" % n, "mod"], writes=["hT%d" % b])

        nblk = T // 256
        slots = [list(range(nblk))[k::2] for k in range(2)]
        g, nx, rnd = [None, None], [0, 0], 0
        while True:
            alive = False
            for k in range(2):
                if g[k] is None and nx[k] < len(slots[k]) and rnd >= k * 34:
                    g[k] = nb(k, slots[k][nx[k]])
                    nx[k] += 1
                if g[k] is not None:
                    alive = True
                    try:
                        next(g[k])
                    except StopIteration:
                        g[k] = None
                elif nx[k] < len(slots[k]):
                    alive = True
            rnd += 1
            if not alive:
                break

    def gemm(self, es, W, ncols, act, actkey, epi, nkc=KC, wname="w", tblk=TBLK):
        S = self.S
        NS = 3
        if not hasattr(self, "_gc"):
            self._gc = {}
        k1, k2 = (id(es), wname, nkc), (id(es), "ps")
        if k1 not in self._gc:
            self._gc[k1] = [self.sb(es, wname, [128, nkc, 256], BF16) for _ in range(NS)]
        if k2 not in self._gc:
            self._gc[k2] = [self.ps(es, "pg", [128, 512], F32) for _ in range(4)]
        wsl, psb = self._gc[k1], self._gc[k2]
        Wv = W.rearrange("(kc p) n -> p kc n", p=128)
        nslab = (ncols + 255) // 256
        pi = 0
        KG = 4
        for s in range(nslab):
            c0 = s * 256
            cw = min(256, ncols - c0)
            buf = wsl[s % NS]
            for kg in range(nkc // KG):
                S.dma("pool", lambda e, buf=buf, kg=kg, c0=c0, cw=cw: e.dma_start(
                    out=buf[:, kg * KG:(kg + 1) * KG, 0:cw], in_=Wv[:, kg * KG:(kg + 1) * KG, c0:c0 + cw]),
                    writes=["%s%d_%d" % (wname, s % NS, kg)])
            for g in range((cw + 127) // 128):
                m = min(128, cw - g * 128)
                for bi, (t0, tn) in enumerate(tblk):
                    PS = psb[pi % 4]
                    pk = "pg%d" % (pi % 4)
                    pi += 1
                    for kc in range(nkc):
                        S.op("pe", lambda e, PS=PS, buf=buf, kc=kc, g=g, m=m, t0=t0, tn=tn: e.matmul(
                            PS[0:m, 0:tn], buf[:, kc, g * 128:g * 128 + m], act[:, kc, t0:t0 + tn],
                            start=(kc == 0), stop=(kc == nkc - 1)),
                            reads=["%s%d_%d" % (wname, s % NS, kc // KG), actkey], writes=[pk])
                    epi(c0 + g * 128, m, bi, t0, tn, PS, pk)

    def phase_norm_inproj(self, l):
        nc, S = self.nc, self.S
        with ExitStack() as es:
            hT = self.sb(es, "hT", [128, KC, T], BF16)
            with ExitStack() as es2, nc.Block() as block:
                self.norm_to_hT(es2, l, 0, hT, 0)
                S.emit(block)
            with ExitStack() as es2, nc.Block() as block:
                stg = [self.sb(es2, "stg", [128, 512], F32) for _ in range(4)]
                cnt = [0]

                def epi(c0, m, bi, t0, tn, PS, pk):
                    i = cnt[0] % 4
                    cnt[0] += 1
                    st = stg[i]
                    S.op("act", lambda e: e.activation(out=st[0:m, 0:tn], in_=PS[0:m, 0:tn], func=AF.Copy),
                         reads=[pk], writes=["stg%d" % i])
                    S.dma("sp", lambda e: e.dma_start(out=self.zT[c0:c0 + m, t0:t0 + tn], in_=st[0:m, 0:tn]),
                          reads=["stg%d" % i])
                self.gemm(es2, self.w_in[l], D_IN, hT, "hT", epi, wname="win")
                S.emit(block)

    def resid_epi(self, es, l, which):
        S = self.S
        NXL = 6
        xl = [self.sb(es, "xl", [128, 512], F32) for _ in range(NXL)]
        cnt = [0]
        mod = self.mod[l]

        def epi(c0, m, bi, t0, tn, PS, pk):
            i = cnt[0] % NXL
            cnt[0] += 1
            xb = xl[i]
            j = 1 if t0 < LC else 0
            c = c0 // 128
            xk = "x_%d_%d" % (c, bi)
            S.dma("act", lambda e: e.dma_start(out=xb[:, 0:tn], in_=self.xT[c0:c0 + 128, t0:t0 + tn]),
                  reads=[xk], writes=["xl%d" % i])
            S.op("dve", lambda e: e.scalar_tensor_tensor(
                out=xb[:, 0:tn], in0=PS[:, 0:tn], scalar=mod[:, which * 16 + c, j:j + 1], in1=xb[:, 0:tn],
                op0=ALU.mult, op1=ALU.add), reads=[pk, "xl%d" % i, "mod"], writes=["xl%d" % i])
            S.dma("sp", lambda e: e.dma_start(out=self.xT[c0:c0 + 128, t0:t0 + tn], in_=xb[:, 0:tn]),
                  reads=["xl%d" % i], writes=[xk])
        return epi

    def phase_dense_tail(self, l):
        nc, S = self.nc, self.S
        with ExitStack() as es, nc.Block() as block:
            mx = self.sb(es, "mx", [128, KC, T], BF16)
            mv = self.mixT.rearrange("(c p) t -> p c t", p=128)
            for c4 in range(4):
                S.dma("sp", lambda e, c4=c4: e.dma_start(out=mx[:, c4 * 4:(c4 + 1) * 4, :], in_=mv[:, c4 * 4:(c4 + 1) * 4, :]),
                      writes=["mx%d" % c4])
            S.op("pool", lambda e: e.memset(self.junk[:], 0.0), reads=["mx0", "mx1", "mx2", "mx3"], writes=["mx"])
            self.gemm(es, self.w_out[l], D, mx, "mx", self.resid_epi(es, l, 2), wname="wout")
            S.emit(block)
        with ExitStack() as es:
            hT = self.sb(es, "hnT", [128, KC, T], BF16)
            with ExitStack() as es2, nc.Block() as block:
                self.norm_to_hT(es2, l, 1, hT, 3)
                S.emit(block)
            with ExitStack() as es2, nc.Block() as block:
                NK2 = 8
                aT = self.sb(es2, "aT", [128, NK2, T], BF16)
                rl = [self.sb(es2, "rl", [128, 512], F32) for _ in range(3)]
                rc = [0]
                repi = self.resid_epi(es2, l, 5)
                for q in range(D_FF // (NK2 * 128)):
                    def epi1(c0, m, bi, t0, tn, PS, pk):
                        i = rc[0] % 3
                        rc[0] += 1
                        r = rl[i]
                        S.op("act", lambda e: e.activation(out=r[:, 0:tn], in_=PS[:, 0:tn], func=AF.Relu),
                             reads=[pk], writes=["rl%d" % i])
                        S.op("pool", lambda e: e.tensor_tensor(out=aT[:, c0 // 128, t0:t0 + tn], in0=r[:, 0:tn],
                                                               in1=r[:, 0:tn], op=ALU.mult),
                             reads=["rl%d" % i], writes=["aT"])
                    self.gemm(es2, self.w_ff1[l][:, q * NK2 * 128:(q + 1) * NK2 * 128], NK2 * 128, hT, "hnT", epi1,
                              wname="wf1")
                    self.gemm(es2, self.w_ff2[l][q * NK2 * 128:(q + 1) * NK2 * 128, :], D, aT, "aT", repi, nkc=NK2,
                              wname="wf2")
                S.emit(block)

    def phase_final(self):
        nc, S = self.nc, self.S
        with ExitStack() as es, nc.Block() as block:
            xb = [self.sb(es, "fxb", [128, KC, 256], F32) for _ in range(2)]
            sq = [self.sb(es, "fsq", [128, 256], F32) for _ in range(2)]
            rs = [self.sb(es, "frs", [128, 256], F32) for _ in range(2)]
            ob = [self.sb(es, "fob", [128, KC, 256], F32) for _ in range(2)]
            nf = self.sb(es, "nf", [128, KC], F32)
            pst = [self.ps(es, "fps", [128, 256], F32) for _ in range(2)]
            xv = self.xT.rearrange("(c p) t -> p c t", p=128)
            ov = self.outT.rearrange("(c p) t -> p c t", p=128)
            ones = self.cst[:, 1, :]
            S.dma("sp", lambda e: e.dma_start(out=nf[:], in_=self.nfin[:, :]), writes=["nf"])
            for b in range(L // 256):
                X, Q, R, PS, O = xb[b % 2], sq[b % 2], rs[b % 2], pst[b % 2], ob[b % 2]
                kx, kq, kr, kp, ko = ["%s%d" % (n, b % 2) for n in ("fx", "fq", "fr", "fp", "fo")]
                S.dma("sp", lambda e, X=X, b=b: e.dma_start(out=X[:], in_=xv[:, :, LC + b * 256:LC + (b + 1) * 256]),
                      writes=[kx])
                for c in range(KC):
                    S.op("act", lambda e, X=X, Q=Q, c=c: e.activation(out=Q[:], in_=X[:, c, :], func=AF.Square),
                         reads=[kx], writes=[kq])
                    S.op("pe", lambda e, Q=Q, PS=PS, c=c: e.matmul(PS[:], ones, Q[:], start=(c == 0), stop=(c == KC - 1)),
                         reads=[kq], writes=[kp])
                S.op("act", lambda e, R=R, PS=PS: e.activation(out=R[:], in_=PS[:], func=AF.Ln, scale=1.0 / D,
                                                              bias=self.epsc[:, 0:1]), reads=[kp], writes=[kr])
                S.op("act", lambda e, R=R: e.activation(out=R[:], in_=R[:], func=AF.Exp, scale=-0.5), reads=[kr], writes=[kr])
                for c in range(KC):
                    S.op("dve", lambda e, X=X, R=R, O=O, c=c: e.scalar_tensor_tensor(
                        out=O[:, c, :], in0=X[:, c, :], scalar=nf[:, c:c + 1], in1=R[:], op0=ALU.mult, op1=ALU.mult),
                        reads=[kx, kr, "nf"], writes=[ko])
                S.dma("sp", lambda e, O=O, b=b: e.dma_start(out=ov[:, :, b * 256:(b + 1) * 256], in_=O[:]), reads=[ko])
            S.emit(block)
    def join(self, keys, newkey):
        self.S.op("pool", lambda e: e.memset(self.junk[:], 0.0), reads=list(keys), writes=[newkey])

    def phase_attn(self, l):
        nc, S = self.nc, self.S
        scale = 128 ** -0.5
        cstb = self.cstb
        with ExitStack() as es, nc.Block() as block:
            rope = self.sb(es, "rope", [128, 2, L], F32)
            snk = self.sb(es, "snk", [128, 8], F32)
            esk = self.sb(es, "esk", [128, 8], F32)
            raw = [self.sb(es, "raw", [128, T], F32) for _ in range(2)]
            rb = self.sb(es, "rb", [128, L], BF16)
            t1 = [self.sb(es, "t1", [128, 512], F32) for _ in range(2)]
            t2 = [self.sb(es, "t2", [128, 512], F32) for _ in range(2)]
            kT = self.sb(es, "kT", [128, T], BF16)
            qT = self.sb(es, "qT", [128, T], BF16)
            vT = self.sb(es, "vT", [128, T], BF16)
            V = self.sb(es, "V", [128, NT, 128], BF16)
            PTc = self.sb(es, "PTc", [128, 2, T], BF16)
            PTl = self.sb(es, "PTl", [128, 16, 3, 128], BF16)
            rc = [self.sb(es, "rc", [128, 128], F32) for _ in range(2)]
            ost = [self.sb(es, "ost", [128, 128], BF16) for _ in range(3)]
            psS = [self.ps(es, "psS", [128, 512], F32) for _ in range(2)]
            psO = [self.ps(es, "psO", [128, 128], F32) for _ in range(2)]
            psD = [self.ps(es, "psD", [128, 128], F32) for _ in range(2)]
            psT = [self.ps(es, "psT", [128, 128], BF16) for _ in range(2)]
            S.dma("sp", lambda e: e.dma_start(out=rope[:], in_=self.ropeT[:, :, :]), writes=["rope"])
            S.dma("sp", lambda e: e.dma_start(out=snk[:], in_=self.sink[l][:, :]), writes=["snk"])
            S.op("act", lambda e: e.activation(out=esk[:], in_=snk[:], func=AF.Exp), reads=["snk"], writes=["esk"])
            cn = {"raw": 0, "ps": 0, "t": 0, "o": 0, "ost": 0}

            def load_rope(row0, dst, dkey):
                ri = cn["raw"] % 2
                cn["raw"] += 1
                R, rk = raw[ri], "raw%d" % ri
                S.dma("sp", lambda e: e.dma_start(out=R[:], in_=self.zT[row0:row0 + 128, :]), writes=[rk])
                S.op("act", lambda e: e.activation(out=dst[:, 0:LC], in_=R[:, 0:LC], func=AF.Copy), reads=[rk],
                     writes=[dkey + "c"])
                S.op("act", lambda e: e.activation(out=rb[:], in_=R[:, LC:T], func=AF.Copy), reads=[rk], writes=["rb"])
                for blk in range(4):
                    a, b_ = blk * 512, (blk + 1) * 512
                    pi = cn["ps"] % 2
                    cn["ps"] += 1
                    ti = cn["t"] % 2
                    cn["t"] += 1
                    PS, pk = psS[pi], "psS%d" % pi
                    S.op("pe", lambda e, PS=PS, a=a, b_=b_: e.matmul(PS[:, :], cstb[:, 2, :], rb[:, a:b_], start=True, stop=True),
                         reads=["rb", "cstb"], writes=[pk])
                    S.op("dve", lambda e, a=a, b_=b_, ti=ti: e.tensor_tensor(out=t1[ti][:], in0=R[:, LC + a:LC + b_],
                                                                           in1=rope[:, 0, a:b_], op=ALU.mult),
                         reads=[rk, "rope"], writes=["t1%d" % ti])
                    S.op("dve", lambda e, PS=PS, a=a, b_=b_, ti=ti: e.tensor_tensor(out=t2[ti][:], in0=PS[:, :],
                                                                                  in1=rope[:, 1, a:b_], op=ALU.mult),
                         reads=[pk, "rope"], writes=["t2%d" % ti])
                    S.op("pool", lambda e, a=a, b_=b_, ti=ti: e.tensor_tensor(out=dst[:, LC + a:LC + b_], in0=t1[ti][:],
                                                                            in1=t2[ti][:], op=ALU.add),
                         reads=["t1%d" % ti, "t2%d" % ti], writes=[dkey + str(blk)])
                self.join([dkey + "c"] + [dkey + str(i) for i in range(4)], dkey)

            for g in range(2):
                load_rope(1024 + g * 128, kT, "kT")
                ri = cn["raw"] % 2
                cn["raw"] += 1
                R, rk = raw[ri], "raw%d" % ri
                S.dma("sp", lambda e, R=R, g=g: e.dma_start(out=R[:], in_=self.zT[1280 + g * 128:1408 + g * 128, :]), writes=[rk])
                S.op("act", lambda e, R=R: e.activation(out=vT[:], in_=R[:], func=AF.Copy), reads=[rk], writes=["vT"])
                for t in range(NT):
                    PT_, ptk = psT[t % 2], "psT%d" % (t % 2)
                    S.op("pe", lambda e, PT_=PT_, t=t: e.transpose(out=PT_[:, :], in_=vT[:, t * 128:(t + 1) * 128],
                                                                   identity=cstb[:, 0, :]), reads=["vT", "cstb"], writes=[ptk])
                    S.op("dve", lambda e, PT_=PT_, t=t: e.tensor_copy(out=V[:, t, :], in_=PT_[:, :]), reads=[ptk],
                         writes=["V%d" % t])
                self.join(["V%d" % t for t in range(NT)], "V")
                for hq in range(4):
                    h = g * 4 + hq
                    load_rope(h * 128, qT, "qT")
                    for ck in range(2):
                        for (t0, tn) in TBLK:
                            pi = cn["ps"] % 2
                            cn["ps"] += 1
                            PS, pk = psS[pi], "psS%d" % pi
                            S.op("pe", lambda e, PS=PS, ck=ck, t0=t0, tn=tn: e.matmul(
                                PS[:, 0:tn], kT[:, ck * 128:(ck + 1) * 128], qT[:, t0:t0 + tn], start=True, stop=True),
                                reads=["kT", "qT"], writes=[pk])
                            S.op("act", lambda e, PS=PS, ck=ck, t0=t0, tn=tn: e.activation(
                                out=PTc[:, ck, t0:t0 + tn], in_=PS[:, 0:tn], func=AF.Exp, scale=scale),
                                reads=[pk], writes=["PTc"])
                    for j in range(16):
                        lo, hi = max(j - 1, 0), min(j + 1, 15)
                        nq = hi - lo + 1
                        pi = cn["ps"] % 2
                        cn["ps"] += 1
                        PS, pk = psS[pi], "psS%d" % pi
                        S.op("pe", lambda e, PS=PS, j=j, lo=lo, nq=nq: e.matmul(
                            PS[:, 0:nq * 128], kT[:, LC + j * 128:LC + (j + 1) * 128],
                            qT[:, LC + lo * 128:LC + (lo + nq) * 128], start=True, stop=True),
                            reads=["kT", "qT"], writes=[pk])
                        r0 = lo - j + 1
                        S.op("act", lambda e, PS=PS, j=j, r0=r0, nq=nq: e.activation(
                            out=PTl[:, j, r0:r0 + nq, :], in_=PS[:, 0:nq * 128].rearrange("p (a b) -> p a b", b=128),
                            func=AF.Exp, scale=scale), reads=[pk], writes=["PTl"])
                        if j >= 1:
                            S.op("pool", lambda e, j=j: e.tensor_tensor(out=PTl[:, j, 0, :], in0=PTl[:, j, 0, :],
                                                                        in1=cstb[:, 3, :], op=ALU.mult),
                                 reads=["PTl", "cstb"], writes=["PTl"])
                        if j <= 14:
                            S.op("pool", lambda e, j=j: e.tensor_tensor(out=PTl[:, j, 2, :], in0=PTl[:, j, 2, :],
                                                                        in1=cstb[:, 4, :], op=ALU.mult),
                                 reads=["PTl", "cstb"], writes=["PTl"])
                    for qt in range(NT):
                        contrib = [(ck, PTc[:, ck, qt * 128:(qt + 1) * 128]) for ck in range(2)]
                        if qt >= 2:
                            n = qt - 2
                            for j in range(max(n - 1, 0), min(n + 1, 15) + 1):
                                contrib.append((2 + j, PTl[:, j, n - j + 1, :]))
                        oi = cn["o"] % 2
                        cn["o"] += 1
                        O, Dn = psO[oi], psD[oi]
                        for idx, (vt, rhs) in enumerate(contrib):
                            st_, sp_ = (idx == 0), (idx == len(contrib) - 1)
                            S.op("pe", lambda e, O=O, vt=vt, rhs=rhs, st_=st_, sp_=sp_: e.matmul(
                                O[:, :], V[:, vt, :], rhs, start=st_, stop=sp_), reads=["V", "PTc", "PTl"], writes=["psO%d" % oi])
                            S.op("pe", lambda e, Dn=Dn, rhs=rhs, st_=st_, sp_=sp_: e.matmul(
                                Dn[:, :], cstb[:, 1, :], rhs, start=st_, stop=sp_), reads=["PTc", "PTl"], writes=["psD%d" % oi])
                        RC = rc[oi]
                        S.op("dve", lambda e, RC=RC, Dn=Dn, h=h: e.tensor_scalar(out=RC[:], in0=Dn[:, :], scalar1=esk[:, h:h + 1],
                                                                             scalar2=None, op0=ALU.add),
                             reads=["psD%d" % oi, "esk"], writes=["rc%d" % oi])
                        S.op("dve", lambda e, RC=RC: e.reciprocal(out=RC[:], in_=RC[:]), reads=["rc%d" % oi], writes=["rc%d" % oi])
                        si = cn["ost"] % 3
                        cn["ost"] += 1
                        OS = ost[si]
                        S.op("dve", lambda e, OS=OS, O=O, RC=RC: e.tensor_tensor(out=OS[:], in0=O[:, :], in1=RC[:], op=ALU.mult),
                             reads=["psO%d" % oi, "rc%d" % oi], writes=["ost%d" % si])
                        S.dma("pool", lambda e, OS=OS, h=h, qt=qt: e.dma_start(
                            out=self.mixT[h * 128:(h + 1) * 128, qt * 128:(qt + 1) * 128], in_=OS[:]), reads=["ost%d" % si])
            S.emit(block)
    def phase_gdn(self, l):
        nc, S = self.nc, self.S
        cst, cstb = self.cst, self.cstb
        ORD = [list(range(NT)), [1, 0] + list(range(NT - 1, 1, -1))]
        with ExitStack() as eso:
            G = self.sb(eso, "G", [128, NT, 16], F32)
            BETA = self.sb(eso, "BETA", [128, NT, 16], F32)
            GC = self.sb(eso, "GC", [128, NT, 16], F32)
            TOT = self.sb(eso, "TOT", [128, NT, 16], F32)
            E = self.sb(eso, "E", [128, NT, 16], F32)
            Fd = self.sb(eso, "Fd", [128, NT, 16], F32)
            BE = self.sb(eso, "BE", [128, NT, 16], F32)
            GL = self.sb(eso, "GL", [128, NT, 16], F32)
            onec = self.sb(eso, "onec", [128, 1], F32)
            with ExitStack() as es, nc.Block() as block:
                abT = self.sb(es, "abT", [32, T], F32)
                abm = self.sb(es, "abm", [128, NT, 32], F32)
                al = self.sb(es, "al", [128, 16], F32)
                dtb = self.sb(es, "dtb", [128, 16], F32)
                cw = self.sb(es, "cw", [128, 24, 5], F32)
                xg = self.sb(es, "xg", [128, NT, 16], F32)
                Zp = [self.sb(es, "Zp", [128, T + 8], F32) for _ in range(3)]
                acc = [self.sb(es, "acc", [128, T], F32) for _ in range(3)]
                sl_ = [self.sb(es, "sl", [128, T], F32) for _ in range(3)]
                sq = [self.sb(es, "gsq", [128, 512], F32) for _ in range(6)]
                ri = [self.sb(es, "gri", [128, 512], F32) for _ in range(6)]
                ob = [self.sb(es, "gob", [128, T], BF16) for _ in range(3)]
                tm = [self.sb(es, "gtm", [128, NT, 128], BF16) for _ in range(3)]
                pabA = self.ps(es, "pabA", [128, 512], F32)
                pabB = self.ps(es, "pabB", [128, 512], F32)
                pgcb = self.ps(es, "pgc", [128, 512], F32)
                ptotb = self.ps(es, "ptot", [128, 512], F32)
                pgc = pgcb[:, 0:NT * 16].rearrange("p (n c) -> p n c", c=16)
                ptot = ptotb[:, 0:NT * 16].rearrange("p (n c) -> p n c", c=16)
                pss = [self.ps(es, "pss", [128, 512], F32) for _ in range(2)]
                ptrb = [self.ps(es, "ptr", [128, 1024], BF16) for _ in range(2)]
                ptr = [ptrb[0][:, 0:128], ptrb[1][:, 0:128]]
                S.op("pool", lambda e: e.memset(onec[:], 1.0), writes=["onec"])
                for z in Zp:
                    S.op("pool", lambda e, z=z: e.memset(z[:], 0.0), writes=["zpc%d" % Zp.index(z), "zpl%d" % Zp.index(z)])
                S.dma("sp", lambda e: e.dma_start(out=abT[:], in_=self.zT[5632:5664, :]), writes=["abT"])
                S.dma("sp", lambda e: e.dma_start(out=al[:], in_=self.alog[l][:, :]), writes=["al"])
                S.dma("sp", lambda e: e.dma_start(out=dtb[:], in_=self.dtb[l][:, :]), writes=["dtb"])
                S.dma("sp", lambda e: e.dma_start(out=cw[:], in_=self.convw[l][:, :, :]), writes=["cw"])
                for t in range(NT):
                    pb_ = pabA if t < 9 else pabB
                    S.op("pe", lambda e, t=t, pb_=pb_: e.transpose(out=pb_[:, (t % 9) * 32:(t % 9 + 1) * 32], in_=abT[:, t * 128:(t + 1) * 128],
                                                                   identity=cst[0:32, 0, 0:32]), reads=["abT", "cst"], writes=["pab"])
                S.op("dve", lambda e: e.tensor_copy(out=abm[:, 0:9, :], in_=pabA[:, 0:288].rearrange("p (n c) -> p n c", c=32)), reads=["pab"], writes=["abm0"])
                S.op("dve", lambda e: e.tensor_copy(out=abm[:, 9:18, :], in_=pabB[:, 0:288].rearrange("p (n c) -> p n c", c=32)), reads=["pab"], writes=["abm1"])
                self.join(["abm0", "abm1"], "abm")
                ab5 = abm[:].rearrange("p n (d t h) -> p n d t h", d=2, t=2)
                v4 = lambda a: a[:].rearrange("p n (d h) -> p n d h", d=2)
                b4 = lambda a: a[:].rearrange("p (d h) -> p d h", d=2).unsqueeze(1).to_broadcast([128, NT, 2, 8])
                S.op("dve", lambda e: e.tensor_tensor(out=v4(xg), in0=ab5[:, :, :, 0, :], in1=b4(dtb), op=ALU.add),
                     reads=["abm", "dtb"], writes=["xg"])
                S.op("act", lambda e: e.activation(out=xg[:], in_=xg[:], func=AF.Exp), reads=["xg"], writes=["xg"])
                S.op("act", lambda e: e.activation(out=xg[:], in_=xg[:], func=AF.Ln, bias=onec[:, 0:1]), reads=["xg", "onec"], writes=["xg"])
                S.op("act", lambda e: e.activation(out=al[:], in_=al[:], func=AF.Exp), reads=["al"], writes=["al"])
                S.op("dve", lambda e: e.tensor_scalar(out=al[:], in0=al[:], scalar1=-1.0, scalar2=None, op0=ALU.mult),
                     reads=["al"], writes=["al"])
                S.op("dve", lambda e: e.tensor_tensor(out=v4(G), in0=v4(xg), in1=b4(al), op=ALU.mult), reads=["xg", "al"], writes=["G"])
                S.op("act", lambda e: e.activation(out=v4(BETA), in_=ab5[:, :, :, 1, :], func=AF.Sigmoid), reads=["abm"], writes=["BETA"])
                for t in range(NT):
                    for d in range(2):
                        S.op("pe", lambda e, t=t, d=d: e.matmul(pgc[:, t, d * 8:(d + 1) * 8], cst[:, 3 + d, :],
                                                                G[:, t, d * 8:(d + 1) * 8], start=True, stop=True),
                             reads=["G", "cst"], writes=["pgc"])
                    S.op("pe", lambda e, t=t: e.matmul(ptot[:, t, :], cst[:, 1, :], G[:, t, :], start=True, stop=True),
                         reads=["G", "cst"], writes=["ptot"])
                S.op("dve", lambda e: e.tensor_copy(out=GC[:], in_=pgc), reads=["pgc"], writes=["GC"])
                S.op("dve", lambda e: e.tensor_copy(out=TOT[:], in_=ptot), reads=["ptot"], writes=["TOT"])
                S.op("act", lambda e: e.activation(out=E[:], in_=GC[:], func=AF.Exp), reads=["GC"], writes=["E"])
                S.op("dve", lambda e: e.tensor_tensor(out=Fd[:], in0=TOT[:], in1=GC[:], op=ALU.subtract), reads=["TOT", "GC"], writes=["Fd"])
                S.op("act", lambda e: e.activation(out=Fd[:], in_=Fd[:], func=AF.Exp), reads=["Fd"], writes=["Fd"])
                S.op("act", lambda e: e.activation(out=GL[:], in_=TOT[:], func=AF.Exp), reads=["TOT"], writes=["GL"])
                S.op("dve", lambda e: e.tensor_tensor(out=BE[:], in0=BETA[:], in1=E[:], op=ALU.mult), reads=["BETA", "E"], writes=["BE"])
                NPS = 2

                def pbody(i, h, which):
                    row0 = 1536 + which * 1024 + h * 128
                    ch = which * 8 + h
                    z, A_, SL, OB, TM = Zp[i], acc[i], sl_[i], ob[i], tm[i]
                    yield S.dma("sp", lambda e, z=z, row0=row0: e.dma_start(out=z[:, 2:2 + LC], in_=self.zT[row0:row0 + 128, 0:LC]),
                          writes=["zpc%d" % i])
                    yield S.dma("sp", lambda e, z=z, row0=row0: e.dma_start(out=z[:, 262:262 + L], in_=self.zT[row0:row0 + 128, LC:T]),
                          writes=["zpl%d" % i])
                    for (o0, n, zo, zk) in ((0, LC, 0, "zpc%d" % i), (LC, L, 260, "zpl%d" % i)):
                        yield S.op("dve", lambda e, z=z, A_=A_, o0=o0, n=n, zo=zo, ch=ch: e.tensor_scalar(
                            out=A_[:, o0:o0 + n], in0=z[:, zo:zo + n], scalar1=cw[:, ch, 0:1], scalar2=None, op0=ALU.mult),
                            reads=[zk, "cw"], writes=["acc%d" % i])
                        for j in range(1, 5):
                            yield S.op("dve", lambda e, z=z, A_=A_, o0=o0, n=n, zo=zo, ch=ch, j=j: e.scalar_tensor_tensor(
                                out=A_[:, o0:o0 + n], in0=z[:, zo + j:zo + j + n], scalar=cw[:, ch, j:j + 1],
                                in1=A_[:, o0:o0 + n], op0=ALU.mult, op1=ALU.add), reads=[zk, "cw", "acc%d" % i], writes=["acc%d" % i])
                    yield S.op("act", lambda e, A_=A_, SL=SL: e.activation(out=SL[:], in_=A_[:], func=AF.Silu),
                         reads=["acc%d" % i], writes=["sl%d" % i])
                    if which < 2:
                        qs = (128 ** -0.5) if which == 0 else 1.0
                        for bi, (t0, tn) in enumerate(TBLK):
                            bb = bi % 2
                            yield S.op("act", lambda e, SL=SL, t0=t0, tn=tn, bb=bb: e.activation(out=sq[i * 2 + bb][:, 0:tn], in_=SL[:, t0:t0 + tn],
                                                                                         func=AF.Square), reads=["sl%d" % i], writes=["gsq%d_%d" % (i, bb)])
                            yield S.op("pe", lambda e, t0=t0, tn=tn, bb=bb: e.matmul(pss[i][:, 0:tn], cst[:, 1, :], sq[i * 2 + bb][:, 0:tn],
                                                                             start=True, stop=True), reads=["gsq%d_%d" % (i, bb)], writes=["pss%d" % i])
                            yield S.op("act", lambda e, tn=tn, bb=bb: e.activation(out=ri[i * 2 + bb][:, 0:tn], in_=pss[i][:, 0:tn], func=AF.Ln,
                                                                           bias=self.epsc[:, 0:1]), reads=["pss%d" % i], writes=["gri%d_%d" % (i, bb)])
                            yield S.op("act", lambda e, tn=tn, bb=bb: e.activation(out=ri[i * 2 + bb][:, 0:tn], in_=ri[i * 2 + bb][:, 0:tn], func=AF.Exp, scale=-0.5),
                                 reads=["gri%d_%d" % (i, bb)], writes=["gri%d_%d" % (i, bb)])
                            yield S.op("dve", lambda e, SL=SL, OB=OB, t0=t0, tn=tn, bb=bb, qs=qs: e.scalar_tensor_tensor(
                                out=OB[:, t0:t0 + tn], in0=SL[:, t0:t0 + tn], scalar=qs, in1=ri[i * 2 + bb][:, 0:tn],
                                op0=ALU.mult, op1=ALU.mult), reads=["sl%d" % i, "gri%d_%d" % (i, bb)], writes=["gob%d" % i])
                        dst = self.gq if which == 0 else self.gk
                        yield S.dma("act", lambda e, OB=OB, dst=dst, h=h: e.dma_start(out=dst[h, :, :], in_=OB[:]), reads=["gob%d" % i])
                    else:
                        yield S.op("act", lambda e, SL=SL, OB=OB: e.activation(out=OB[:], in_=SL[:], func=AF.Copy),
                             reads=["sl%d" % i], writes=["gob%d" % i])
                    if which >= 1:
                        for t in range(NT):
                            yield S.op("pe", lambda e, OB=OB, t=t: e.transpose(out=ptr[i], in_=OB[:, t * 128:(t + 1) * 128],
                                                                         identity=cstb[:, 0, :]), reads=["gob%d" % i], writes=["ptr%d" % i])
                            yield S.op("dve", lambda e, TM=TM, t=t: e.tensor_copy(out=TM[:, t, :], in_=ptr[i]),
                                 reads=["ptr%d" % i], writes=["gtm%d" % i])
                        dst = self.gkm if which == 1 else self.gvm
                        yield S.dma("act", lambda e, TM=TM, dst=dst, h=h: e.dma_start(
                            out=dst[h].rearrange("(n p) d -> p n d", p=128), in_=TM[:]), reads=["gtm%d" % i])

                items = [(h_, w_) for h_ in range(8) for w_ in range(3)]
                pslots = [items[k::NPS] for k in range(NPS)]
                pg = [None] * NPS
                pn = [0] * NPS
                rnd = 0
                while True:
                    alive = False
                    for k in range(NPS):
                        if pg[k] is None and pn[k] < len(pslots[k]) and rnd >= k * 25:
                            pg[k] = pbody(k, *pslots[k][pn[k]])
                            pn[k] += 1
                        if pg[k] is not None:
                            alive = True
                            try:
                                next(pg[k])
                            except StopIteration:
                                pg[k] = None
                        elif pn[k] < len(pslots[k]):
                            alive = True
                    rnd += 1
                    if not alive:
                        break
                S.emit(block)
            if getattr(self, "gdn_stop", 9) < 2:
                return
            with ExitStack() as es, nc.Block() as block:
                NSET = 8
                f32t = lambda n: [self.sb(es, n, [128, 128], F32) for _ in range(NSET)]
                b16t = lambda n: [self.sb(es, n, [128, 128], BF16) for _ in range(NSET)]
                rhsG = [self.sb(es, "rhsG", [128, 256], F32) for _ in range(NSET)]
                t1, Ex, t2, Ey, EB, Mx, XA, My, QA, usb, osb, gcs = [f32t(n) for n in
                    ("t1", "Ex", "t2", "Ey", "EB", "Mx", "XA", "My", "QA", "usb", "osb", "gcs")]
                QKm, qdec, TT, bv, kbe, kdec, wsb, vnew, kt, qt, km, vm = [b16t(n) for n in
                    ("QKm", "qdec", "TT", "bv", "kbe", "kdec", "wsb", "vnew", "kt", "qt", "km", "vm")]
                Pm = [[self.sb(es, "Pm", [128, 128], F32) for _ in range(2)] for _ in range(NSET)]
                Xm = [[self.sb(es, "Xm", [128, 128], F32) for _ in range(2)] for _ in range(NSET)]
                Yb = [[self.sb(es, "Yb", [128, 128], F32) for _ in range(2)] for _ in range(NSET)]
                Yo = [[self.sb(es, "Yo", [128, 128], F32) for _ in range(3)] for _ in range(NSET)]
                XF, YF, Qs, M1s = [f32t(n) for n in ("XF", "YF", "Qs", "M1s")]
                Sf = [[self.sb(es, "Sf", [128, 128], F32) for _ in range(2)] for _ in range(8)]
                Sb = [[self.sb(es, "Sb", [128, 128], BF16) for _ in range(2)] for _ in range(8)]
                bk = [[self.ps(es, "bk", [128, 512], F32) for _ in range(1)] for _ in range(NSET)]
                for h in range(8):
                    for d in range(2):
                        S.op("pool", lambda e, h=h, d=d: e.memset(Sf[h][d][:], 0.0), writes=["Sf%d%d" % (h, d)])
                        S.op("pool", lambda e, h=h, d=d: e.memset(Sb[h][d][:], 0.0), writes=["Sb%d%d" % (h, d)])
                def body(i, h, d, t):
                    K = lambda n: "%s%d" % (n, i)
                    col = d * 8 + h
                    gcol, bcol, gccol = G[:, t, col:col + 1], BETA[:, t, col:col + 1], GC[:, t, col:col + 1]
                    becol, fcol, glcol = BE[:, t, col:col + 1], Fd[:, t, col:col + 1], GL[:, t, col:col + 1]
                    tsl = slice(t * 128, (t + 1) * 128)
                    Bk = bk[i][0]
                    A, B, C = Bk[:, 0:128], Bk[:, 128:256], Bk[:, 256:512]
                    OP, M1, OX, Yp = Bk[:, 0:128], Bk[:, 128:256], Bk[:, 256:384], Bk[:, 384:512]
                    Sp, U, W, O2 = Bk[:, 0:128], Bk[:, 128:256], Bk[:, 256:384], Bk[:, 384:512]
                    Dp = Bk[:, 128:256]
                    yield S.dma("sp", lambda e, i=i, h=h, tsl=tsl: e.dma_start(out=kt[i][:], in_=self.gk[h, :, tsl]), writes=[K("kt")])
                    yield S.dma("sp", lambda e, i=i, h=h, tsl=tsl: e.dma_start(out=qt[i][:], in_=self.gq[h, :, tsl]), writes=[K("qt")])
                    yield S.dma("sp", lambda e, i=i, h=h, tsl=tsl: e.dma_start(out=km[i][:], in_=self.gkm[h, tsl, :]), writes=[K("km")])
                    yield S.dma("sp", lambda e, i=i, h=h, tsl=tsl: e.dma_start(out=vm[i][:], in_=self.gvm[h, tsl, :]), writes=[K("vm")])
                    yield S.op("pe", lambda e, A=A, i=i: e.matmul(A, kt[i][:], kt[i][:], start=True, stop=True), reads=[K("kt")], writes=[K("bank0"), K("A")])
                    yield S.op("pe", lambda e, B=B, i=i: e.matmul(B, kt[i][:], qt[i][:], start=True, stop=True), reads=[K("kt"), K("qt")], writes=[K("bank0"), K("B")])
                    yield S.op("pool", lambda e, i=i, d=d, gcol=gcol: e.tensor_scalar(out=rhsG[i][:, 0:128], in0=cst[:, 3 + d, :], scalar1=gcol,
                                                                             scalar2=None, op0=ALU.mult), reads=["G"], writes=[K("rhsGa")])
                    yield S.op("act", lambda e, i=i, bcol=bcol: e.activation(out=rhsG[i][:, 128:256], in_=cst[:, 0, :], func=AF.Copy, scale=bcol),
                               reads=["BETA"], writes=[K("rhsGb")])
                    yield S.op("pe", lambda e, C=C, i=i: e.matmul(C, cst[:, 1, :], rhsG[i][:], start=True, stop=True), reads=[K("rhsGa"), K("rhsGb")],
                               writes=[K("bank0"), K("C")])
                    yield S.op("dve", lambda e, C=C, i=i, gccol=gccol: e.tensor_scalar(out=t1[i][:], in0=C[:, 0:128], scalar1=gccol, scalar2=0.0,
                                                                               op0=ALU.subtract, op1=ALU.min), reads=[K("C")], writes=[K("bank0"), K("t1")])
                    yield S.op("act", lambda e, i=i: e.activation(out=Ex[i][:], in_=t1[i][:], func=AF.Exp), reads=[K("t1")], writes=[K("Ex")])
                    yield S.op("dve", lambda e, C=C, i=i, gccol=gccol: e.tensor_scalar(out=t2[i][:], in0=C[:, 0:128], scalar1=gccol, scalar2=0.0,
                                                                               op0=ALU.subtract, op1=ALU.max), reads=[K("C")], writes=[K("bank0"), K("t2")])
                    yield S.op("act", lambda e, i=i: e.activation(out=Ey[i][:], in_=t2[i][:], func=AF.Exp, scale=-1.0), reads=[K("t2")], writes=[K("Ey")])
                    yield S.op("dve", lambda e, C=C, i=i: e.tensor_copy(out=gcs[i][:], in_=C[:, 0:128]), reads=[K("C")], writes=[K("bank0"), K("gcs")])
                    yield S.op("act", lambda e, i=i: e.activation(out=EB[i][:], in_=gcs[i][:], func=AF.Exp), reads=[K("gcs")], writes=[K("EB")])
                    yield S.op("dve", lambda e, C=C, i=i, d=d: e.tensor_tensor(out=Mx[i][:], in0=C[:, 128:256], in1=cst[:, 11 + d, :], op=ALU.mult),
                               reads=[K("C")], writes=[K("bank0"), K("Mx")])
                    yield S.op("dve", lambda e, A=A, i=i: e.tensor_tensor(out=XA[i][:], in0=A, in1=Ex[i][:], op=ALU.mult), reads=[K("A"), K("Ex")],
                               writes=[K("bank0"), K("XA")])
                    yield S.op("pool", lambda e, i=i: e.tensor_tensor(out=Xm[i][0][:], in0=XA[i][:], in1=Mx[i][:], op=ALU.mult),
                               reads=[K("XA"), K("Mx")], writes=[K("X0")])
                    yield S.op("pool", lambda e, i=i: e.tensor_tensor(out=Pm[i][1][:], in0=Xm[i][0][:], in1=cst[:, 0, :], op=ALU.add),
                               reads=[K("X0")], writes=[K("P1")])
                    yield S.op("dve", lambda e, A=A, i=i: e.tensor_tensor(out=YF[i][:], in0=A, in1=Ey[i][:], op=ALU.mult), reads=[K("A"), K("Ey")],
                               writes=[K("bank0"), K("YF")])
                    yield S.op("dve", lambda e, i=i, d=d, bcol=bcol: e.scalar_tensor_tensor(out=Yb[i][0][:], in0=YF[i][:], scalar=bcol, in1=cst[:, 13 + d * 4, :],
                                                                                     op0=ALU.mult, op1=ALU.mult), reads=[K("YF"), "BETA"], writes=[K("Y0")])
                    for lv in range(3):
                        yield S.op("dve", lambda e, i=i, d=d, lv=lv, bcol=bcol: e.scalar_tensor_tensor(
                            out=Yo[i][lv][:], in0=YF[i][:], scalar=bcol, in1=cst[:, 14 + d * 4 + lv, :], op0=ALU.mult, op1=ALU.mult),
                            reads=[K("YF"), "BETA"], writes=[K("Yo%d" % lv)])
                    yield S.op("dve", lambda e, B=B, i=i: e.tensor_tensor(out=QA[i][:], in0=B, in1=Ex[i][:], op=ALU.mult), reads=[K("B"), K("Ex")],
                               writes=[K("bank0"), K("QA")])
                    yield S.op("pool", lambda e, i=i, d=d: e.tensor_tensor(out=QKm[i][:], in0=QA[i][:], in1=cst[:, 3 + d, :], op=ALU.mult),
                               reads=[K("QA")], writes=[K("QKm")])
                    yield S.op("pool", lambda e, i=i: e.tensor_tensor(out=qdec[i][:], in0=qt[i][:], in1=EB[i][:], op=ALU.mult), reads=[K("qt"), K("EB")],
                               writes=[K("qdec")])
                    yield S.op("act", lambda e, i=i, bcol=bcol: e.activation(out=bv[i][:], in_=vm[i][:], func=AF.Copy, scale=bcol),
                               reads=[K("vm"), "BETA"], writes=[K("bv")])
                    yield S.op("act", lambda e, i=i, becol=becol: e.activation(out=kbe[i][:], in_=km[i][:], func=AF.Copy, scale=becol),
                               reads=[K("km"), "BE"], writes=[K("kbe")])
                    yield S.op("act", lambda e, i=i, fcol=fcol: e.activation(out=kdec[i][:], in_=km[i][:], func=AF.Copy, scale=fcol),
                               reads=[K("km"), "Fd"], writes=[K("kdec")])
                    a, b = 0, 1
                    for m in range(4):
                        ka, kb_ = str(a), str(b)
                        if m > 0:
                            yield S.op("pe", lambda e, OP=OP, i=i, a=a: e.matmul(OP, Yb[i][a][:], Pm[i][a][:], start=True, stop=True),
                                       reads=[K("Y" + ka), K("P" + ka)], writes=[K("bank0"), K("OP")])
                        if m < 3:
                            yield S.op("pe", lambda e, OX=OX, i=i, a=a: e.matmul(OX, Yb[i][a][:], Xm[i][a][:], start=True, stop=True),
                                       reads=[K("Y" + ka), K("X" + ka)], writes=[K("bank0"), K("OX")])
                            yield S.op("pe", lambda e, Yp=Yp, i=i, a=a: e.matmul(Yp, Xm[i][a][:], Yb[i][a][:], start=True, stop=True),
                                       reads=[K("Y" + ka), K("X" + ka)], writes=[K("bank0"), K("Yp")])
                        if m > 0:
                            yield S.op("dve", lambda e, OP=OP, i=i, a=a, b=b: e.tensor_tensor(out=Pm[i][b][:], in0=OP, in1=Pm[i][a][:], op=ALU.add),
                                       reads=[K("OP"), K("P" + ka)], writes=[K("bank0"), K("P" + kb_)])
                        if m < 3:
                            yield S.op("act", lambda e, OX=OX, i=i, b=b: e.activation(out=Xm[i][b][:], in_=OX, func=AF.Copy),
                                       reads=[K("OX")], writes=[K("bank0"), K("X" + kb_)])
                            yield S.op("act", lambda e, Yp=Yp, i=i, b=b: e.activation(out=Yb[i][b][:], in_=Yp, func=AF.Copy),
                                       reads=[K("Yp")], writes=[K("bank0"), K("Y" + kb_)])
                        a, b = b, a
                    yield S.op("pe", lambda e, M1=M1, i=i, a=a: e.transpose(out=M1, in_=Pm[i][a][:], identity=cst[:, 0, :]),
                         reads=[K("P" + str(a))], writes=[K("bank0"), K("M1")])
                    yield S.op("act", lambda e, M1=M1, i=i: e.activation(out=Qs[i][:], in_=M1, func=AF.Copy), reads=[K("M1")], writes=[K("bank0"), K("Qs")])
                    for lv in range(3):
                        ka, kb_ = str(a), str(b)
                        yield S.op("pe", lambda e, M1=M1, i=i, a=a, lv=lv: e.matmul(M1, Yo[i][lv][:], Pm[i][a][:], start=True, stop=True),
                             reads=[K("Yo%d" % lv), K("P" + ka)], writes=[K("bank0"), K("M1")])
                        yield S.op("act", lambda e, M1=M1, i=i: e.activation(out=M1s[i][:], in_=M1, func=AF.Copy), reads=[K("M1")], writes=[K("bank0"), K("M1s")])
                        yield S.op("pe", lambda e, OP=OP, i=i: e.matmul(OP, Qs[i][:], M1s[i][:], start=True, stop=True),
                             reads=[K("Qs"), K("M1s")], writes=[K("bank0"), K("OP")])
                        if lv < 2:
                            yield S.op("dve", lambda e, OP=OP, i=i, a=a, b=b: e.tensor_tensor(out=Pm[i][b][:], in0=OP, in1=Pm[i][a][:], op=ALU.add),
                                 reads=[K("OP"), K("P" + ka)], writes=[K("bank0"), K("P" + kb_)])
                            yield S.op("pe", lambda e, M1=M1, i=i, b=b: e.transpose(out=M1, in_=Pm[i][b][:], identity=cst[:, 0, :]),
                                 reads=[K("P" + kb_)], writes=[K("bank0"), K("M1")])
                            yield S.op("act", lambda e, M1=M1, i=i: e.activation(out=Qs[i][:], in_=M1, func=AF.Copy), reads=[K("M1")], writes=[K("bank0"), K("Qs")])
                        else:
                            yield S.op("dve", lambda e, OP=OP, i=i, a=a: e.tensor_tensor(out=TT[i][:], in0=OP, in1=Pm[i][a][:], op=ALU.add),
                                 reads=[K("OP"), K("P" + ka)], writes=[K("bank0"), K("TT")])
                        a, b = b, a
                    yield S.op("pe", lambda e, U=U, i=i: e.matmul(U, TT[i][:], bv[i][:], start=True, stop=True), reads=[K("TT"), K("bv")], writes=[K("bank0"), K("U")])
                    yield S.op("pe", lambda e, W=W, i=i: e.matmul(W, kbe[i][:], TT[i][:], start=True, stop=True), reads=[K("TT"), K("kbe")], writes=[K("bank0"), K("W")])
                    yield S.op("dve", lambda e, U=U, i=i: e.tensor_copy(out=usb[i][:], in_=U), reads=[K("U")], writes=[K("bank0"), K("usb")])
                    yield S.op("act", lambda e, W=W, i=i: e.activation(out=wsb[i][:], in_=W, func=AF.Copy), reads=[K("W")], writes=[K("bank0"), K("wsb")])
                    sk, sbk = "Sf%d%d" % (h, d), "Sb%d%d" % (h, d)
                    SF, SB = Sf[h][d], Sb[h][d]
                    yield S.op("pe", lambda e, Sp=Sp, i=i, SB=SB: e.matmul(Sp, wsb[i][:], SB[:], start=True, stop=True), reads=[K("wsb"), sbk], writes=[K("bank0"), K("Sp")])
                    yield S.op("dve", lambda e, Sp=Sp, i=i: e.tensor_tensor(out=vnew[i][:], in0=usb[i][:], in1=Sp, op=ALU.subtract),
                         reads=[K("usb"), K("Sp")], writes=[K("bank0"), K("vnew")])
                    yield S.op("pe", lambda e, O2=O2, i=i, SB=SB: e.matmul(O2, SB[:], qdec[i][:], start=True, stop=False), reads=[sbk, K("qdec")], writes=[K("bank0"), K("O2")])
                    yield S.op("pe", lambda e, O2=O2, i=i: e.matmul(O2, vnew[i][:], QKm[i][:], start=False, stop=True), reads=[K("vnew"), K("QKm")], writes=[K("bank0"), K("O2")])
                    yield S.op("act", lambda e, O2=O2, i=i: e.activation(out=osb[i][:], in_=O2, func=AF.Copy), reads=[K("O2")], writes=[K("bank0"), K("osb")])
                    yield S.dma("act", lambda e, i=i, d=d, h=h, tsl=tsl: e.dma_start(out=self.oT[d, h, :, tsl], in_=osb[i][:]), reads=[K("osb")])
                    yield S.op("pe", lambda e, Dp=Dp, i=i: e.matmul(Dp, kdec[i][:], vnew[i][:], start=True, stop=True), reads=[K("kdec"), K("vnew")], writes=[K("bank0"), K("Dp")])
                    yield S.op("dve", lambda e, Dp=Dp, SF=SF, glcol=glcol: e.scalar_tensor_tensor(out=SF[:], in0=SF[:], scalar=glcol, in1=Dp,
                                                                                           op0=ALU.mult, op1=ALU.add), reads=[sk, K("Dp"), "GL"], writes=[K("bank0"), sk])
                    yield S.op("act", lambda e, SF=SF, SB=SB: e.activation(out=SB[:], in_=SF[:], func=AF.Copy), reads=[sk], writes=[sbk])

                order = [(s_, h_, d_) for s_ in range(getattr(self, 'scan_steps', NT))
                         for h_ in range(getattr(self, 'scan_heads', 8)) for d_ in range(2)]
                STAG = getattr(self, "scan_stag", 9)
                slots = [order[k::NSET] for k in range(NSET)]
                gens = [None] * NSET
                nxt = [0] * NSET
                rnd = 0
                while True:
                    alive = False
                    for k in range(NSET):
                        if gens[k] is None and nxt[k] < len(slots[k]) and rnd >= k * STAG:
                            s_, h_, d_ = slots[k][nxt[k]]
                            nxt[k] += 1
                            gens[k] = body(k, h_, d_, ORD[d_][s_])
                        if gens[k] is not None:
                            alive = True
                            try:
                                next(gens[k])
                            except StopIteration:
                                gens[k] = None
                        elif nxt[k] < len(slots[k]):
                            alive = True
                    rnd += 1
                    if not alive:
                        break
                S.limit = None
                S.emit(block)
        if getattr(self, "gdn_stop", 9) < 3:
            return
        with ExitStack() as es, nc.Block() as block:
            o0 = [self.sb(es, "o0", [128, T], F32) for _ in range(2)]
            o1 = [self.sb(es, "o1", [128, T], F32) for _ in range(2)]
            gt = [self.sb(es, "gt", [128, T], F32) for _ in range(2)]
            sq = [self.sb(es, "nsq", [128, 512], F32) for _ in range(2)]
            ri = [self.sb(es, "nri", [128, 512], F32) for _ in range(2)]
            yb = [self.sb(es, "nyb", [128, T], BF16) for _ in range(2)]
            gn = self.sb(es, "gn", [128, 1], F32)
            pss = [self.ps(es, "npss", [128, 512], F32) for _ in range(2)]
            S.dma("sp", lambda e: e.dma_start(out=gn[:], in_=self.gnorm[l][:, :]), writes=["gn"])
            for h in range(8):
                i = h % 2
                S.dma("sp", lambda e, i=i, h=h: e.dma_start(out=o0[i][:], in_=self.oT[0, h, :, :]), writes=["o0%d" % i])
                S.dma("sp", lambda e, i=i, h=h: e.dma_start(out=o1[i][:], in_=self.oT[1, h, :, :]), writes=["o1%d" % i])
                S.dma("sp", lambda e, i=i, h=h: e.dma_start(out=gt[i][:], in_=self.zT[4608 + h * 128:4736 + h * 128, :]), writes=["gt%d" % i])
                S.op("pool", lambda e, i=i: e.tensor_tensor(out=o0[i][:], in0=o0[i][:], in1=o1[i][:], op=ALU.add), reads=["o0%d" % i, "o1%d" % i], writes=["o0%d" % i])
                S.op("act", lambda e, i=i: e.activation(out=gt[i][:], in_=gt[i][:], func=AF.Silu), reads=["gt%d" % i], writes=["gt%d" % i])
                for bi, (t0, tn) in enumerate(TBLK):
                    bb = bi % 2
                    S.op("act", lambda e, i=i, t0=t0, tn=tn, bb=bb: e.activation(out=sq[bb][:, 0:tn], in_=o0[i][:, t0:t0 + tn], func=AF.Square),
                         reads=["o0%d" % i], writes=["nsq%d" % bb])
                    S.op("pe", lambda e, tn=tn, bb=bb: e.matmul(pss[bb][:, 0:tn], cst[:, 1, :], sq[bb][:, 0:tn], start=True, stop=True),
                         reads=["nsq%d" % bb], writes=["npss%d" % bb])
                    S.op("act", lambda e, tn=tn, bb=bb: e.activation(out=ri[bb][:, 0:tn], in_=pss[bb][:, 0:tn], func=AF.Ln, scale=1.0 / 128,
                                                                   bias=self.epsc[:, 0:1]), reads=["npss%d" % bb], writes=["nri%d" % bb])
                    S.op("act", lambda e, tn=tn, bb=bb: e.activation(out=ri[bb][:, 0:tn], in_=ri[bb][:, 0:tn], func=AF.Exp, scale=-0.5), reads=["nri%d" % bb], writes=["nri%d" % bb])
                    S.op("dve", lambda e, i=i, t0=t0, tn=tn, bb=bb: e.scalar_tensor_tensor(out=ri[bb][:, 0:tn], in0=o0[i][:, t0:t0 + tn], scalar=gn[:, 0:1],
                                                                                         in1=ri[bb][:, 0:tn], op0=ALU.mult, op1=ALU.mult),
                         reads=["o0%d" % i, "nri%d" % bb, "gn"], writes=["nri%d" % bb])
                    S.op("pool", lambda e, i=i, t0=t0, tn=tn, bb=bb: e.tensor_tensor(out=yb[i][:, t0:t0 + tn], in0=ri[bb][:, 0:tn], in1=gt[i][:, t0:t0 + tn],
                                                                                   op=ALU.mult), reads=["nri%d" % bb, "gt%d" % i], writes=["nyb%d" % i])
                S.dma("act", lambda e, i=i, h=h: e.dma_start(out=self.mixT[1024 + h * 128:1152 + h * 128, :], in_=yb[i][:]), reads=["nyb%d" % i])
            S.emit(block)

    def phase_mixer(self, l):
        if not getattr(self, "skip_attn", False):
            self.phase_attn(l)
        if not getattr(self, "skip_gdn", False):
            self.phase_gdn(l)
    def build_gdnprobe(self):
        nc, es, P = self.nc, self.es, self
        l = 0
        self.consts = P.din("consts", [128, 21, 128])
        self.zT = P.din("zT", [D_IN, T])
        self.alog = {l: P.din("alog0", [128, 16])}
        self.dtb = {l: P.din("dtb0", [128, 16])}
        self.convw = {l: P.din("convw0", [128, 24, 5])}
        self.gnorm = {l: P.din("gnorm0", [128, 1])}
        self.gq = P.dout("gq", [8, 128, T], BF16)
        self.gk = P.dout("gk", [8, 128, T], BF16)
        self.gkm = P.dout("gkm", [8, T, 128], BF16)
        self.gvm = P.dout("gvm", [8, T, 128], BF16)
        self.oT = P.dout("oT", [2, 8, 128, T])
        self.mixT = P.dout("mixT", [D, T], BF16)
        self.S = Sch(nc, es)
        self.cst = P.sb(es, "cst", [128, 21, 128], F32)
        self.cstb = P.sb(es, "cstb", [128, 21, 128], BF16)
        self.junk = P.sb(es, "junk", [128, 8], F32)
        self.epsc = P.sb(es, "epsc", [128, 1], F32)
        S = self.S
        with nc.Block() as block:
            S.dma("sp", lambda e: e.dma_start(out=self.cst[:], in_=self.consts[:, :, :]), writes=["cst"])
            S.op("dve", lambda e: e.tensor_copy(out=self.cstb[:], in_=self.cst[:]), reads=["cst"], writes=["cstb"])
            S.op("pool", lambda e: e.memset(self.epsc[:], EPS), writes=["epsc"])
            S.emit(block)
        self.phase_gdn(0)
        es.close()
        return nc

    def build(self):
        nc = self.nc
        es = self.es
        P = self
        self.xT_in = P.din("xT", [D, T])
        self.c2 = P.din("c2", [128, KC, 2])
        self.consts = P.din("consts", [128, 21, 128])
        self.w_ada = {l: P.din("w_ada%d" % l, [D, 6 * D]) for l in self.layers}
        self.b_ada = {l: P.din("b_ada%d" % l, [128, 96]) for l in self.layers}
        self.nrm = {l: P.din("nrm%d" % l, [128, 2, KC]) for l in self.layers}
        self.w_in = {l: P.din("w_in%d" % l, [D, D_IN]) for l in self.layers}
        if self.upto != "inproj":
            self.w_out = {l: P.din("w_out%d" % l, [D, D]) for l in self.layers}
            self.w_ff1 = {l: P.din("w_ff1%d" % l, [D, D_FF]) for l in self.layers}
            self.w_ff2 = {l: P.din("w_ff2%d" % l, [D_FF, D]) for l in self.layers}
            self.nfin = P.din("nfin", [128, KC])
            self.outT = P.dout("outT", [D, L])
        if self.upto != "inproj":
            self.ropeT = P.din("ropeT", [128, 2, L])
            self.sink = {l: P.din("sink%d" % l, [128, 8]) for l in self.layers}
            self.alog = {l: P.din("alog%d" % l, [128, 16]) for l in self.layers}
            self.dtb = {l: P.din("dtb%d" % l, [128, 16]) for l in self.layers}
            self.convw = {l: P.din("convw%d" % l, [128, 24, 5]) for l in self.layers}
            self.gnorm = {l: P.din("gnorm%d" % l, [128, 1]) for l in self.layers}
        self.gq = P.dscr("gq", [8, 128, T], BF16)
        self.gk = P.dscr("gk", [8, 128, T], BF16)
        self.gkm = P.dscr("gkm", [8, T, 128], BF16)
        self.gvm = P.dscr("gvm", [8, T, 128], BF16)
        self.oT = P.dout("oT", [2, 8, 128, T]) if "oT" in [d[0] for d in self.dbg] else P.dscr("oT", [2, 8, 128, T])
        dn = [d[0] for d in self.dbg]
        self.xT = P.dout("xTs", [D, T]) if "xT" in dn else P.dscr("xTs", [D, T])
        self.zT = P.dscr("zT", [D_IN, T]) if "zT" not in [d[0] for d in self.dbg] else P.dout("zT", [D_IN, T])
        self.mixT = P.dout("mixT", [D, T], BF16) if "mixT" in dn else P.dscr("mixT", [D, T], BF16)
        self.S = Sch(nc, es)
        self.cst = P.sb(es, "cst", [128, 21, 128], F32)
        self.cstb = P.sb(es, "cstb", [128, 21, 128], BF16)
        self.junk = P.sb(es, "junk", [128, 8], F32)
        self.epsc = P.sb(es, "epsc", [128, 1], F32)
        self.mod = {l: P.sb(es, "mod%d" % l, [128, 96, 2], F32) for l in self.layers}
        self.gm1 = {l: P.sb(es, "gm1_%d" % l, [128, 2, KC, 2], F32) for l in self.layers}
        self.phase_init()
        for l in self.layers:
            self.phase_mod(l)
        for l in self.layers:
            self.phase_norm_inproj(l)
            if self.upto == "inproj":
                break
            self.phase_mixer(l)
            if not getattr(self, "skip_tail", False):
                self.phase_dense_tail(l)
        if self.upto == "all":
            self.phase_final()
        es.close()
        return nc


def make_consts():
    c = np.zeros((128, 21, 128), np.float32)
    p = np.arange(128)[:, None]
    f = np.arange(128)[None, :]
    c[:, 0, :] = (p == f)
    c[:, 1, :] = 1.0
    partner = np.where((np.arange(128) % 64) < 32, np.arange(128) + 32, np.arange(128) - 32)
    c[:, 2, :] = (p == partner[None, :])
    c[:, 3, :] = (p <= f)
    c[:, 4, :] = (p >= f)
    c[:, 5, :] = -(p < f).astype(np.float32)
    c[:, 6, :] = -(p > f).astype(np.float32)
    c[:, 7, :] = (p // 16 == f // 16)
    for lv, sz in enumerate((16, 32, 64)):
        c[:, 8 + lv, :] = (p // (2 * sz) == f // (2 * sz)) & (p // sz != f // sz)
    for d in range(2):
        c[:, 11 + d, :] = c[:, 5 + d, :] * c[:, 7, :]
        for k in range(4):
            c[:, 13 + d * 4 + k, :] = c[:, 6 - d, :] * c[:, 7 + k, :]
    return c


def rope_tables():
    pos = np.arange(L)
    row = (pos // 64).astype(np.float32)
    col = (pos % 64).astype(np.float32)
    inv = (10000.0 ** (-np.arange(0, 64, 2, dtype=np.float32) / 64.0)).astype(np.float32)
    ang = np.concatenate([row[:, None] * inv[None, :], col[:, None] * inv[None, :]], axis=-1).astype(np.float32)
    cos, sin = np.cos(ang).astype(np.float32), np.sin(ang).astype(np.float32)
    out = np.zeros((128, 2, L), np.float32)
    for m_ in range(128):
        idx = (m_ % 32) if m_ < 64 else 32 + ((m_ - 64) % 32)
        sign = -1.0 if (m_ % 64) < 32 else 1.0
        out[m_, 0, :] = cos[:, idx]
        out[m_, 1, :] = sign * sin[:, idx]
    return out


def fm_vec(v, nchunk):
    return np.ascontiguousarray(np.asarray(v, np.float32).reshape(nchunk, 128).T)


def core_inputs(inp, b, layers, upto="all"):
    m = {}
    xt = np.concatenate([inp["ctx"][b], inp["x"][b]], axis=0)
    m["xT"] = np.ascontiguousarray(xt.T)
    c2 = np.stack([fm_vec(inp["c"][b], KC), fm_vec(inp["c_ctx"], KC)], axis=-1)
    m["c2"] = np.ascontiguousarray(c2)
    m["consts"] = make_consts()
    for l in layers:
        m["w_ada%d" % l] = inp["w_ada"][l]
        m["b_ada%d" % l] = fm_vec(inp["b_ada"][l], 96)
        m["nrm%d" % l] = np.ascontiguousarray(np.stack([fm_vec(inp["norm_mix"][l], KC), fm_vec(inp["norm_ffn"][l], KC)], axis=1))
        m["w_in%d" % l] = inp["w_in"][l]
        if upto != "inproj":
            m["w_out%d" % l] = inp["w_out"][l]
            m["w_ff1%d" % l] = inp["w_ff1"][l]
            m["w_ff2%d" % l] = inp["w_ff2"][l]
    if upto != "inproj":
        m["nfin"] = fm_vec(inp["norm_final"], KC)
        m["ropeT"] = rope_tables()
        for l in layers:
            m["sink%d" % l] = np.ascontiguousarray(np.broadcast_to(inp["attn_sink"][l][None, :], (128, 8))).astype(np.float32)
            m["alog%d" % l] = np.ascontiguousarray(np.broadcast_to(inp["a_log"][l].reshape(1, 16), (128, 16))).astype(np.float32)
            m["dtb%d" % l] = np.ascontiguousarray(np.broadcast_to(inp["dt_bias"][l].reshape(1, 16), (128, 16))).astype(np.float32)
            cw = inp["conv_w"][l]
            m["convw%d" % l] = np.ascontiguousarray(cw.T.reshape(24, 128, 5).transpose(1, 0, 2)).astype(np.float32)
            m["gnorm%d" % l] = np.ascontiguousarray(inp["gdn_norm"][l].reshape(128, 1)).astype(np.float32)
    return m


_PROG = {}


def kernel(**inputs):
    inp = {k: np.asarray(v) for k, v in inputs.items()}
    layers = list(range(DEPTH))
    if "nc" not in _PROG:
        _PROG["nc"] = Prog(layers).build()
    nc = _PROG["nc"]
    n = 4
    maps = [core_inputs(inp, b, layers) for b in range(n)]
    res = run_bass_kernel_spmd(nc, maps, core_ids=list(range(n)))
    out = np.stack([res.results[b]["outT"].T for b in range(n)], axis=0)
    return np.ascontiguousarray(out.astype(np.float32))
```

```python
import math
from contextlib import ExitStack
import numpy as np
import concourse.bass as bass
import concourse.mybir as mybir
from concourse.bass_utils import run_bass_kernel_spmd

F32 = mybir.dt.float32
BF16 = mybir.dt.bfloat16
ALU = mybir.AluOpType
AF = mybir.ActivationFunctionType

D = 2048
KC = 16
L = 2048
LC = 256
T = L + LC
NT = T // 128
DEPTH = 4
D_IN = 5664
D_FF = 8192
EPS = 1e-6
TBLK = [(0, 256), (256, 512), (768, 512), (1280, 512), (1792, 512)]
NCORES = 8


class Op:
    __slots__ = ("e", "fn", "deps", "idx", "signal", "dma", "dsem", "dval", "prewait")

    def __init__(self, e, fn, deps, idx, dma=False):
        self.e, self.fn, self.deps, self.idx, self.dma = e, fn, deps, idx, dma
        self.signal = False
        self.dsem = None
        self.dval = 0
        self.prewait = None


class Sch:
    ENG = ("pe", "act", "dve", "pool", "sp")
    NDS = 12

    def __init__(self, nc, es):
        self.nc = nc
        self.eng = {"pe": nc.tensor, "act": nc.scalar, "dve": nc.vector, "pool": nc.gpsimd, "sp": nc.sync}
        self.sem = {e: es.enter_context(nc.semaphore("sem_" + e)) for e in self.ENG}
        self.cnt = {e: 0 for e in self.ENG}
        self.dsem = {q: [es.enter_context(nc.semaphore("dsem_%s_%d" % (q, i))) for i in range(self.NDS)]
                     for q in ("sp", "pool", "act")}
        self.dcnt = {q: 0 for q in ("sp", "pool", "act")}
        self.reset()

    def reset(self):
        self.ops = {e: [] for e in self.ENG}
        self.res = {}

    def _mk(self, e, fn, reads, writes, dma):
        deps = set()
        for r in reads:
            st = self.res.get(r)
            if st is not None and st[0] is not None:
                deps.add(st[0])
        for w in writes:
            st = self.res.get(w)
            if st is not None:
                if st[0] is not None:
                    deps.add(st[0])
                deps.update(st[1].values())
        if e == "pe" and not dma:
            deps = {d for d in deps if d.dma or d.e != "pe"}
        o = Op(e, fn, deps, len(self.ops[e]), dma)
        self.ops[e].append(o)
        for d in deps:
            d.signal = True
        for r in reads:
            st = self.res.setdefault(r, [None, {}])
            st[1][e + ("d" if dma else "")] = o
        for w in writes:
            self.res[w] = [o, {}]
        return o

    limit = None
    nrec = 0

    def op(self, e, fn, reads=(), writes=()):
        if self.limit is not None:
            self.nrec += 1
            if self.nrec > self.limit:
                return None
        return self._mk(e, fn, reads, writes, False)

    def dma(self, q, fn, reads=(), writes=()):
        if self.limit is not None:
            self.nrec += 1
            if self.nrec > self.limit:
                return None
        o = self._mk(q, fn, reads, writes, True)
        m = self.dcnt[q]
        self.dcnt[q] += 1
        o.dsem = self.dsem[q][m % self.NDS]
        o.dval = 16 * (m // self.NDS + 1)
        o.prewait = 16 * (m // self.NDS)
        return o

    def emit(self, block, final_wait=True):
        val = {}
        for e in self.ENG:
            c = self.cnt[e]
            pend = []
            for o in self.ops[e]:
                if o.dma:
                    continue
                pend.append(o)
                if o.signal:
                    c += 1
                    for p in pend:
                        val[p] = c
                    pend = []
            self.cnt[e] = c
        dtot = {}

        def run(e):
            def body(eng):
                waited = {}

                def w(sem, v):
                    k = id(sem)
                    if waited.get(k, -1) >= v:
                        return
                    waited[k] = v
                    eng.wait_ge(sem, v)
                for o in self.ops[e]:
                    for d in o.deps:
                        if d.dma:
                            w(d.dsem, d.dval)
                        else:
                            w(self.sem[d.e], val[d])
                    if o.dma:
                        if o.prewait:
                            w(o.dsem, o.prewait)
                        ins = o.fn(eng)
                        ins.then_inc(o.dsem, 16)
                        dtot[id(o.dsem)] = (o.dsem, o.dval)
                    else:
                        ins = o.fn(eng)
                        if o.signal:
                            ins.then_inc(self.sem[e], 1)
                if e == "sp" and final_wait:
                    pass
            return body

        for e in ("pe", "act", "dve", "pool"):
            if self.ops[e]:
                getattr(block, {"pe": "tensor", "act": "scalar", "dve": "vector", "pool": "gpsimd"}[e])(run(e))
        spbody = run("sp")

        def spfull(eng):
            spbody(eng)
            for q in ("pool", "act"):
                for o in self.ops[q]:
                    if o.dma:
                        dtot[id(o.dsem)] = (o.dsem, max(o.dval, dtot.get(id(o.dsem), (None, 0))[1]))
            for sem, v in dtot.values():
                eng.wait_ge(sem, v)
        block.sync(spfull)
        self.reset()


class Prog:
    def __init__(self, layers, upto="all", dbg=()):
        self.layers = layers
        self.upto = upto
        self.dbg = dbg
        self.nc = bass.Bass("TRN2", target_bir_lowering=False)
        self.es = ExitStack()
        self.uid = 0

    def din(self, name, shape, dt=F32):
        return self.nc.dram_tensor(name, list(shape), dt, kind="ExternalInput").ap()

    def dout(self, name, shape, dt=F32):
        return self.nc.dram_tensor(name, list(shape), dt, kind="ExternalOutput").ap()

    def dscr(self, name, shape, dt=F32):
        return self.nc.dram_tensor(name, list(shape), dt).ap()

    def sb(self, es, name, shape, dt):
        self.uid += 1
        return es.enter_context(self.nc.sbuf_tensor("%s_%d" % (name, self.uid), list(shape), dt))

    def ps(self, es, name, shape, dt=F32):
        self.uid += 1
        return es.enter_context(self.nc.psum_tensor("%s_%d" % (name, self.uid), list(shape), dt))

    def build(self):
        nc = self.nc
        es = self.es
        P = self
        self.xT_in = P.din("xT", [D, T])
        self.c2 = P.din("c2", [128, KC, 2])
        self.consts = P.din("consts", [128, 21, 128])
        self.ropeT = P.din("ropeT", [128, 2, L])
        self.w_ada = {l: P.din("w_ada%d" % l, [D, 6 * D]) for l in self.layers}
        self.b_ada = {l: P.din("b_ada%d" % l, [128, 96]) for l in self.layers}
        self.nrm = {l: P.din("nrm%d" % l, [128, 2, KC]) for l in self.layers}
        self.w_in = {l: P.din("w_in%d" % l, [D, D_IN]) for l in self.layers}
        self.w_out = {l: P.din("w_out%d" % l, [D, D]) for l in self.layers}
        self.w_ff1 = {l: P.din("w_ff1%d" % l, [D, D_FF]) for l in self.layers}
        self.w_ff2 = {l: P.din("w_ff2%d" % l, [D_FF, D]) for l in self.layers}
        self.nfin = P.din("nfin", [128, KC])
        self.outT = P.dout("outT", [D, L])
        self.xT = P.dscr("xTs", [D, T])
        self.zqkT = P.dscr("zqkT", [1280, T], BF16)
        self.zv = P.dscr("zv", [T, 256], BF16)
        self.zg = P.dscr("zg", [3072, T])
        self.zgate = P.dscr("zgate", [1024, T], BF16)
        self.zab = P.dscr("zab", [T, 32])
        self.mixT = P.dscr("mixT", [D, T], BF16)
        self.aT = P.dscr("aT", [D_FF, T], BF16)
        self.dbgout = {}
        for name, shape, dt in self.dbg:
            self.dbgout[name] = P.dout("dbg_" + name, shape, dt)

        self.S = Sch(nc, es)
        self.cst = P.sb(es, "cst", [128, 21, 128], F32)
        self.cstb = P.sb(es, "cstb", [128, 21, 128], BF16)
        self.mod = {l: P.sb(es, "mod%d" % l, [128, 96, 2], F32) for l in self.layers}
        self.gm1 = {l: P.sb(es, "gm1_%d" % l, [128, 2, KC, 2], F32) for l in self.layers}

        self.phase_init()
        for l in self.layers:
            self.phase_mod(l)
        for l in self.layers:
            self.phase_norm_inproj(l)
            if self.upto == "inproj":
                break
            if not getattr(self, "skip_tail", False):
                self.phase_dense_tail(l)
        if self.upto == "all":
            self.phase_final()
        return nc

    def phase_init(self):
        nc, S = self.nc, self.S
        with ExitStack() as es, nc.Block() as block:
            S.dma("sp", lambda e: e.dma_start(out=self.cst[:], in_=self.consts[:, :, :]), writes=["cst"])
            S.op("dve", lambda e: e.tensor_copy(out=self.cstb[:], in_=self.cst[:]), reads=["cst"], writes=["cstb"])
            S.op("pool", lambda e: e.memset(self.epsc[:], EPS), writes=["epsc"])
            for i in range(4):
                S.dma("sp", lambda e, i=i: e.dma_start(out=self.xT[i * 512:(i + 1) * 512, :],
                                                      in_=self.xT_in[i * 512:(i + 1) * 512, :]))
            S.emit(block)

    def phase_mod(self, l):
        nc, S = self.nc, self.S
        with ExitStack() as es, nc.Block() as block:
            c2 = self.sb(es, "c2", [128, KC, 2], F32)
            s2 = self.sb(es, "s2", [128, KC, 2], BF16)
            bt = self.sb(es, "bt", [128, 96], F32)
            nr = self.sb(es, "nr", [128, 2, KC], F32)
            NS = 3
            wsl = [self.sb(es, "wad", [128, KC, 512], BF16) for _ in range(NS)]
            pm = self.ps(es, "pm", [128, 96, 2], F32)
            S.dma("sp", lambda e: e.dma_start(out=c2[:], in_=self.c2[:, :, :]), writes=["c2"])
            S.dma("sp", lambda e: e.dma_start(out=bt[:], in_=self.b_ada[l][:, :]), writes=["bt"])
            S.dma("sp", lambda e: e.dma_start(out=nr[:], in_=self.nrm[l][:, :, :]), writes=["nr"])
            S.op("act", lambda e: e.activation(out=s2[:], in_=c2[:], func=AF.Silu), reads=["c2"], writes=["s2"])
            wv = self.w_ada[l].rearrange("(kc p) n -> p kc n", p=128)
            for s in range(24):
                buf = wsl[s % NS]
                key = "wad%d" % (s % NS)
                S.dma("pool", lambda e, s=s, buf=buf: e.dma_start(out=buf[:], in_=wv[:, :, s * 512:(s + 1) * 512]),
                      writes=[key])
                for j in range(4):
                    cc = s * 4 + j
                    for kc in range(KC):
                        S.op("pe", lambda e, cc=cc, kc=kc, j=j, buf=buf: e.matmul(
                            pm[:, cc, :], buf[:, kc, j * 128:(j + 1) * 128], s2[:, kc, :],
                            start=(kc == 0), stop=(kc == KC - 1)), reads=[key, "s2"], writes=["pm"])
            mod = self.mod[l]
            S.op("dve", lambda e: e.tensor_tensor(out=mod[:], in0=pm[:], in1=bt[:].unsqueeze(2).to_broadcast([128, 96, 2]),
                                                  op=ALU.add), reads=["pm", "bt"], writes=["mod"])
            gm = self.gm1[l]
            for n, which in ((0, 1), (1, 4)):
                S.op("dve", lambda e, n=n, which=which: e.tensor_scalar(
                    out=gm[:, n, :, :], in0=mod[:, which * 16:(which + 1) * 16, :], scalar1=1.0, scalar2=None,
                    op0=ALU.add), reads=["mod"], writes=["gm%d" % n])
                S.op("dve", lambda e, n=n: e.tensor_tensor(
                    out=gm[:, n, :, :], in0=gm[:, n, :, :], in1=nr[:, n, :].unsqueeze(2).to_broadcast([128, KC, 2]),
                    op=ALU.mult), reads=["gm%d" % n, "nr"], writes=["gm%d" % n])
            S.emit(block)

    def norm_to_hT(self, es, l, n, hT, shift_which):
        nc, S = self.nc, self.S
        xb = [self.sb(es, "xb", [128, KC, 256], F32) for _ in range(2)]
        sq = [self.sb(es, "sq", [128, 256], F32) for _ in range(8)]
        rs = [self.sb(es, "rs", [128, 256], F32) for _ in range(2)]
        tmp = [self.sb(es, "tmp", [128, 256], F32) for _ in range(6)]
        pst = [self.ps(es, "pst", [128, 512], F32) for _ in range(2)]
        xv = self.xT.rearrange("(c p) t -> p c t", p=128)
        gm, mod = self.gm1[l], self.mod[l]
        ones = self.cst[:, 1, :]

        def nb(k, b):
            j = 1 if b == 0 else 0
            X, R, PS = xb[k], rs[k], pst[k][:, 0:256]
            kx, kr, kp = "xb%d" % k, "rs%d" % k, "pst%d" % k
            yield S.dma("sp", lambda e: e.dma_start(out=X[:], in_=xv[:, :, b * 256:(b + 1) * 256]), writes=[kx])
            for c in range(KC):
                Q, kq = sq[k * 4 + c % 4], "sq%d_%d" % (k, c % 4)
                yield S.op("act", lambda e, Q=Q, c=c: e.activation(out=Q[:], in_=X[:, c, :], func=AF.Square),
                           reads=[kx], writes=[kq])
                yield S.op("pe", lambda e, Q=Q, c=c: e.matmul(PS, ones, Q[:], start=(c == 0), stop=(c == KC - 1)),
                           reads=[kq, "cst"], writes=[kp])
            yield S.op("act", lambda e: e.activation(out=R[:], in_=PS, func=AF.Ln, scale=1.0 / D, bias=self.epsc[:, 0:1]),
                       reads=[kp], writes=[kr])
            yield S.op("act", lambda e: e.activation(out=R[:], in_=R[:], func=AF.Exp, scale=-0.5), reads=[kr], writes=[kr])
            for c in range(KC):
                tb, kt = tmp[k * 3 + c % 3], "tmp%d_%d" % (k, c % 3)
                yield S.op("dve", lambda e, tb=tb, c=c: e.tensor_tensor(out=tb[:], in0=X[:, c, :], in1=R[:], op=ALU.mult),
                           reads=[kx, kr], writes=[kt])
                yield S.op("act", lambda e, tb=tb, c=c: e.activation(
                    out=hT[:, c, b * 256:(b + 1) * 256], in_=tb[:], func=AF.Identity,
                    scale=gm[:, n, c, j:j + 1], bias=mod[:, shift_which * 16 + c, j:j + 1]),
                    reads=[kt, "gm%d" % n, "mod"], writes=["hT%d" % b])

        nblk = T // 256
        slots = [list(range(nblk))[k::2] for k in range(2)]
        g, nx, rnd = [None, None], [0, 0], 0
        while True:
            alive = False
            for k in range(2):
                if g[k] is None and nx[k] < len(slots[k]) and rnd >= k * 34:
                    g[k] = nb(k, slots[k][nx[k]])
                    nx[k] += 1
                if g[k] is not None:
                    alive = True
                    try:
                        next(g[k])
                    except StopIteration:
                        g[k] = None
                elif nx[k] < len(slots[k]):
                    alive = True
            rnd += 1
            if not alive:
                break

    def gemm(self, es, W, ncols, act, actkey, epi, nkc=KC, wname="w", tblk=TBLK):
        S = self.S
        NS = 3
        if not hasattr(self, "_gc"):
            self._gc = {}
        k1, k2 = (id(es), wname, nkc), (id(es), "ps")
        if k1 not in self._gc:
            self._gc[k1] = [self.sb(es, wname, [128, nkc, 256], BF16) for _ in range(NS)]
        if k2 not in self._gc:
            self._gc[k2] = [self.ps(es, "pg", [128, 512], F32) for _ in range(4)]
        wsl, psb = self._gc[k1], self._gc[k2]
        Wv = W.rearrange("(kc p) n -> p kc n", p=128)
        nslab = (ncols + 255) // 256
        pi = 0
        KG = 4
        for s in range(nslab):
            c0 = s * 256
            cw = min(256, ncols - c0)
            buf = wsl[s % NS]
            for kg in range(nkc // KG):
                S.dma("pool", lambda e, buf=buf, kg=kg, c0=c0, cw=cw: e.dma_start(
                    out=buf[:, kg * KG:(kg + 1) * KG, 0:cw], in_=Wv[:, kg * KG:(kg + 1) * KG, c0:c0 + cw]),
                    writes=["%s%d_%d" % (wname, s % NS, kg)])
            for g in range((cw + 127) // 128):
                m = min(128, cw - g * 128)
                for bi, (t0, tn) in enumerate(tblk):
                    PS = psb[pi % 4]
                    pk = "pg%d" % (pi % 4)
                    pi += 1
                    for kc in range(nkc):
                        S.op("pe", lambda e, PS=PS, buf=buf, kc=kc, g=g, m=m, t0=t0, tn=tn: e.matmul(
                            PS[0:m, 0:tn], buf[:, kc, g * 128:g * 128 + m], act[:, kc, t0:t0 + tn],
                            start=(kc == 0), stop=(kc == nkc - 1)),
                            reads=["%s%d_%d" % (wname, s % NS, kc // KG), (actkey(kc) if callable(actkey) else actkey)], writes=[pk])
                    epi(c0 + g * 128, m, bi, t0, tn, PS, pk)

    def phase_norm_inproj(self, l):
        nc, S = self.nc, self.S
        with ExitStack() as es:
            hT = self.sb(es, "hT", [128, KC, T], BF16)
            with ExitStack() as es2, nc.Block() as block:
                self.norm_to_hT(es2, l, 0, hT, 0)
                S.emit(block)
            with ExitStack() as es2, nc.Block() as block:
                stg = [self.sb(es2, "stg", [128, 512], F32) for _ in range(4)]
                cnt = [0]

                def epi(c0, m, bi, t0, tn, PS, pk):
                    i = cnt[0] % 4
                    cnt[0] += 1
                    st = stg[i]
                    S.op("act", lambda e: e.activation(out=st[0:m, 0:tn], in_=PS[0:m, 0:tn], func=AF.Copy),
                         reads=[pk], writes=["stg%d" % i])
                    S.dma("sp", lambda e: e.dma_start(out=self.zT[c0:c0 + m, t0:t0 + tn], in_=st[0:m, 0:tn]),
                          reads=["stg%d" % i])
                self.gemm(es2, self.w_in[l], D_IN, hT, "hT", epi, wname="win")
                S.emit(block)

    def resid_epi(self, es, l, which):
        S = self.S
        NXL = 6
        xl = [self.sb(es, "xl", [128, 512], F32) for _ in range(NXL)]
        cnt = [0]
        mod = self.mod[l]

        def epi(c0, m, bi, t0, tn, PS, pk):
            i = cnt[0] % NXL
            cnt[0] += 1
            xb = xl[i]
            j = 1 if t0 < LC else 0
            c = c0 // 128
            xk = "x_%d_%d" % (c, bi)
            S.dma("act", lambda e: e.dma_start(out=xb[:, 0:tn], in_=self.xT[c0:c0 + 128, t0:t0 + tn]),
                  reads=[xk], writes=["xl%d" % i])
            S.op("dve", lambda e: e.scalar_tensor_tensor(
                out=xb[:, 0:tn], in0=PS[:, 0:tn], scalar=mod[:, which * 16 + c, j:j + 1], in1=xb[:, 0:tn],
                op0=ALU.mult, op1=ALU.add), reads=[pk, "xl%d" % i, "mod"], writes=["xl%d" % i])
            S.dma("sp", lambda e: e.dma_start(out=self.xT[c0:c0 + 128, t0:t0 + tn], in_=xb[:, 0:tn]),
                  reads=["xl%d" % i], writes=[xk])
        return epi

    def phase_dense_tail(self, l):
        nc, S = self.nc, self.S
        with ExitStack() as es, nc.Block() as block:
            mx = self.sb(es, "mx", [128, KC, T], BF16)
            mv = self.mixT.rearrange("(c p) t -> p c t", p=128)
            for c4 in range(4):
                S.dma("sp", lambda e, c4=c4: e.dma_start(out=mx[:, c4 * 4:(c4 + 1) * 4, :], in_=mv[:, c4 * 4:(c4 + 1) * 4, :]),
                      writes=["mx%d" % c4])
            S.op("pool", lambda e: e.memset(self.junk[:], 0.0), reads=["mx0", "mx1", "mx2", "mx3"], writes=["mx"])
            self.gemm(es, self.w_out[l], D, mx, "mx", self.resid_epi(es, l, 2), wname="wout")
            S.emit(block)
        with ExitStack() as es:
            hT = self.sb(es, "hnT", [128, KC, T], BF16)
            with ExitStack() as es2, nc.Block() as block:
                self.norm_to_hT(es2, l, 1, hT, 3)
                S.emit(block)
            with ExitStack() as es2, nc.Block() as block:
                NK2 = 8
                aT = self.sb(es2, "aT", [128, NK2, T], BF16)
                rl = [self.sb(es2, "rl", [128, 512], F32) for _ in range(3)]
                rc = [0]
                repi = self.resid_epi(es2, l, 5)
                for q in range(D_FF // (NK2 * 128)):
                    def epi1(c0, m, bi, t0, tn, PS, pk):
                        i = rc[0] % 3
                        rc[0] += 1
                        r = rl[i]
                        S.op("act", lambda e: e.activation(out=r[:, 0:tn], in_=PS[:, 0:tn], func=AF.Relu),
                             reads=[pk], writes=["rl%d" % i])
                        S.op("pool", lambda e: e.tensor_tensor(out=aT[:, c0 // 128, t0:t0 + tn], in0=r[:, 0:tn],
                                                               in1=r[:, 0:tn], op=ALU.mult),
                             reads=["rl%d" % i], writes=["aT%d" % (c0 // 128)])
                    self.gemm(es2, self.w_ff1[l][:, q * NK2 * 128:(q + 1) * NK2 * 128], NK2 * 128, hT, "hnT", epi1,
                              wname="wf1")
                    self.gemm(es2, self.w_ff2[l][q * NK2 * 128:(q + 1) * NK2 * 128, :], D, aT, (lambda kc: "aT%d" % kc), repi, nkc=NK2,
                              wname="wf2")
                S.emit(block)

    def phase_final(self):
        nc, S = self.nc, self.S
        with ExitStack() as es, nc.Block() as block:
            xb = [self.sb(es, "fxb", [128, KC, 256], F32) for _ in range(2)]
            sq = [self.sb(es, "fsq", [128, 256], F32) for _ in range(2)]
            rs = [self.sb(es, "frs", [128, 256], F32) for _ in range(2)]
            ob = [self.sb(es, "fob", [128, KC, 256], F32) for _ in range(2)]
            nf = self.sb(es, "nf", [128, KC], F32)
            pst = [self.ps(es, "fps", [128, 256], F32) for _ in range(2)]
            xv = self.xT.rearrange("(c p) t -> p c t", p=128)
            ov = self.outT.rearrange("(c p) t -> p c t", p=128)
            ones = self.cst[:, 1, :]
            S.dma("sp", lambda e: e.dma_start(out=nf[:], in_=self.nfin[:, :]), writes=["nf"])
            for b in range(L // 256):
                X, Q, R, PS, O = xb[b % 2], sq[b % 2], rs[b % 2], pst[b % 2], ob[b % 2]
                kx, kq, kr, kp, ko = ["%s%d" % (n, b % 2) for n in ("fx", "fq", "fr", "fp", "fo")]
                S.dma("sp", lambda e, X=X, b=b: e.dma_start(out=X[:], in_=xv[:, :, LC + b * 256:LC + (b + 1) * 256]),
                      writes=[kx])
                for c in range(KC):
                    S.op("act", lambda e, X=X, Q=Q, c=c: e.activation(out=Q[:], in_=X[:, c, :], func=AF.Square),
                         reads=[kx], writes=[kq])
                    S.op("pe", lambda e, Q=Q, PS=PS, c=c: e.matmul(PS[:], ones, Q[:], start=(c == 0), stop=(c == KC - 1)),
                         reads=[kq], writes=[kp])
                S.op("act", lambda e, R=R, PS=PS: e.activation(out=R[:], in_=PS[:], func=AF.Ln, scale=1.0 / D,
                                                              bias=self.epsc[:, 0:1]), reads=[kp], writes=[kr])
                S.op("act", lambda e, R=R: e.activation(out=R[:], in_=R[:], func=AF.Exp, scale=-0.5), reads=[kr], writes=[kr])
                for c in range(KC):
                    S.op("dve", lambda e, X=X, R=R, O=O, c=c: e.scalar_tensor_tensor(
                        out=O[:, c, :], in0=X[:, c, :], scalar=nf[:, c:c + 1], in1=R[:], op0=ALU.mult, op1=ALU.mult),
                        reads=[kx, kr, "nf"], writes=[ko])
                S.dma("sp", lambda e, O=O, b=b: e.dma_start(out=ov[:, :, b * 256:(b + 1) * 256], in_=O[:]), reads=[ko])
            S.emit(block)
    def join(self, keys, newkey):
        self.S.op("pool", lambda e: e.memset(self.junk[:], 0.0), reads=list(keys), writes=[newkey])

    def phase_attn(self, l):
        nc, S = self.nc, self.S
        scale = 128 ** -0.5
        cstb = self.cstb
        with ExitStack() as es, nc.Block() as block:
            rope = self.sb(es, "rope", [128, 2, L], F32)
            snk = self.sb(es, "snk", [128, 8], F32)
            esk = self.sb(es, "esk", [128, 8], F32)
            raw = [self.sb(es, "raw", [128, T], F32) for _ in range(2)]
            rb = self.sb(es, "rb", [128, L], BF16)
            t1 = [self.sb(es, "t1", [128, 512], F32) for _ in range(2)]
            t2 = [self.sb(es, "t2", [128, 512], F32) for _ in range(2)]
            kT = self.sb(es, "kT", [128, T], BF16)
            qT = self.sb(es, "qT", [128, T], BF16)
            vT = self.sb(es, "vT", [128, T], BF16)
            V = self.sb(es, "V", [128, NT, 128], BF16)
            PTc = self.sb(es, "PTc", [128, 2, T], BF16)
            PTl = self.sb(es, "PTl", [128, 16, 3, 128], BF16)
            rc = [self.sb(es, "rc", [128, 128], F32) for _ in range(2)]
            ost = [self.sb(es, "ost", [128, 128], BF16) for _ in range(3)]
            psS = [self.ps(es, "psS", [128, 512], F32) for _ in range(2)]
            psO = [self.ps(es, "psO", [128, 128], F32) for _ in range(2)]
            psD = [self.ps(es, "psD", [128, 128], F32) for _ in range(2)]
            psT = [self.ps(es, "psT", [128, 128], BF16) for _ in range(2)]
            S.dma("sp", lambda e: e.dma_start(out=rope[:], in_=self.ropeT[:, :, :]), writes=["rope"])
            S.dma("sp", lambda e: e.dma_start(out=snk[:], in_=self.sink[l][:, :]), writes=["snk"])
            S.op("act", lambda e: e.activation(out=esk[:], in_=snk[:], func=AF.Exp), reads=["snk"], writes=["esk"])
            cn = {"raw": 0, "ps": 0, "t": 0, "o": 0, "ost": 0}

            def load_rope(row0, dst, dkey):
                ri = cn["raw"] % 2
                cn["raw"] += 1
                R, rk = raw[ri], "raw%d" % ri
                S.dma("sp", lambda e: e.dma_start(out=R[:], in_=self.zT[row0:row0 + 128, :]), writes=[rk])
                S.op("act", lambda e: e.activation(out=dst[:, 0:LC], in_=R[:, 0:LC], func=AF.Copy), reads=[rk],
                     writes=[dkey + "c"])
                S.op("act", lambda e: e.activation(out=rb[:], in_=R[:, LC:T], func=AF.Copy), reads=[rk], writes=["rb"])
                for blk in range(4):
                    a, b_ = blk * 512, (blk + 1) * 512
                    pi = cn["ps"] % 2
                    cn["ps"] += 1
                    ti = cn["t"] % 2
                    cn["t"] += 1
                    PS, pk = psS[pi], "psS%d" % pi
                    S.op("pe", lambda e, PS=PS, a=a, b_=b_: e.matmul(PS[:, :], cstb[:, 2, :], rb[:, a:b_], start=True, stop=True),
                         reads=["rb", "cstb"], writes=[pk])
                    S.op("dve", lambda e, a=a, b_=b_, ti=ti: e.tensor_tensor(out=t1[ti][:], in0=R[:, LC + a:LC + b_],
                                                                           in1=rope[:, 0, a:b_], op=ALU.mult),
                         reads=[rk, "rope"], writes=["t1%d" % ti])
                    S.op("dve", lambda e, PS=PS, a=a, b_=b_, ti=ti: e.tensor_tensor(out=t2[ti][:], in0=PS[:, :],
                                                                                  in1=rope[:, 1, a:b_], op=ALU.mult),
                         reads=[pk, "rope"], writes=["t2%d" % ti])
                    S.op("pool", lambda e, a=a, b_=b_, ti=ti: e.tensor_tensor(out=dst[:, LC + a:LC + b_], in0=t1[ti][:],
                                                                            in1=t2[ti][:], op=ALU.add),
                         reads=["t1%d" % ti, "t2%d" % ti], writes=[dkey + str(blk)])
                self.join([dkey + "c"] + [dkey + str(i) for i in range(4)], dkey)

            for g in range(2):
                load_rope(1024 + g * 128, kT, "kT")
                ri = cn["raw"] % 2
                cn["raw"] += 1
                R, rk = raw[ri], "raw%d" % ri
                S.dma("sp", lambda e, R=R, g=g: e.dma_start(out=R[:], in_=self.zT[1280 + g * 128:1408 + g * 128, :]), writes=[rk])
                S.op("act", lambda e, R=R: e.activation(out=vT[:], in_=R[:], func=AF.Copy), reads=[rk], writes=["vT"])
                for t in range(NT):
                    PT_, ptk = psT[t % 2], "psT%d" % (t % 2)
                    S.op("pe", lambda e, PT_=PT_, t=t: e.transpose(out=PT_[:, :], in_=vT[:, t * 128:(t + 1) * 128],
                                                                   identity=cstb[:, 0, :]), reads=["vT", "cstb"], writes=[ptk])
                    S.op("dve", lambda e, PT_=PT_, t=t: e.tensor_copy(out=V[:, t, :], in_=PT_[:, :]), reads=[ptk],
                         writes=["V%d" % t])
                self.join(["V%d" % t for t in range(NT)], "V")
                for hq in range(4):
                    h = g * 4 + hq
                    load_rope(h * 128, qT, "qT")
                    for ck in range(2):
                        for (t0, tn) in TBLK:
                            pi = cn["ps"] % 2
                            cn["ps"] += 1
                            PS, pk = psS[pi], "psS%d" % pi
                            S.op("pe", lambda e, PS=PS, ck=ck, t0=t0, tn=tn: e.matmul(
                                PS[:, 0:tn], kT[:, ck * 128:(ck + 1) * 128], qT[:, t0:t0 + tn], start=True, stop=True),
                                reads=["kT", "qT"], writes=[pk])
                            S.op("act", lambda e, PS=PS, ck=ck, t0=t0, tn=tn: e.activation(
                                out=PTc[:, ck, t0:t0 + tn], in_=PS[:, 0:tn], func=AF.Exp, scale=scale),
                                reads=[pk], writes=["PTc"])
                    for j in range(16):
                        lo, hi = max(j - 1, 0), min(j + 1, 15)
                        nq = hi - lo + 1
                        pi = cn["ps"] % 2
                        cn["ps"] += 1
                        PS, pk = psS[pi], "psS%d" % pi
                        S.op("pe", lambda e, PS=PS, j=j, lo=lo, nq=nq: e.matmul(
                            PS[:, 0:nq * 128], kT[:, LC + j * 128:LC + (j + 1) * 128],
                            qT[:, LC + lo * 128:LC + (lo + nq) * 128], start=True, stop=True),
                            reads=["kT", "qT"], writes=[pk])
                        r0 = lo - j + 1
                        S.op("act", lambda e, PS=PS, j=j, r0=r0, nq=nq: e.activation(
                            out=PTl[:, j, r0:r0 + nq, :], in_=PS[:, 0:nq * 128].rearrange("p (a b) -> p a b", b=128),
                            func=AF.Exp, scale=scale), reads=[pk], writes=["PTl"])
                        if j >= 1:
                            S.op("pool", lambda e, j=j: e.tensor_tensor(out=PTl[:, j, 0, :], in0=PTl[:, j, 0, :],
                                                                        in1=cstb[:, 3, :], op=ALU.mult),
                                 reads=["PTl", "cstb"], writes=["PTl"])
                        if j <= 14:
                            S.op("pool", lambda e, j=j: e.tensor_tensor(out=PTl[:, j, 2, :], in0=PTl[:, j, 2, :],
                                                                        in1=cstb[:, 4, :], op=ALU.mult),
                                 reads=["PTl", "cstb"], writes=["PTl"])
                    for qt in range(NT):
                        contrib = [(ck, PTc[:, ck, qt * 128:(qt + 1) * 128]) for ck in range(2)]
                        if qt >= 2:
                            n = qt - 2
                            for j in range(max(n - 1, 0), min(n + 1, 15) + 1):
                                contrib.append((2 + j, PTl[:, j, n - j + 1, :]))
                        oi = cn["o"] % 2
                        cn["o"] += 1
                        O, Dn = psO[oi], psD[oi]
                        for idx, (vt, rhs) in enumerate(contrib):
                            st_, sp_ = (idx == 0), (idx == len(contrib) - 1)
                            S.op("pe", lambda e, O=O, vt=vt, rhs=rhs, st_=st_, sp_=sp_: e.matmul(
                                O[:, :], V[:, vt, :], rhs, start=st_, stop=sp_), reads=["V", "PTc", "PTl"], writes=["psO%d" % oi])
                            S.op("pe", lambda e, Dn=Dn, rhs=rhs, st_=st_, sp_=sp_: e.matmul(
                                Dn[:, :], cstb[:, 1, :], rhs, start=st_, stop=sp_), reads=["PTc", "PTl"], writes=["psD%d" % oi])
                        RC = rc[oi]
                        S.op("dve", lambda e, RC=RC, Dn=Dn, h=h: e.tensor_scalar(out=RC[:], in0=Dn[:, :], scalar1=esk[:, h:h + 1],
                                                                             scalar2=None, op0=ALU.add),
                             reads=["psD%d" % oi, "esk"], writes=["rc%d" % oi])
                        S.op("dve", lambda e, RC=RC: e.reciprocal(out=RC[:], in_=RC[:]), reads=["rc%d" % oi], writes=["rc%d" % oi])
                        si = cn["ost"] % 3
                        cn["ost"] += 1
                        OS = ost[si]
                        S.op("dve", lambda e, OS=OS, O=O, RC=RC: e.tensor_tensor(out=OS[:], in0=O[:, :], in1=RC[:], op=ALU.mult),
                             reads=["psO%d" % oi, "rc%d" % oi], writes=["ost%d" % si])
                        S.dma("pool", lambda e, OS=OS, h=h, qt=qt: e.dma_start(
                            out=self.mixT[h * 128:(h + 1) * 128, qt * 128:(qt + 1) * 128], in_=OS[:]), reads=["ost%d" % si])
            S.emit(block)
    def phase_gdn(self, l):
        nc, S = self.nc, self.S
        cst, cstb = self.cst, self.cstb
        ORD = [list(range(NT)), [1, 0] + list(range(NT - 1, 1, -1))]
        with ExitStack() as eso:
            G = self.sb(eso, "G", [128, NT, 16], F32)
            BETA = self.sb(eso, "BETA", [128, NT, 16], F32)
            GC = self.sb(eso, "GC", [128, NT, 16], F32)
            TOT = self.sb(eso, "TOT", [128, NT, 16], F32)
            E = self.sb(eso, "E", [128, NT, 16], F32)
            Fd = self.sb(eso, "Fd", [128, NT, 16], F32)
            BE = self.sb(eso, "BE", [128, NT, 16], F32)
            GL = self.sb(eso, "GL", [128, NT, 16], F32)
            onec = self.sb(eso, "onec", [128, 1], F32)
            with ExitStack() as es, nc.Block() as block:
                abT = self.sb(es, "abT", [32, T], F32)
                abm = self.sb(es, "abm", [128, NT, 32], F32)
                al = self.sb(es, "al", [128, 16], F32)
                dtb = self.sb(es, "dtb", [128, 16], F32)
                cw = self.sb(es, "cw", [128, 24, 5], F32)
                xg = self.sb(es, "xg", [128, NT, 16], F32)
                Zp = [self.sb(es, "Zp", [128, T + 8], F32) for _ in range(3)]
                acc = [self.sb(es, "acc", [128, T], F32) for _ in range(3)]
                sl_ = [self.sb(es, "sl", [128, T], F32) for _ in range(3)]
                sq = [self.sb(es, "gsq", [128, 512], F32) for _ in range(6)]
                ri = [self.sb(es, "gri", [128, 512], F32) for _ in range(6)]
                ob = [self.sb(es, "gob", [128, T], BF16) for _ in range(3)]
                tm = [self.sb(es, "gtm", [128, NT, 128], BF16) for _ in range(3)]
                pabA = self.ps(es, "pabA", [128, 512], F32)
                pabB = self.ps(es, "pabB", [128, 512], F32)
                pgcb = self.ps(es, "pgc", [128, 512], F32)
                ptotb = self.ps(es, "ptot", [128, 512], F32)
                pgc = pgcb[:, 0:NT * 16].rearrange("p (n c) -> p n c", c=16)
                ptot = ptotb[:, 0:NT * 16].rearrange("p (n c) -> p n c", c=16)
                pss = [self.ps(es, "pss", [128, 512], F32) for _ in range(2)]
                ptrb = [self.ps(es, "ptr", [128, 1024], BF16) for _ in range(2)]
                ptr = [ptrb[0][:, 0:128], ptrb[1][:, 0:128]]
                S.op("pool", lambda e: e.memset(onec[:], 1.0), writes=["onec"])
                for z in Zp:
                    S.op("pool", lambda e, z=z: e.memset(z[:], 0.0), writes=["zpc%d" % Zp.index(z), "zpl%d" % Zp.index(z)])
                S.dma("sp", lambda e: e.dma_start(out=abT[:], in_=self.zT[5632:5664, :]), writes=["abT"])
                S.dma("sp", lambda e: e.dma_start(out=al[:], in_=self.alog[l][:, :]), writes=["al"])
                S.dma("sp", lambda e: e.dma_start(out=dtb[:], in_=self.dtb[l][:, :]), writes=["dtb"])
                S.dma("sp", lambda e: e.dma_start(out=cw[:], in_=self.convw[l][:, :, :]), writes=["cw"])
                for t in range(NT):
                    pb_ = pabA if t < 9 else pabB
                    S.op("pe", lambda e, t=t, pb_=pb_: e.transpose(out=pb_[:, (t % 9) * 32:(t % 9 + 1) * 32], in_=abT[:, t * 128:(t + 1) * 128],
                                                                   identity=cst[0:32, 0, 0:32]), reads=["abT", "cst"], writes=["pab"])
                S.op("dve", lambda e: e.tensor_copy(out=abm[:, 0:9, :], in_=pabA[:, 0:288].rearrange("p (n c) -> p n c", c=32)), reads=["pab"], writes=["abm0"])
                S.op("dve", lambda e: e.tensor_copy(out=abm[:, 9:18, :], in_=pabB[:, 0:288].rearrange("p (n c) -> p n c", c=32)), reads=["pab"], writes=["abm1"])
                self.join(["abm0", "abm1"], "abm")
                ab5 = abm[:].rearrange("p n (d t h) -> p n d t h", d=2, t=2)
                v4 = lambda a: a[:].rearrange("p n (d h) -> p n d h", d=2)
                b4 = lambda a: a[:].rearrange("p (d h) -> p d h", d=2).unsqueeze(1).to_broadcast([128, NT, 2, 8])
                S.op("dve", lambda e: e.tensor_tensor(out=v4(xg), in0=ab5[:, :, :, 0, :], in1=b4(dtb), op=ALU.add),
                     reads=["abm", "dtb"], writes=["xg"])
                S.op("act", lambda e: e.activation(out=xg[:], in_=xg[:], func=AF.Exp), reads=["xg"], writes=["xg"])
                S.op("act", lambda e: e.activation(out=xg[:], in_=xg[:], func=AF.Ln, bias=onec[:, 0:1]), reads=["xg", "onec"], writes=["xg"])
                S.op("act", lambda e: e.activation(out=al[:], in_=al[:], func=AF.Exp), reads=["al"], writes=["al"])
                S.op("dve", lambda e: e.tensor_scalar(out=al[:], in0=al[:], scalar1=-1.0, scalar2=None, op0=ALU.mult),
                     reads=["al"], writes=["al"])
                S.op("dve", lambda e: e.tensor_tensor(out=v4(G), in0=v4(xg), in1=b4(al), op=ALU.mult), reads=["xg", "al"], writes=["G"])
                S.op("act", lambda e: e.activation(out=v4(BETA), in_=ab5[:, :, :, 1, :], func=AF.Sigmoid), reads=["abm"], writes=["BETA"])
                for t in range(NT):
                    for d in range(2):
                        S.op("pe", lambda e, t=t, d=d: e.matmul(pgc[:, t, d * 8:(d + 1) * 8], cst[:, 3 + d, :],
                                                                G[:, t, d * 8:(d + 1) * 8], start=True, stop=True),
                             reads=["G", "cst"], writes=["pgc"])
                    S.op("pe", lambda e, t=t: e.matmul(ptot[:, t, :], cst[:, 1, :], G[:, t, :], start=True, stop=True),
                         reads=["G", "cst"], writes=["ptot"])
                S.op("dve", lambda e: e.tensor_copy(out=GC[:], in_=pgc), reads=["pgc"], writes=["GC"])
                S.op("dve", lambda e: e.tensor_copy(out=TOT[:], in_=ptot), reads=["ptot"], writes=["TOT"])
                S.op("act", lambda e: e.activation(out=E[:], in_=GC[:], func=AF.Exp), reads=["GC"], writes=["E"])
                S.op("dve", lambda e: e.tensor_tensor(out=Fd[:], in0=TOT[:], in1=GC[:], op=ALU.subtract), reads=["TOT", "GC"], writes=["Fd"])
                S.op("act", lambda e: e.activation(out=Fd[:], in_=Fd[:], func=AF.Exp), reads=["Fd"], writes=["Fd"])
                S.op("act", lambda e: e.activation(out=GL[:], in_=TOT[:], func=AF.Exp), reads=["TOT"], writes=["GL"])
                S.op("dve", lambda e: e.tensor_tensor(out=BE[:], in0=BETA[:], in1=E[:], op=ALU.mult), reads=["BETA", "E"], writes=["BE"])
                NPS = 2

                def pbody(i, h, which):
                    row0 = 1536 + which * 1024 + h * 128
                    ch = which * 8 + h
                    z, A_, SL, OB, TM = Zp[i], acc[i], sl_[i], ob[i], tm[i]
                    yield S.dma("sp", lambda e, z=z, row0=row0: e.dma_start(out=z[:, 2:2 + LC], in_=self.zT[row0:row0 + 128, 0:LC]),
                          writes=["zpc%d" % i])
                    yield S.dma("sp", lambda e, z=z, row0=row0: e.dma_start(out=z[:, 262:262 + L], in_=self.zT[row0:row0 + 128, LC:T]),
                          writes=["zpl%d" % i])
                    for (o0, n, zo, zk) in ((0, LC, 0, "zpc%d" % i), (LC, L, 260, "zpl%d" % i)):
                        yield S.op("dve", lambda e, z=z, A_=A_, o0=o0, n=n, zo=zo, ch=ch: e.tensor_scalar(
                            out=A_[:, o0:o0 + n], in0=z[:, zo:zo + n], scalar1=cw[:, ch, 0:1], scalar2=None, op0=ALU.mult),
                            reads=[zk, "cw"], writes=["acc%d" % i])
                        for j in range(1, 5):
                            yield S.op("dve", lambda e, z=z, A_=A_, o0=o0, n=n, zo=zo, ch=ch, j=j: e.scalar_tensor_tensor(
                                out=A_[:, o0:o0 + n], in0=z[:, zo + j:zo + j + n], scalar=cw[:, ch, j:j + 1],
                                in1=A_[:, o0:o0 + n], op0=ALU.mult, op1=ALU.add), reads=[zk, "cw", "acc%d" % i], writes=["acc%d" % i])
                    yield S.op("act", lambda e, A_=A_, SL=SL: e.activation(out=SL[:], in_=A_[:], func=AF.Silu),
                         reads=["acc%d" % i], writes=["sl%d" % i])
                    if which < 2:
                        qs = (128 ** -0.5) if which == 0 else 1.0
                        for bi, (t0, tn) in enumerate(TBLK):
                            bb = bi % 2
                            yield S.op("act", lambda e, SL=SL, t0=t0, tn=tn, bb=bb: e.activation(out=sq[i * 2 + bb][:, 0:tn], in_=SL[:, t0:t0 + tn],
                                                                                         func=AF.Square), reads=["sl%d" % i], writes=["gsq%d_%d" % (i, bb)])
                            yield S.op("pe", lambda e, t0=t0, tn=tn, bb=bb: e.matmul(pss[i][:, 0:tn], cst[:, 1, :], sq[i * 2 + bb][:, 0:tn],
                                                                             start=True, stop=True), reads=["gsq%d_%d" % (i, bb)], writes=["pss%d" % i])
                            yield S.op("act", lambda e, tn=tn, bb=bb: e.activation(out=ri[i * 2 + bb][:, 0:tn], in_=pss[i][:, 0:tn], func=AF.Ln,
                                                                           bias=self.epsc[:, 0:1]), reads=["pss%d" % i], writes=["gri%d_%d" % (i, bb)])
                            yield S.op("act", lambda e, tn=tn, bb=bb: e.activation(out=ri[i * 2 + bb][:, 0:tn], in_=ri[i * 2 + bb][:, 0:tn], func=AF.Exp, scale=-0.5),
                                 reads=["gri%d_%d" % (i, bb)], writes=["gri%d_%d" % (i, bb)])
                            yield S.op("dve", lambda e, SL=SL, OB=OB, t0=t0, tn=tn, bb=bb, qs=qs: e.scalar_tensor_tensor(
                                out=OB[:, t0:t0 + tn], in0=SL[:, t0:t0 + tn], scalar=qs, in1=ri[i * 2 + bb][:, 0:tn],
                                op0=ALU.mult, op1=ALU.mult), reads=["sl%d" % i, "gri%d_%d" % (i, bb)], writes=["gob%d" % i])
                        dst = self.gq if which == 0 else self.gk
                        yield S.dma("act", lambda e, OB=OB, dst=dst, h=h: e.dma_start(out=dst[h, :, :], in_=OB[:]), reads=["gob%d" % i])
                    else:
                        yield S.op("act", lambda e, SL=SL, OB=OB: e.activation(out=OB[:], in_=SL[:], func=AF.Copy),
                             reads=["sl%d" % i], writes=["gob%d" % i])
                    if which >= 1:
                        for t in range(NT):
                            yield S.op("pe", lambda e, OB=OB, t=t: e.transpose(out=ptr[i], in_=OB[:, t * 128:(t + 1) * 128],
                                                                         identity=cstb[:, 0, :]), reads=["gob%d" % i], writes=["ptr%d" % i])
                            yield S.op("dve", lambda e, TM=TM, t=t: e.tensor_copy(out=TM[:, t, :], in_=ptr[i]),
                                 reads=["ptr%d" % i], writes=["gtm%d" % i])
                        dst = self.gkm if which == 1 else self.gvm
                        yield S.dma("act", lambda e, TM=TM, dst=dst, h=h: e.dma_start(
                            out=dst[h].rearrange("(n p) d -> p n d", p=128), in_=TM[:]), reads=["gtm%d" % i])

                items = [(h_, w_) for h_ in range(8) for w_ in range(3)]
                pslots = [items[k::NPS] for k in range(NPS)]
                pg = [None] * NPS
                pn = [0] * NPS
                rnd = 0
                while True:
                    alive = False
                    for k in range(NPS):
                        if pg[k] is None and pn[k] < len(pslots[k]) and rnd >= k * 25:
                            pg[k] = pbody(k, *pslots[k][pn[k]])
                            pn[k] += 1
                        if pg[k] is not None:
                            alive = True
                            try:
                                next(pg[k])
                            except StopIteration:
                                pg[k] = None
                        elif pn[k] < len(pslots[k]):
                            alive = True
                    rnd += 1
                    if not alive:
                        break
                S.emit(block)
            if getattr(self, "gdn_stop", 9) < 2:
                return
            with ExitStack() as es, nc.Block() as block:
                NSET = 8
                f32t = lambda n: [self.sb(es, n, [128, 128], F32) for _ in range(NSET)]
                b16t = lambda n: [self.sb(es, n, [128, 128], BF16) for _ in range(NSET)]
                rhsG = [self.sb(es, "rhsG", [128, 256], F32) for _ in range(NSET)]
                t1, Ex, t2, Ey, EB, Mx, XA, My, QA, usb, osb, gcs = [f32t(n) for n in
                    ("t1", "Ex", "t2", "Ey", "EB", "Mx", "XA", "My", "QA", "usb", "osb", "gcs")]
                QKm, qdec, TT, bv, kbe, kdec, wsb, vnew, kt, qt, km, vm = [b16t(n) for n in
                    ("QKm", "qdec", "TT", "bv", "kbe", "kdec", "wsb", "vnew", "kt", "qt", "km", "vm")]
                Pm = [[self.sb(es, "Pm", [128, 128], F32) for _ in range(2)] for _ in range(NSET)]
                Xm = [[self.sb(es, "Xm", [128, 128], F32) for _ in range(2)] for _ in range(NSET)]
                Yb = [[self.sb(es, "Yb", [128, 128], F32) for _ in range(2)] for _ in range(NSET)]
                Yo = [[self.sb(es, "Yo", [128, 128], F32) for _ in range(3)] for _ in range(NSET)]
                XF, YF, Qs, M1s = [f32t(n) for n in ("XF", "YF", "Qs", "M1s")]
                Sf = [[self.sb(es, "Sf", [128, 128], F32) for _ in range(2)] for _ in range(8)]
                Sb = [[self.sb(es, "Sb", [128, 128], BF16) for _ in range(2)] for _ in range(8)]
                bk = [[self.ps(es, "bk", [128, 512], F32) for _ in range(1)] for _ in range(NSET)]
                for h in range(8):
                    for d in range(2):
                        S.op("pool", lambda e, h=h, d=d: e.memset(Sf[h][d][:], 0.0), writes=["Sf%d%d" % (h, d)])
                        S.op("pool", lambda e, h=h, d=d: e.memset(Sb[h][d][:], 0.0), writes=["Sb%d%d" % (h, d)])
                def body(i, h, d, t):
                    K = lambda n: "%s%d" % (n, i)
                    col = d * 8 + h
                    gcol, bcol, gccol = G[:, t, col:col + 1], BETA[:, t, col:col + 1], GC[:, t, col:col + 1]
                    becol, fcol, glcol = BE[:, t, col:col + 1], Fd[:, t, col:col + 1], GL[:, t, col:col + 1]
                    tsl = slice(t * 128, (t + 1) * 128)
                    Bk = bk[i][0]
                    A, B, C = Bk[:, 0:128], Bk[:, 128:256], Bk[:, 256:512]
                    OP, M1, OX, Yp = Bk[:, 0:128], Bk[:, 128:256], Bk[:, 256:384], Bk[:, 384:512]
                    Sp, U, W, O2 = Bk[:, 0:128], Bk[:, 128:256], Bk[:, 256:384], Bk[:, 384:512]
                    Dp = Bk[:, 128:256]
                    yield S.dma("sp", lambda e, i=i, h=h, tsl=tsl: e.dma_start(out=kt[i][:], in_=self.gk[h, :, tsl]), writes=[K("kt")])
                    yield S.dma("sp", lambda e, i=i, h=h, tsl=tsl: e.dma_start(out=qt[i][:], in_=self.gq[h, :, tsl]), writes=[K("qt")])
                    yield S.dma("sp", lambda e, i=i, h=h, tsl=tsl: e.dma_start(out=km[i][:], in_=self.gkm[h, tsl, :]), writes=[K("km")])
                    yield S.dma("sp", lambda e, i=i, h=h, tsl=tsl: e.dma_start(out=vm[i][:], in_=self.gvm[h, tsl, :]), writes=[K("vm")])
                    yield S.op("pe", lambda e, A=A, i=i: e.matmul(A, kt[i][:], kt[i][:], start=True, stop=True), reads=[K("kt")], writes=[K("bank0"), K("A")])
                    yield S.op("pe", lambda e, B=B, i=i: e.matmul(B, kt[i][:], qt[i][:], start=True, stop=True), reads=[K("kt"), K("qt")], writes=[K("bank0"), K("B")])
                    yield S.op("pool", lambda e, i=i, d=d, gcol=gcol: e.tensor_scalar(out=rhsG[i][:, 0:128], in0=cst[:, 3 + d, :], scalar1=gcol,
                                                                             scalar2=None, op0=ALU.mult), reads=["G"], writes=[K("rhsGa")])
                    yield S.op("act", lambda e, i=i, bcol=bcol: e.activation(out=rhsG[i][:, 128:256], in_=cst[:, 0, :], func=AF.Copy, scale=bcol),
                               reads=["BETA"], writes=[K("rhsGb")])
                    yield S.op("pe", lambda e, C=C, i=i: e.matmul(C, cst[:, 1, :], rhsG[i][:], start=True, stop=True), reads=[K("rhsGa"), K("rhsGb")],
                               writes=[K("bank0"), K("C")])
                    yield S.op("dve", lambda e, C=C, i=i, gccol=gccol: e.tensor_scalar(out=t1[i][:], in0=C[:, 0:128], scalar1=gccol, scalar2=0.0,
                                                                               op0=ALU.subtract, op1=ALU.min), reads=[K("C")], writes=[K("bank0"), K("t1")])
                    yield S.op("act", lambda e, i=i: e.activation(out=Ex[i][:], in_=t1[i][:], func=AF.Exp), reads=[K("t1")], writes=[K("Ex")])
                    yield S.op("dve", lambda e, C=C, i=i, gccol=gccol: e.tensor_scalar(out=t2[i][:], in0=C[:, 0:128], scalar1=gccol, scalar2=0.0,
                                                                               op0=ALU.subtract, op1=ALU.max), reads=[K("C")], writes=[K("bank0"), K("t2")])
                    yield S.op("act", lambda e, i=i: e.activation(out=Ey[i][:], in_=t2[i][:], func=AF.Exp, scale=-1.0), reads=[K("t2")], writes=[K("Ey")])
                    yield S.op("dve", lambda e, C=C, i=i: e.tensor_copy(out=gcs[i][:], in_=C[:, 0:128]), reads=[K("C")], writes=[K("bank0"), K("gcs")])
                    yield S.op("act", lambda e, i=i: e.activation(out=EB[i][:], in_=gcs[i][:], func=AF.Exp), reads=[K("gcs")], writes=[K("EB")])
                    yield S.op("dve", lambda e, C=C, i=i, d=d: e.tensor_tensor(out=Mx[i][:], in0=C[:, 128:256], in1=cst[:, 11 + d, :], op=ALU.mult),
                               reads=[K("C")], writes=[K("bank0"), K("Mx")])
                    yield S.op("dve", lambda e, A=A, i=i: e.tensor_tensor(out=XA[i][:], in0=A, in1=Ex[i][:], op=ALU.mult), reads=[K("A"), K("Ex")],
                               writes=[K("bank0"), K("XA")])
                    yield S.op("pool", lambda e, i=i: e.tensor_tensor(out=Xm[i][0][:], in0=XA[i][:], in1=Mx[i][:], op=ALU.mult),
                               reads=[K("XA"), K("Mx")], writes=[K("X0")])
                    yield S.op("pool", lambda e, i=i: e.tensor_tensor(out=Pm[i][1][:], in0=Xm[i][0][:], in1=cst[:, 0, :], op=ALU.add),
                               reads=[K("X0")], writes=[K("P1")])
                    yield S.op("dve", lambda e, A=A, i=i: e.tensor_tensor(out=YF[i][:], in0=A, in1=Ey[i][:], op=ALU.mult), reads=[K("A"), K("Ey")],
                               writes=[K("bank0"), K("YF")])
                    yield S.op("dve", lambda e, i=i, d=d, bcol=bcol: e.scalar_tensor_tensor(out=Yb[i][0][:], in0=YF[i][:], scalar=bcol, in1=cst[:, 13 + d * 4, :],
                                                                                     op0=ALU.mult, op1=ALU.mult), reads=[K("YF"), "BETA"], writes=[K("Y0")])
                    for lv in range(3):
                        yield S.op("dve", lambda e, i=i, d=d, lv=lv, bcol=bcol: e.scalar_tensor_tensor(
                            out=Yo[i][lv][:], in0=YF[i][:], scalar=bcol, in1=cst[:, 14 + d * 4 + lv, :], op0=ALU.mult, op1=ALU.mult),
                            reads=[K("YF"), "BETA"], writes=[K("Yo%d" % lv)])
                    yield S.op("dve", lambda e, B=B, i=i: e.tensor_tensor(out=QA[i][:], in0=B, in1=Ex[i][:], op=ALU.mult), reads=[K("B"), K("Ex")],
                               writes=[K("bank0"), K("QA")])
                    yield S.op("pool", lambda e, i=i, d=d: e.tensor_tensor(out=QKm[i][:], in0=QA[i][:], in1=cst[:, 3 + d, :], op=ALU.mult),
                               reads=[K("QA")], writes=[K("QKm")])
                    yield S.op("pool", lambda e, i=i: e.tensor_tensor(out=qdec[i][:], in0=qt[i][:], in1=EB[i][:], op=ALU.mult), reads=[K("qt"), K("EB")],
                               writes=[K("qdec")])
                    yield S.op("act", lambda e, i=i, bcol=bcol: e.activation(out=bv[i][:], in_=vm[i][:], func=AF.Copy, scale=bcol),
                               reads=[K("vm"), "BETA"], writes=[K("bv")])
                    yield S.op("act", lambda e, i=i, becol=becol: e.activation(out=kbe[i][:], in_=km[i][:], func=AF.Copy, scale=becol),
                               reads=[K("km"), "BE"], writes=[K("kbe")])
                    yield S.op("act", lambda e, i=i, fcol=fcol: e.activation(out=kdec[i][:], in_=km[i][:], func=AF.Copy, scale=fcol),
                               reads=[K("km"), "Fd"], writes=[K("kdec")])
                    a, b = 0, 1
                    for m in range(4):
                        ka, kb_ = str(a), str(b)
                        if m > 0:
                            yield S.op("pe", lambda e, OP=OP, i=i, a=a: e.matmul(OP, Yb[i][a][:], Pm[i][a][:], start=True, stop=True),
                                       reads=[K("Y" + ka), K("P" + ka)], writes=[K("bank0"), K("OP")])
                        if m < 3:
                            yield S.op("pe", lambda e, OX=OX, i=i, a=a: e.matmul(OX, Yb[i][a][:], Xm[i][a][:], start=True, stop=True),
                                       reads=[K("Y" + ka), K("X" + ka)], writes=[K("bank0"), K("OX")])
                            yield S.op("pe", lambda e, Yp=Yp, i=i, a=a: e.matmul(Yp, Xm[i][a][:], Yb[i][a][:], start=True, stop=True),
                                       reads=[K("Y" + ka), K("X" + ka)], writes=[K("bank0"), K("Yp")])
                        if m > 0:
                            yield S.op("dve", lambda e, OP=OP, i=i, a=a, b=b: e.tensor_tensor(out=Pm[i][b][:], in0=OP, in1=Pm[i][a][:], op=ALU.add),
                                       reads=[K("OP"), K("P" + ka)], writes=[K("bank0"), K("P" + kb_)])
                        if m < 3:
                            yield S.op("act", lambda e, OX=OX, i=i, b=b: e.activation(out=Xm[i][b][:], in_=OX, func=AF.Copy),
                                       reads=[K("OX")], writes=[K("bank0"), K("X" + kb_)])
                            yield S.op("act", lambda e, Yp=Yp, i=i, b=b: e.activation(out=Yb[i][b][:], in_=Yp, func=AF.Copy),
                                       reads=[K("Yp")], writes=[K("bank0"), K("Y" + kb_)])
                        a, b = b, a
                    yield S.op("pe", lambda e, M1=M1, i=i, a=a: e.transpose(out=M1, in_=Pm[i][a][:], identity=cst[:, 0, :]),
                         reads=[K("P" + str(a))], writes=[K("bank0"), K("M1")])
                    yield S.op("act", lambda e, M1=M1, i=i: e.activation(out=Qs[i][:], in_=M1, func=AF.Copy), reads=[K("M1")], writes=[K("bank0"), K("Qs")])
                    for lv in range(3):
                        ka, kb_ = str(a), str(b)
                        yield S.op("pe", lambda e, M1=M1, i=i, a=a, lv=lv: e.matmul(M1, Yo[i][lv][:], Pm[i][a][:], start=True, stop=True),
                             reads=[K("Yo%d" % lv), K("P" + ka)], writes=[K("bank0"), K("M1")])
                        yield S.op("act", lambda e, M1=M1, i=i: e.activation(out=M1s[i][:], in_=M1, func=AF.Copy), reads=[K("M1")], writes=[K("bank0"), K("M1s")])
                        yield S.op("pe", lambda e, OP=OP, i=i: e.matmul(OP, Qs[i][:], M1s[i][:], start=True, stop=True),
                             reads=[K("Qs"), K("M1s")], writes=[K("bank0"), K("OP")])
                        if lv < 2:
                            yield S.op("dve", lambda e, OP=OP, i=i, a=a, b=b: e.tensor_tensor(out=Pm[i][b][:], in0=OP, in1=Pm[i][a][:], op=ALU.add),
                                 reads=[K("OP"), K("P" + ka)], writes=[K("bank0"), K("P" + kb_)])
                            yield S.op("pe", lambda e, M1=M1, i=i, b=b: e.transpose(out=M1, in_=Pm[i][b][:], identity=cst[:, 0, :]),
                                 reads=[K("P" + kb_)], writes=[K("bank0"), K("M1")])
                            yield S.op("act", lambda e, M1=M1, i=i: e.activation(out=Qs[i][:], in_=M1, func=AF.Copy), reads=[K("M1")], writes=[K("bank0"), K("Qs")])
                        else:
                            yield S.op("dve", lambda e, OP=OP, i=i, a=a: e.tensor_tensor(out=TT[i][:], in0=OP, in1=Pm[i][a][:], op=ALU.add),
                                 reads=[K("OP"), K("P" + ka)], writes=[K("bank0"), K("TT")])
                        a, b = b, a
                    yield S.op("pe", lambda e, U=U, i=i: e.matmul(U, TT[i][:], bv[i][:], start=True, stop=True), reads=[K("TT"), K("bv")], writes=[K("bank0"), K("U")])
                    yield S.op("pe", lambda e, W=W, i=i: e.matmul(W, kbe[i][:], TT[i][:], start=True, stop=True), reads=[K("TT"), K("kbe")], writes=[K("bank0"), K("W")])
                    yield S.op("dve", lambda e, U=U, i=i: e.tensor_copy(out=usb[i][:], in_=U), reads=[K("U")], writes=[K("bank0"), K("usb")])
                    yield S.op("act", lambda e, W=W, i=i: e.activation(out=wsb[i][:], in_=W, func=AF.Copy), reads=[K("W")], writes=[K("bank0"), K("wsb")])
                    sk, sbk = "Sf%d%d" % (h, d), "Sb%d%d" % (h, d)
                    SF, SB = Sf[h][d], Sb[h][d]
                    yield S.op("pe", lambda e, Sp=Sp, i=i, SB=SB: e.matmul(Sp, wsb[i][:], SB[:], start=True, stop=True), reads=[K("wsb"), sbk], writes=[K("bank0"), K("Sp")])
                    yield S.op("dve", lambda e, Sp=Sp, i=i: e.tensor_tensor(out=vnew[i][:], in0=usb[i][:], in1=Sp, op=ALU.subtract),
                         reads=[K("usb"), K("Sp")], writes=[K("bank0"), K("vnew")])
                    yield S.op("pe", lambda e, O2=O2, i=i, SB=SB: e.matmul(O2, SB[:], qdec[i][:], start=True, stop=False), reads=[sbk, K("qdec")], writes=[K("bank0"), K("O2")])
                    yield S.op("pe", lambda e, O2=O2, i=i: e.matmul(O2, vnew[i][:], QKm[i][:], start=False, stop=True), reads=[K("vnew"), K("QKm")], writes=[K("bank0"), K("O2")])
                    yield S.op("act", lambda e, O2=O2, i=i: e.activation(out=osb[i][:], in_=O2, func=AF.Copy), reads=[K("O2")], writes=[K("bank0"), K("osb")])
                    yield S.dma("act", lambda e, i=i, d=d, h=h, tsl=tsl: e.dma_start(out=self.oT[d, h, :, tsl], in_=osb[i][:]), reads=[K("osb")])
                    yield S.op("pe", lambda e, Dp=Dp, i=i: e.matmul(Dp, kdec[i][:], vnew[i][:], start=True, stop=True), reads=[K("kdec"), K("vnew")], writes=[K("bank0"), K("Dp")])
                    yield S.op("dve", lambda e, Dp=Dp, SF=SF, glcol=glcol: e.scalar_tensor_tensor(out=SF[:], in0=SF[:], scalar=glcol, in1=Dp,
                                                                                           op0=ALU.mult, op1=ALU.add), reads=[sk, K("Dp"), "GL"], writes=[K("bank0"), sk])
                    yield S.op("act", lambda e, SF=SF, SB=SB: e.activation(out=SB[:], in_=SF[:], func=AF.Copy), reads=[sk], writes=[sbk])

                order = [(s_, h_, d_) for s_ in range(getattr(self, 'scan_steps', NT))
                         for h_ in range(getattr(self, 'scan_heads', 8)) for d_ in range(2)]
                STAG = getattr(self, "scan_stag", 9)
                slots = [order[k::NSET] for k in range(NSET)]
                gens = [None] * NSET
                nxt = [0] * NSET
                rnd = 0
                while True:
                    alive = False
                    for k in range(NSET):
                        if gens[k] is None and nxt[k] < len(slots[k]) and rnd >= k * STAG:
                            s_, h_, d_ = slots[k][nxt[k]]
                            nxt[k] += 1
                            gens[k] = body(k, h_, d_, ORD[d_][s_])
                        if gens[k] is not None:
                            alive = True
                            try:
                                next(gens[k])
                            except StopIteration:
                                gens[k] = None
                        elif nxt[k] < len(slots[k]):
                            alive = True
                    rnd += 1
                    if not alive:
                        break
                S.limit = None
                S.emit(block)
        if getattr(self, "gdn_stop", 9) < 3:
            return
        with ExitStack() as es, nc.Block() as block:
            o0 = [self.sb(es, "o0", [128, T], F32) for _ in range(2)]
            o1 = [self.sb(es, "o1", [128, T], F32) for _ in range(2)]
            gt = [self.sb(es, "gt", [128, T], F32) for _ in range(2)]
            sq = [self.sb(es, "nsq", [128, 512], F32) for _ in range(2)]
            ri = [self.sb(es, "nri", [128, 512], F32) for _ in range(2)]
            yb = [self.sb(es, "nyb", [128, T], BF16) for _ in range(2)]
            gn = self.sb(es, "gn", [128, 1], F32)
            pss = [self.ps(es, "npss", [128, 512], F32) for _ in range(2)]
            S.dma("sp", lambda e: e.dma_start(out=gn[:], in_=self.gnorm[l][:, :]), writes=["gn"])
            for h in range(8):
                i = h % 2
                S.dma("sp", lambda e, i=i, h=h: e.dma_start(out=o0[i][:], in_=self.oT[0, h, :, :]), writes=["o0%d" % i])
                S.dma("sp", lambda e, i=i, h=h: e.dma_start(out=o1[i][:], in_=self.oT[1, h, :, :]), writes=["o1%d" % i])
                S.dma("sp", lambda e, i=i, h=h: e.dma_start(out=gt[i][:], in_=self.zT[4608 + h * 128:4736 + h * 128, :]), writes=["gt%d" % i])
                S.op("pool", lambda e, i=i: e.tensor_tensor(out=o0[i][:], in0=o0[i][:], in1=o1[i][:], op=ALU.add), reads=["o0%d" % i, "o1%d" % i], writes=["o0%d" % i])
                S.op("act", lambda e, i=i: e.activation(out=gt[i][:], in_=gt[i][:], func=AF.Silu), reads=["gt%d" % i], writes=["gt%d" % i])
                for bi, (t0, tn) in enumerate(TBLK):
                    bb = bi % 2
                    S.op("act", lambda e, i=i, t0=t0, tn=tn, bb=bb: e.activation(out=sq[bb][:, 0:tn], in_=o0[i][:, t0:t0 + tn], func=AF.Square),
                         reads=["o0%d" % i], writes=["nsq%d" % bb])
                    S.op("pe", lambda e, tn=tn, bb=bb: e.matmul(pss[bb][:, 0:tn], cst[:, 1, :], sq[bb][:, 0:tn], start=True, stop=True),
                         reads=["nsq%d" % bb], writes=["npss%d" % bb])
                    S.op("act", lambda e, tn=tn, bb=bb: e.activation(out=ri[bb][:, 0:tn], in_=pss[bb][:, 0:tn], func=AF.Ln, scale=1.0 / 128,
                                                                   bias=self.epsc[:, 0:1]), reads=["npss%d" % bb], writes=["nri%d" % bb])
                    S.op("act", lambda e, tn=tn, bb=bb: e.activation(out=ri[bb][:, 0:tn], in_=ri[bb][:, 0:tn], func=AF.Exp, scale=-0.5), reads=["nri%d" % bb], writes=["nri%d" % bb])
                    S.op("dve", lambda e, i=i, t0=t0, tn=tn, bb=bb: e.scalar_tensor_tensor(out=ri[bb][:, 0:tn], in0=o0[i][:, t0:t0 + tn], scalar=gn[:, 0:1],
                                                                                         in1=ri[bb][:, 0:tn], op0=ALU.mult, op1=ALU.mult),
                         reads=["o0%d" % i, "nri%d" % bb, "gn"], writes=["nri%d" % bb])
                    S.op("pool", lambda e, i=i, t0=t0, tn=tn, bb=bb: e.tensor_tensor(out=yb[i][:, t0:t0 + tn], in0=ri[bb][:, 0:tn], in1=gt[i][:, t0:t0 + tn],
                                                                                   op=ALU.mult), reads=["nri%d" % bb, "gt%d" % i], writes=["nyb%d" % i])
                S.dma("act", lambda e, i=i, h=h: e.dma_start(out=self.mixT[1024 + h * 128:1152 + h * 128, :], in_=yb[i][:]), reads=["nyb%d" % i])
            S.emit(block)

    def phase_mixer(self, l):
        if not getattr(self, "skip_attn", False):
            self.phase_attn(l)
        if not getattr(self, "skip_gdn", False):
            self.phase_gdn(l)
    def build_gdnprobe(self):
        nc, es, P = self.nc, self.es, self
        l = 0
        self.consts = P.din("consts", [128, 21, 128])
        self.zT = P.din("zT", [D_IN, T])
        self.alog = {l: P.din("alog0", [128, 16])}
        self.dtb = {l: P.din("dtb0", [128, 16])}
        self.convw = {l: P.din("convw0", [128, 24, 5])}
        self.gnorm = {l: P.din("gnorm0", [128, 1])}
        self.gq = P.dout("gq", [8, 128, T], BF16)
        self.gk = P.dout("gk", [8, 128, T], BF16)
        self.gkm = P.dout("gkm", [8, T, 128], BF16)
        self.gvm = P.dout("gvm", [8, T, 128], BF16)
        self.oT = P.dout("oT", [2, 8, 128, T])
        self.mixT = P.dout("mixT", [D, T], BF16)
        self.S = Sch(nc, es)
        self.cst = P.sb(es, "cst", [128, 21, 128], F32)
        self.cstb = P.sb(es, "cstb", [128, 21, 128], BF16)
        self.junk = P.sb(es, "junk", [128, 8], F32)
        self.epsc = P.sb(es, "epsc", [128, 1], F32)
        S = self.S
        with nc.Block() as block:
            S.dma("sp", lambda e: e.dma_start(out=self.cst[:], in_=self.consts[:, :, :]), writes=["cst"])
            S.op("dve", lambda e: e.tensor_copy(out=self.cstb[:], in_=self.cst[:]), reads=["cst"], writes=["cstb"])
            S.op("pool", lambda e: e.memset(self.epsc[:], EPS), writes=["epsc"])
            S.emit(block)
        self.phase_gdn(0)
        es.close()
        return nc

    def build(self):
        nc = self.nc
        es = self.es
        P = self
        self.xT_in = P.din("xT", [D, T])
        self.c2 = P.din("c2", [128, KC, 2])
        self.consts = P.din("consts", [128, 21, 128])
        self.w_ada = {l: P.din("w_ada%d" % l, [D, 6 * D]) for l in self.layers}
        self.b_ada = {l: P.din("b_ada%d" % l, [128, 96]) for l in self.layers}
        self.nrm = {l: P.din("nrm%d" % l, [128, 2, KC]) for l in self.layers}
        self.w_in = {l: P.din("w_in%d" % l, [D, D_IN]) for l in self.layers}
        if self.upto != "inproj":
            self.w_out = {l: P.din("w_out%d" % l, [D, D]) for l in self.layers}
            self.w_ff1 = {l: P.din("w_ff1%d" % l, [D, D_FF]) for l in self.layers}
            self.w_ff2 = {l: P.din("w_ff2%d" % l, [D_FF, D]) for l in self.layers}
            self.nfin = P.din("nfin", [128, KC])
            self.outT = P.dout("outT", [D, L])
        if self.upto != "inproj":
            self.ropeT = P.din("ropeT", [128, 2, L])
            self.sink = {l: P.din("sink%d" % l, [128, 8]) for l in self.layers}
            self.alog = {l: P.din("alog%d" % l, [128, 16]) for l in self.layers}
            self.dtb = {l: P.din("dtb%d" % l, [128, 16]) for l in self.layers}
            self.convw = {l: P.din("convw%d" % l, [128, 24, 5]) for l in self.layers}
            self.gnorm = {l: P.din("gnorm%d" % l, [128, 1]) for l in self.layers}
        self.gq = P.dscr("gq", [8, 128, T], BF16)
        self.gk = P.dscr("gk", [8, 128, T], BF16)
        self.gkm = P.dscr("gkm", [8, T, 128], BF16)
        self.gvm = P.dscr("gvm", [8, T, 128], BF16)
        self.oT = P.dout("oT", [2, 8, 128, T]) if "oT" in [d[0] for d in self.dbg] else P.dscr("oT", [2, 8, 128, T])
        dn = [d[0] for d in self.dbg]
        self.xT = P.dout("xTs", [D, T]) if "xT" in dn else P.dscr("xTs", [D, T])
        self.zT = P.dscr("zT", [D_IN, T]) if "zT" not in [d[0] for d in self.dbg] else P.dout("zT", [D_IN, T])
        self.mixT = P.dout("mixT", [D, T], BF16) if "mixT" in dn else P.dscr("mixT", [D, T], BF16)
        self.S = Sch(nc, es)
        self.cst = P.sb(es, "cst", [128, 21, 128], F32)
        self.cstb = P.sb(es, "cstb", [128, 21, 128], BF16)
        self.junk = P.sb(es, "junk", [128, 8], F32)
        self.epsc = P.sb(es, "epsc", [128, 1], F32)
        self.mod = {l: P.sb(es, "mod%d" % l, [128, 96, 2], F32) for l in self.layers}
        self.gm1 = {l: P.sb(es, "gm1_%d" % l, [128, 2, KC, 2], F32) for l in self.layers}
        self.phase_init()
        for l in self.layers:
            self.phase_mod(l)
        for l in self.layers:
            self.phase_norm_inproj(l)
            if self.upto == "inproj":
                break
            self.phase_mixer(l)
            if not getattr(self, "skip_tail", False):
                self.phase_dense_tail(l)
        if self.upto == "all":
            self.phase_final()
        es.close()
        return nc


def make_consts():
    c = np.zeros((128, 21, 128), np.float32)
    p = np.arange(128)[:, None]
    f = np.arange(128)[None, :]
    c[:, 0, :] = (p == f)
    c[:, 1, :] = 1.0
    partner = np.where((np.arange(128) % 64) < 32, np.arange(128) + 32, np.arange(128) - 32)
    c[:, 2, :] = (p == partner[None, :])
    c[:, 3, :] = (p <= f)
    c[:, 4, :] = (p >= f)
    c[:, 5, :] = -(p < f).astype(np.float32)
    c[:, 6, :] = -(p > f).astype(np.float32)
    c[:, 7, :] = (p // 16 == f // 16)
    for lv, sz in enumerate((16, 32, 64)):
        c[:, 8 + lv, :] = (p // (2 * sz) == f // (2 * sz)) & (p // sz != f // sz)
    for d in range(2):
        c[:, 11 + d, :] = c[:, 5 + d, :] * c[:, 7, :]
        for k in range(4):
            c[:, 13 + d * 4 + k, :] = c[:, 6 - d, :] * c[:, 7 + k, :]
    return c


def rope_tables():
    pos = np.arange(L)
    row = (pos // 64).astype(np.float32)
    col = (pos % 64).astype(np.float32)
    inv = (10000.0 ** (-np.arange(0, 64, 2, dtype=np.float32) / 64.0)).astype(np.float32)
    ang = np.concatenate([row[:, None] * inv[None, :], col[:, None] * inv[None, :]], axis=-1).astype(np.float32)
    cos, sin = np.cos(ang).astype(np.float32), np.sin(ang).astype(np.float32)
    out = np.zeros((128, 2, L), np.float32)
    for m_ in range(128):
        idx = (m_ % 32) if m_ < 64 else 32 + ((m_ - 64) % 32)
        sign = -1.0 if (m_ % 64) < 32 else 1.0
        out[m_, 0, :] = cos[:, idx]
        out[m_, 1, :] = sign * sin[:, idx]
    return out


def fm_vec(v, nchunk):
    return np.ascontiguousarray(np.asarray(v, np.float32).reshape(nchunk, 128).T)


def core_inputs(inp, b, layers, upto="all"):
    m = {}
    xt = np.concatenate([inp["ctx"][b], inp["x"][b]], axis=0)
    m["xT"] = np.ascontiguousarray(xt.T)
    c2 = np.stack([fm_vec(inp["c"][b], KC), fm_vec(inp["c_ctx"], KC)], axis=-1)
    m["c2"] = np.ascontiguousarray(c2)
    m["consts"] = make_consts()
    for l in layers:
        m["w_ada%d" % l] = inp["w_ada"][l]
        m["b_ada%d" % l] = fm_vec(inp["b_ada"][l], 96)
        m["nrm%d" % l] = np.ascontiguousarray(np.stack([fm_vec(inp["norm_mix"][l], KC), fm_vec(inp["norm_ffn"][l], KC)], axis=1))
        m["w_in%d" % l] = inp["w_in"][l]
        if upto != "inproj":
            m["w_out%d" % l] = inp["w_out"][l]
            m["w_ff1%d" % l] = inp["w_ff1"][l]
            m["w_ff2%d" % l] = inp["w_ff2"][l]
    if upto != "inproj":
        m["nfin"] = fm_vec(inp["norm_final"], KC)
        m["ropeT"] = rope_tables()
        for l in layers:
            m["sink%d" % l] = np.ascontiguousarray(np.broadcast_to(inp["attn_sink"][l][None, :], (128, 8))).astype(np.float32)
            m["alog%d" % l] = np.ascontiguousarray(np.broadcast_to(inp["a_log"][l].reshape(1, 16), (128, 16))).astype(np.float32)
            m["dtb%d" % l] = np.ascontiguousarray(np.broadcast_to(inp["dt_bias"][l].reshape(1, 16), (128, 16))).astype(np.float32)
            cw = inp["conv_w"][l]
            m["convw%d" % l] = np.ascontiguousarray(cw.T.reshape(24, 128, 5).transpose(1, 0, 2)).astype(np.float32)
            m["gnorm%d" % l] = np.ascontiguousarray(inp["gdn_norm"][l].reshape(128, 1)).astype(np.float32)
    return m


_PROG = {}


def kernel(**inputs):
    inp = {k: np.asarray(v) for k, v in inputs.items()}
    layers = list(range(DEPTH))
    if "nc" not in _PROG:
        _PROG["nc"] = Prog(layers).build()
    nc = _PROG["nc"]
    n = 4
    maps = [core_inputs(inp, b, layers) for b in range(n)]
    res = run_bass_kernel_spmd(nc, maps, core_ids=list(range(n)))
    out = np.stack([res.results[b]["outT"].T for b in range(n)], axis=0)
    return np.ascontiguousarray(out.astype(np.float32))
```
